# Optimizing a Trainium2 kernel written in Bass

```python
import jax
import jax.numpy as jnp
from jax import lax
import numpy as np

D_MODEL = 1024
BATCH = 2
SEQ = 16384
DEPTH = 2

D_MIX = D_MODEL
POOL_WIDTH = D_MIX // 4
POOL_WINDOWS = (2, 4, 8, 16)
POOL_GROUP_DIM = POOL_WIDTH // len(POOL_WINDOWS)
NSA_WIDTH = D_MIX - POOL_WIDTH
N_Q_HEADS = 8
HEAD_DIM = NSA_WIDTH // N_Q_HEADS
N_KV_GROUPS = 2
HEADS_PER_GROUP = N_Q_HEADS // N_KV_GROUPS
KV_WIDTH = N_KV_GROUPS * HEAD_DIM
N_BRANCHES = 3
CMP_STRIDE = 16
CMP_BLOCK = 2 * CMP_STRIDE
SLC_BLOCK = 64
N_SELECT = 16
WINDOW = 512
Q_BLOCK = 128
D_FF = 2816
D_IN = POOL_WIDTH + NSA_WIDTH + 6 * KV_WIDTH + N_BRANCHES * N_Q_HEADS
ALPHA = (2.0 * DEPTH) ** 0.25
BETA = (8.0 * DEPTH) ** -0.25
LN_EPS = 1e-5
NEG_BIG = -1e30
SEL_BIG = 1e30

kernel_name = 'hymba_pool_nsa_macaron_deepnorm'


def layer_norm(x, g, b):
    xf = x.astype(jnp.float32)
    mu = jnp.mean(xf, axis=-1, keepdims=True)
    var = jnp.mean(jnp.square(xf - mu), axis=-1, keepdims=True)
    y = (xf - mu) * lax.rsqrt(var + LN_EPS) * g.astype(jnp.float32) + b.astype(jnp.float32)
    return y.astype(x.dtype)


def swiglu(x, w_gate, w_up, w_down):
    return (jax.nn.silu(x @ w_gate) * (x @ w_up)) @ w_down


def masked_softmax(s, mask):
    s = jnp.where(mask, s, NEG_BIG)
    m = jnp.max(s, axis=-1, keepdims=True)
    e = jnp.exp(s - m) * mask
    return e / jnp.maximum(jnp.sum(e, axis=-1, keepdims=True), 1e-30)


def pool_mixer(u, pool_w, pool_scale):
    B, S, _ = u.shape
    uf = u.astype(jnp.float32)
    csum = jnp.pad(jnp.cumsum(uf, axis=1), ((0, 0), (1, 0), (0, 0)))
    t = jnp.arange(S)
    groups = []
    for g, w in enumerate(POOL_WINDOWS):
        sl = slice(g * POOL_GROUP_DIM, (g + 1) * POOL_GROUP_DIM)
        cg = csum[..., sl]
        lo = jnp.maximum(t + 1 - w, 0)
        win_sum = cg[:, 1:] - cg[:, lo]
        cnt = jnp.minimum(t + 1, w).astype(jnp.float32)[None, :, None]
        groups.append(win_sum / cnt - uf[..., sl])
    pooled = jnp.stack(groups, axis=2).astype(u.dtype)
    mixed = jnp.einsum('bsgc,gcd->bsgd', pooled, pool_w)
    return mixed.reshape(B, S, POOL_WIDTH) * pool_scale


def compress(kv, pos, w1, w2):
    B, G, S, dh = kv.shape
    chunks = kv.reshape(B, G, S // CMP_STRIDE, CMP_STRIDE, dh)
    blocks = jnp.concatenate([chunks[:, :, :-1], chunks[:, :, 1:]], axis=3)
    flat = (blocks + pos).reshape(B, G, S // CMP_STRIDE - 1, CMP_BLOCK * dh)
    return jax.nn.gelu(flat @ w1) @ w2


def nsa_mixer(q, kc, vc, k_slc, v_slc, k_win, v_win, gates):
    B, S = q.shape[0], q.shape[1]
    G = N_KV_GROUPS
    NQB = S // Q_BLOCK
    NS = S // SLC_BLOCK
    NC = S // CMP_STRIDE - 1
    n_sel = min(N_SELECT, NS)
    scale = HEAD_DIM ** -0.5
    qb = (q * scale).reshape(B, NQB, Q_BLOCK, G, HEADS_PER_GROUP, HEAD_DIM).transpose(1, 0, 3, 4, 2, 5)
    gb = gates.reshape(B, NQB, Q_BLOCK, G, HEADS_PER_GROUP, N_BRANCHES).transpose(1, 0, 3, 4, 2, 5)
    ks_blocks = k_slc.reshape(B, G, NS, SLC_BLOCK, HEAD_DIM)
    vs_blocks = v_slc.reshape(B, G, NS, SLC_BLOCK, HEAD_DIM)
    kw_pad = jnp.pad(k_win, ((0, 0), (0, 0), (WINDOW, 0), (0, 0)))
    vw_pad = jnp.pad(v_win, ((0, 0), (0, 0), (WINDOW, 0), (0, 0)))
    cmp_end = jnp.arange(NC) * CMP_STRIDE + CMP_BLOCK - 1
    blk = jnp.arange(NS)
    bi = jnp.arange(B)[:, None, None, None]
    gi = jnp.arange(G)[None, :, None, None]
    slc_w = jnp.array([0.5, 1.0, 1.0, 1.0], jnp.float32)

    def one_block(args):
        c, q_c, g_c = args
        t = c * Q_BLOCK + jnp.arange(Q_BLOCK)
        s = jnp.einsum('bghtd,bgnd->bghtn', q_c, kc).astype(jnp.float32)
        p_cmp = masked_softmax(s, cmp_end[None, :] <= t[:, None])
        o_cmp = jnp.einsum('bghtn,bgnd->bghtd', p_cmp.astype(vc.dtype), vc)
        imp = jnp.pad(p_cmp.sum(axis=2), ((0, 0), (0, 0), (0, 0), (1, 1)))
        imp_slc = imp[..., :4 * NS].reshape(B, G, Q_BLOCK, NS, 4) @ slc_w + 0.5 * imp[..., 4::4]
        jt = (t // SLC_BLOCK)[:, None]
        forced = (blk == 0) | (blk == jt) | (blk == jt - 1)
        score = jnp.where(blk <= jt, jnp.where(forced, SEL_BIG, imp_slc), NEG_BIG)
        top_s, idx = lax.top_k(score, n_sel)
        sel_ok = top_s > 0.5 * NEG_BIG
        ks = ks_blocks[bi, gi, idx].reshape(B, G, Q_BLOCK, n_sel * SLC_BLOCK, HEAD_DIM)
        vs = vs_blocks[bi, gi, idx].reshape(B, G, Q_BLOCK, n_sel * SLC_BLOCK, HEAD_DIM)
        pos = (idx[..., None] * SLC_BLOCK + jnp.arange(SLC_BLOCK)).reshape(B, G, Q_BLOCK, n_sel * SLC_BLOCK)
        m_slc = (pos <= t[:, None]) & jnp.repeat(sel_ok, SLC_BLOCK, axis=-1)
        s = jnp.einsum('bghtd,bgtkd->bghtk', q_c, ks).astype(jnp.float32)
        p = masked_softmax(s, m_slc[:, :, None])
        o_slc = jnp.einsum('bghtk,bgtkd->bghtd', p.astype(vs.dtype), vs)
        start = c * Q_BLOCK
        kw = lax.dynamic_slice_in_dim(kw_pad, start, WINDOW + Q_BLOCK, axis=2)
        vw = lax.dynamic_slice_in_dim(vw_pad, start, WINDOW + Q_BLOCK, axis=2)
        s_pos = start - WINDOW + jnp.arange(WINDOW + Q_BLOCK)
        diff = t[:, None] - s_pos[None, :]
        m_win = (s_pos[None, :] >= 0) & (diff >= 0) & (diff < WINDOW)
        s = jnp.einsum('bghtd,bgkd->bghtk', q_c, kw).astype(jnp.float32)
        p = masked_softmax(s, m_win)
        o_win = jnp.einsum('bghtk,bgkd->bghtd', p.astype(vw.dtype), vw)
        o = g_c[..., 0:1] * o_cmp + g_c[..., 1:2] * o_slc + g_c[..., 2:3] * o_win
        return o.transpose(0, 3, 1, 2, 4).reshape(B, Q_BLOCK, NSA_WIDTH)

    out = lax.map(one_block, (jnp.arange(NQB), qb, gb))
    return out.transpose(1, 0, 2, 3).reshape(B, S, NSA_WIDTH)


def hybrid_mixer(x, w_in, b_gate, pool_w, pool_scale, cmp_pos_k, cmp_k_w1, cmp_k_w2,
                 cmp_pos_v, cmp_v_w1, cmp_v_w2, w_out):
    B, S, _ = x.shape
    proj = x @ w_in
    cuts = tuple(int(v) for v in np.cumsum([POOL_WIDTH, NSA_WIDTH] + [KV_WIDTH] * 6))
    u, q, kc, vc, ks, vs, kw, vw, g = jnp.split(proj, cuts, axis=-1)
    y_pool = pool_mixer(u, pool_w, pool_scale)

    def heads(a):
        return a.reshape(B, S, N_KV_GROUPS, HEAD_DIM).transpose(0, 2, 1, 3)

    k_cmp = compress(heads(kc), cmp_pos_k, cmp_k_w1, cmp_k_w2)
    v_cmp = compress(heads(vc), cmp_pos_v, cmp_v_w1, cmp_v_w2)
    gates = jax.nn.sigmoid((g + b_gate).astype(jnp.float32)).astype(x.dtype)
    gates = gates.reshape(B, S, N_Q_HEADS, N_BRANCHES)
    y_nsa = nsa_mixer(q.reshape(B, S, N_Q_HEADS, HEAD_DIM), k_cmp, v_cmp,
                      heads(ks), heads(vs), heads(kw), heads(vw), gates)
    return jnp.concatenate([y_pool, y_nsa], axis=-1) @ w_out


def setup_inputs(seed: int = 0) -> dict:
    key = jax.random.key(seed)
    k = jax.random.split(key, 24)
    L = DEPTH

    def nrm(kk, shape, scale):
        return jax.random.normal(kk, shape, jnp.float32) * scale

    return {
        'x': nrm(k[0], (BATCH, SEQ, D_MODEL), 1.0),
        'ln1_g': 1.0 + nrm(k[1], (L, D_MODEL), 0.02),
        'ln1_b': nrm(k[2], (L, D_MODEL), 0.02),
        'ffn1_w_gate': nrm(k[3], (L, D_MODEL, D_FF), D_MODEL ** -0.5),
        'ffn1_w_up': nrm(k[4], (L, D_MODEL, D_FF), D_MODEL ** -0.5),
        'ffn1_w_down': nrm(k[5], (L, D_FF, D_MODEL), BETA * D_FF ** -0.5),
        'w_in': nrm(k[6], (L, D_MODEL, D_IN), D_MODEL ** -0.5),
        'b_gate': nrm(k[7], (L, N_BRANCHES * N_Q_HEADS), 0.5),
        'pool_w': nrm(k[8], (L, len(POOL_WINDOWS), POOL_GROUP_DIM, POOL_GROUP_DIM), POOL_GROUP_DIM ** -0.5),
        'pool_scale': 1.0 + nrm(k[9], (L, POOL_WIDTH), 0.02),
        'cmp_pos_k': nrm(k[10], (L, CMP_BLOCK, HEAD_DIM), 0.02),
        'cmp_k_w1': nrm(k[11], (L, CMP_BLOCK * HEAD_DIM, HEAD_DIM), (CMP_BLOCK * HEAD_DIM) ** -0.5),
        'cmp_k_w2': nrm(k[12], (L, HEAD_DIM, HEAD_DIM), HEAD_DIM ** -0.5),
        'cmp_pos_v': nrm(k[13], (L, CMP_BLOCK, HEAD_DIM), 0.02),
        'cmp_v_w1': nrm(k[14], (L, CMP_BLOCK * HEAD_DIM, HEAD_DIM), (CMP_BLOCK * HEAD_DIM) ** -0.5),
        'cmp_v_w2': nrm(k[15], (L, HEAD_DIM, HEAD_DIM), HEAD_DIM ** -0.5),
        'w_out': nrm(k[16], (L, D_MIX, D_MODEL), BETA * D_MIX ** -0.5),
        'ln2_g': 1.0 + nrm(k[17], (L, D_MODEL), 0.02),
        'ln2_b': nrm(k[18], (L, D_MODEL), 0.02),
        'ffn2_w_gate': nrm(k[19], (L, D_MODEL, D_FF), D_MODEL ** -0.5),
        'ffn2_w_up': nrm(k[20], (L, D_MODEL, D_FF), D_MODEL ** -0.5),
        'ffn2_w_down': nrm(k[21], (L, D_FF, D_MODEL), BETA * D_FF ** -0.5),
        'ln3_g': 1.0 + nrm(k[22], (L, D_MODEL), 0.02),
        'ln3_b': nrm(k[23], (L, D_MODEL), 0.02),
    }


def reference(x, ln1_g, ln1_b, ffn1_w_gate, ffn1_w_up, ffn1_w_down, w_in, b_gate,
              pool_w, pool_scale, cmp_pos_k, cmp_k_w1, cmp_k_w2, cmp_pos_v, cmp_v_w1,
              cmp_v_w2, w_out, ln2_g, ln2_b, ffn2_w_gate, ffn2_w_up, ffn2_w_down,
              ln3_g, ln3_b):
    for l in range(DEPTH):
        x = layer_norm(ALPHA * x + 0.5 * swiglu(x, ffn1_w_gate[l], ffn1_w_up[l], ffn1_w_down[l]),
                       ln1_g[l], ln1_b[l])
        y = hybrid_mixer(x, w_in[l], b_gate[l], pool_w[l], pool_scale[l], cmp_pos_k[l],
                         cmp_k_w1[l], cmp_k_w2[l], cmp_pos_v[l], cmp_v_w1[l], cmp_v_w2[l], w_out[l])
        x = layer_norm(ALPHA * x + y, ln2_g[l], ln2_b[l])
        x = layer_norm(ALPHA * x + 0.5 * swiglu(x, ffn2_w_gate[l], ffn2_w_up[l], ffn2_w_down[l]),
                       ln3_g[l], ln3_b[l])
    return x
```

```python
import numpy as np
import concourse.bass as bass
import concourse.mybir as mybir
from contextlib import ExitStack

F32 = mybir.dt.float32
BF16 = mybir.dt.bfloat16
AF = mybir.ActivationFunctionType
ALU = mybir.AluOpType
AX = mybir.AxisListType

ENGS = ("sync", "tensor", "vector", "scalar", "gpsimd")
N_DMA_SEMS = 12


class Buf:
    __slots__ = ("name", "w", "r", "psum")

    def __init__(self, name, psum=False):
        self.name = name
        self.psum = psum
        self.w = None
        self.r = {}


class Prog:
    def __init__(self, nc):
        self.nc = nc
        self.es = ExitStack()
        self.streams = {e: [] for e in ENGS}
        self.sem = {}
        self.cnt = {}
        for e in ENGS:
            self.sem[e] = nc.alloc_semaphore("c_" + e)
            self.cnt[e] = 0
        self.dsem = {}
        self.dcnt = {}
        self.drr = {}
        for e in ("sync", "gpsimd", "scalar"):
            for i in range(N_DMA_SEMS):
                k = "d_%s_%d" % (e, i)
                self.sem[k] = nc.alloc_semaphore(k)
                self.cnt[k] = 0
            self.drr[e] = 0
        self.sem["cc"] = nc.alloc_semaphore("cc_sem")
        self.cnt["cc"] = 0
        self.seen = {e: {} for e in ENGS}
        self.nbuf = 0
        self.block_hooks = []
        self.scopes = []
        self.scope_ctr = 0
        self.scope_id = 0

    def sbuf(self, name, shape, dtype):
        t = self.es.enter_context(self.nc.sbuf_tensor("sb%d_%s" % (self.scope_id, name), list(shape), dtype))
        return t

    def psum(self, name, shape, dtype=F32):
        t = self.es.enter_context(self.nc.psum_tensor("pp%d_%s" % (self.scope_id, name), list(shape), dtype))
        return t

    def push_scope(self):
        self.scopes.append(self.es)
        self.es = ExitStack()
        self.scope_ctr += 1
        self.scope_id = self.scope_ctr

    def pop_scope(self):
        self.barrier()
        self.es.close()
        self.es = self.scopes.pop()

    def buf(self, name=None, psum=False):
        self.nbuf += 1
        return Buf(name or ("b%d" % self.nbuf), psum)

    def pbuf(self):
        return self.buf(psum=True)

    def _need(self, eng, deps):
        out = []
        seen = self.seen[eng]
        for k, v in deps.items():
            if seen.get(k, 0) < v:
                seen[k] = v
                out.append((k, v))
        return out

    def op(self, eng, fn, reads=(), writes=(), dma=False, pe_accum=False, cc=False):
        deps = {}

        def add(tok):
            if tok is None:
                return
            k, v = tok
            if deps.get(k, 0) < v:
                deps[k] = v

        for b in reads:
            add(b.w)
            if b.psum:
                for k, v in b.r.items():
                    if k != eng:
                        add((k, v))
        for b in writes:
            if not (pe_accum and b.w is not None and b.w[0] == "tensor"):
                add(b.w)
            for k, v in b.r.items():
                add((k, v))
        if dma:
            rr = self.drr[eng]
            self.drr[eng] = (rr + 1) % N_DMA_SEMS
            dk = "d_%s_%d" % (eng, rr)
            if self.cnt[dk] > 0:
                add((dk, self.cnt[dk]))
        if eng == "tensor":
            deps.pop("tensor", None)
        waits = self._need(eng, deps)
        if cc:
            self.cnt["cc"] += 1
            tok = ("cc", self.cnt["cc"])
            inc = 1
        elif dma:
            self.cnt[dk] += 16
            tok = (dk, self.cnt[dk])
            inc = 16
        else:
            self.cnt[eng] += 1
            tok = (eng, self.cnt[eng])
            inc = 1
        semh = self.sem[tok[0]]
        wl = [(self.sem[k], v) for k, v in waits]

        def emit(e, fn=fn, wl=wl, semh=semh, inc=inc):
            for s, v in wl:
                e.wait_ge(s, v)
            fn(e).then_inc(semh, inc)

        self.streams[eng].append(emit)
        for b in writes:
            b.w = tok
            b.r = {}
        for b in reads:
            if b.r.get(tok[0], 0) < tok[1]:
                b.r[tok[0]] = tok[1]
        return tok

    def wait_all(self, eng, bufs):
        deps = {}
        for b in bufs:
            if b.w is not None:
                k, v = b.w
                if deps.get(k, 0) < v:
                    deps[k] = v
        wl = [(self.sem[k], v) for k, v in self._need(eng, deps)]

        def emit(e, wl=wl):
            for s, v in wl:
                e.wait_ge(s, v)

        self.streams[eng].append(emit)

    def barrier(self):
        allc = {k: v for k, v in self.cnt.items() if v > 0}
        for eng in ENGS:
            wl = [(self.sem[k], v) for k, v in self._need(eng, dict(allc))]

            def emit(e, wl=wl):
                for s, v in wl:
                    e.wait_ge(s, v)

            self.streams[eng].append(emit)

    def block_break(self):
        self.barrier()
        for e in ENGS:
            self.streams[e].append(None)

    def finish(self):
        nc = self.nc
        segs = {e: [[]] for e in ENGS}
        for e in ENGS:
            for f in self.streams[e]:
                if f is None:
                    segs[e].append([])
                else:
                    segs[e][-1].append(f)
        nseg = len(segs["sync"])
        for i in range(nseg):
            for hook in self.block_hooks:
                hook()
            with nc.Block() as block:
                @block.sync
                def _(e):
                    for f in segs["sync"][i]:
                        f(e)

                @block.tensor
                def _(e):
                    for f in segs["tensor"][i]:
                        f(e)

                @block.vector
                def _(e):
                    for f in segs["vector"][i]:
                        f(e)

                @block.scalar
                def _(e):
                    for f in segs["scalar"][i]:
                        f(e)

                @block.gpsimd
                def _(e):
                    for f in segs["gpsimd"][i]:
                        f(e)
        self.es.close()


class Dram:
    def __init__(self, nc, P):
        self.nc = nc
        self.P = P
        self.ap = {}
        self.b = {}
        self.outs = []
        self.dump = ()
        self.arena = {}
        self.ARENA_ELEMS = {"bf16": 120 * 1024 * 1024, "f32": 60 * 1024 * 1024}

    def din(self, name, shape, dt):
        if name not in self.ap:
            self.ap[name] = self.nc.dram_tensor(name, list(shape), dt, kind="ExternalInput").ap()
        return self.ap[name]

    def dout(self, name, shape, dt):
        self.ap[name] = self.nc.dram_tensor(name, list(shape), dt, kind="ExternalOutput").ap()
        self.b[name] = self.P.buf(name)
        self.outs.append(self.b[name])
        return self.ap[name]

    def dscr(self, name, shape, dt):
        if name not in self.ap:
            if name in self.dump:
                self.ap[name] = self.nc.dram_tensor(name, list(shape), dt, kind="ExternalOutput").ap()
            else:
                n = 1
                for d in shape:
                    n *= int(d)
                key = "bf16" if dt == BF16 else "f32"
                if key not in self.arena:
                    cap = self.ARENA_ELEMS[key]
                    self.arena[key] = [self.nc.dram_tensor("arena_" + key, [cap], dt).ap(), 0, cap]
                ar = self.arena[key]
                off = ar[1]
                n_al = (n + 2047) // 2048 * 2048
                assert off + n_al <= ar[2], ("arena overflow", key, name)
                ar[1] = off + n_al
                flat = ar[0][off:off + n]
                if len(shape) == 1:
                    self.ap[name] = flat
                else:
                    names = ["a%d" % i for i in range(len(shape))]
                    pat = "(" + " ".join(names) + ") -> " + " ".join(names)
                    kw = {nm: int(d) for nm, d in zip(names[1:], shape[1:])}
                    self.ap[name] = flat.rearrange(pat, **kw)
            self.b[name] = self.P.buf(name)
        return self.ap[name]

D = 1024
DFF = 2816
NF = DFF // 128
DIN = 2200
NTOK = 4096
TT = 512
NTILE = NTOK // TT
ALPHA = float((2.0 * 2) ** 0.25)
LN_EPS = 1e-5
QSCALE = float(96 ** -0.5)
TCOLS = [256 + 96 * h for h in range(8)] + [1024, 1120, 1216, 1312, 1408, 1504, 1792, 1888]


class TB:
    def __init__(self, nc, P, ident, dram):
        self.nc = nc
        self.P = P
        self.dram = dram
        self.ident = ident
        self.b_dram = dram.b
        self.idt = P.sbuf("idt", [128, 128], F32); self.b_idt = P.buf()
        self.xt = [P.sbuf("xt%d" % i, [128, 4, D], F32) for i in range(2)]
        self.b_xt = [[P.buf() for s in range(4)] for i in range(2)]
        self.xT = P.sbuf("xT", [128, 8, TT], BF16); self.b_xT = [P.buf() for c in range(8)]
        self.hT = P.sbuf("hT", [128, NF, TT], BF16); self.b_hT = [P.buf() for f in range(NF)]
        self.wres = P.sbuf("wres", [128, NF * D], BF16); self.b_wres = [P.buf() for f in range(NF)]
        self.ring = P.sbuf("ring", [128, 3, 2, 8, 128], BF16); self.b_ring = [P.buf() for i in range(3)]
        self.stg = [P.sbuf("stg%d" % i, [128, DFF], F32) for i in range(2)]; self.b_stg = [P.buf() for i in range(2)]
        self.stgb = [P.sbuf("stgb%d" % i, [128, DFF], BF16) for i in range(2)]; self.b_stgb = [P.buf() for i in range(2)]
        self.lng = P.sbuf("lng", [128, D], F32); self.b_lng = P.buf()
        self.lnb = P.sbuf("lnb", [128, D], F32); self.b_lnb = P.buf()
        self.sg = [P.sbuf("sg%d" % i, [128, TT], F32) for i in range(2)]; self.b_sg = [P.buf() for i in range(2)]
        self.st = P.sbuf("st", [128, 4, 2, 6], F32); self.b_st = [P.buf() for s in range(4)]
        self.mv = P.sbuf("mv", [128, 4, 4], F32); self.b_mv = [P.buf() for s in range(4)]
        self.oT = P.sbuf("oT", [128, 16, TT], BF16); self.b_oT = [P.buf() for i in range(16)]
        P.op("gpsimd", lambda e: e.memset(self.oT[96:128, :, :], 0.0), writes=self.b_oT)
        self.ou = P.sbuf("ou", [128, 2, TT], F32); self.b_ou = [P.buf() for i in range(2)]
        self.ov = P.sbuf("ov", [128, 4, 384], BF16); self.b_ov = [P.buf() for i in range(4)]
        self.og = P.sbuf("og", [128, 4, 24], F32); self.b_og = [P.buf() for i in range(4)]
        self.bg = P.sbuf("bg", [128, 24], F32); self.b_bg = P.buf()
        self.ps = [P.psum("ps%d" % i, [128, 512], F32) for i in range(8)]
        self.b_ps = [P.pbuf() for i in range(8)]
        self.rr = 0
        P.op("sync", lambda e: e.dma_start(out=self.idt[:], in_=self.ident), writes=[self.b_idt], dma=True)

    def din(self, name, shape, dt):
        return self.dram.din(name, shape, dt)

    def dscr(self, name, shape, dt):
        return self.dram.dscr(name, shape, dt)

    def cast_eng(self):
        self.rr += 1
        return ("vector", "gpsimd", "scalar")[self.rr % 3]

    def cast(self, eng, out, in_, reads, writes):
        if eng == "scalar":
            self.P.op("scalar", lambda e: e.activation(out=out, in_=in_, func=AF.Copy), reads=reads, writes=writes)
        else:
            self.P.op(eng, lambda e: e.tensor_copy(out=out, in_=in_), reads=reads, writes=writes)

    def load_res(self, w, nchunks, width):
        P = self.P
        per = max(1, DFF // width)
        f = 0
        i = 0
        while f < nchunks:
            n = min(per, nchunks - f)
            k = i % 2
            src = w[f * 128:(f + n) * 128, :].rearrange("(n p) d -> p n d", p=128)
            dst = self.stg[k][:, 0:n * width].rearrange("p (n d) -> p n d", d=width)
            P.op("sync", lambda e, dst=dst, src=src: e.dma_start(out=dst, in_=src), writes=[self.b_stg[k]], dma=True)
            self.cast(self.cast_eng(), self.wres[:, f * width:(f + n) * width], self.stg[k][:, 0:n * width],
                      [self.b_stg[k]], [self.b_wres[j] for j in range(f, f + n)])
            f += n
            i += 1

    def prep_gu(self, wg, wu, scr, b_scr):
        P = self.P
        i = 0
        for c in range(8):
            for gi, w in ((0, wg), (1, wu)):
                k = i % 2
                P.op("sync", lambda e, k=k, w=w, c=c: e.dma_start(out=self.stg[k][:], in_=w[c * 128:(c + 1) * 128, :]),
                     writes=[self.b_stg[k]], dma=True)
                self.cast(self.cast_eng(), self.stgb[k][:], self.stg[k][:], [self.b_stg[k]], [self.b_stgb[k]])
                dst = scr[:, :, gi, c, :].rearrange("f p j -> p f j")
                src = self.stgb[k][:].rearrange("p (f j) -> p f j", j=128)
                P.op("gpsimd", lambda e, dst=dst, src=src: e.dma_start(out=dst, in_=src), reads=[self.b_stgb[k]],
                     writes=[b_scr], dma=True)
                i += 1

    def load_ln(self, g, b):
        P = self.P
        P.op("sync", lambda e: e.dma_start(out=self.lng[:], in_=g.partition_broadcast(128)), writes=[self.b_lng], dma=True)
        P.op("sync", lambda e: e.dma_start(out=self.lnb[:], in_=b.partition_broadcast(128)), writes=[self.b_lnb], dma=True)

    def load_x(self, src, b_src, t, k):
        P = self.P
        s_ap = src[t * TT:(t + 1) * TT, :].rearrange("(s p) d -> p s d", p=128)
        P.op("sync", lambda e: e.dma_start(out=self.xt[k][:], in_=s_ap), reads=[b_src] if b_src else [], writes=self.b_xt[k], dma=True)

    def transposes(self, k):
        P = self.P
        for c in range(8):
            pk = c % 2
            for s in range(4):
                P.op("tensor", lambda e, c=c, s=s, pk=pk: e.transpose(out=self.ps[pk][:, s * 128:(s + 1) * 128],
                                                                      in_=self.xt[k][:, s, c * 128:(c + 1) * 128], identity=self.idt[:]),
                     reads=[self.b_xt[k][s], self.b_idt], writes=[self.b_ps[pk]], pe_accum=True)
            self.cast("scalar" if c % 2 else "vector", self.xT[:, c, :], self.ps[pk][:], [self.b_ps[pk]], [self.b_xT[c]])

    def gate_up(self, scr, b_scr, t):
        P = self.P
        for f in range(NF):
            slot = (t * NF + f) % 3
            P.op("gpsimd" if f % 2 else "sync", lambda e, f=f, slot=slot: e.dma_start(out=self.ring[:, slot], in_=scr[f]),
                 reads=[b_scr], writes=[self.b_ring[slot]], dma=True)
            pg = 2 + (f % 2) * 2
            pu = pg + 1
            for gi, pp in ((0, pg), (1, pu)):
                for c in range(8):
                    P.op("tensor", lambda e, c=c, gi=gi, pp=pp, slot=slot: e.matmul(self.ps[pp][:], lhsT=self.ring[:, slot, gi, c, :], rhs=self.xT[:, c, :],
                                                                                 start=(c == 0), stop=(c == 7)),
                         reads=[self.b_ring[slot], self.b_xT[c]], writes=[self.b_ps[pp]], pe_accum=True)
            k = f % 2
            P.op("scalar", lambda e, k=k, pg=pg: e.activation(out=self.sg[k][:], in_=self.ps[pg][:], func=AF.Silu),
                 reads=[self.b_ps[pg]], writes=[self.b_sg[k]])
            P.op("vector", lambda e, k=k, pu=pu, f=f: e.tensor_tensor(out=self.hT[:, f, :], in0=self.sg[k][:], in1=self.ps[pu][:], op=ALU.mult),
                 reads=[self.b_sg[k], self.b_ps[pu]], writes=[self.b_hT[f]])

    def down_ln(self, k, lhs, b_lhs, nch, width_w, ysc, dst, b_dst, t):
        P = self.P
        for s in range(4):
            P.op("gpsimd", lambda e, s=s: e.tensor_scalar(out=self.xt[k][:, s, :], in0=self.xt[k][:, s, :], scalar1=ALPHA, scalar2=None, op0=ALU.mult),
                 reads=[self.b_xt[k][s]], writes=[self.b_xt[k][s]])
        for s in range(4):
            for n in range(2):
                pp = 6 + (s * 2 + n) % 2
                for f in range(nch):
                    P.op("tensor", lambda e, s=s, n=n, f=f, pp=pp: e.matmul(self.ps[pp][:], lhsT=lhs[:, f, s * 128:(s + 1) * 128],
                                                                          rhs=self.wres[:, f * width_w + n * 512: f * width_w + (n + 1) * 512],
                                                                          start=(f == 0), stop=(f == nch - 1)),
                         reads=[b_lhs[f], self.b_wres[f]], writes=[self.b_ps[pp]], pe_accum=True)
                P.op("vector", lambda e, s=s, n=n, pp=pp: e.scalar_tensor_tensor(out=self.xt[k][:, s, n * 512:(n + 1) * 512], in0=self.ps[pp][:], scalar=ysc,
                                                                                in1=self.xt[k][:, s, n * 512:(n + 1) * 512], op0=ALU.mult, op1=ALU.add),
                     reads=[self.b_ps[pp], self.b_xt[k][s]], writes=[self.b_xt[k][s]])
            self.layernorm(k, s)
        d_ap = dst[t * TT:(t + 1) * TT, :].rearrange("(s p) d -> p s d", p=128)
        P.op("sync", lambda e: e.dma_start(out=d_ap, in_=self.xt[k][:]), reads=self.b_xt[k], writes=[b_dst], dma=True)

    def layernorm(self, k, s):
        P = self.P
        z = self.xt[k]
        for h in range(2):
            P.op("vector", lambda e, h=h: e.bn_stats(self.st[:, s, h, :], z[:, s, h * 512:(h + 1) * 512]),
                 reads=[self.b_xt[k][s]], writes=[self.b_st[s]])
        P.op("vector", lambda e: e.bn_aggr(self.mv[:, s, 0:2], self.st[:, s, :, :]), reads=[self.b_st[s]], writes=[self.b_mv[s]])
        P.op("vector", lambda e: e.tensor_scalar(out=self.mv[:, s, 2:3], in0=self.mv[:, s, 1:2], scalar1=LN_EPS, scalar2=None, op0=ALU.add),
             reads=[self.b_mv[s]], writes=[self.b_mv[s]])
        P.op("scalar", lambda e: e.sqrt(self.mv[:, s, 3:4], self.mv[:, s, 2:3]), reads=[self.b_mv[s]], writes=[self.b_mv[s]])
        P.op("vector", lambda e: e.reciprocal(self.mv[:, s, 2:3], self.mv[:, s, 3:4]), reads=[self.b_mv[s]], writes=[self.b_mv[s]])
        P.op("vector", lambda e: e.tensor_scalar(out=z[:, s, :], in0=z[:, s, :], scalar1=self.mv[:, s, 0:1], scalar2=self.mv[:, s, 2:3],
                                                 op0=ALU.subtract, op1=ALU.mult),
             reads=[self.b_xt[k][s], self.b_mv[s]], writes=[self.b_xt[k][s]])
        P.op("gpsimd", lambda e: e.tensor_tensor(out=z[:, s, :], in0=z[:, s, :], in1=self.lng[:], op=ALU.mult),
             reads=[self.b_xt[k][s], self.b_lng], writes=[self.b_xt[k][s]])
        P.op("gpsimd", lambda e: e.tensor_tensor(out=z[:, s, :], in0=z[:, s, :], in1=self.lnb[:], op=ALU.add),
             reads=[self.b_xt[k][s], self.b_lnb], writes=[self.b_xt[k][s]])

    def stage_ffn(self, pfx, src, b_src, dst, b_dst, tcount):
        wg = self.din(pfx + "_wg", [D, DFF], F32)
        wu = self.din(pfx + "_wu", [D, DFF], F32)
        wd = self.din(pfx + "_wd", [DFF, D], F32)
        g = self.din(pfx + "_lg", [D], F32)
        b = self.din(pfx + "_lb", [D], F32)
        scr = self.dscr("gus", [NF, 128, 2, 8, 128], BF16)
        b_scr = self.b_dram["gus"]
        self.load_ln(g, b)
        self.prep_gu(wg, wu, scr, b_scr)
        self.load_res(wd, NF, D)
        for t in range(NTILE):
            k = (tcount + t) % 2
            self.load_x(src, b_src, t, k)
            self.transposes(k)
            self.gate_up(scr, b_scr, t)
            self.down_ln(k, self.hT, self.b_hT, NF, D, 0.5, dst, b_dst, t)
        return tcount + NTILE

    def stage_mix(self, pfx, ym, b_ym, src, b_src, dst, b_dst, tcount):
        wo = self.din(pfx + "_w_out", [D, D], F32)
        g = self.din(pfx + "_ln2_g", [D], F32)
        b = self.din(pfx + "_ln2_b", [D], F32)
        self.load_ln(g, b)
        self.load_res(wo, 8, D)
        for t in range(NTILE):
            k = (tcount + t) % 2
            self.load_x(src, b_src, t, k)
            y_ap = ym[:, t * TT:(t + 1) * TT].rearrange("(c p) t -> p c t", p=128)
            self.P.op("gpsimd", lambda e, y_ap=y_ap: e.dma_start(out=self.hT[:, 0:8, :], in_=y_ap), reads=[b_ym], writes=self.b_hT[0:8], dma=True)
            self.down_ln(k, self.hT, self.b_hT, 8, D, 1.0, dst, b_dst, t)
        return tcount + NTILE

    def stage_proj(self, pfx, src, b_src, outs, tcount):
        P = self.P
        win = self.din(pfx + "_w_in", [D, DIN], F32)
        bgate = self.din(pfx + "_b_gate", [24], F32)
        qT, kT, uT, vs, vw, gates = (outs[k][0] for k in ("qT", "kT", "uT", "vs", "vw", "gates"))
        bd = {k: outs[k][1] for k in ("qT", "kT", "uT", "vs", "vw", "gates")}
        P.op("sync", lambda e: e.dma_start(out=self.bg[:], in_=bgate.partition_broadcast(128)), writes=[self.b_bg], dma=True)
        for c in range(8):
            k = c % 2
            P.op("sync", lambda e, k=k, c=c: e.dma_start(out=self.stg[k][:, 0:DIN], in_=win[c * 128:(c + 1) * 128, :]), writes=[self.b_stg[k]], dma=True)
            self.cast(self.cast_eng(), self.wres[:, c * DIN:(c + 1) * DIN], self.stg[k][:, 0:DIN], [self.b_stg[k]], [self.b_wres[c]])
        W = lambda c, a, b_: self.wres[:, c * DIN + a: c * DIN + b_]
        for t in range(NTILE):
            k = (tcount + t) % 2
            self.load_x(src, b_src, t, k)
            self.transposes(k)
            tok = slice(t * TT, (t + 1) * TT)
            for i, col in enumerate(TCOLS):
                pp = 2 + i % 4
                for c in range(8):
                    P.op("tensor", lambda e, c=c, col=col, pp=pp: e.matmul(self.ps[pp][0:96, :], lhsT=W(c, col, col + 96), rhs=self.xT[:, c, :],
                                                                         start=(c == 0), stop=(c == 7)),
                         reads=[self.b_wres[c], self.b_xT[c]], writes=[self.b_ps[pp]], pe_accum=True)
                if i < 8:
                    P.op("scalar", lambda e, i=i, pp=pp: e.mul(self.oT[0:96, i, :], self.ps[pp][0:96, :], QSCALE),
                         reads=[self.b_ps[pp]], writes=[self.b_oT[i]])
                else:
                    P.op("vector", lambda e, i=i, pp=pp: e.tensor_copy(out=self.oT[0:96, i, :], in_=self.ps[pp][0:96, :]),
                         reads=[self.b_ps[pp]], writes=[self.b_oT[i]])
            P.op("sync", lambda e, tok=tok: e.dma_start(out=qT[:, :, tok].rearrange("h d t -> d h t"), in_=self.oT[:, 0:8, :]),
                 reads=self.b_oT[0:8], writes=[bd["qT"]], dma=True)
            P.op("sync", lambda e, tok=tok: e.dma_start(out=kT[:, :, tok].rearrange("h d t -> d h t"), in_=self.oT[0:96, 8:16, :]),
                 reads=self.b_oT[8:16], writes=[bd["kT"]], dma=True)
            for j in range(2):
                pp = 6 + j
                for c in range(8):
                    P.op("tensor", lambda e, c=c, j=j, pp=pp: e.matmul(self.ps[pp][:], lhsT=W(c, j * 128, (j + 1) * 128), rhs=self.xT[:, c, :],
                                                                     start=(c == 0), stop=(c == 7)),
                         reads=[self.b_wres[c], self.b_xT[c]], writes=[self.b_ps[pp]], pe_accum=True)
                P.op("vector", lambda e, j=j, pp=pp: e.tensor_copy(out=self.ou[:, j, :], in_=self.ps[pp][:]), reads=[self.b_ps[pp]], writes=[self.b_ou[j]])
            for j2 in range(2):
                for half in range(2):
                    P.op("sync", lambda e, tok=tok, j2=j2, half=half: e.dma_start(out=uT[2 * j2 + half][:, tok], in_=self.ou[half * 64:(half + 1) * 64, j2, :]),
                         reads=[self.b_ou[j2]], writes=[bd["uT"]], dma=True)
            for s in range(4):
                pp = 2 + s
                for c in range(8):
                    P.op("tensor", lambda e, c=c, s=s, pp=pp: e.matmul(self.ps[pp][:, 0:192], lhsT=self.xT[:, c, s * 128:(s + 1) * 128], rhs=W(c, 1600, 1792),
                                                                     start=(c == 0), stop=(c == 7)),
                         reads=[self.b_wres[c], self.b_xT[c]], writes=[self.b_ps[pp]], pe_accum=True)
                for c in range(8):
                    P.op("tensor", lambda e, c=c, s=s, pp=pp: e.matmul(self.ps[pp][:, 256:472], lhsT=self.xT[:, c, s * 128:(s + 1) * 128], rhs=W(c, 1984, 2200),
                                                                     start=(c == 0), stop=(c == 7), skip_group_check=True),
                         reads=[self.b_wres[c], self.b_xT[c]], writes=[self.b_ps[pp]], pe_accum=True)
                P.op("vector", lambda e, s=s, pp=pp: e.tensor_copy(out=self.ov[:, s, 0:192], in_=self.ps[pp][:, 0:192]), reads=[self.b_ps[pp]], writes=[self.b_ov[s]])
                P.op("vector", lambda e, s=s, pp=pp: e.tensor_copy(out=self.ov[:, s, 192:384], in_=self.ps[pp][:, 256:448]), reads=[self.b_ps[pp]], writes=[self.b_ov[s]])
                P.op("vector", lambda e, s=s, pp=pp: e.tensor_tensor(out=self.og[:, s, :], in0=self.ps[pp][:, 448:472], in1=self.bg[:], op=ALU.add),
                     reads=[self.b_ps[pp], self.b_bg], writes=[self.b_og[s]])
                P.op("scalar", lambda e, s=s: e.activation(out=self.og[:, s, :], in_=self.og[:, s, :], func=AF.Sigmoid), reads=[self.b_og[s]], writes=[self.b_og[s]])
            for g_ in range(2):
                P.op("gpsimd", lambda e, tok=tok, g_=g_: e.dma_start(out=vs[g_, tok, :].rearrange("(s p) d -> p s d", p=128), in_=self.ov[:, :, 96 * g_:96 * g_ + 96]),
                     reads=self.b_ov, writes=[bd["vs"]], dma=True)
                P.op("gpsimd", lambda e, tok=tok, g_=g_: e.dma_start(out=vw[g_, tok, :].rearrange("(s p) d -> p s d", p=128), in_=self.ov[:, :, 192 + 96 * g_:192 + 96 * g_ + 96]),
                     reads=self.b_ov, writes=[bd["vw"]], dma=True)
            P.op("gpsimd", lambda e, tok=tok: e.dma_start(out=gates[tok, :].rearrange("(s p) d -> p s d", p=128), in_=self.og[:]),
                 reads=self.b_og, writes=[bd["gates"]], dma=True)
        return tcount + NTILE


S = 16384
NQT = S // 512
NEG = -30000.0
VP = 112
GELU_C = float(2.0 * (2.0 / np.pi) ** 0.5)


def b_consts():
    import ml_dtypes
    c = {}
    c["ident_f"] = np.eye(128, dtype=np.float32)
    c["ident_b"] = np.eye(128, dtype=np.float32).astype(ml_dtypes.bfloat16)
    k = np.arange(128)[:, None]
    t = np.arange(512)[None, :]
    masks = np.zeros((16, 128, 512), np.float32)
    for i in range(8):
        kp = 128 * i - 512 + k
        masks[i] = np.where((t - kp >= 0) & (t - kp < 512), 1.0, 0.0)
    for r in range(4):
        masks[8 + r] = np.where(16 * k + 31 <= 512 * r + t, 1.0, 0.0)
    for i in range(4):
        masks[12 + i] = np.where(128 * i + k <= t, 1.0, 0.0)
    c["masks"] = masks.astype(ml_dtypes.bfloat16)
    kk = np.arange(2048)[None, :]
    r = np.arange(32)[:, None]
    c["kpat"] = (((kk // 64) % 32) == r).astype(np.float32).astype(ml_dtypes.bfloat16)
    W = np.zeros((1024, 256), np.float32)
    for j in range(256):
        for n, w in ((4 * j - 1, 0.5), (4 * j, 1.0), (4 * j + 1, 1.0), (4 * j + 2, 1.0), (4 * j + 3, 0.5)):
            if 0 <= n < 1023:
                W[n, j] = w
    c["wimp"] = W.reshape(8, 128, 256).astype(ml_dtypes.bfloat16)
    fp = np.zeros((128, 4), np.float32)
    fp[:64, 0] = 1000.0; fp[:64, 1] = 2000.0
    fp[64:, 1] = 2000.0; fp[64:, 2] = 3000.0
    c["fpat"] = fp
    return c


def pool_consts(j):
    w = 2 ** (j + 1)
    a = np.zeros((64, 4), np.float32)
    a[:, j] = 1.0 / w
    A = np.zeros((64, 4, 16), np.float32)
    tt = np.arange(16)
    A[:, j, :] = 1.0 / np.minimum(tt + 1, w)
    return a, A


def _V(x, e):
    return x(e) if callable(x) else x


class BB:
    def __init__(self, nc, P, dram):
        self.nc = nc
        self.P = P
        self.dram = dram

    def din(self, name, shape, dt):
        return self.dram.din(name, shape, dt)

    def build(self, pfx, src, ymc, b_ymc):
        P = self.P
        nc = self.nc
        op = P.op
        qT, b_qT = src["qT"]; gates, b_gates = src["gates"]
        kcT, b_kcT = src["kcT"]; vcT, b_vcT = src["vcT"]; ksT, b_ksT = src["ksT"]; kwT, b_kwT = src["kwT"]
        vs, b_vs = src["vs"]; vw, b_vw = src["vw"]; uT, b_uT = src["uT"]
        cw = {}
        for kv in ("k", "v"):
            cw[kv + "w1"] = self.din(pfx + "_cmp_%s_w1" % kv, [3072, 96], F32)
            cw[kv + "w2"] = self.din(pfx + "_cmp_%s_w2" % kv, [96, 96], F32)
            cw[kv + "pos"] = self.din(pfx + "_cmp_%s_pos" % kv, [32, 96], F32)
        pool_w = self.din(pfx + "_pool_w", [64, 64], F32)
        pool_sc = self.din(pfx + "_pool_sc", [64, 1], F32)
        pool_a = self.din("pool_a", [64, 4], F32)
        pool_A = self.din("pool_A", [64, 4, 16], F32)
        d_identf = self.din("ident", [128, 128], F32)
        d_identb = self.din("ident_b", [128, 128], BF16)
        d_masks = self.din("masks", [16, 128, 512], BF16)
        d_kpat = self.din("kpat", [32, 2048], BF16)
        d_wimp = self.din("wimp", [8, 128, 256], BF16)
        d_fpat = self.din("fpat", [128, 4], F32)
        d_gsel = self.din("gsel", [6, 24], F32)
        def ysl(r0, r1, t0, n):
            return ymc[t0 // 4096, r0:r1, (t0 % 4096):(t0 % 4096) + n]

        ps_s = [P.psum("ps_s%d" % i, [128, 512]) for i in range(2)]; b_ps_s = [P.pbuf() for i in range(2)]
        ps_o = [P.psum("ps_o%d" % i, [128, 512]) for i in range(2)]; b_ps_o = [P.pbuf() for i in range(2)]
        ps_i = [P.psum("ps_i%d" % i, [128, 512]) for i in range(2)]; b_ps_i = [P.pbuf() for i in range(2)]
        ps_t = [P.psum("ps_t%d" % i, [128, 512]) for i in range(2)]; b_ps_t = [P.pbuf() for i in range(2)]

        idf = P.sbuf("idf", [128, 128], F32); b_idf = P.buf()
        idb = P.sbuf("idb", [128, 128], BF16); b_idb = P.buf()
        masks = P.sbuf("masks", [128, 16, 512], BF16); b_masks = P.buf()
        wimp = P.sbuf("wimp", [128, 8, 256], BF16); b_wimp = P.buf()
        fpat = P.sbuf("fpat", [128, 4], F32); b_fpat = P.buf()
        ksa = P.sbuf("ksa", [128, S], BF16); b_ksa = P.buf()
        vsa = P.sbuf("vsa", [128, 128, VP], BF16); b_vsa = P.buf()
        kcmpT = P.sbuf("kcmpT", [96, 1024], BF16); b_kcmpT = P.buf()
        vcmpa = P.sbuf("vcmpa", [128, 8, VP], BF16); b_vcmpa = P.buf()
        op("sync", lambda e: e.dma_start(out=idf[:], in_=d_identf), writes=[b_idf], dma=True)
        op("sync", lambda e: e.dma_start(out=idb[:], in_=d_identb), writes=[b_idb], dma=True)
        op("sync", lambda e: e.dma_start(out=masks[:], in_=d_masks.rearrange("m p t -> p m t")), writes=[b_masks], dma=True)
        op("sync", lambda e: e.dma_start(out=wimp[:], in_=d_wimp.rearrange("c p j -> p c j")), writes=[b_wimp], dma=True)
        op("sync", lambda e: e.dma_start(out=fpat[:], in_=d_fpat), writes=[b_fpat], dma=True)
        gsel = P.sbuf("gsel", [128, 6, 24], F32); b_gsel = P.buf()
        op("sync", lambda e: e.dma_start(out=gsel[:].rearrange("p a b -> p (a b)"), in_=d_gsel.rearrange("a b -> (a b)").partition_broadcast(128)), writes=[b_gsel], dma=True)
        op("gpsimd", lambda e: e.memset(kcmpT[:], 0.0), writes=[b_kcmpT])
        op("gpsimd", lambda e: e.memset(vcmpa[:], 1.0), writes=[b_vcmpa])
        op("gpsimd", lambda e: e.memset(vsa[:], 1.0), writes=[b_vsa])

        q_sb = [P.sbuf("q_sb%d" % i, [96, 4, 512], BF16) for i in range(2)]; b_q = [P.buf() for i in range(2)]
        g_sb = [P.sbuf("g_sb%d" % i, [128, 4, 6], F32) for i in range(2)]; b_g = [P.buf() for i in range(2)]
        g_full = P.sbuf("g_full", [128, 4, 24], F32); b_gfull = P.buf()
        g_tmp = P.sbuf("g_tmp", [128, 4, 6, 24], F32); b_gtmp = P.buf()
        kw_sb = [P.sbuf("kw_sb%d" % i, [96, 1024], BF16) for i in range(2)]; b_kw = [P.buf() for i in range(2)]
        vwa = [P.sbuf("vwa%d" % i, [128, 8, VP], BF16) for i in range(2)]; b_vwa = [P.buf() for i in range(2)]
        e_all = [P.sbuf("e_all%d" % i, [128, 8, 512], BF16) for i in range(2)]; b_eall = [P.buf() for i in range(2)]
        e_sb = [P.sbuf("e_sb%d" % i, [128, 512], BF16) for i in range(3)]; b_e = [P.buf() for i in range(3)]
        oTs = [P.sbuf("oTs%d" % i, [97, 512], F32) for i in range(2)]; b_oTs = [P.buf() for i in range(2)]
        impacc = P.sbuf("impacc", [128, 4, 258], F32); b_imp = [P.buf() for i in range(4)]
        mr = P.sbuf("mr", [128, 256], F32); b_mr = P.buf()
        m8 = P.sbuf("m8", [128, 3, 8], F32); b_m8 = P.buf()
        selb = P.sbuf("selb", [128, 4, 352], F32); b_selb = [P.buf() for i in range(4)]
        qaug = P.sbuf("qaug", [128, 2, 8, 512], BF16); b_qaug = [[P.buf() for v in range(8)] for h in range(2)]
        oacc = [P.sbuf("oacc%d" % i, [128, 4, 2, 96], F32) for i in range(2)]; b_oacc = [P.buf() for i in range(2)]
        oTo = [P.sbuf("oTo%d" % i, [96, 2, 512], BF16) for i in range(2)]; b_oTo = [P.buf() for i in range(2)]
        rd = P.sbuf("rd", [128, 8, 4], F32); b_rd = P.buf()
        qa_f = qaug[:].rearrange("p a b c -> p (a b c)").bitcast(F32)
        ea0 = e_all[0][:].rearrange("p a b -> p (a b)")
        ea1_f = e_all[1][:].rearrange("p a b -> p (a b)").bitcast(F32)
        w1s = qa_f[0:96, 0:3072].rearrange("d (p e) -> d p e", e=96); b_w1s = P.buf()
        gx = qa_f[0:96, 3072:4096]; b_gx = P.buf()
        w1b = ea0[0:96, 0:3072].rearrange("d (p e) -> d p e", e=96); b_w1b = P.buf()
        gb = ea0[0:96, 3072:4096]; b_gb = P.buf()
        gy = ea1_f[0:96, 0:1024]; b_gy = P.buf()
        w2s = P.sbuf("w2s", [96, 96], F32); b_w2s = P.buf()
        w2b = P.sbuf("w2b", [96, 96], BF16); b_w2b = P.buf()
        poss = P.sbuf("poss", [32, 96], F32); b_poss = P.buf()
        posT = P.sbuf("posT", [96, 32], BF16); b_posT = P.buf()
        cb = P.sbuf("cb", [96, 1], F32); b_cb = P.buf()
        stage = getattr(self, 'stage', 99)
        for i in range(8):
            op("gpsimd", lambda e, i=i: e.dma_start(out=ksa[96:128, i * 2048:(i + 1) * 2048], in_=d_kpat), writes=[b_ksa], dma=True)
        for i in range(16 if stage >= 2 else 0):
            op("gpsimd", lambda e, i=i: e.dma_start(out=vsa[:, i * 8:(i + 1) * 8, 0:96],
                                                                        in_=_V(vs, e)[i * 1024:(i + 1) * 1024, :].rearrange("(k p) d -> p k d", p=128)),
               reads=[b_vs], writes=[b_vsa], dma=True)

        for kv, srcT, b_srcT in ((("k", kcT, b_kcT), ("v", vcT, b_vcT)) if stage >= 1 else ()):
            op("sync", lambda e, srcT=srcT: e.dma_start(out=ksa[0:96, :], in_=srcT), reads=[b_srcT], writes=[b_ksa], dma=True)
            op("sync", lambda e, kv=kv: e.dma_start(out=w1s, in_=cw[kv + "w1"].rearrange("(p d) e -> d p e", d=96)), writes=[b_w1s], dma=True)
            op("sync", lambda e, kv=kv: e.dma_start(out=w2s[:], in_=cw[kv + "w2"]), writes=[b_w2s], dma=True)
            op("sync", lambda e, kv=kv: e.dma_start(out=poss[:], in_=cw[kv + "pos"]), writes=[b_poss], dma=True)
            op("vector", lambda e: e.tensor_copy(out=w1b, in_=w1s), reads=[b_w1s], writes=[b_w1b])
            op("vector", lambda e: e.tensor_copy(out=w2b[:], in_=w2s[:]), reads=[b_w2s], writes=[b_w2b])
            op("tensor", lambda e: e.transpose(out=ps_t[0][0:96, 0:32], in_=poss[:], identity=idf[0:32, 0:32]), reads=[b_poss, b_idf], writes=[b_ps_t[0]], pe_accum=True)
            op("vector", lambda e: e.tensor_copy(out=posT[:], in_=ps_t[0][0:96, 0:32]), reads=[b_ps_t[0]], writes=[b_posT])
            for p in range(32):
                op("tensor", lambda e, p=p: e.matmul(ps_t[1][0:96, 0:1], lhsT=w1b[:, p, :], rhs=posT[:, p:p + 1], start=(p == 0), stop=(p == 31)),
                   reads=[b_w1b, b_posT], writes=[b_ps_t[1]], pe_accum=True)
            op("vector", lambda e: e.tensor_copy(out=cb[:], in_=ps_t[1][0:96, 0:1]), reads=[b_ps_t[1]], writes=[b_cb])
            if stage < 1.2:
                continue
            op("gpsimd", lambda e: e.memset(gb, 0.0), writes=[b_gb])
            kview = ksa[0:96, :].rearrange("d (n s) -> d n s", s=16)
            for half, (n0, N) in enumerate(((0, 512), (512, 511))):
                pp = ps_s[half]
                for p in range(32):
                    rhs = kview[:, n0:n0 + N, p] if p < 16 else kview[:, n0 + 1:n0 + 1 + N, p - 16]
                    op("tensor", lambda e, p=p, rhs=rhs, pp=pp, N=N: e.matmul(pp[0:96, 0:N], lhsT=w1b[:, p, :], rhs=rhs, start=(p == 0), stop=(p == 31)),
                       reads=[b_w1b, b_ksa], writes=[b_ps_s[half]], pe_accum=True)
                sl = slice(n0, n0 + N)
                op("vector", lambda e, pp=pp, N=N, sl=sl: e.tensor_scalar(out=gx[:, sl], in0=pp[0:96, 0:N], scalar1=cb[:, 0:1], scalar2=None, op0=ALU.add),
                   reads=[b_ps_s[half], b_cb], writes=[b_gx])
                op("vector", lambda e, sl=sl: e.tensor_tensor(out=gy[:, sl], in0=gx[:, sl], in1=gx[:, sl], op=ALU.mult), reads=[b_gx], writes=[b_gy])
                op("vector", lambda e, sl=sl: e.tensor_scalar(out=gy[:, sl], in0=gy[:, sl], scalar1=0.044715, scalar2=1.0, op0=ALU.mult, op1=ALU.add), reads=[b_gy], writes=[b_gy])
                op("vector", lambda e, sl=sl: e.tensor_tensor(out=gy[:, sl], in0=gy[:, sl], in1=gx[:, sl], op=ALU.mult), reads=[b_gy, b_gx], writes=[b_gy])
                op("scalar", lambda e, sl=sl: e.activation(out=gy[:, sl], in_=gy[:, sl], func=AF.Sigmoid, scale=GELU_C), reads=[b_gy], writes=[b_gy])
                op("vector", lambda e, sl=sl: e.tensor_tensor(out=gb[:, sl], in0=gy[:, sl], in1=gx[:, sl], op=ALU.mult), reads=[b_gy, b_gx], writes=[b_gb])
            if stage < 1.3:
                continue
            if kv == "k":
                for half, (n0, N) in enumerate(((0, 512), (512, 511))):
                    op("tensor", lambda e, half=half, n0=n0, N=N: e.matmul(ps_o[half][0:96, 0:N], lhsT=w2b[:], rhs=gb[:, n0:n0 + N], start=True, stop=True),
                       reads=[b_w2b, b_gb], writes=[b_ps_o[half]], pe_accum=True)
                    op("vector", lambda e, half=half, n0=n0, N=N: e.tensor_copy(out=kcmpT[:, n0:n0 + N], in_=ps_o[half][0:96, 0:N]), reads=[b_ps_o[half]], writes=[b_kcmpT])
            else:
                for c in range(8):
                    k2 = c % 2
                    op("tensor", lambda e, c=c, k2=k2: e.matmul(ps_o[k2][:, 0:96], lhsT=gb[:, c * 128:(c + 1) * 128], rhs=w2b[:], start=True, stop=True),
                       reads=[b_w2b, b_gb], writes=[b_ps_o[k2]], pe_accum=True)
                    op("vector", lambda e, c=c, k2=k2: e.tensor_copy(out=vcmpa[:, c, 0:96], in_=ps_o[k2][:, 0:96]), reads=[b_ps_o[k2]], writes=[b_vcmpa])

        stage = getattr(self, 'stage', 99)
        op("sync", lambda e: e.dma_start(out=ksa[0:96, :], in_=ksT), reads=[b_ksT], writes=[b_ksa], dma=True)
        CH = 512
        pw_s = P.sbuf("pw_s", [64, 64], F32); b_pw_s = P.buf()
        pw_b = P.sbuf("pw_b", [64, 64], BF16); b_pw_b = P.buf()
        psc = P.sbuf("psc", [64, 1], F32); b_psc = P.buf()
        pa = P.sbuf("pa", [64, 4], F32); b_pa = P.buf()
        pA = P.sbuf("pA", [64, 4, 16], F32); b_pA = P.buf()
        L_ = 16 + CH
        ub = [P.sbuf("ub%d" % i, [64, L_], F32)[:] for i in range(2)]; b_ub = [P.buf() for i in range(2)]
        sw = [P.sbuf("sw%d" % i, [64, L_], F32)[:] for i in range(4)]; b_sw = [P.buf() for i in range(4)]
        acc = P.sbuf("pacc", [64, CH], F32)[:]; b_acc = P.buf()
        a16 = P.sbuf("a16", [64, 2, 16], F32)[:]; b_a16 = P.buf()
        accb = P.sbuf("paccb", [64, CH], BF16)[:]; b_accb = P.buf()
        pout = [P.sbuf("pout%d" % i, [64, CH], BF16)[:] for i in range(2)]; b_pout = [P.buf() for i in range(2)]
        op("sync", lambda e: e.dma_start(out=pw_s[:], in_=pool_w), writes=[b_pw_s], dma=True)
        op("sync", lambda e: e.dma_start(out=psc[:], in_=pool_sc), writes=[b_psc], dma=True)
        op("sync", lambda e: e.dma_start(out=pa[:], in_=pool_a), writes=[b_pa], dma=True)
        op("sync", lambda e: e.dma_start(out=pA[:], in_=pool_A), writes=[b_pA], dma=True)
        op("vector", lambda e: e.tensor_copy(out=pw_b[:], in_=pw_s[:]), reads=[b_pw_s], writes=[b_pw_b])
        def pool_chunk(ci):
            k = ci % 2
            u = ub[k]
            if ci == 0:
                op("gpsimd", lambda e, u=u: e.memset(u[:, 0:16], 0.0), writes=[b_ub[k]])
                op("sync", lambda e, u=u: e.dma_start(out=u[:, 16:], in_=uT[:, 0:CH]), reads=[b_uT], writes=[b_ub[k]], dma=True)
            else:
                op("sync", lambda e, u=u, ci=ci: e.dma_start(out=u, in_=uT[:, ci * CH - 16:(ci + 1) * CH]), reads=[b_uT], writes=[b_ub[k]], dma=True)
            L = 16 + CH
            prev, b_prev = u, b_ub[k]
            for wi, sh in enumerate((1, 2, 4, 8)):
                lo = 2 * sh - 1
                dst = sw[wi]
                op("gpsimd", lambda e, dst=dst, prev=prev, lo=lo, sh=sh, L=L: e.tensor_tensor(out=dst[:, lo:L], in0=prev[:, lo:L], in1=prev[:, lo - sh:L - sh], op=ALU.add),
                   reads=[b_prev], writes=[b_sw[wi]])
                prev, b_prev = dst, b_sw[wi]
            op("vector", lambda e: e.tensor_scalar(out=acc, in0=sw[0][:, 16:], scalar1=pa[:, 0:1], scalar2=None, op0=ALU.mult), reads=[b_sw[0], b_pa], writes=[b_acc])
            for wi in range(1, 4):
                op("vector", lambda e, wi=wi: e.scalar_tensor_tensor(out=acc, in0=sw[wi][:, 16:], scalar=pa[:, wi:wi + 1], in1=acc, op0=ALU.mult, op1=ALU.add),
                   reads=[b_sw[wi], b_pa, b_acc], writes=[b_acc])
            if ci == 0:
                op("vector", lambda e: e.tensor_tensor(out=a16[:, 0, :], in0=sw[0][:, 16:32], in1=pA[:, 0, :], op=ALU.mult), reads=[b_sw[0], b_pA], writes=[b_a16])
                for wi in range(1, 4):
                    op("vector", lambda e, wi=wi: e.tensor_tensor(out=a16[:, 1, :], in0=sw[wi][:, 16:32], in1=pA[:, wi, :], op=ALU.mult), reads=[b_sw[wi], b_pA, b_a16], writes=[b_a16])
                    op("vector", lambda e: e.tensor_tensor(out=a16[:, 0, :], in0=a16[:, 0, :], in1=a16[:, 1, :], op=ALU.add), reads=[b_a16], writes=[b_a16])
                op("vector", lambda e: e.tensor_copy(out=acc[:, 0:16], in_=a16[:, 0, :]), reads=[b_a16, b_acc], writes=[b_acc])
            op("vector", lambda e, u=u: e.tensor_tensor(out=accb, in0=acc, in1=u[:, 16:], op=ALU.subtract), reads=[b_acc, b_ub[k]], writes=[b_accb])
            for hh in range(CH // 512):
                pbi = nxt("t", 2)
                op("tensor", lambda e, hh=hh, pbi=pbi: e.matmul(ps_t[pbi][0:64, :], lhsT=pw_b[:], rhs=accb[:, hh * 512:(hh + 1) * 512], start=True, stop=True),
                   reads=[b_pw_b, b_accb], writes=[b_ps_t[pbi]], pe_accum=True)
                op("vector", lambda e, hh=hh, k=k, pbi=pbi: e.tensor_scalar(out=pout[k][:, hh * 512:(hh + 1) * 512], in0=ps_t[pbi][0:64, :], scalar1=psc[:, 0:1], scalar2=None, op0=ALU.mult),
                   reads=[b_ps_t[pbi], b_psc], writes=[b_pout[k]])
            op("sync", lambda e, k=k, ci=ci: e.dma_start(out=ysl(0, 64, ci * CH, CH), in_=pout[k]), reads=[b_pout[k]], writes=[b_ymc], dma=True)


        P.barrier()

        for i in range(2):
            op("gpsimd", lambda e, i=i: e.memset(vwa[i][:], 1.0), writes=[b_vwa[i]])
        op("gpsimd", lambda e: e.memset(selb[:], 0.0), writes=b_selb)
        op("gpsimd", lambda e: e.memset(impacc[:], 0.0), writes=b_imp)

        cnt = {"s": 0, "o": 0, "e": 0, "t": 0, "rd": 0}

        def nxt(key, n):
            v = cnt[key] % n
            cnt[key] += 1
            return v

        def load_tile(qt):
            k = qt % 2
            tok = slice(qt * 512, (qt + 1) * 512)
            op("sync", lambda e: e.dma_start(out=q_sb[k][:], in_=qT[:, :, tok].rearrange("h d t -> d h t")), reads=[b_qT], writes=[b_q[k]], dma=True)
            op("sync", lambda e: e.dma_start(out=g_full[:], in_=gates[tok, :].rearrange("(s p) c -> p s c", p=128)), reads=[b_gates], writes=[b_gfull], dma=True)
            op("gpsimd", lambda e: e.tensor_tensor(out=g_tmp[:], in0=g_full[:].unsqueeze(2).to_broadcast([128, 4, 6, 24]), in1=gsel[:].unsqueeze(1).to_broadcast([128, 4, 6, 24]), op=ALU.mult),
               reads=[b_gfull, b_gsel], writes=[b_gtmp])
            op("vector", lambda e: e.tensor_reduce(out=g_sb[k][:], in_=g_tmp[:], axis=AX.X, op=ALU.add), reads=[b_gtmp], writes=[b_g[k]])
            if qt == 0:
                op("sync", lambda e: e.dma_start(out=kw_sb[k][:, 512:1024], in_=kwT[:, 0:512]), reads=[b_kwT], writes=[b_kw[k]], dma=True)
                op("sync", lambda e: e.dma_start(out=vwa[k][:, 4:8, 0:96], in_=_V(vw, e)[0:512, :].rearrange("(k p) d -> p k d", p=128)), reads=[b_vw], writes=[b_vwa[k]], dma=True)
            else:
                op("sync", lambda e: e.dma_start(out=kw_sb[k][:], in_=kwT[:, qt * 512 - 512: qt * 512 + 512]), reads=[b_kwT], writes=[b_kw[k]], dma=True)
                op("sync", lambda e: e.dma_start(out=vwa[k][:, :, 0:96], in_=_V(vw, e)[qt * 512 - 512: qt * 512 + 512, :].rearrange("(k p) d -> p k d", p=128)), reads=[b_vw], writes=[b_vwa[k]], dma=True)

        def finish_branch(qt, po, h_own, gcol, first, rd_out=None):
            k = qt % 2
            bi = nxt("t", 2)
            osb = oTs[bi]
            op("vector", lambda e: e.tensor_copy(out=osb[:], in_=ps_o[po][0:97, :]), reads=[b_ps_o[po]], writes=[b_oTs[bi]])
            pt = ps_t[bi]
            for sub in range(4):
                op("tensor", lambda e, sub=sub: e.transpose(out=pt[:, sub * 128: sub * 128 + 97], in_=osb[0:97, sub * 128:(sub + 1) * 128], identity=idf[0:97, 0:97]),
                   reads=[b_oTs[bi], b_idf], writes=[b_ps_t[bi]], pe_accum=True)
            ri = nxt("rd", 8)
            ptv = pt[:].rearrange("p (s c) -> p s c", c=128)
            op("vector", lambda e: e.tensor_scalar(out=rd[:, ri, :], in0=ptv[:, :, 96], scalar1=1e-30, scalar2=None, op0=ALU.max), reads=[b_ps_t[bi]], writes=[b_rd])
            op("vector", lambda e: e.reciprocal(rd[:, ri, :], rd[:, ri, :]), reads=[b_rd], writes=[b_rd])
            if h_own is not None:
                ci = nxt("rd", 8)
                op("vector", lambda e: e.tensor_tensor(out=rd[:, ci, :], in0=rd[:, ri, :], in1=g_sb[k][:, :, h_own * 3 + gcol], op=ALU.mult), reads=[b_rd, b_g[k]], writes=[b_rd])
                dbg = getattr(self, "dbg_branch", None)
                if dbg is not None and dbg != gcol:
                    op("vector", lambda e: e.memset(rd[:, ci, :], 0.0), reads=[b_rd], writes=[b_rd])
                for sub in range(4):
                    if first:
                        op("vector", lambda e, sub=sub: e.tensor_scalar(out=oacc[k][:, sub, h_own, :], in0=ptv[:, sub, 0:96], scalar1=rd[:, ci, sub:sub + 1], scalar2=None, op0=ALU.mult),
                           reads=[b_ps_t[bi], b_rd], writes=[b_oacc[k]])
                    else:
                        op("vector", lambda e, sub=sub: e.scalar_tensor_tensor(out=oacc[k][:, sub, h_own, :], in0=ptv[:, sub, 0:96], scalar=rd[:, ci, sub:sub + 1],
                                                                              in1=oacc[k][:, sub, h_own, :], op0=ALU.mult, op1=ALU.add),
                           reads=[b_ps_t[bi], b_rd, b_oacc[k]], writes=[b_oacc[k]])
            return ri

        def cmp_part(qt):
            k = qt % 2
            nct = min(8, qt // 4 + 1)
            for h in range(4):
                ea = e_all[h % 2]
                for c in range(nct):
                    si = nxt("s", 2)
                    partial = qt < 4 * c + 4
                    op("tensor", lambda e, c=c, h=h, si=si: e.matmul(ps_s[si][:], lhsT=kcmpT[:, c * 128:(c + 1) * 128], rhs=q_sb[k][:, h, :], start=True, stop=True),
                       reads=[b_kcmpT, b_q[k]], writes=[b_ps_s[si]], pe_accum=True)
                    op("scalar", lambda e, c=c, si=si, ea=ea: e.activation(out=ea[:, c, :], in_=ps_s[si][:], func=AF.Exp), reads=[b_ps_s[si]], writes=[b_eall[h % 2]])
                    if partial:
                        r = qt - 4 * c
                        op("vector", lambda e, c=c, r=r, ea=ea: e.tensor_tensor(out=ea[:, c, :], in0=ea[:, c, :], in1=masks[:, 8 + r, :], op=ALU.mult),
                           reads=[b_eall[h % 2], b_masks], writes=[b_eall[h % 2]])
                po = nxt("o", 2)
                for c in range(nct):
                    op("tensor", lambda e, c=c, ea=ea, po=po: e.matmul(ps_o[po][0:97, :], lhsT=vcmpa[:, c, 0:97], rhs=ea[:, c, :], start=(c == 0), stop=(c == nct - 1)),
                       reads=[b_vcmpa, b_eall[h % 2]], writes=[b_ps_o[po]], pe_accum=True)
                for sub in range(4):
                    pi = ps_i[sub // 2]
                    for c in range(nct):
                        op("tensor", lambda e, c=c, sub=sub, ea=ea, pi=pi: e.matmul(pi[:, (sub % 2) * 256:(sub % 2) * 256 + 256], lhsT=ea[:, c, sub * 128:(sub + 1) * 128], rhs=wimp[:, c, :],
                                                                               start=(c == 0), stop=(c == nct - 1)),
                           reads=[b_wimp, b_eall[h % 2]], writes=[b_ps_i[sub // 2]], pe_accum=True)
                ri = finish_branch(qt, po, h if h < 2 else None, 0, True)
                for sub in range(4):
                    pi = ps_i[sub // 2]
                    src = pi[:, (sub % 2) * 256:(sub % 2) * 256 + 256]
                    if h == 0:
                        op("vector", lambda e, sub=sub, src=src, ri=ri: e.tensor_scalar(out=impacc[:, sub, 1:257], in0=src, scalar1=rd[:, ri, sub:sub + 1], scalar2=None, op0=ALU.mult),
                           reads=[b_ps_i[sub // 2], b_rd], writes=[b_imp[sub]])
                    else:
                        op("vector", lambda e, sub=sub, src=src, ri=ri: e.scalar_tensor_tensor(out=impacc[:, sub, 1:257], in0=src, scalar=rd[:, ri, sub:sub + 1], in1=impacc[:, sub, 1:257],
                                                                                            op0=ALU.mult, op1=ALU.add),
                           reads=[b_ps_i[sub // 2], b_rd, b_imp[sub]], writes=[b_imp[sub]])

        def topk_part(qt):
            for sub in range(4):
                st = 4 * qt + sub
                sc = impacc[:, sub, 1:257]
                op("vector", lambda e, sub=sub, st=st: e.tensor_tensor(out=impacc[:, sub, 2 * st:2 * st + 3], in0=impacc[:, sub, 2 * st:2 * st + 3], in1=fpat[:, 0:3], op=ALU.add),
                   reads=[b_imp[sub], b_fpat], writes=[b_imp[sub]])
                op("vector", lambda e, sub=sub: e.tensor_scalar(out=impacc[:, sub, 1:2], in0=impacc[:, sub, 1:2], scalar1=4000.0, scalar2=None, op0=ALU.add), reads=[b_imp[sub]], writes=[b_imp[sub]])
                op("vector", lambda e, sc=sc: e.max(out=m8[:, 0, :], in_=sc), reads=[b_imp[sub]], writes=[b_m8])
                op("vector", lambda e, sc=sc: e.match_replace(out=mr[:], in_to_replace=m8[:, 0, :], in_values=sc, imm_value=-1e9), reads=[b_imp[sub], b_m8], writes=[b_mr])
                op("vector", lambda e: e.max(out=m8[:, 1, :], in_=mr[:]), reads=[b_mr], writes=[b_m8])
                op("vector", lambda e: e.tensor_reduce(out=m8[:, 2, 0:1], in_=m8[:, 1, :], axis=AX.X, op=ALU.min), reads=[b_m8], writes=[b_m8])
                op("vector", lambda e, sub=sub, sc=sc: e.tensor_scalar(out=selb[:, sub, 96:352], in0=sc, scalar1=m8[:, 2, 0:1], scalar2=NEG, op0=ALU.is_lt, op1=ALU.mult),
                   reads=[b_imp[sub], b_m8], writes=[b_selb[sub]])

        def qaug_part(qt):
            k = qt % 2
            nv = (qt + 1 + 3) // 4
            for v in range(nv):
                bi = nxt("t", 2)
                pt = ps_t[bi]
                for sub in range(4):
                    op("tensor", lambda e, sub=sub, v=v, pt=pt: e.transpose(out=pt[:, sub * 128:(sub + 1) * 128], in_=selb[:, sub, 32 * v:32 * v + 128], identity=idf[:]),
                       reads=[b_selb[sub], b_idf], writes=[b_ps_t[bi]], pe_accum=True)
                for h in range(2):
                    op("vector", lambda e, h=h, v=v, pt=pt: e.tensor_copy(out=qaug[96:128, h, v, :], in_=pt[96:128, :]),
                       reads=[b_ps_t[bi]], writes=[b_qaug[h][v]])
                    op("gpsimd", lambda e, h=h, v=v: e.tensor_copy(out=qaug[0:96, h, v, :], in_=q_sb[k][:, h, :]), reads=[b_q[k]], writes=[b_qaug[h][v]])

        def win_part(qt):
            k = qt % 2
            tiles = list(range(4, 8)) if qt == 0 else list(range(8))
            items = [(h, n, i) for h in range(2) for n, i in enumerate(tiles)]
            pos = {}
            pend = None

            def pv(item):
                h, n, i = item
                ei, po = pos[item]
                op("tensor", lambda e, i=i, ei=ei, po=po, n=n: e.matmul(ps_o[po][0:97, :], lhsT=vwa[k][:, i, 0:97], rhs=e_sb[ei][:], start=(n == 0), stop=(n == len(tiles) - 1)),
                   reads=[b_vwa[k], b_e[ei]], writes=[b_ps_o[po]], pe_accum=True)
                if n == len(tiles) - 1:
                    finish_branch(qt, po, h, 2, False)

            po_h = {}
            for item in items:
                h, n, i = item
                if n == 0:
                    po_h[h] = nxt("o", 2)
                si = nxt("s", 2)
                ei = nxt("e", 3)
                pos[item] = (ei, po_h[h])
                op("tensor", lambda e, i=i, si=si, h=h: e.matmul(ps_s[si][:], lhsT=kw_sb[k][:, i * 128:(i + 1) * 128], rhs=q_sb[k][:, h, :], start=True, stop=True),
                   reads=[b_kw[k], b_q[k]], writes=[b_ps_s[si]], pe_accum=True)
                op("scalar", lambda e, si=si, ei=ei: e.activation(out=e_sb[ei][:], in_=ps_s[si][:], func=AF.Exp), reads=[b_ps_s[si]], writes=[b_e[ei]])
                op("vector", lambda e, i=i, ei=ei: e.tensor_tensor(out=e_sb[ei][:], in0=e_sb[ei][:], in1=masks[:, i, :], op=ALU.mult),
                   reads=[b_e[ei], b_masks], writes=[b_e[ei]])
                if pend is not None:
                    pv(pend)
                pend = item
            pv(pend)

        def slc_part(qt):
            k = qt % 2
            nkt = 4 * (qt + 1)
            items = [(h, kt) for h in range(2) for kt in range(nkt)]
            pos = {}
            pend = None
            po_h = {}

            def pv(item):
                h, kt = item
                ei, po = pos[item]
                op("tensor", lambda e, kt=kt, ei=ei, po=po: e.matmul(ps_o[po][0:97, :], lhsT=vsa[:, kt, 0:97], rhs=e_sb[ei][:], start=(kt == 0), stop=(kt == nkt - 1)),
                   reads=[b_vsa, b_e[ei]], writes=[b_ps_o[po]], pe_accum=True)
                if kt == nkt - 1:
                    finish_branch(qt, po, h, 1, False)

            for item in items:
                h, kt = item
                if kt == 0:
                    po_h[h] = nxt("o", 2)
                si = nxt("s", 2)
                ei = nxt("e", 3)
                pos[item] = (ei, po_h[h])
                v = kt // 16
                diag = kt >= 4 * qt
                op("tensor", lambda e, kt=kt, si=si, v=v, h=h: e.matmul(ps_s[si][:], lhsT=ksa[:, kt * 128:(kt + 1) * 128], rhs=qaug[:, h, v, :], start=True, stop=True),
                   reads=[b_ksa, b_qaug[h][v]], writes=[b_ps_s[si]], pe_accum=True)
                op("scalar", lambda e, si=si, ei=ei: e.activation(out=e_sb[ei][:], in_=ps_s[si][:], func=AF.Exp), reads=[b_ps_s[si]], writes=[b_e[ei]])
                if diag:
                    op("vector", lambda e, kt=kt, ei=ei: e.tensor_tensor(out=e_sb[ei][:], in0=e_sb[ei][:], in1=masks[:, 12 + kt - 4 * qt, :], op=ALU.mult),
                       reads=[b_e[ei], b_masks], writes=[b_e[ei]])
                if pend is not None:
                    pv(pend)
                pend = item
            pv(pend)

        def out_part(qt):
            k = qt % 2
            tok = slice(qt * 512, (qt + 1) * 512)
            for h in range(2):
                bi = nxt("t", 2)
                pt = ps_t[bi]
                for sub in range(4):
                    op("tensor", lambda e, sub=sub, h=h, pt=pt: e.transpose(out=pt[0:96, sub * 128:(sub + 1) * 128], in_=oacc[k][:, sub, h, :], identity=idf[:]),
                       reads=[b_oacc[k], b_idf], writes=[b_ps_t[bi]], pe_accum=True)
                op("vector", lambda e, h=h, pt=pt: e.tensor_copy(out=oTo[k][:, h, :], in_=pt[0:96, :]), reads=[b_ps_t[bi]], writes=[b_oTo[k]])
            op("sync", lambda e: e.dma_start(out=ysl(64, 256, qt * 512, 512).rearrange("(h d) t -> d h t", d=96), in_=oTo[k][:]), reads=[b_oTo[k]], writes=[b_ymc], dma=True)

        nq = self.nqt if hasattr(self, "nqt") else NQT
        if stage < 4:
            nq = 0
        if nq > 0:
            load_tile(0)
            cmp_part(0)
            if stage >= 4.2:
                topk_part(0)
        for qt in range(nq):
            if stage >= 4.3:
                qaug_part(qt)
            if stage >= 4.4:
                win_part(qt)
            pool_chunk(qt)
            if qt + 1 < nq and stage >= 4.7:
                load_tile(qt + 1)
                cmp_part(qt + 1)
                topk_part(qt + 1)
            if stage >= 4.5:
                slc_part(qt)
            if stage >= 4.6:
                out_part(qt)
            if stage < 4.7:
                break
import ml_dtypes
from concourse.bass_utils import run_bass_kernel_spmd

_BF = ml_dtypes.bfloat16
_NCORE = 8
_GROUPS = [[0, 1, 2, 3], [4, 5, 6, 7]]
_DBG = {}
_CC_CHAIN = Buf("cc_chain")


def _reset_chain():
    _CC_CHAIN.w = None
    _CC_CHAIN.r = {}


def _gather(P, c_ap, b_c, g_ap, b_g):
    if _DBG.get("nocc"):
        rows = c_ap.shape[0]
        for r in range(4):
            P.op("gpsimd", lambda e, r=r: e.dma_start(out=g_ap[r * rows:(r + 1) * rows, :], in_=c_ap), reads=[b_c], writes=[b_g], dma=True)
        return
    P.op("gpsimd", lambda e: e.collective_compute("AllGather", ALU.bypass, replica_groups=_GROUPS, ins=[c_ap], outs=[g_ap]),
         reads=[b_c, _CC_CHAIN], writes=[b_g, _CC_CHAIN], cc=True)


def build_fused():
    nc = bass.Bass("TRN2", target_bir_lowering=False)
    _reset_chain()
    P = Prog(nc)
    dram = Dram(nc, P)
    dram.dump = tuple(_DBG.get("dump", ()))
    ident = dram.din("ident", [128, 128], F32)
    xin = dram.din("xin", [NTOK, D], F32)
    xout = dram.dout("xout", [NTOK, D], F32)
    op = P.op

    jcache = {}

    def jexpr(e):
        if "v" not in jcache:
            pid = e.partition_id()
            j = e.snap(pid % 4, min_val=0, max_val=3)
            g = e.snap(j // 2, min_val=0, max_val=1)
            jo = e.snap(g * 2 + (1 - (j % 2)), min_val=0, max_val=3)
            jcache["v"] = dict(j=j, g=g, jo=jo)
        return jcache["v"]

    cur, b_cur = xin, None
    tcount = 0
    ym_loc = b_ym_loc = None
    for l in range(3):
        P.push_scope()
        tb = TB(nc, P, ident, dram)
        if l > 0:
            pl = "l%d" % (l - 1)
            x2 = dram.dscr("x2", [NTOK, D], F32)
            tcount = tb.stage_mix(pl, ym_loc, b_ym_loc, cur, b_cur, x2, dram.b["x2"], tcount)
            if l == 2:
                tcount = tb.stage_ffn(pl + "_ffn2", x2, dram.b["x2"], xout, dram.b["xout"], tcount)
                P.pop_scope()
                break
            x3 = dram.dscr("x3", [NTOK, D], F32)
            tcount = tb.stage_ffn(pl + "_ffn2", x2, dram.b["x2"], x3, dram.b["x3"], tcount)
            cur, b_cur = x3, dram.b["x3"]
        ll = "l%d" % l
        x1 = dram.dscr("x1", [NTOK, D], F32)
        tcount = tb.stage_ffn(ll + "_ffn1", cur, b_cur, x1, dram.b["x1"], tcount)
        cur, b_cur = x1, dram.b["x1"]
        def ctensor(name, shp, dt):
            ap = dram.dscr(name, shp, dt)
            return ap, dram.b[name]
        def gtensor(name, shp, dt):
            if name not in dram.ap:
                dram.ap[name] = nc.dram_tensor(name, list(shp), dt).ap()
                dram.b[name] = P.buf(name)
            return dram.ap[name], dram.b[name]
        C_q, b_Cq = ctensor("c_q", [8, 128, NTOK], BF16)
        C_u, b_Cu = ctensor("c_u", [4, 64, NTOK], F32)
        C_k, b_Ck = ctensor("c_k", [8, 96, NTOK], BF16)
        C_vs, b_Cvs = ctensor("c_vs", [2, NTOK, 96], BF16)
        C_vw, b_Cvw = ctensor("c_vw", [2, NTOK, 96], BF16)
        C_g, b_Cg = ctensor("c_g", [NTOK, 24], F32)
        G_q, b_Gq = gtensor("g_q", [8, 4, 128, NTOK], BF16)
        G_u, b_Gu = gtensor("g_u", [4, 4, 64, NTOK], F32)
        G_k, b_Gk = gtensor("g_k", [8, 4, 96, NTOK], BF16)
        G_vs, b_Gvs = gtensor("g_vs", [2, 4, NTOK, 96], BF16)
        G_vw, b_Gvw = gtensor("g_vw", [2, 4, NTOK, 96], BF16)
        G_g, b_Gg = ctensor("g_g", [4 * NTOK, 24], F32)
        outs = {"gates": (C_g, b_Cg)}
        outs["qT"] = (C_q, b_Cq)
        outs["uT"] = ([C_u[p] for p in range(4)], b_Cu)
        outs["kT"] = (C_k, b_Ck)
        outs["vs"] = (C_vs, b_Cvs)
        outs["vw"] = (C_vw, b_Cvw)
        tcount = tb.stage_proj(ll, cur, b_cur, outs, tcount)
        P.pop_scope()
        if _DBG.get("stop3") and l == 1:
            P.wait_all("sync", list(dram.b.values()))
            P.finish()
            return nc
        if _DBG.get("dump") and l == 0:
            d0 = dram.dscr("dbg_ck0_pre", [96, NTOK], BF16)
            op("sync", lambda e: e.dma_start(out=d0, in_=C_k[0]), reads=[b_Ck], writes=[dram.b["dbg_ck0_pre"]], dma=True)
            dram.outs += [dram.b["dbg_ck0_pre"]]
            P.barrier()
        for i in range(8):
            _gather(P, C_q[i], b_Cq, G_q[i].rearrange("r d t -> (r d) t"), b_Gq)
        for i in range(8):
            _gather(P, C_k[i], b_Ck, G_k[i].rearrange("r d t -> (r d) t"), b_Gk)
        for i in range(4):
            _gather(P, C_u[i], b_Cu, G_u[i].rearrange("r c t -> (r c) t"), b_Gu)
        for i in range(2):
            _gather(P, C_vs[i], b_Cvs, G_vs[i].rearrange("r t d -> (r t) d"), b_Gvs)
            _gather(P, C_vw[i], b_Cvw, G_vw[i].rearrange("r t d -> (r t) d"), b_Gvw)
        _gather(P, C_g, b_Cg, G_g, b_Gg)
        P.barrier()
        if _DBG.get("dump") and l == 0:
            d1 = dram.dscr("dbg_ck0", [96, NTOK], BF16); d2 = dram.dscr("dbg_gk0", [4 * 96, NTOK], BF16)
            op("sync", lambda e: e.dma_start(out=d1, in_=C_k[0]), reads=[b_Ck], writes=[dram.b["dbg_ck0"]], dma=True)
            op("sync", lambda e: e.dma_start(out=d2, in_=G_k[0].rearrange("r d t -> (r d) t")), reads=[b_Gk], writes=[dram.b["dbg_gk0"]], dma=True)
            dram.outs += [dram.b["dbg_ck0"], dram.b["dbg_gk0"]]
        loc = {}

        def mk(name, shp, dt):
            ap = dram.dscr("l_" + name, shp, dt)
            loc[name] = (ap, dram.b["l_" + name])
            return ap, loc[name][1]

        l_qT, b_lq = mk("qT", [4, 96, S], BF16)
        kinds = ["kcT", "vcT", "ksT", "kwT"]
        for name in kinds:
            mk(name, [96, S], BF16)
        mk("uT", [64, S], F32); mk("vs", [S, 96], BF16); mk("vw", [S, 96], BF16)
        gk5 = G_k.rearrange("(k g) r d t -> k g r d t", g=2)
        gq5 = G_q.rearrange("(p h) r d t -> p h r d t", h=2)
        for h2 in range(2):
            def q_own(e, h2=h2):
                return e.dma_start(out=l_qT[h2].rearrange("d (r t) -> d r t", r=4),
                                   in_=gq5[bass.ds(jexpr(e)["j"], 1), h2, :, 0:96, :].rearrange("o r d t -> (o d) r t"))

            def q_oth(e, h2=h2):
                return e.dma_start(out=l_qT[2 + h2].rearrange("d (r t) -> d r t", r=4),
                                   in_=gq5[bass.ds(jexpr(e)["jo"], 1), h2, :, 0:96, :].rearrange("o r d t -> (o d) r t"))
            op("sync", q_own, reads=[b_Gq], writes=[b_lq], dma=True)
            op("sync", q_oth, reads=[b_Gq], writes=[b_lq], dma=True)

        def uf(e):
            return e.dma_start(out=loc["uT"][0].rearrange("c (r t) -> c r t", r=4),
                               in_=G_u[bass.ds(jexpr(e)["j"], 1), :, :, :].rearrange("o r c t -> (o c) r t"))
        op("sync", uf, reads=[b_Gu], writes=[loc["uT"][1]], dma=True)
        for ki, name in enumerate(kinds):
            def kf(e, name=name, ki=ki):
                return e.dma_start(out=loc[name][0].rearrange("d (r t) -> d r t", r=4),
                                   in_=gk5[ki, bass.ds(jexpr(e)["g"], 1), :, :, :].rearrange("o r d t -> (o d) r t"))
            op("sync", kf, reads=[b_Gk], writes=[loc[name][1]], dma=True)
        for name, Gv, b_Gv in (("vs", G_vs, b_Gvs), ("vw", G_vw, b_Gvw)):
            for r in range(4):
                rows = slice(r * NTOK, (r + 1) * NTOK)

                def vf(e, name=name, rows=rows, r=r, Gv=Gv):
                    return e.dma_start(out=loc[name][0][rows, :], in_=Gv[bass.ds(jexpr(e)["g"], 1), r, :, :].rearrange("o t d -> (o t) d"))
                op("sync", vf, reads=[b_Gv], writes=[loc[name][1]], dma=True)
        loc["gates"] = (G_g, b_Gg)
        if _DBG.get("stop1") is not None and _DBG.get("stop1") == l:
            P.wait_all("sync", list(dram.b.values()))
            P.finish()
            return nc
        ymc = dram.dscr("c_ym", [4, 256, NTOK], BF16)
        b_ymc = dram.b["c_ym"]
        P.push_scope()
        bb = BB(nc, P, dram)
        bb.build(ll, loc, ymc, b_ymc)
        P.pop_scope()
        g_ym, b_gym = gtensor("g_ym", [4, 2, 4, 128, NTOK], BF16)
        for tj in range(4):
            for hf in range(2):
                _gather(P, ymc[tj, hf * 128:(hf + 1) * 128, :], b_ymc, g_ym[tj, hf].rearrange("r i t -> (r i) t"), b_gym)
        P.barrier()
        ym_loc = dram.dscr("l_ym", [D, NTOK], BF16)
        b_ym_loc = dram.b["l_ym"]
        for hf in range(2):
            def yf(e, g_ym=g_ym, ym_loc=ym_loc, hf=hf):
                return e.dma_start(out=ym_loc[hf * 512:(hf + 1) * 512, :], in_=g_ym[bass.ds(jexpr(e)["j"], 1), hf, :, :, :].rearrange("o r i t -> (o r i) t"))
            op("sync", yf, reads=[b_gym], writes=[b_ym_loc], dma=True)
        if _DBG.get("stop2") is not None and _DBG.get("stop2") == l:
            P.wait_all("sync", list(dram.b.values()))
            P.finish()
            return nc
    P.wait_all("sync", dram.outs)
    P.finish()
    return nc


def _yperm():
    src_of = lambda r, row: (64 * r + row) if row < 64 else (256 + 192 * r + (row - 64))
    return np.array([src_of(r, hf * 128 + i) for hf in range(2) for r in range(4) for i in range(128)])


_YPERM = _yperm()


def kernel(**inputs):
    inputs = {k: np.asarray(v) for k, v in inputs.items()}
    x = inputs["x"].reshape(2 * S, D).astype(np.float32, copy=False)
    nc = build_fused()
    C = b_consts()
    maps = []
    for c in range(_NCORE):
        b, j = divmod(c, 4)
        m = {"ident": C["ident_f"], "ident_b": C["ident_b"], "masks": C["masks"], "kpat": C["kpat"], "wimp": C["wimp"], "fpat": C["fpat"]}
        a, A = pool_consts(j)
        m["pool_a"] = a
        gs = np.zeros((6, 24), np.float32)
        gs[np.arange(6), 6 * j + np.arange(6)] = 1.0
        m["gsel"] = gs
        m["pool_A"] = A
        m["xin"] = x[c * NTOK:(c + 1) * NTOK]
        for l in range(2):
            p = "l%d" % l
            m[p + "_ffn1_wg"] = inputs["ffn1_w_gate"][l]; m[p + "_ffn1_wu"] = inputs["ffn1_w_up"][l]; m[p + "_ffn1_wd"] = inputs["ffn1_w_down"][l]
            m[p + "_ffn1_lg"] = inputs["ln1_g"][l]; m[p + "_ffn1_lb"] = inputs["ln1_b"][l]
            m[p + "_ffn2_wg"] = inputs["ffn2_w_gate"][l]; m[p + "_ffn2_wu"] = inputs["ffn2_w_up"][l]; m[p + "_ffn2_wd"] = inputs["ffn2_w_down"][l]
            m[p + "_ffn2_lg"] = inputs["ln3_g"][l]; m[p + "_ffn2_lb"] = inputs["ln3_b"][l]
            m[p + "_w_in"] = inputs["w_in"][l]; m[p + "_b_gate"] = inputs["b_gate"][l]
            m[p + "_w_out"] = inputs["w_out"][l][_YPERM]
            m[p + "_ln2_g"] = inputs["ln2_g"][l]; m[p + "_ln2_b"] = inputs["ln2_b"][l]
            m[p + "_cmp_k_w1"] = inputs["cmp_k_w1"][l]; m[p + "_cmp_k_w2"] = inputs["cmp_k_w2"][l]; m[p + "_cmp_k_pos"] = inputs["cmp_pos_k"][l]
            m[p + "_cmp_v_w1"] = inputs["cmp_v_w1"][l]; m[p + "_cmp_v_w2"] = inputs["cmp_v_w2"][l]; m[p + "_cmp_v_pos"] = inputs["cmp_pos_v"][l]
            m[p + "_pool_w"] = inputs["pool_w"][l][j]
            m[p + "_pool_sc"] = inputs["pool_scale"][l][64 * j:64 * j + 64].reshape(64, 1)
        maps.append({k: np.ascontiguousarray(v) for k, v in m.items()})
    res = run_bass_kernel_spmd(nc, maps, core_ids=list(range(_NCORE))).results
    _DBG["res"] = res if _DBG.get("dump") else None
    out = np.concatenate([r["xout"] for r in res], axis=0)
    return out.reshape(2, S, D).astype(np.float32, copy=False)
```

```python
import numpy as np
import concourse.bass as bass
import concourse.mybir as mybir
from contextlib import ExitStack

F32 = mybir.dt.float32
BF16 = mybir.dt.bfloat16
AF = mybir.ActivationFunctionType
ALU = mybir.AluOpType
AX = mybir.AxisListType

ENGS = ("sync", "tensor", "vector", "scalar", "gpsimd")
N_DMA_SEMS = 12


class Buf:
    __slots__ = ("name", "w", "r", "psum")

    def __init__(self, name, psum=False):
        self.name = name
        self.psum = psum
        self.w = None
        self.r = {}


class Prog:
    def __init__(self, nc):
        self.nc = nc
        self.es = ExitStack()
        self.streams = {e: [] for e in ENGS}
        self.sem = {}
        self.cnt = {}
        for e in ENGS:
            self.sem[e] = nc.alloc_semaphore("c_" + e)
            self.cnt[e] = 0
        self.dsem = {}
        self.dcnt = {}
        self.drr = {}
        for e in ("sync", "gpsimd", "scalar"):
            for i in range(N_DMA_SEMS):
                k = "d_%s_%d" % (e, i)
                self.sem[k] = nc.alloc_semaphore(k)
                self.cnt[k] = 0
            self.drr[e] = 0
        self.sem["cc"] = nc.alloc_semaphore("cc_sem")
        self.cnt["cc"] = 0
        self.seen = {e: {} for e in ENGS}
        self.nbuf = 0
        self.block_hooks = []
        self.scopes = []
        self.scope_ctr = 0
        self.scope_id = 0

    def sbuf(self, name, shape, dtype):
        t = self.es.enter_context(self.nc.sbuf_tensor("sb%d_%s" % (self.scope_id, name), list(shape), dtype))
        return t

    def psum(self, name, shape, dtype=F32):
        t = self.es.enter_context(self.nc.psum_tensor("pp%d_%s" % (self.scope_id, name), list(shape), dtype))
        return t

    def push_scope(self):
        self.scopes.append(self.es)
        self.es = ExitStack()
        self.scope_ctr += 1
        self.scope_id = self.scope_ctr

    def pop_scope(self):
        self.barrier()
        self.es.close()
        self.es = self.scopes.pop()

    def buf(self, name=None, psum=False):
        self.nbuf += 1
        return Buf(name or ("b%d" % self.nbuf), psum)

    def pbuf(self):
        return self.buf(psum=True)

    def _need(self, eng, deps):
        out = []
        seen = self.seen[eng]
        for k, v in deps.items():
            if seen.get(k, 0) < v:
                seen[k] = v
                out.append((k, v))
        return out

    def op(self, eng, fn, reads=(), writes=(), dma=False, pe_accum=False, cc=False):
        deps = {}

        def add(tok):
            if tok is None:
                return
            k, v = tok
            if deps.get(k, 0) < v:
                deps[k] = v

        for b in reads:
            add(b.w)
            if b.psum:
                for k, v in b.r.items():
                    if k != eng:
                        add((k, v))
        for b in writes:
            if not (pe_accum and b.w is not None and b.w[0] == "tensor"):
                add(b.w)
            for k, v in b.r.items():
                add((k, v))
        if dma:
            rr = self.drr[eng]
            self.drr[eng] = (rr + 1) % N_DMA_SEMS
            dk = "d_%s_%d" % (eng, rr)
            if self.cnt[dk] > 0:
                add((dk, self.cnt[dk]))
        if eng == "tensor":
            deps.pop("tensor", None)
        waits = self._need(eng, deps)
        if cc:
            self.cnt["cc"] += 1
            tok = ("cc", self.cnt["cc"])
            inc = 1
        elif dma:
            self.cnt[dk] += 16
            tok = (dk, self.cnt[dk])
            inc = 16
        else:
            self.cnt[eng] += 1
            tok = (eng, self.cnt[eng])
            inc = 1
        semh = self.sem[tok[0]]
        wl = [(self.sem[k], v) for k, v in waits]

        def emit(e, fn=fn, wl=wl, semh=semh, inc=inc):
            for s, v in wl:
                e.wait_ge(s, v)
            fn(e).then_inc(semh, inc)

        self.streams[eng].append(emit)
        for b in writes:
            b.w = tok
            b.r = {}
        for b in reads:
            if b.r.get(tok[0], 0) < tok[1]:
                b.r[tok[0]] = tok[1]
        return tok

    def wait_all(self, eng, bufs):
        deps = {}
        for b in bufs:
            if b.w is not None:
                k, v = b.w
                if deps.get(k, 0) < v:
                    deps[k] = v
        wl = [(self.sem[k], v) for k, v in self._need(eng, deps)]

        def emit(e, wl=wl):
            for s, v in wl:
                e.wait_ge(s, v)

        self.streams[eng].append(emit)

    def barrier(self):
        allc = {k: v for k, v in self.cnt.items() if v > 0}
        for eng in ENGS:
            wl = [(self.sem[k], v) for k, v in self._need(eng, dict(allc))]

            def emit(e, wl=wl):
                for s, v in wl:
                    e.wait_ge(s, v)

            self.streams[eng].append(emit)

    def block_break(self):
        self.barrier()
        for e in ENGS:
            self.streams[e].append(None)

    def finish(self):
        nc = self.nc
        segs = {e: [[]] for e in ENGS}
        for e in ENGS:
            for f in self.streams[e]:
                if f is None:
                    segs[e].append([])
                else:
                    segs[e][-1].append(f)
        nseg = len(segs["sync"])
        for i in range(nseg):
            for hook in self.block_hooks:
                hook()
            with nc.Block() as block:
                @block.sync
                def _(e):
                    for f in segs["sync"][i]:
                        f(e)

                @block.tensor
                def _(e):
                    for f in segs["tensor"][i]:
                        f(e)

                @block.vector
                def _(e):
                    for f in segs["vector"][i]:
                        f(e)

                @block.scalar
                def _(e):
                    for f in segs["scalar"][i]:
                        f(e)

                @block.gpsimd
                def _(e):
                    for f in segs["gpsimd"][i]:
                        f(e)
        self.es.close()


class Dram:
    def __init__(self, nc, P):
        self.nc = nc
        self.P = P
        self.ap = {}
        self.b = {}
        self.outs = []
        self.dump = ()
        self.arena = {}
        self.ARENA_ELEMS = {"bf16": 120 * 1024 * 1024, "f32": 60 * 1024 * 1024}

    def din(self, name, shape, dt):
        if name not in self.ap:
            self.ap[name] = self.nc.dram_tensor(name, list(shape), dt, kind="ExternalInput").ap()
        return self.ap[name]

    def dout(self, name, shape, dt):
        self.ap[name] = self.nc.dram_tensor(name, list(shape), dt, kind="ExternalOutput").ap()
        self.b[name] = self.P.buf(name)
        self.outs.append(self.b[name])
        return self.ap[name]

    def dscr(self, name, shape, dt):
        if name not in self.ap:
            if name in self.dump:
                self.ap[name] = self.nc.dram_tensor(name, list(shape), dt, kind="ExternalOutput").ap()
            else:
                n = 1
                for d in shape:
                    n *= int(d)
                key = "bf16" if dt == BF16 else "f32"
                if key not in self.arena:
                    cap = self.ARENA_ELEMS[key]
                    self.arena[key] = [self.nc.dram_tensor("arena_" + key, [cap], dt).ap(), 0, cap]
                ar = self.arena[key]
                off = ar[1]
                n_al = (n + 2047) // 2048 * 2048
                assert off + n_al <= ar[2], ("arena overflow", key, name)
                ar[1] = off + n_al
                flat = ar[0][off:off + n]
                if len(shape) == 1:
                    self.ap[name] = flat
                else:
                    names = ["a%d" % i for i in range(len(shape))]
                    pat = "(" + " ".join(names) + ") -> " + " ".join(names)
                    kw = {nm: int(d) for nm, d in zip(names[1:], shape[1:])}
                    self.ap[name] = flat.rearrange(pat, **kw)
            self.b[name] = self.P.buf(name)
        return self.ap[name]

D = 1024
DFF = 2816
NF = DFF // 128
DIN = 2200
NTOK = 4096
TT = 512
NTILE = NTOK // TT
ALPHA = float((2.0 * 2) ** 0.25)
LN_EPS = 1e-5
QSCALE = float(96 ** -0.5)
TCOLS = [256 + 96 * h for h in range(8)] + [1024, 1120, 1216, 1312, 1408, 1504, 1792, 1888]


class TB:
    def __init__(self, nc, P, ident, dram):
        self.nc = nc
        self.P = P
        self.dram = dram
        self.ident = ident
        self.b_dram = dram.b
        self.idt = P.sbuf("idt", [128, 128], F32); self.b_idt = P.buf()
        self.xt = [P.sbuf("xt%d" % i, [128, 4, D], F32) for i in range(2)]
        self.b_xt = [[P.buf() for s in range(4)] for i in range(2)]
        self.xT = P.sbuf("xT", [128, 8, TT], BF16); self.b_xT = [P.buf() for c in range(8)]
        self.hT = P.sbuf("hT", [128, NF, TT], BF16); self.b_hT = [P.buf() for f in range(NF)]
        self.wres = P.sbuf("wres", [128, NF * D], BF16); self.b_wres = [P.buf() for f in range(NF)]
        self.ring = P.sbuf("ring", [128, 3, 2, 8, 128], BF16); self.b_ring = [P.buf() for i in range(3)]
        self.stg = [P.sbuf("stg%d" % i, [128, DFF], F32) for i in range(2)]; self.b_stg = [P.buf() for i in range(2)]
        self.stgb = [P.sbuf("stgb%d" % i, [128, DFF], BF16) for i in range(2)]; self.b_stgb = [P.buf() for i in range(2)]
        self.lng = P.sbuf("lng", [128, D], F32); self.b_lng = P.buf()
        self.lnb = P.sbuf("lnb", [128, D], F32); self.b_lnb = P.buf()
        self.sg = [P.sbuf("sg%d" % i, [128, TT], F32) for i in range(2)]; self.b_sg = [P.buf() for i in range(2)]
        self.st = P.sbuf("st", [128, 4, 2, 6], F32); self.b_st = [P.buf() for s in range(4)]
        self.mv = P.sbuf("mv", [128, 4, 4], F32); self.b_mv = [P.buf() for s in range(4)]
        self.oT = P.sbuf("oT", [128, 16, TT], BF16); self.b_oT = [P.buf() for i in range(16)]
        P.op("gpsimd", lambda e: e.memset(self.oT[96:128, :, :], 0.0), writes=self.b_oT)
        self.ou = P.sbuf("ou", [128, 2, TT], F32); self.b_ou = [P.buf() for i in range(2)]
        self.ov = P.sbuf("ov", [128, 4, 384], BF16); self.b_ov = [P.buf() for i in range(4)]
        self.og = P.sbuf("og", [128, 4, 24], F32); self.b_og = [P.buf() for i in range(4)]
        self.bg = P.sbuf("bg", [128, 24], F32); self.b_bg = P.buf()
        self.ps = [P.psum("ps%d" % i, [128, 512], F32) for i in range(8)]
        self.b_ps = [P.pbuf() for i in range(8)]
        self.rr = 0
        P.op("sync", lambda e: e.dma_start(out=self.idt[:], in_=self.ident), writes=[self.b_idt], dma=True)

    def din(self, name, shape, dt):
        return self.dram.din(name, shape, dt)

    def dscr(self, name, shape, dt):
        return self.dram.dscr(name, shape, dt)

    def cast_eng(self):
        self.rr += 1
        return ("vector", "gpsimd", "scalar")[self.rr % 3]

    def cast(self, eng, out, in_, reads, writes):
        if eng == "scalar":
            self.P.op("scalar", lambda e: e.activation(out=out, in_=in_, func=AF.Copy), reads=reads, writes=writes)
        else:
            self.P.op(eng, lambda e: e.tensor_copy(out=out, in_=in_), reads=reads, writes=writes)

    def load_res(self, w, nchunks, width):
        P = self.P
        per = max(1, DFF // width)
        f = 0
        i = 0
        while f < nchunks:
            n = min(per, nchunks - f)
            k = i % 2
            src = w[f * 128:(f + n) * 128, :].rearrange("(n p) d -> p n d", p=128)
            dst = self.stg[k][:, 0:n * width].rearrange("p (n d) -> p n d", d=width)
            P.op("sync", lambda e, dst=dst, src=src: e.dma_start(out=dst, in_=src), writes=[self.b_stg[k]], dma=True)
            self.cast(self.cast_eng(), self.wres[:, f * width:(f + n) * width], self.stg[k][:, 0:n * width],
                      [self.b_stg[k]], [self.b_wres[j] for j in range(f, f + n)])
            f += n
            i += 1

    def prep_gu(self, wg, wu, scr, b_scr):
        P = self.P
        i = 0
        for c in range(8):
            for gi, w in ((0, wg), (1, wu)):
                k = i % 2
                P.op("sync", lambda e, k=k, w=w, c=c: e.dma_start(out=self.stg[k][:], in_=w[c * 128:(c + 1) * 128, :]),
                     writes=[self.b_stg[k]], dma=True)
                self.cast(self.cast_eng(), self.stgb[k][:], self.stg[k][:], [self.b_stg[k]], [self.b_stgb[k]])
                dst = scr[:, :, gi, c, :].rearrange("f p j -> p f j")
                src = self.stgb[k][:].rearrange("p (f j) -> p f j", j=128)
                P.op("gpsimd", lambda e, dst=dst, src=src: e.dma_start(out=dst, in_=src), reads=[self.b_stgb[k]],
                     writes=[b_scr], dma=True)
                i += 1

    def load_ln(self, g, b):
        P = self.P
        P.op("sync", lambda e: e.dma_start(out=self.lng[:], in_=g.partition_broadcast(128)), writes=[self.b_lng], dma=True)
        P.op("sync", lambda e: e.dma_start(out=self.lnb[:], in_=b.partition_broadcast(128)), writes=[self.b_lnb], dma=True)

    def load_x(self, src, b_src, t, k):
        P = self.P
        s_ap = src[t * TT:(t + 1) * TT, :].rearrange("(s p) d -> p s d", p=128)
        P.op("sync", lambda e: e.dma_start(out=self.xt[k][:], in_=s_ap), reads=[b_src] if b_src else [], writes=self.b_xt[k], dma=True)

    def transposes(self, k):
        P = self.P
        for c in range(8):
            pk = c % 2
            for s in range(4):
                P.op("tensor", lambda e, c=c, s=s, pk=pk: e.transpose(out=self.ps[pk][:, s * 128:(s + 1) * 128],
                                                                      in_=self.xt[k][:, s, c * 128:(c + 1) * 128], identity=self.idt[:]),
                     reads=[self.b_xt[k][s], self.b_idt], writes=[self.b_ps[pk]], pe_accum=True)
            self.cast("scalar" if c % 2 else "vector", self.xT[:, c, :], self.ps[pk][:], [self.b_ps[pk]], [self.b_xT[c]])

    def gate_up(self, scr, b_scr, t):
        P = self.P
        for f in range(NF):
            slot = (t * NF + f) % 3
            P.op("gpsimd" if f % 2 else "sync", lambda e, f=f, slot=slot: e.dma_start(out=self.ring[:, slot], in_=scr[f]),
                 reads=[b_scr], writes=[self.b_ring[slot]], dma=True)
            pg = 2 + (f % 2) * 2
            pu = pg + 1
            for gi, pp in ((0, pg), (1, pu)):
                for c in range(8):
                    P.op("tensor", lambda e, c=c, gi=gi, pp=pp, slot=slot: e.matmul(self.ps[pp][:], lhsT=self.ring[:, slot, gi, c, :], rhs=self.xT[:, c, :],
                                                                                 start=(c == 0), stop=(c == 7)),
                         reads=[self.b_ring[slot], self.b_xT[c]], writes=[self.b_ps[pp]], pe_accum=True)
            k = f % 2
            P.op("scalar", lambda e, k=k, pg=pg: e.activation(out=self.sg[k][:], in_=self.ps[pg][:], func=AF.Silu),
                 reads=[self.b_ps[pg]], writes=[self.b_sg[k]])
            P.op("vector", lambda e, k=k, pu=pu, f=f: e.tensor_tensor(out=self.hT[:, f, :], in0=self.sg[k][:], in1=self.ps[pu][:], op=ALU.mult),
                 reads=[self.b_sg[k], self.b_ps[pu]], writes=[self.b_hT[f]])

    def down_ln(self, k, lhs, b_lhs, nch, width_w, ysc, dst, b_dst, t):
        P = self.P
        for s in range(4):
            P.op("gpsimd", lambda e, s=s: e.tensor_scalar(out=self.xt[k][:, s, :], in0=self.xt[k][:, s, :], scalar1=ALPHA, scalar2=None, op0=ALU.mult),
                 reads=[self.b_xt[k][s]], writes=[self.b_xt[k][s]])
        for s in range(4):
            for n in range(2):
                pp = 6 + (s * 2 + n) % 2
                for f in range(nch):
                    P.op("tensor", lambda e, s=s, n=n, f=f, pp=pp: e.matmul(self.ps[pp][:], lhsT=lhs[:, f, s * 128:(s + 1) * 128],
                                                                          rhs=self.wres[:, f * width_w + n * 512: f * width_w + (n + 1) * 512],
                                                                          start=(f == 0), stop=(f == nch - 1)),
                         reads=[b_lhs[f], self.b_wres[f]], writes=[self.b_ps[pp]], pe_accum=True)
                P.op("vector", lambda e, s=s, n=n, pp=pp: e.scalar_tensor_tensor(out=self.xt[k][:, s, n * 512:(n + 1) * 512], in0=self.ps[pp][:], scalar=ysc,
                                                                                in1=self.xt[k][:, s, n * 512:(n + 1) * 512], op0=ALU.mult, op1=ALU.add),
                     reads=[self.b_ps[pp], self.b_xt[k][s]], writes=[self.b_xt[k][s]])
            self.layernorm(k, s)
        d_ap = dst[t * TT:(t + 1) * TT, :].rearrange("(s p) d -> p s d", p=128)
        P.op("sync", lambda e: e.dma_start(out=d_ap, in_=self.xt[k][:]), reads=self.b_xt[k], writes=[b_dst], dma=True)

    def layernorm(self, k, s):
        P = self.P
        z = self.xt[k]
        for h in range(2):
            P.op("vector", lambda e, h=h: e.bn_stats(self.st[:, s, h, :], z[:, s, h * 512:(h + 1) * 512]),
                 reads=[self.b_xt[k][s]], writes=[self.b_st[s]])
        P.op("vector", lambda e: e.bn_aggr(self.mv[:, s, 0:2], self.st[:, s, :, :]), reads=[self.b_st[s]], writes=[self.b_mv[s]])
        P.op("vector", lambda e: e.tensor_scalar(out=self.mv[:, s, 2:3], in0=self.mv[:, s, 1:2], scalar1=LN_EPS, scalar2=None, op0=ALU.add),
             reads=[self.b_mv[s]], writes=[self.b_mv[s]])
        P.op("scalar", lambda e: e.sqrt(self.mv[:, s, 3:4], self.mv[:, s, 2:3]), reads=[self.b_mv[s]], writes=[self.b_mv[s]])
        P.op("vector", lambda e: e.reciprocal(self.mv[:, s, 2:3], self.mv[:, s, 3:4]), reads=[self.b_mv[s]], writes=[self.b_mv[s]])
        P.op("vector", lambda e: e.tensor_scalar(out=z[:, s, :], in0=z[:, s, :], scalar1=self.mv[:, s, 0:1], scalar2=self.mv[:, s, 2:3],
                                                 op0=ALU.subtract, op1=ALU.mult),
             reads=[self.b_xt[k][s], self.b_mv[s]], writes=[self.b_xt[k][s]])
        P.op("gpsimd", lambda e: e.tensor_tensor(out=z[:, s, :], in0=z[:, s, :], in1=self.lng[:], op=ALU.mult),
             reads=[self.b_xt[k][s], self.b_lng], writes=[self.b_xt[k][s]])
        P.op("gpsimd", lambda e: e.tensor_tensor(out=z[:, s, :], in0=z[:, s, :], in1=self.lnb[:], op=ALU.add),
             reads=[self.b_xt[k][s], self.b_lnb], writes=[self.b_xt[k][s]])

    def stage_ffn(self, pfx, src, b_src, dst, b_dst, tcount):
        wg = self.din(pfx + "_wg", [D, DFF], F32)
        wu = self.din(pfx + "_wu", [D, DFF], F32)
        wd = self.din(pfx + "_wd", [DFF, D], F32)
        g = self.din(pfx + "_lg", [D], F32)
        b = self.din(pfx + "_lb", [D], F32)
        scr = self.dscr("gus", [NF, 128, 2, 8, 128], BF16)
        b_scr = self.b_dram["gus"]
        self.load_ln(g, b)
        self.prep_gu(wg, wu, scr, b_scr)
        self.load_res(wd, NF, D)
        for t in range(NTILE):
            k = (tcount + t) % 2
            self.load_x(src, b_src, t, k)
            self.transposes(k)
            self.gate_up(scr, b_scr, t)
            self.down_ln(k, self.hT, self.b_hT, NF, D, 0.5, dst, b_dst, t)
        return tcount + NTILE

    def stage_mix(self, pfx, ym, b_ym, src, b_src, dst, b_dst, tcount):
        wo = self.din(pfx + "_w_out", [D, D], F32)
        g = self.din(pfx + "_ln2_g", [D], F32)
        b = self.din(pfx + "_ln2_b", [D], F32)
        self.load_ln(g, b)
        self.load_res(wo, 8, D)
        for t in range(NTILE):
            k = (tcount + t) % 2
            self.load_x(src, b_src, t, k)
            y_ap = ym[:, t * TT:(t + 1) * TT].rearrange("(c p) t -> p c t", p=128)
            self.P.op("gpsimd", lambda e, y_ap=y_ap: e.dma_start(out=self.hT[:, 0:8, :], in_=y_ap), reads=[b_ym], writes=self.b_hT[0:8], dma=True)
            self.down_ln(k, self.hT, self.b_hT, 8, D, 1.0, dst, b_dst, t)
        return tcount + NTILE

    def stage_proj(self, pfx, src, b_src, outs, tcount):
        P = self.P
        win = self.din(pfx + "_w_in", [D, DIN], F32)
        bgate = self.din(pfx + "_b_gate", [24], F32)
        qT, kT, uT, vs, vw, gates = (outs[k][0] for k in ("qT", "kT", "uT", "vs", "vw", "gates"))
        bd = {k: outs[k][1] for k in ("qT", "kT", "uT", "vs", "vw", "gates")}
        P.op("sync", lambda e: e.dma_start(out=self.bg[:], in_=bgate.partition_broadcast(128)), writes=[self.b_bg], dma=True)
        for c in range(8):
            k = c % 2
            P.op("sync", lambda e, k=k, c=c: e.dma_start(out=self.stg[k][:, 0:DIN], in_=win[c * 128:(c + 1) * 128, :]), writes=[self.b_stg[k]], dma=True)
            self.cast(self.cast_eng(), self.wres[:, c * DIN:(c + 1) * DIN], self.stg[k][:, 0:DIN], [self.b_stg[k]], [self.b_wres[c]])
        W = lambda c, a, b_: self.wres[:, c * DIN + a: c * DIN + b_]
        for t in range(NTILE):
            k = (tcount + t) % 2
            self.load_x(src, b_src, t, k)
            self.transposes(k)
            tok = slice(t * TT, (t + 1) * TT)
            for i, col in enumerate(TCOLS):
                pp = 2 + i % 4
                for c in range(8):
                    P.op("tensor", lambda e, c=c, col=col, pp=pp: e.matmul(self.ps[pp][0:96, :], lhsT=W(c, col, col + 96), rhs=self.xT[:, c, :],
                                                                         start=(c == 0), stop=(c == 7)),
                         reads=[self.b_wres[c], self.b_xT[c]], writes=[self.b_ps[pp]], pe_accum=True)
                if i < 8:
                    P.op("scalar", lambda e, i=i, pp=pp: e.mul(self.oT[0:96, i, :], self.ps[pp][0:96, :], QSCALE),
                         reads=[self.b_ps[pp]], writes=[self.b_oT[i]])
                else:
                    P.op("vector", lambda e, i=i, pp=pp: e.tensor_copy(out=self.oT[0:96, i, :], in_=self.ps[pp][0:96, :]),
                         reads=[self.b_ps[pp]], writes=[self.b_oT[i]])
            P.op("sync", lambda e, tok=tok: e.dma_start(out=qT[:, :, tok].rearrange("h d t -> d h t"), in_=self.oT[:, 0:8, :]),
                 reads=self.b_oT[0:8], writes=[bd["qT"]], dma=True)
            P.op("sync", lambda e, tok=tok: e.dma_start(out=kT[:, :, tok].rearrange("h d t -> d h t"), in_=self.oT[0:96, 8:16, :]),
                 reads=self.b_oT[8:16], writes=[bd["kT"]], dma=True)
            for j in range(2):
                pp = 6 + j
                for c in range(8):
                    P.op("tensor", lambda e, c=c, j=j, pp=pp: e.matmul(self.ps[pp][:], lhsT=W(c, j * 128, (j + 1) * 128), rhs=self.xT[:, c, :],
                                                                     start=(c == 0), stop=(c == 7)),
                         reads=[self.b_wres[c], self.b_xT[c]], writes=[self.b_ps[pp]], pe_accum=True)
                P.op("vector", lambda e, j=j, pp=pp: e.tensor_copy(out=self.ou[:, j, :], in_=self.ps[pp][:]), reads=[self.b_ps[pp]], writes=[self.b_ou[j]])
            for j2 in range(2):
                for half in range(2):
                    P.op("sync", lambda e, tok=tok, j2=j2, half=half: e.dma_start(out=uT[2 * j2 + half][:, tok], in_=self.ou[half * 64:(half + 1) * 64, j2, :]),
                         reads=[self.b_ou[j2]], writes=[bd["uT"]], dma=True)
            for s in range(4):
                pp = 2 + s
                for c in range(8):
                    P.op("tensor", lambda e, c=c, s=s, pp=pp: e.matmul(self.ps[pp][:, 0:192], lhsT=self.xT[:, c, s * 128:(s + 1) * 128], rhs=W(c, 1600, 1792),
                                                                     start=(c == 0), stop=(c == 7)),
                         reads=[self.b_wres[c], self.b_xT[c]], writes=[self.b_ps[pp]], pe_accum=True)
                for c in range(8):
                    P.op("tensor", lambda e, c=c, s=s, pp=pp: e.matmul(self.ps[pp][:, 256:472], lhsT=self.xT[:, c, s * 128:(s + 1) * 128], rhs=W(c, 1984, 2200),
                                                                     start=(c == 0), stop=(c == 7), skip_group_check=True),
                         reads=[self.b_wres[c], self.b_xT[c]], writes=[self.b_ps[pp]], pe_accum=True)
                P.op("vector", lambda e, s=s, pp=pp: e.tensor_copy(out=self.ov[:, s, 0:192], in_=self.ps[pp][:, 0:192]), reads=[self.b_ps[pp]], writes=[self.b_ov[s]])
                P.op("vector", lambda e, s=s, pp=pp: e.tensor_copy(out=self.ov[:, s, 192:384], in_=self.ps[pp][:, 256:448]), reads=[self.b_ps[pp]], writes=[self.b_ov[s]])
                P.op("vector", lambda e, s=s, pp=pp: e.tensor_tensor(out=self.og[:, s, :], in0=self.ps[pp][:, 448:472], in1=self.bg[:], op=ALU.add),
                     reads=[self.b_ps[pp], self.b_bg], writes=[self.b_og[s]])
                P.op("scalar", lambda e, s=s: e.activation(out=self.og[:, s, :], in_=self.og[:, s, :], func=AF.Sigmoid), reads=[self.b_og[s]], writes=[self.b_og[s]])
            for g_ in range(2):
                P.op("gpsimd", lambda e, tok=tok, g_=g_: e.dma_start(out=vs[g_, tok, :].rearrange("(s p) d -> p s d", p=128), in_=self.ov[:, :, 96 * g_:96 * g_ + 96]),
                     reads=self.b_ov, writes=[bd["vs"]], dma=True)
                P.op("gpsimd", lambda e, tok=tok, g_=g_: e.dma_start(out=vw[g_, tok, :].rearrange("(s p) d -> p s d", p=128), in_=self.ov[:, :, 192 + 96 * g_:192 + 96 * g_ + 96]),
                     reads=self.b_ov, writes=[bd["vw"]], dma=True)
            P.op("gpsimd", lambda e, tok=tok: e.dma_start(out=gates[tok, :].rearrange("(s p) d -> p s d", p=128), in_=self.og[:]),
                 reads=self.b_og, writes=[bd["gates"]], dma=True)
        return tcount + NTILE


S = 16384
NQT = S // 512
NEG = -30000.0
VP = 128
GELU_C = float(2.0 * (2.0 / np.pi) ** 0.5)


def b_consts():
    import ml_dtypes
    c = {}
    c["ident_f"] = np.eye(128, dtype=np.float32)
    c["ident_b"] = np.eye(128, dtype=np.float32).astype(ml_dtypes.bfloat16)
    k = np.arange(128)[:, None]
    t = np.arange(512)[None, :]
    masks = np.zeros((16, 128, 512), np.float32)
    for i in range(8):
        kp = 128 * i - 512 + k
        masks[i] = np.where((t - kp >= 0) & (t - kp < 512), 0.0, NEG)
    for r in range(4):
        masks[8 + r] = np.where(16 * k + 31 <= 512 * r + t, 0.0, NEG)
    for i in range(4):
        masks[12 + i] = np.where(128 * i + k <= t, 0.0, NEG)
    c["masks"] = masks.astype(ml_dtypes.bfloat16)
    kk = np.arange(2048)[None, :]
    r = np.arange(32)[:, None]
    c["kpat"] = (((kk // 64) % 32) == r).astype(np.float32).astype(ml_dtypes.bfloat16)
    W = np.zeros((1024, 256), np.float32)
    for j in range(256):
        for n, w in ((4 * j - 1, 0.5), (4 * j, 1.0), (4 * j + 1, 1.0), (4 * j + 2, 1.0), (4 * j + 3, 0.5)):
            if 0 <= n < 1023:
                W[n, j] = w
    c["wimp"] = W.reshape(8, 128, 256).astype(ml_dtypes.bfloat16)
    fp = np.zeros((128, 4), np.float32)
    fp[:64, 0] = 1000.0; fp[:64, 1] = 2000.0
    fp[64:, 1] = 2000.0; fp[64:, 2] = 3000.0
    c["fpat"] = fp
    return c


def pool_consts(j):
    w = 2 ** (j + 1)
    a = np.zeros((64, 4), np.float32)
    a[:, j] = 1.0 / w
    A = np.zeros((64, 4, 16), np.float32)
    tt = np.arange(16)
    A[:, j, :] = 1.0 / np.minimum(tt + 1, w)
    return a, A


def _V(x, e):
    return x(e) if callable(x) else x


class BB:
    def __init__(self, nc, P, dram):
        self.nc = nc
        self.P = P
        self.dram = dram

    def din(self, name, shape, dt):
        return self.dram.din(name, shape, dt)

    def build(self, pfx, src, ymc, b_ymc):
        P = self.P
        nc = self.nc
        op = P.op
        qT, b_qT = src["qT"]; gates, b_gates = src["gates"]
        kcT, b_kcT = src["kcT"]; vcT, b_vcT = src["vcT"]; ksT, b_ksT = src["ksT"]; kwT, b_kwT = src["kwT"]
        vs, b_vs = src["vs"]; vw, b_vw = src["vw"]; uT, b_uT = src["uT"]
        cw = {}
        for kv in ("k", "v"):
            cw[kv + "w1"] = self.din(pfx + "_cmp_%s_w1" % kv, [3072, 96], F32)
            cw[kv + "w2"] = self.din(pfx + "_cmp_%s_w2" % kv, [96, 96], F32)
            cw[kv + "pos"] = self.din(pfx + "_cmp_%s_pos" % kv, [32, 96], F32)
        pool_w = self.din(pfx + "_pool_w", [64, 64], F32)
        pool_sc = self.din(pfx + "_pool_sc", [64, 1], F32)
        pool_a = self.din("pool_a", [64, 4], F32)
        pool_A = self.din("pool_A", [64, 4, 16], F32)
        d_identf = self.din("ident", [128, 128], F32)
        d_identb = self.din("ident_b", [128, 128], BF16)
        d_masks = self.din("masks", [16, 128, 512], BF16)
        d_kpat = self.din("kpat", [32, 2048], BF16)
        d_wimp = self.din("wimp", [8, 128, 256], BF16)
        d_fpat = self.din("fpat", [128, 4], F32)
        d_gsel = self.din("gsel", [6, 24], F32)
        def ysl(r0, r1, t0, n):
            return ymc[t0 // 4096, r0:r1, (t0 % 4096):(t0 % 4096) + n]

        ps_s = [P.psum("ps_s%d" % i, [128, 512]) for i in range(2)]; b_ps_s = [P.pbuf() for i in range(2)]
        ps_o = [P.psum("ps_o%d" % i, [128, 512]) for i in range(2)]; b_ps_o = [P.pbuf() for i in range(2)]
        ps_i = [P.psum("ps_i%d" % i, [128, 512]) for i in range(2)]; b_ps_i = [P.pbuf() for i in range(2)]
        ps_t = [P.psum("ps_t%d" % i, [128, 512]) for i in range(2)]; b_ps_t = [P.pbuf() for i in range(2)]

        idf = P.sbuf("idf", [128, 128], F32); b_idf = P.buf()
        idb = P.sbuf("idb", [128, 128], BF16); b_idb = P.buf()
        masks = P.sbuf("masks", [128, 16, 512], BF16); b_masks = P.buf()
        wimp = P.sbuf("wimp", [128, 8, 256], BF16); b_wimp = P.buf()
        fpat = P.sbuf("fpat", [128, 4], F32); b_fpat = P.buf()
        ksa = P.sbuf("ksa", [128, S], BF16); b_ksa = P.buf()
        vsa = P.sbuf("vsa", [128, 128, VP], BF16); b_vsa = P.buf()
        kcmpT = P.sbuf("kcmpT", [96, 1024], BF16); b_kcmpT = P.buf()
        vcmpa = P.sbuf("vcmpa", [128, 8, VP], BF16); b_vcmpa = P.buf()
        op("sync", lambda e: e.dma_start(out=idf[:], in_=d_identf), writes=[b_idf], dma=True)
        op("sync", lambda e: e.dma_start(out=idb[:], in_=d_identb), writes=[b_idb], dma=True)
        op("sync", lambda e: e.dma_start(out=masks[:], in_=d_masks.rearrange("m p t -> p m t")), writes=[b_masks], dma=True)
        op("sync", lambda e: e.dma_start(out=wimp[:], in_=d_wimp.rearrange("c p j -> p c j")), writes=[b_wimp], dma=True)
        op("sync", lambda e: e.dma_start(out=fpat[:], in_=d_fpat), writes=[b_fpat], dma=True)
        gsel = P.sbuf("gsel", [128, 6, 24], F32); b_gsel = P.buf()
        op("sync", lambda e: e.dma_start(out=gsel[:].rearrange("p a b -> p (a b)"), in_=d_gsel.rearrange("a b -> (a b)").partition_broadcast(128)), writes=[b_gsel], dma=True)
        op("gpsimd", lambda e: e.memset(kcmpT[:], 0.0), writes=[b_kcmpT])
        op("gpsimd", lambda e: e.memset(vcmpa[:], 1.0), writes=[b_vcmpa])
        op("gpsimd", lambda e: e.memset(vsa[:], 1.0), writes=[b_vsa])

        q_sb = [P.sbuf("q_sb%d" % i, [96, 4, 512], BF16) for i in range(2)]; b_q = [P.buf() for i in range(2)]
        g_sb = [P.sbuf("g_sb%d" % i, [128, 4, 6], F32) for i in range(2)]; b_g = [P.buf() for i in range(2)]
        g_full = P.sbuf("g_full", [128, 4, 24], F32); b_gfull = P.buf()
        g_tmp = P.sbuf("g_tmp", [128, 4, 6, 24], F32); b_gtmp = P.buf()
        kw_sb = [P.sbuf("kw_sb%d" % i, [96, 1024], BF16) for i in range(2)]; b_kw = [P.buf() for i in range(2)]
        vwa = [P.sbuf("vwa%d" % i, [128, 8, VP], BF16) for i in range(2)]; b_vwa = [P.buf() for i in range(2)]
        e_all = [P.sbuf("e_all%d" % i, [128, 8, 512], BF16) for i in range(2)]; b_eall = [P.buf() for i in range(2)]
        e_sb = [P.sbuf("e_sb%d" % i, [128, 512], BF16) for i in range(3)]; b_e = [P.buf() for i in range(3)]
        oTs = [P.sbuf("oTs%d" % i, [97, 512], F32) for i in range(2)]; b_oTs = [P.buf() for i in range(2)]
        impacc = P.sbuf("impacc", [128, 4, 258], F32); b_imp = [P.buf() for i in range(4)]
        mr = P.sbuf("mr", [128, 256], F32); b_mr = P.buf()
        m8 = P.sbuf("m8", [128, 3, 8], F32); b_m8 = P.buf()
        selb = P.sbuf("selb", [128, 4, 352], F32); b_selb = [P.buf() for i in range(4)]
        qaug = P.sbuf("qaug", [128, 2, 8, 512], BF16); b_qaug = [[P.buf() for v in range(8)] for h in range(2)]
        oacc = [P.sbuf("oacc%d" % i, [128, 4, 2, 96], F32) for i in range(2)]; b_oacc = [P.buf() for i in range(2)]
        oTo = [P.sbuf("oTo%d" % i, [96, 2, 512], BF16) for i in range(2)]; b_oTo = [P.buf() for i in range(2)]
        rd = P.sbuf("rd", [128, 8, 4], F32); b_rd = P.buf()
        qa_f = qaug[:].rearrange("p a b c -> p (a b c)").bitcast(F32)
        ea0 = e_all[0][:].rearrange("p a b -> p (a b)")
        ea1_f = e_all[1][:].rearrange("p a b -> p (a b)").bitcast(F32)
        w1s = qa_f[0:96, 0:3072].rearrange("d (p e) -> d p e", e=96); b_w1s = P.buf()
        gx = qa_f[0:96, 3072:4096]; b_gx = P.buf()
        w1b = ea0[0:96, 0:3072].rearrange("d (p e) -> d p e", e=96); b_w1b = P.buf()
        gb = ea0[0:96, 3072:4096]; b_gb = P.buf()
        gy = ea1_f[0:96, 0:1024]; b_gy = P.buf()
        w2s = P.sbuf("w2s", [96, 96], F32); b_w2s = P.buf()
        w2b = P.sbuf("w2b", [96, 96], BF16); b_w2b = P.buf()
        poss = P.sbuf("poss", [32, 96], F32); b_poss = P.buf()
        posT = P.sbuf("posT", [96, 32], BF16); b_posT = P.buf()
        cb = P.sbuf("cb", [96, 1], F32); b_cb = P.buf()
        stage = getattr(self, 'stage', 99)
        for i in range(8):
            op("gpsimd", lambda e, i=i: e.dma_start(out=ksa[96:128, i * 2048:(i + 1) * 2048], in_=d_kpat), writes=[b_ksa], dma=True)
        for i in range(16 if stage >= 2 else 0):
            op("gpsimd", lambda e, i=i: e.dma_start(out=vsa[:, i * 8:(i + 1) * 8, 0:96],
                                                                        in_=_V(vs, e)[i * 1024:(i + 1) * 1024, :].rearrange("(k p) d -> p k d", p=128)),
               reads=[b_vs], writes=[b_vsa], dma=True)

        for kv, srcT, b_srcT in ((("k", kcT, b_kcT), ("v", vcT, b_vcT)) if stage >= 1 else ()):
            op("sync", lambda e, srcT=srcT: e.dma_start(out=ksa[0:96, :], in_=srcT), reads=[b_srcT], writes=[b_ksa], dma=True)
            op("sync", lambda e, kv=kv: e.dma_start(out=w1s, in_=cw[kv + "w1"].rearrange("(p d) e -> d p e", d=96)), writes=[b_w1s], dma=True)
            op("sync", lambda e, kv=kv: e.dma_start(out=w2s[:], in_=cw[kv + "w2"]), writes=[b_w2s], dma=True)
            op("sync", lambda e, kv=kv: e.dma_start(out=poss[:], in_=cw[kv + "pos"]), writes=[b_poss], dma=True)
            op("vector", lambda e: e.tensor_copy(out=w1b, in_=w1s), reads=[b_w1s], writes=[b_w1b])
            op("vector", lambda e: e.tensor_copy(out=w2b[:], in_=w2s[:]), reads=[b_w2s], writes=[b_w2b])
            op("tensor", lambda e: e.transpose(out=ps_t[0][0:96, 0:32], in_=poss[:], identity=idf[0:32, 0:32]), reads=[b_poss, b_idf], writes=[b_ps_t[0]], pe_accum=True)
            op("vector", lambda e: e.tensor_copy(out=posT[:], in_=ps_t[0][0:96, 0:32]), reads=[b_ps_t[0]], writes=[b_posT])
            for p in range(32):
                op("tensor", lambda e, p=p: e.matmul(ps_t[1][0:96, 0:1], lhsT=w1b[:, p, :], rhs=posT[:, p:p + 1], start=(p == 0), stop=(p == 31)),
                   reads=[b_w1b, b_posT], writes=[b_ps_t[1]], pe_accum=True)
            op("vector", lambda e: e.tensor_copy(out=cb[:], in_=ps_t[1][0:96, 0:1]), reads=[b_ps_t[1]], writes=[b_cb])
            if stage < 1.2:
                continue
            op("gpsimd", lambda e: e.memset(gb, 0.0), writes=[b_gb])
            kview = ksa[0:96, :].rearrange("d (n s) -> d n s", s=16)
            for half, (n0, N) in enumerate(((0, 512), (512, 511))):
                pp = ps_s[half]
                for p in range(32):
                    rhs = kview[:, n0:n0 + N, p] if p < 16 else kview[:, n0 + 1:n0 + 1 + N, p - 16]
                    op("tensor", lambda e, p=p, rhs=rhs, pp=pp, N=N: e.matmul(pp[0:96, 0:N], lhsT=w1b[:, p, :], rhs=rhs, start=(p == 0), stop=(p == 31)),
                       reads=[b_w1b, b_ksa], writes=[b_ps_s[half]], pe_accum=True)
                sl = slice(n0, n0 + N)
                op("vector", lambda e, pp=pp, N=N, sl=sl: e.tensor_scalar(out=gx[:, sl], in0=pp[0:96, 0:N], scalar1=cb[:, 0:1], scalar2=None, op0=ALU.add),
                   reads=[b_ps_s[half], b_cb], writes=[b_gx])
                op("vector", lambda e, sl=sl: e.tensor_tensor(out=gy[:, sl], in0=gx[:, sl], in1=gx[:, sl], op=ALU.mult), reads=[b_gx], writes=[b_gy])
                op("vector", lambda e, sl=sl: e.tensor_scalar(out=gy[:, sl], in0=gy[:, sl], scalar1=0.044715, scalar2=1.0, op0=ALU.mult, op1=ALU.add), reads=[b_gy], writes=[b_gy])
                op("vector", lambda e, sl=sl: e.tensor_tensor(out=gy[:, sl], in0=gy[:, sl], in1=gx[:, sl], op=ALU.mult), reads=[b_gy, b_gx], writes=[b_gy])
                op("scalar", lambda e, sl=sl: e.activation(out=gy[:, sl], in_=gy[:, sl], func=AF.Sigmoid, scale=GELU_C), reads=[b_gy], writes=[b_gy])
                op("vector", lambda e, sl=sl: e.tensor_tensor(out=gb[:, sl], in0=gy[:, sl], in1=gx[:, sl], op=ALU.mult), reads=[b_gy, b_gx], writes=[b_gb])
            if stage < 1.3:
                continue
            if kv == "k":
                for half, (n0, N) in enumerate(((0, 512), (512, 511))):
                    op("tensor", lambda e, half=half, n0=n0, N=N: e.matmul(ps_o[half][0:96, 0:N], lhsT=w2b[:], rhs=gb[:, n0:n0 + N], start=True, stop=True),
                       reads=[b_w2b, b_gb], writes=[b_ps_o[half]], pe_accum=True)
                    op("vector", lambda e, half=half, n0=n0, N=N: e.tensor_copy(out=kcmpT[:, n0:n0 + N], in_=ps_o[half][0:96, 0:N]), reads=[b_ps_o[half]], writes=[b_kcmpT])
            else:
                for c in range(8):
                    k2 = c % 2
                    op("tensor", lambda e, c=c, k2=k2: e.matmul(ps_o[k2][:, 0:96], lhsT=gb[:, c * 128:(c + 1) * 128], rhs=w2b[:], start=True, stop=True),
                       reads=[b_w2b, b_gb], writes=[b_ps_o[k2]], pe_accum=True)
                    op("vector", lambda e, c=c, k2=k2: e.tensor_copy(out=vcmpa[:, c, 0:96], in_=ps_o[k2][:, 0:96]), reads=[b_ps_o[k2]], writes=[b_vcmpa])

        stage = getattr(self, 'stage', 99)
        op("sync", lambda e: e.dma_start(out=ksa[0:96, :], in_=ksT), reads=[b_ksT], writes=[b_ksa], dma=True)
        CH = 512
        pw_s = P.sbuf("pw_s", [64, 64], F32); b_pw_s = P.buf()
        pw_b = P.sbuf("pw_b", [64, 64], BF16); b_pw_b = P.buf()
        psc = P.sbuf("psc", [64, 1], F32); b_psc = P.buf()
        pa = P.sbuf("pa", [64, 4], F32); b_pa = P.buf()
        pA = P.sbuf("pA", [64, 4, 16], F32); b_pA = P.buf()
        L_ = 16 + CH
        ub = [P.sbuf("ub%d" % i, [64, L_], F32)[:] for i in range(2)]; b_ub = [P.buf() for i in range(2)]
        sw = [P.sbuf("sw%d" % i, [64, L_], F32)[:] for i in range(4)]; b_sw = [P.buf() for i in range(4)]
        acc = P.sbuf("pacc", [64, CH], F32)[:]; b_acc = P.buf()
        a16 = P.sbuf("a16", [64, 2, 16], F32)[:]; b_a16 = P.buf()
        accb = P.sbuf("paccb", [64, CH], BF16)[:]; b_accb = P.buf()
        pout = [P.sbuf("pout%d" % i, [64, CH], BF16)[:] for i in range(2)]; b_pout = [P.buf() for i in range(2)]
        op("sync", lambda e: e.dma_start(out=pw_s[:], in_=pool_w), writes=[b_pw_s], dma=True)
        op("sync", lambda e: e.dma_start(out=psc[:], in_=pool_sc), writes=[b_psc], dma=True)
        op("sync", lambda e: e.dma_start(out=pa[:], in_=pool_a), writes=[b_pa], dma=True)
        op("sync", lambda e: e.dma_start(out=pA[:], in_=pool_A), writes=[b_pA], dma=True)
        op("vector", lambda e: e.tensor_copy(out=pw_b[:], in_=pw_s[:]), reads=[b_pw_s], writes=[b_pw_b])
        def pool_chunk(ci):
            k = ci % 2
            u = ub[k]
            if ci == 0:
                op("gpsimd", lambda e, u=u: e.memset(u[:, 0:16], 0.0), writes=[b_ub[k]])
                op("sync", lambda e, u=u: e.dma_start(out=u[:, 16:], in_=uT[:, 0:CH]), reads=[b_uT], writes=[b_ub[k]], dma=True)
            else:
                op("sync", lambda e, u=u, ci=ci: e.dma_start(out=u, in_=uT[:, ci * CH - 16:(ci + 1) * CH]), reads=[b_uT], writes=[b_ub[k]], dma=True)
            L = 16 + CH
            prev, b_prev = u, b_ub[k]
            for wi, sh in enumerate((1, 2, 4, 8)):
                lo = 2 * sh - 1
                dst = sw[wi]
                op("gpsimd", lambda e, dst=dst, prev=prev, lo=lo, sh=sh, L=L: e.tensor_tensor(out=dst[:, lo:L], in0=prev[:, lo:L], in1=prev[:, lo - sh:L - sh], op=ALU.add),
                   reads=[b_prev], writes=[b_sw[wi]])
                prev, b_prev = dst, b_sw[wi]
            op("vector", lambda e: e.tensor_scalar(out=acc, in0=sw[0][:, 16:], scalar1=pa[:, 0:1], scalar2=None, op0=ALU.mult), reads=[b_sw[0], b_pa], writes=[b_acc])
            for wi in range(1, 4):
                op("vector", lambda e, wi=wi: e.scalar_tensor_tensor(out=acc, in0=sw[wi][:, 16:], scalar=pa[:, wi:wi + 1], in1=acc, op0=ALU.mult, op1=ALU.add),
                   reads=[b_sw[wi], b_pa, b_acc], writes=[b_acc])
            if ci == 0:
                op("vector", lambda e: e.tensor_tensor(out=a16[:, 0, :], in0=sw[0][:, 16:32], in1=pA[:, 0, :], op=ALU.mult), reads=[b_sw[0], b_pA], writes=[b_a16])
                for wi in range(1, 4):
                    op("vector", lambda e, wi=wi: e.tensor_tensor(out=a16[:, 1, :], in0=sw[wi][:, 16:32], in1=pA[:, wi, :], op=ALU.mult), reads=[b_sw[wi], b_pA, b_a16], writes=[b_a16])
                    op("vector", lambda e: e.tensor_tensor(out=a16[:, 0, :], in0=a16[:, 0, :], in1=a16[:, 1, :], op=ALU.add), reads=[b_a16], writes=[b_a16])
                op("vector", lambda e: e.tensor_copy(out=acc[:, 0:16], in_=a16[:, 0, :]), reads=[b_a16, b_acc], writes=[b_acc])
            op("vector", lambda e, u=u: e.tensor_tensor(out=accb, in0=acc, in1=u[:, 16:], op=ALU.subtract), reads=[b_acc, b_ub[k]], writes=[b_accb])
            for hh in range(CH // 512):
                pbi = nxt("t", 2)
                op("tensor", lambda e, hh=hh, pbi=pbi: e.matmul(ps_t[pbi][0:64, :], lhsT=pw_b[:], rhs=accb[:, hh * 512:(hh + 1) * 512], start=True, stop=True),
                   reads=[b_pw_b, b_accb], writes=[b_ps_t[pbi]], pe_accum=True)
                op("vector", lambda e, hh=hh, k=k, pbi=pbi: e.tensor_scalar(out=pout[k][:, hh * 512:(hh + 1) * 512], in0=ps_t[pbi][0:64, :], scalar1=psc[:, 0:1], scalar2=None, op0=ALU.mult),
                   reads=[b_ps_t[pbi], b_psc], writes=[b_pout[k]])
            op("sync", lambda e, k=k, ci=ci: e.dma_start(out=ysl(0, 64, ci * CH, CH), in_=pout[k]), reads=[b_pout[k]], writes=[b_ymc], dma=True)


        P.barrier()

        for i in range(2):
            op("gpsimd", lambda e, i=i: e.memset(vwa[i][:], 1.0), writes=[b_vwa[i]])
        op("gpsimd", lambda e: e.memset(selb[:], 0.0), writes=b_selb)
        op("gpsimd", lambda e: e.memset(impacc[:], 0.0), writes=b_imp)

        cnt = {"s": 0, "o": 0, "e": 0, "t": 0, "rd": 0}

        def nxt(key, n):
            v = cnt[key] % n
            cnt[key] += 1
            return v

        def load_tile(qt):
            k = qt % 2
            tok = slice(qt * 512, (qt + 1) * 512)
            op("sync", lambda e: e.dma_start(out=q_sb[k][:], in_=qT[:, :, tok].rearrange("h d t -> d h t")), reads=[b_qT], writes=[b_q[k]], dma=True)
            op("sync", lambda e: e.dma_start(out=g_full[:], in_=gates[tok, :].rearrange("(s p) c -> p s c", p=128)), reads=[b_gates], writes=[b_gfull], dma=True)
            op("gpsimd", lambda e: e.tensor_tensor(out=g_tmp[:], in0=g_full[:].unsqueeze(2).to_broadcast([128, 4, 6, 24]), in1=gsel[:].unsqueeze(1).to_broadcast([128, 4, 6, 24]), op=ALU.mult),
               reads=[b_gfull, b_gsel], writes=[b_gtmp])
            op("vector", lambda e: e.tensor_reduce(out=g_sb[k][:], in_=g_tmp[:], axis=AX.X, op=ALU.add), reads=[b_gtmp], writes=[b_g[k]])
            if qt == 0:
                op("sync", lambda e: e.dma_start(out=kw_sb[k][:, 512:1024], in_=kwT[:, 0:512]), reads=[b_kwT], writes=[b_kw[k]], dma=True)
                op("sync", lambda e: e.dma_start(out=vwa[k][:, 4:8, 0:96], in_=_V(vw, e)[0:512, :].rearrange("(k p) d -> p k d", p=128)), reads=[b_vw], writes=[b_vwa[k]], dma=True)
            else:
                op("sync", lambda e: e.dma_start(out=kw_sb[k][:], in_=kwT[:, qt * 512 - 512: qt * 512 + 512]), reads=[b_kwT], writes=[b_kw[k]], dma=True)
                op("sync", lambda e: e.dma_start(out=vwa[k][:, :, 0:96], in_=_V(vw, e)[qt * 512 - 512: qt * 512 + 512, :].rearrange("(k p) d -> p k d", p=128)), reads=[b_vw], writes=[b_vwa[k]], dma=True)

        def finish_branch(qt, po, h_own, gcol, first, rd_out=None):
            k = qt % 2
            bi = nxt("t", 2)
            osb = oTs[bi]
            op("vector", lambda e: e.tensor_copy(out=osb[:], in_=ps_o[po][0:97, :]), reads=[b_ps_o[po]], writes=[b_oTs[bi]])
            pt = ps_t[bi]
            for sub in range(4):
                op("tensor", lambda e, sub=sub: e.transpose(out=pt[:, sub * 128: sub * 128 + 97], in_=osb[0:97, sub * 128:(sub + 1) * 128], identity=idf[0:97, 0:97]),
                   reads=[b_oTs[bi], b_idf], writes=[b_ps_t[bi]], pe_accum=True)
            ri = nxt("rd", 8)
            ptv = pt[:].rearrange("p (s c) -> p s c", c=128)
            op("vector", lambda e: e.tensor_scalar(out=rd[:, ri, :], in0=ptv[:, :, 96], scalar1=1e-30, scalar2=None, op0=ALU.max), reads=[b_ps_t[bi]], writes=[b_rd])
            op("vector", lambda e: e.reciprocal(rd[:, ri, :], rd[:, ri, :]), reads=[b_rd], writes=[b_rd])
            if h_own is not None:
                ci = nxt("rd", 8)
                op("vector", lambda e: e.tensor_tensor(out=rd[:, ci, :], in0=rd[:, ri, :], in1=g_sb[k][:, :, h_own * 3 + gcol], op=ALU.mult), reads=[b_rd, b_g[k]], writes=[b_rd])
                dbg = getattr(self, "dbg_branch", None)
                if dbg is not None and dbg != gcol:
                    op("vector", lambda e: e.memset(rd[:, ci, :], 0.0), reads=[b_rd], writes=[b_rd])
                for sub in range(4):
                    if first:
                        op("vector", lambda e, sub=sub: e.tensor_scalar(out=oacc[k][:, sub, h_own, :], in0=ptv[:, sub, 0:96], scalar1=rd[:, ci, sub:sub + 1], scalar2=None, op0=ALU.mult),
                           reads=[b_ps_t[bi], b_rd], writes=[b_oacc[k]])
                    else:
                        op("vector", lambda e, sub=sub: e.scalar_tensor_tensor(out=oacc[k][:, sub, h_own, :], in0=ptv[:, sub, 0:96], scalar=rd[:, ci, sub:sub + 1],
                                                                              in1=oacc[k][:, sub, h_own, :], op0=ALU.mult, op1=ALU.add),
                           reads=[b_ps_t[bi], b_rd, b_oacc[k]], writes=[b_oacc[k]])
            return ri

        def cmp_part(qt):
            k = qt % 2
            nct = min(8, qt // 4 + 1)
            for h in range(4):
                ea = e_all[h % 2]
                for c in range(nct):
                    si = nxt("s", 2)
                    partial = qt < 4 * c + 4
                    op("tensor", lambda e, c=c, h=h, si=si, partial=partial: e.matmul(ps_s[si][:], lhsT=kcmpT[:, c * 128:(c + 1) * 128], rhs=q_sb[k][:, h, :], start=True, stop=not partial),
                       reads=[b_kcmpT, b_q[k]], writes=[b_ps_s[si]], pe_accum=True)
                    if partial:
                        r = qt - 4 * c
                        op("tensor", lambda e, si=si, r=r: e.matmul(ps_s[si][:], lhsT=idb[:], rhs=masks[:, 8 + r, :], start=False, stop=True),
                           reads=[b_idb, b_masks], writes=[b_ps_s[si]], pe_accum=True)
                    op("scalar", lambda e, c=c, si=si, ea=ea: e.activation(out=ea[:, c, :], in_=ps_s[si][:], func=AF.Exp), reads=[b_ps_s[si]], writes=[b_eall[h % 2]])
                po = nxt("o", 2)
                for c in range(nct):
                    op("tensor", lambda e, c=c, ea=ea, po=po: e.matmul(ps_o[po][:, :], lhsT=vcmpa[:, c, 0:128], rhs=ea[:, c, :], start=(c == 0), stop=(c == nct - 1)),
                       reads=[b_vcmpa, b_eall[h % 2]], writes=[b_ps_o[po]], pe_accum=True)
                for sub in range(4):
                    pi = ps_i[sub // 2]
                    for c in range(nct):
                        op("tensor", lambda e, c=c, sub=sub, ea=ea, pi=pi: e.matmul(pi[:, (sub % 2) * 256:(sub % 2) * 256 + 256], lhsT=ea[:, c, sub * 128:(sub + 1) * 128], rhs=wimp[:, c, :],
                                                                               start=(c == 0), stop=(c == nct - 1)),
                           reads=[b_wimp, b_eall[h % 2]], writes=[b_ps_i[sub // 2]], pe_accum=True)
                ri = finish_branch(qt, po, h if h < 2 else None, 0, True)
                for sub in range(4):
                    pi = ps_i[sub // 2]
                    src = pi[:, (sub % 2) * 256:(sub % 2) * 256 + 256]
                    if h == 0:
                        op("vector", lambda e, sub=sub, src=src, ri=ri: e.tensor_scalar(out=impacc[:, sub, 1:257], in0=src, scalar1=rd[:, ri, sub:sub + 1], scalar2=None, op0=ALU.mult),
                           reads=[b_ps_i[sub // 2], b_rd], writes=[b_imp[sub]])
                    else:
                        op("vector", lambda e, sub=sub, src=src, ri=ri: e.scalar_tensor_tensor(out=impacc[:, sub, 1:257], in0=src, scalar=rd[:, ri, sub:sub + 1], in1=impacc[:, sub, 1:257],
                                                                                            op0=ALU.mult, op1=ALU.add),
                           reads=[b_ps_i[sub // 2], b_rd, b_imp[sub]], writes=[b_imp[sub]])

        def topk_part(qt):
            for sub in range(4):
                st = 4 * qt + sub
                sc = impacc[:, sub, 1:257]
                op("vector", lambda e, sub=sub, st=st: e.tensor_tensor(out=impacc[:, sub, 2 * st:2 * st + 3], in0=impacc[:, sub, 2 * st:2 * st + 3], in1=fpat[:, 0:3], op=ALU.add),
                   reads=[b_imp[sub], b_fpat], writes=[b_imp[sub]])
                op("vector", lambda e, sub=sub: e.tensor_scalar(out=impacc[:, sub, 1:2], in0=impacc[:, sub, 1:2], scalar1=4000.0, scalar2=None, op0=ALU.add), reads=[b_imp[sub]], writes=[b_imp[sub]])
                op("vector", lambda e, sc=sc: e.max(out=m8[:, 0, :], in_=sc), reads=[b_imp[sub]], writes=[b_m8])
                op("vector", lambda e, sc=sc: e.match_replace(out=mr[:], in_to_replace=m8[:, 0, :], in_values=sc, imm_value=-1e9), reads=[b_imp[sub], b_m8], writes=[b_mr])
                op("vector", lambda e: e.max(out=m8[:, 1, :], in_=mr[:]), reads=[b_mr], writes=[b_m8])
                op("vector", lambda e: e.tensor_reduce(out=m8[:, 2, 0:1], in_=m8[:, 1, :], axis=AX.X, op=ALU.min), reads=[b_m8], writes=[b_m8])
                op("vector", lambda e, sub=sub, sc=sc: e.tensor_scalar(out=selb[:, sub, 96:352], in0=sc, scalar1=m8[:, 2, 0:1], scalar2=NEG, op0=ALU.is_lt, op1=ALU.mult),
                   reads=[b_imp[sub], b_m8], writes=[b_selb[sub]])

        def qaug_part(qt):
            k = qt % 2
            nv = (qt + 1 + 3) // 4
            for v in range(nv):
                bi = nxt("t", 2)
                pt = ps_t[bi]
                for sub in range(4):
                    op("tensor", lambda e, sub=sub, v=v, pt=pt: e.transpose(out=pt[:, sub * 128:(sub + 1) * 128], in_=selb[:, sub, 32 * v:32 * v + 128], identity=idf[:]),
                       reads=[b_selb[sub], b_idf], writes=[b_ps_t[bi]], pe_accum=True)
                for h in range(2):
                    op("vector", lambda e, h=h, v=v, pt=pt: e.tensor_copy(out=qaug[96:128, h, v, :], in_=pt[96:128, :]),
                       reads=[b_ps_t[bi]], writes=[b_qaug[h][v]])
                    op("gpsimd", lambda e, h=h, v=v: e.tensor_copy(out=qaug[0:96, h, v, :], in_=q_sb[k][:, h, :]), reads=[b_q[k]], writes=[b_qaug[h][v]])

        def win_part(qt):
            k = qt % 2
            tiles = list(range(4, 8)) if qt == 0 else list(range(8))
            items = [(h, n, i) for h in range(2) for n, i in enumerate(tiles)]
            pos = {}
            pend = None

            def pv(item):
                h, n, i = item
                ei, po = pos[item]
                op("tensor", lambda e, i=i, ei=ei, po=po, n=n: e.matmul(ps_o[po][:, :], lhsT=vwa[k][:, i, 0:128], rhs=e_sb[ei][:], start=(n == 0), stop=(n == len(tiles) - 1)),
                   reads=[b_vwa[k], b_e[ei]], writes=[b_ps_o[po]], pe_accum=True)
                if n == len(tiles) - 1:
                    finish_branch(qt, po, h, 2, False)

            po_h = {}
            for item in items:
                h, n, i = item
                if n == 0:
                    po_h[h] = nxt("o", 2)
                si = nxt("s", 2)
                ei = nxt("e", 3)
                pos[item] = (ei, po_h[h])
                op("tensor", lambda e, i=i, si=si, h=h: e.matmul(ps_s[si][:], lhsT=kw_sb[k][:, i * 128:(i + 1) * 128], rhs=q_sb[k][:, h, :], start=True, stop=False),
                   reads=[b_kw[k], b_q[k]], writes=[b_ps_s[si]], pe_accum=True)
                op("tensor", lambda e, i=i, si=si: e.matmul(ps_s[si][:], lhsT=idb[:], rhs=masks[:, i, :], start=False, stop=True),
                   reads=[b_idb, b_masks], writes=[b_ps_s[si]], pe_accum=True)
                op("scalar", lambda e, si=si, ei=ei: e.activation(out=e_sb[ei][:], in_=ps_s[si][:], func=AF.Exp), reads=[b_ps_s[si]], writes=[b_e[ei]])
                if pend is not None:
                    pv(pend)
                pend = item
            pv(pend)

        def slc_part(qt):
            k = qt % 2
            nkt = 4 * (qt + 1)
            items = [(h, kt) for h in range(2) for kt in range(nkt)]
            pos = {}
            pend = None
            po_h = {}

            def pv(item):
                h, kt = item
                ei, po = pos[item]
                op("tensor", lambda e, kt=kt, ei=ei, po=po: e.matmul(ps_o[po][:, :], lhsT=vsa[:, kt, 0:128], rhs=e_sb[ei][:], start=(kt == 0), stop=(kt == nkt - 1)),
                   reads=[b_vsa, b_e[ei]], writes=[b_ps_o[po]], pe_accum=True)
                if kt == nkt - 1:
                    finish_branch(qt, po, h, 1, False)

            for item in items:
                h, kt = item
                if kt == 0:
                    po_h[h] = nxt("o", 2)
                si = nxt("s", 2)
                ei = nxt("e", 3)
                pos[item] = (ei, po_h[h])
                v = kt // 16
                diag = kt >= 4 * qt
                op("tensor", lambda e, kt=kt, si=si, v=v, diag=diag, h=h: e.matmul(ps_s[si][:], lhsT=ksa[:, kt * 128:(kt + 1) * 128], rhs=qaug[:, h, v, :], start=True, stop=not diag),
                   reads=[b_ksa, b_qaug[h][v]], writes=[b_ps_s[si]], pe_accum=True)
                if diag:
                    op("tensor", lambda e, kt=kt, si=si: e.matmul(ps_s[si][:], lhsT=idb[:], rhs=masks[:, 12 + kt - 4 * qt, :], start=False, stop=True),
                       reads=[b_idb, b_masks], writes=[b_ps_s[si]], pe_accum=True)
                op("scalar", lambda e, si=si, ei=ei: e.activation(out=e_sb[ei][:], in_=ps_s[si][:], func=AF.Exp), reads=[b_ps_s[si]], writes=[b_e[ei]])
                if pend is not None:
                    pv(pend)
                pend = item
            pv(pend)

        def out_part(qt):
            k = qt % 2
            tok = slice(qt * 512, (qt + 1) * 512)
            for h in range(2):
                bi = nxt("t", 2)
                pt = ps_t[bi]
                for sub in range(4):
                    op("tensor", lambda e, sub=sub, h=h, pt=pt: e.transpose(out=pt[0:96, sub * 128:(sub + 1) * 128], in_=oacc[k][:, sub, h, :], identity=idf[:]),
                       reads=[b_oacc[k], b_idf], writes=[b_ps_t[bi]], pe_accum=True)
                op("vector", lambda e, h=h, pt=pt: e.tensor_copy(out=oTo[k][:, h, :], in_=pt[0:96, :]), reads=[b_ps_t[bi]], writes=[b_oTo[k]])
            op("sync", lambda e: e.dma_start(out=ysl(64, 256, qt * 512, 512).rearrange("(h d) t -> d h t", d=96), in_=oTo[k][:]), reads=[b_oTo[k]], writes=[b_ymc], dma=True)

        nq = self.nqt if hasattr(self, "nqt") else NQT
        if stage < 4:
            nq = 0
        if nq > 0:
            load_tile(0)
            cmp_part(0)
            if stage >= 4.2:
                topk_part(0)
        for qt in range(nq):
            if stage >= 4.3:
                qaug_part(qt)
            if stage >= 4.4:
                win_part(qt)
            pool_chunk(qt)
            if qt + 1 < nq and stage >= 4.7:
                load_tile(qt + 1)
                cmp_part(qt + 1)
                topk_part(qt + 1)
            if stage >= 4.5:
                slc_part(qt)
            if stage >= 4.6:
                out_part(qt)
            if stage < 4.7:
                break
import ml_dtypes
from concourse.bass_utils import run_bass_kernel_spmd

_BF = ml_dtypes.bfloat16
_NCORE = 8
_GROUPS = [[0, 1, 2, 3], [4, 5, 6, 7]]
_DBG = {}
_CC_CHAIN = Buf("cc_chain")


def _reset_chain():
    _CC_CHAIN.w = None
    _CC_CHAIN.r = {}


def _gather(P, c_ap, b_c, g_ap, b_g):
    if _DBG.get("nocc"):
        rows = c_ap.shape[0]
        for r in range(4):
            P.op("gpsimd", lambda e, r=r: e.dma_start(out=g_ap[r * rows:(r + 1) * rows, :], in_=c_ap), reads=[b_c], writes=[b_g], dma=True)
        return
    P.op("gpsimd", lambda e: e.collective_compute("AllGather", ALU.bypass, replica_groups=_GROUPS, ins=[c_ap], outs=[g_ap]),
         reads=[b_c, _CC_CHAIN], writes=[b_g, _CC_CHAIN], cc=True)


def build_fused():
    nc = bass.Bass("TRN2", target_bir_lowering=False)
    _reset_chain()
    P = Prog(nc)
    dram = Dram(nc, P)
    dram.dump = tuple(_DBG.get("dump", ()))
    ident = dram.din("ident", [128, 128], F32)
    xin = dram.din("xin", [NTOK, D], F32)
    xout = dram.dout("xout", [NTOK, D], F32)
    op = P.op

    jcache = {}

    def jexpr(e):
        if "v" not in jcache:
            pid = e.partition_id()
            j = e.snap(pid % 4, min_val=0, max_val=3)
            g = e.snap(j // 2, min_val=0, max_val=1)
            jo = e.snap(g * 2 + (1 - (j % 2)), min_val=0, max_val=3)
            jcache["v"] = dict(j=j, g=g, jo=jo)
        return jcache["v"]

    cur, b_cur = xin, None
    tcount = 0
    ym_loc = b_ym_loc = None
    for l in range(3):
        P.push_scope()
        tb = TB(nc, P, ident, dram)
        if l > 0:
            pl = "l%d" % (l - 1)
            x2 = dram.dscr("x2", [NTOK, D], F32)
            tcount = tb.stage_mix(pl, ym_loc, b_ym_loc, cur, b_cur, x2, dram.b["x2"], tcount)
            if l == 2:
                tcount = tb.stage_ffn(pl + "_ffn2", x2, dram.b["x2"], xout, dram.b["xout"], tcount)
                P.pop_scope()
                break
            x3 = dram.dscr("x3", [NTOK, D], F32)
            tcount = tb.stage_ffn(pl + "_ffn2", x2, dram.b["x2"], x3, dram.b["x3"], tcount)
            cur, b_cur = x3, dram.b["x3"]
        ll = "l%d" % l
        x1 = dram.dscr("x1", [NTOK, D], F32)
        tcount = tb.stage_ffn(ll + "_ffn1", cur, b_cur, x1, dram.b["x1"], tcount)
        cur, b_cur = x1, dram.b["x1"]
        def ctensor(name, shp, dt):
            ap = dram.dscr(name, shp, dt)
            return ap, dram.b[name]
        def gtensor(name, shp, dt):
            if name not in dram.ap:
                dram.ap[name] = nc.dram_tensor(name, list(shp), dt).ap()
                dram.b[name] = P.buf(name)
            return dram.ap[name], dram.b[name]
        C_q, b_Cq = ctensor("c_q", [8, 128, NTOK], BF16)
        C_u, b_Cu = ctensor("c_u", [4, 64, NTOK], F32)
        C_k, b_Ck = ctensor("c_k", [8, 96, NTOK], BF16)
        C_vs, b_Cvs = ctensor("c_vs", [2, NTOK, 96], BF16)
        C_vw, b_Cvw = ctensor("c_vw", [2, NTOK, 96], BF16)
        C_g, b_Cg = ctensor("c_g", [NTOK, 24], F32)
        G_q, b_Gq = gtensor("g_q", [8, 4, 128, NTOK], BF16)
        G_u, b_Gu = gtensor("g_u", [4, 4, 64, NTOK], F32)
        G_k, b_Gk = gtensor("g_k", [8, 4, 96, NTOK], BF16)
        G_vs, b_Gvs = gtensor("g_vs", [2, 4, NTOK, 96], BF16)
        G_vw, b_Gvw = gtensor("g_vw", [2, 4, NTOK, 96], BF16)
        G_g, b_Gg = ctensor("g_g", [4 * NTOK, 24], F32)
        outs = {"gates": (C_g, b_Cg)}
        outs["qT"] = (C_q, b_Cq)
        outs["uT"] = ([C_u[p] for p in range(4)], b_Cu)
        outs["kT"] = (C_k, b_Ck)
        outs["vs"] = (C_vs, b_Cvs)
        outs["vw"] = (C_vw, b_Cvw)
        tcount = tb.stage_proj(ll, cur, b_cur, outs, tcount)
        P.pop_scope()
        if _DBG.get("stop3") and l == 1:
            P.wait_all("sync", list(dram.b.values()))
            P.finish()
            return nc
        if _DBG.get("dump") and l == 0:
            d0 = dram.dscr("dbg_ck0_pre", [96, NTOK], BF16)
            op("sync", lambda e: e.dma_start(out=d0, in_=C_k[0]), reads=[b_Ck], writes=[dram.b["dbg_ck0_pre"]], dma=True)
            dram.outs += [dram.b["dbg_ck0_pre"]]
            P.barrier()
        for i in range(8):
            _gather(P, C_q[i], b_Cq, G_q[i].rearrange("r d t -> (r d) t"), b_Gq)
        for i in range(8):
            _gather(P, C_k[i], b_Ck, G_k[i].rearrange("r d t -> (r d) t"), b_Gk)
        for i in range(4):
            _gather(P, C_u[i], b_Cu, G_u[i].rearrange("r c t -> (r c) t"), b_Gu)
        for i in range(2):
            _gather(P, C_vs[i], b_Cvs, G_vs[i].rearrange("r t d -> (r t) d"), b_Gvs)
            _gather(P, C_vw[i], b_Cvw, G_vw[i].rearrange("r t d -> (r t) d"), b_Gvw)
        _gather(P, C_g, b_Cg, G_g, b_Gg)
        P.barrier()
        if _DBG.get("dump") and l == 0:
            d1 = dram.dscr("dbg_ck0", [96, NTOK], BF16); d2 = dram.dscr("dbg_gk0", [4 * 96, NTOK], BF16)
            op("sync", lambda e: e.dma_start(out=d1, in_=C_k[0]), reads=[b_Ck], writes=[dram.b["dbg_ck0"]], dma=True)
            op("sync", lambda e: e.dma_start(out=d2, in_=G_k[0].rearrange("r d t -> (r d) t")), reads=[b_Gk], writes=[dram.b["dbg_gk0"]], dma=True)
            dram.outs += [dram.b["dbg_ck0"], dram.b["dbg_gk0"]]
        loc = {}

        def mk(name, shp, dt):
            ap = dram.dscr("l_" + name, shp, dt)
            loc[name] = (ap, dram.b["l_" + name])
            return ap, loc[name][1]

        l_qT, b_lq = mk("qT", [4, 96, S], BF16)
        kinds = ["kcT", "vcT", "ksT", "kwT"]
        for name in kinds:
            mk(name, [96, S], BF16)
        mk("uT", [64, S], F32); mk("vs", [S, 96], BF16); mk("vw", [S, 96], BF16)
        gk5 = G_k.rearrange("(k g) r d t -> k g r d t", g=2)
        gq5 = G_q.rearrange("(p h) r d t -> p h r d t", h=2)
        for h2 in range(2):
            def q_own(e, h2=h2):
                return e.dma_start(out=l_qT[h2].rearrange("d (r t) -> d r t", r=4),
                                   in_=gq5[bass.ds(jexpr(e)["j"], 1), h2, :, 0:96, :].rearrange("o r d t -> (o d) r t"))

            def q_oth(e, h2=h2):
                return e.dma_start(out=l_qT[2 + h2].rearrange("d (r t) -> d r t", r=4),
                                   in_=gq5[bass.ds(jexpr(e)["jo"], 1), h2, :, 0:96, :].rearrange("o r d t -> (o d) r t"))
            op("sync", q_own, reads=[b_Gq], writes=[b_lq], dma=True)
            op("sync", q_oth, reads=[b_Gq], writes=[b_lq], dma=True)

        def uf(e):
            return e.dma_start(out=loc["uT"][0].rearrange("c (r t) -> c r t", r=4),
                               in_=G_u[bass.ds(jexpr(e)["j"], 1), :, :, :].rearrange("o r c t -> (o c) r t"))
        op("sync", uf, reads=[b_Gu], writes=[loc["uT"][1]], dma=True)
        for ki, name in enumerate(kinds):
            def kf(e, name=name, ki=ki):
                return e.dma_start(out=loc[name][0].rearrange("d (r t) -> d r t", r=4),
                                   in_=gk5[ki, bass.ds(jexpr(e)["g"], 1), :, :, :].rearrange("o r d t -> (o d) r t"))
            op("sync", kf, reads=[b_Gk], writes=[loc[name][1]], dma=True)
        for name, Gv, b_Gv in (("vs", G_vs, b_Gvs), ("vw", G_vw, b_Gvw)):
            for r in range(4):
                rows = slice(r * NTOK, (r + 1) * NTOK)

                def vf(e, name=name, rows=rows, r=r, Gv=Gv):
                    return e.dma_start(out=loc[name][0][rows, :], in_=Gv[bass.ds(jexpr(e)["g"], 1), r, :, :].rearrange("o t d -> (o t) d"))
                op("sync", vf, reads=[b_Gv], writes=[loc[name][1]], dma=True)
        loc["gates"] = (G_g, b_Gg)
        if _DBG.get("stop1") is not None and _DBG.get("stop1") == l:
            P.wait_all("sync", list(dram.b.values()))
            P.finish()
            return nc
        ymc = dram.dscr("c_ym", [4, 256, NTOK], BF16)
        b_ymc = dram.b["c_ym"]
        P.push_scope()
        bb = BB(nc, P, dram)
        bb.build(ll, loc, ymc, b_ymc)
        P.pop_scope()
        g_ym, b_gym = gtensor("g_ym", [4, 2, 4, 128, NTOK], BF16)
        for tj in range(4):
            for hf in range(2):
                _gather(P, ymc[tj, hf * 128:(hf + 1) * 128, :], b_ymc, g_ym[tj, hf].rearrange("r i t -> (r i) t"), b_gym)
        P.barrier()
        ym_loc = dram.dscr("l_ym", [D, NTOK], BF16)
        b_ym_loc = dram.b["l_ym"]
        for hf in range(2):
            def yf(e, g_ym=g_ym, ym_loc=ym_loc, hf=hf):
                return e.dma_start(out=ym_loc[hf * 512:(hf + 1) * 512, :], in_=g_ym[bass.ds(jexpr(e)["j"], 1), hf, :, :, :].rearrange("o r i t -> (o r i) t"))
            op("sync", yf, reads=[b_gym], writes=[b_ym_loc], dma=True)
        if _DBG.get("stop2") is not None and _DBG.get("stop2") == l:
            P.wait_all("sync", list(dram.b.values()))
            P.finish()
            return nc
    P.wait_all("sync", dram.outs)
    P.finish()
    return nc


def _yperm():
    src_of = lambda r, row: (64 * r + row) if row < 64 else (256 + 192 * r + (row - 64))
    return np.array([src_of(r, hf * 128 + i) for hf in range(2) for r in range(4) for i in range(128)])


_YPERM = _yperm()


def kernel(**inputs):
    inputs = {k: np.asarray(v) for k, v in inputs.items()}
    x = inputs["x"].reshape(2 * S, D).astype(np.float32, copy=False)
    nc = build_fused()
    C = b_consts()
    maps = []
    for c in range(_NCORE):
        b, j = divmod(c, 4)
        m = {"ident": C["ident_f"], "ident_b": C["ident_b"], "masks": C["masks"], "kpat": C["kpat"], "wimp": C["wimp"], "fpat": C["fpat"]}
        a, A = pool_consts(j)
        m["pool_a"] = a
        gs = np.zeros((6, 24), np.float32)
        gs[np.arange(6), 6 * j + np.arange(6)] = 1.0
        m["gsel"] = gs
        m["pool_A"] = A
        m["xin"] = x[c * NTOK:(c + 1) * NTOK]
        for l in range(2):
            p = "l%d" % l
            m[p + "_ffn1_wg"] = inputs["ffn1_w_gate"][l]; m[p + "_ffn1_wu"] = inputs["ffn1_w_up"][l]; m[p + "_ffn1_wd"] = inputs["ffn1_w_down"][l]
            m[p + "_ffn1_lg"] = inputs["ln1_g"][l]; m[p + "_ffn1_lb"] = inputs["ln1_b"][l]
            m[p + "_ffn2_wg"] = inputs["ffn2_w_gate"][l]; m[p + "_ffn2_wu"] = inputs["ffn2_w_up"][l]; m[p + "_ffn2_wd"] = inputs["ffn2_w_down"][l]
            m[p + "_ffn2_lg"] = inputs["ln3_g"][l]; m[p + "_ffn2_lb"] = inputs["ln3_b"][l]
            m[p + "_w_in"] = inputs["w_in"][l]; m[p + "_b_gate"] = inputs["b_gate"][l]
            m[p + "_w_out"] = inputs["w_out"][l][_YPERM]
            m[p + "_ln2_g"] = inputs["ln2_g"][l]; m[p + "_ln2_b"] = inputs["ln2_b"][l]
            m[p + "_cmp_k_w1"] = inputs["cmp_k_w1"][l]; m[p + "_cmp_k_w2"] = inputs["cmp_k_w2"][l]; m[p + "_cmp_k_pos"] = inputs["cmp_pos_k"][l]
            m[p + "_cmp_v_w1"] = inputs["cmp_v_w1"][l]; m[p + "_cmp_v_w2"] = inputs["cmp_v_w2"][l]; m[p + "_cmp_v_pos"] = inputs["cmp_pos_v"][l]
            m[p + "_pool_w"] = inputs["pool_w"][l][j]
            m[p + "_pool_sc"] = inputs["pool_scale"][l][64 * j:64 * j + 64].reshape(64, 1)
        maps.append({k: np.ascontiguousarray(v) for k, v in m.items()})
    res = run_bass_kernel_spmd(nc, maps, core_ids=list(range(_NCORE))).results
    _DBG["res"] = res if _DBG.get("dump") else None
    out = np.concatenate([r["xout"] for r in res], axis=0)
    return out.reshape(2, S, D).astype(np.float32, copy=False)
```

```python
import numpy as np
import concourse.bass as bass
import concourse.mybir as mybir
from contextlib import ExitStack

F32 = mybir.dt.float32
BF16 = mybir.dt.bfloat16
AF = mybir.ActivationFunctionType
ALU = mybir.AluOpType
AX = mybir.AxisListType

ENGS = ("sync", "tensor", "vector", "scalar", "gpsimd")
N_DMA_SEMS = 12


class Buf:
    __slots__ = ("name", "w", "r", "psum")

    def __init__(self, name, psum=False):
        self.name = name
        self.psum = psum
        self.w = None
        self.r = {}


class Prog:
    def __init__(self, nc):
        self.nc = nc
        self.es = ExitStack()
        self.streams = {e: [] for e in ENGS}
        self.sem = {}
        self.cnt = {}
        for e in ENGS:
            self.sem[e] = nc.alloc_semaphore("c_" + e)
            self.cnt[e] = 0
        self.dsem = {}
        self.dcnt = {}
        self.drr = {}
        for e in ("sync", "gpsimd", "scalar"):
            for i in range(N_DMA_SEMS):
                k = "d_%s_%d" % (e, i)
                self.sem[k] = nc.alloc_semaphore(k)
                self.cnt[k] = 0
            self.drr[e] = 0
        self.sem["cc"] = nc.alloc_semaphore("cc_sem")
        self.cnt["cc"] = 0
        self.seen = {e: {} for e in ENGS}
        self.nbuf = 0
        self.block_hooks = []
        self.scopes = []
        self.scope_ctr = 0
        self.scope_id = 0

    def sbuf(self, name, shape, dtype):
        t = self.es.enter_context(self.nc.sbuf_tensor("sb%d_%s" % (self.scope_id, name), list(shape), dtype))
        return t

    def psum(self, name, shape, dtype=F32):
        t = self.es.enter_context(self.nc.psum_tensor("pp%d_%s" % (self.scope_id, name), list(shape), dtype))
        return t

    def push_scope(self):
        self.scopes.append(self.es)
        self.es = ExitStack()
        self.scope_ctr += 1
        self.scope_id = self.scope_ctr

    def pop_scope(self):
        self.barrier()
        self.es.close()
        self.es = self.scopes.pop()

    def buf(self, name=None, psum=False):
        self.nbuf += 1
        return Buf(name or ("b%d" % self.nbuf), psum)

    def pbuf(self):
        return self.buf(psum=True)

    def _need(self, eng, deps):
        out = []
        seen = self.seen[eng]
        for k, v in deps.items():
            if seen.get(k, 0) < v:
                seen[k] = v
                out.append((k, v))
        return out

    def op(self, eng, fn, reads=(), writes=(), dma=False, pe_accum=False, cc=False):
        deps = {}

        def add(tok):
            if tok is None:
                return
            k, v = tok
            if deps.get(k, 0) < v:
                deps[k] = v

        for b in reads:
            add(b.w)
            if b.psum:
                for k, v in b.r.items():
                    if k != eng:
                        add((k, v))
        for b in writes:
            if not (pe_accum and b.w is not None and b.w[0] == "tensor"):
                add(b.w)
            for k, v in b.r.items():
                add((k, v))
        if dma:
            rr = self.drr[eng]
            self.drr[eng] = (rr + 1) % N_DMA_SEMS
            dk = "d_%s_%d" % (eng, rr)
            if self.cnt[dk] > 0:
                add((dk, self.cnt[dk]))
        if eng == "tensor":
            deps.pop("tensor", None)
        waits = self._need(eng, deps)
        if cc:
            self.cnt["cc"] += 1
            tok = ("cc", self.cnt["cc"])
            inc = 1
        elif dma:
            self.cnt[dk] += 16
            tok = (dk, self.cnt[dk])
            inc = 16
        else:
            self.cnt[eng] += 1
            tok = (eng, self.cnt[eng])
            inc = 1
        semh = self.sem[tok[0]]
        wl = [(self.sem[k], v) for k, v in waits]

        def emit(e, fn=fn, wl=wl, semh=semh, inc=inc):
            for s, v in wl:
                e.wait_ge(s, v)
            fn(e).then_inc(semh, inc)

        self.streams[eng].append(emit)
        for b in writes:
            b.w = tok
            b.r = {}
        for b in reads:
            if b.r.get(tok[0], 0) < tok[1]:
                b.r[tok[0]] = tok[1]
        return tok

    def wait_all(self, eng, bufs):
        deps = {}
        for b in bufs:
            if b.w is not None:
                k, v = b.w
                if deps.get(k, 0) < v:
                    deps[k] = v
        wl = [(self.sem[k], v) for k, v in self._need(eng, deps)]

        def emit(e, wl=wl):
            for s, v in wl:
                e.wait_ge(s, v)

        self.streams[eng].append(emit)

    def barrier(self):
        allc = {k: v for k, v in self.cnt.items() if v > 0}
        for eng in ENGS:
            wl = [(self.sem[k], v) for k, v in self._need(eng, dict(allc))]

            def emit(e, wl=wl):
                for s, v in wl:
                    e.wait_ge(s, v)

            self.streams[eng].append(emit)

    def block_break(self):
        self.barrier()
        for e in ENGS:
            self.streams[e].append(None)

    def finish(self):
        nc = self.nc
        segs = {e: [[]] for e in ENGS}
        for e in ENGS:
            for f in self.streams[e]:
                if f is None:
                    segs[e].append([])
                else:
                    segs[e][-1].append(f)
        nseg = len(segs["sync"])
        for i in range(nseg):
            for hook in self.block_hooks:
                hook()
            with nc.Block() as block:
                @block.sync
                def _(e):
                    for f in segs["sync"][i]:
                        f(e)

                @block.tensor
                def _(e):
                    for f in segs["tensor"][i]:
                        f(e)

                @block.vector
                def _(e):
                    for f in segs["vector"][i]:
                        f(e)

                @block.scalar
                def _(e):
                    for f in segs["scalar"][i]:
                        f(e)

                @block.gpsimd
                def _(e):
                    for f in segs["gpsimd"][i]:
                        f(e)
        self.es.close()


class Dram:
    def __init__(self, nc, P):
        self.nc = nc
        self.P = P
        self.ap = {}
        self.b = {}
        self.outs = []
        self.dump = ()
        self.arena = {}
        self.ARENA_ELEMS = {"bf16": 120 * 1024 * 1024, "f32": 60 * 1024 * 1024}

    def din(self, name, shape, dt):
        if name not in self.ap:
            self.ap[name] = self.nc.dram_tensor(name, list(shape), dt, kind="ExternalInput").ap()
        return self.ap[name]

    def dout(self, name, shape, dt):
        self.ap[name] = self.nc.dram_tensor(name, list(shape), dt, kind="ExternalOutput").ap()
        self.b[name] = self.P.buf(name)
        self.outs.append(self.b[name])
        return self.ap[name]

    def dscr(self, name, shape, dt):
        if name not in self.ap:
            if name in self.dump:
                self.ap[name] = self.nc.dram_tensor(name, list(shape), dt, kind="ExternalOutput").ap()
            else:
                n = 1
                for d in shape:
                    n *= int(d)
                key = "bf16" if dt == BF16 else "f32"
                if key not in self.arena:
                    cap = self.ARENA_ELEMS[key]
                    self.arena[key] = [self.nc.dram_tensor("arena_" + key, [cap], dt).ap(), 0, cap]
                ar = self.arena[key]
                off = ar[1]
                n_al = (n + 2047) // 2048 * 2048
                assert off + n_al <= ar[2], ("arena overflow", key, name)
                ar[1] = off + n_al
                flat = ar[0][off:off + n]
                if len(shape) == 1:
                    self.ap[name] = flat
                else:
                    names = ["a%d" % i for i in range(len(shape))]
                    pat = "(" + " ".join(names) + ") -> " + " ".join(names)
                    kw = {nm: int(d) for nm, d in zip(names[1:], shape[1:])}
                    self.ap[name] = flat.rearrange(pat, **kw)
            self.b[name] = self.P.buf(name)
        return self.ap[name]

D = 1024
DFF = 2816
NF = DFF // 128
DIN = 2200
NTOK = 4096
TT = 512
NTILE = NTOK // TT
ALPHA = float((2.0 * 2) ** 0.25)
LN_EPS = 1e-5
QSCALE = float(96 ** -0.5)
TCOLS = [256 + 96 * h for h in range(8)] + [1024, 1120, 1216, 1312, 1408, 1504, 1792, 1888]


class TB:
    gus_ctr = 0

    def __init__(self, nc, P, ident, dram):
        self.nc = nc
        self.P = P
        self.dram = dram
        self.ident = ident
        self.b_dram = dram.b
        self.idt = P.sbuf("idt", [128, 128], F32); self.b_idt = P.buf()
        self.xt = [P.sbuf("xt%d" % i, [128, 4, D], F32) for i in range(2)]
        self.b_xt = [[P.buf() for s in range(4)] for i in range(2)]
        self.xT = P.sbuf("xT", [128, 8, TT], BF16); self.b_xT = [P.buf() for c in range(8)]
        self.hT = P.sbuf("hT", [128, NF, TT], BF16); self.b_hT = [P.buf() for f in range(NF)]
        self.wres = P.sbuf("wres", [128, NF * D], BF16); self.b_wres = [P.buf() for f in range(NF)]
        self.ring = P.sbuf("ring", [128, 3, 2, 8, 128], BF16); self.b_ring = [P.buf() for i in range(3)]
        self.stg = [P.sbuf("stg%d" % i, [128, DFF], F32) for i in range(2)]; self.b_stg = [P.buf() for i in range(2)]
        self.stgb = [P.sbuf("stgb%d" % i, [128, DFF], BF16) for i in range(2)]; self.b_stgb = [P.buf() for i in range(2)]
        self.lng = P.sbuf("lng", [128, D], F32); self.b_lng = P.buf()
        self.lnb = P.sbuf("lnb", [128, D], F32); self.b_lnb = P.buf()
        self.sg = [P.sbuf("sg%d" % i, [128, TT], F32) for i in range(2)]; self.b_sg = [P.buf() for i in range(2)]
        self.st = P.sbuf("st", [128, 4, 2, 6], F32); self.b_st = [P.buf() for s in range(4)]
        self.mv = P.sbuf("mv", [128, 4, 4], F32); self.b_mv = [P.buf() for s in range(4)]
        self.oT = P.sbuf("oT", [128, 16, TT], BF16); self.b_oT = [P.buf() for i in range(16)]
        P.op("gpsimd", lambda e: e.memset(self.oT[96:128, :, :], 0.0), writes=self.b_oT)
        self.ou = P.sbuf("ou", [128, 2, TT], F32); self.b_ou = [P.buf() for i in range(2)]
        self.ov = P.sbuf("ov", [128, 4, 384], BF16); self.b_ov = [P.buf() for i in range(4)]
        self.og = P.sbuf("og", [128, 4, 24], F32); self.b_og = [P.buf() for i in range(4)]
        self.bg = P.sbuf("bg", [128, 24], F32); self.b_bg = P.buf()
        self.ps = [P.psum("ps%d" % i, [128, 512], F32) for i in range(8)]
        self.b_ps = [P.pbuf() for i in range(8)]
        self.rr = 0
        self.prep_i = 0
        P.op("sync", lambda e: e.dma_start(out=self.idt[:], in_=self.ident), writes=[self.b_idt], dma=True)

    def din(self, name, shape, dt):
        return self.dram.din(name, shape, dt)

    def dscr(self, name, shape, dt):
        return self.dram.dscr(name, shape, dt)

    def cast_eng(self):
        self.rr += 1
        return ("vector", "gpsimd", "scalar")[self.rr % 3]

    def cast(self, eng, out, in_, reads, writes):
        if eng == "scalar":
            self.P.op("scalar", lambda e: e.activation(out=out, in_=in_, func=AF.Copy), reads=reads, writes=writes)
        else:
            self.P.op(eng, lambda e: e.tensor_copy(out=out, in_=in_), reads=reads, writes=writes)

    def load_res(self, w, nchunks, width):
        P = self.P
        per = max(1, DFF // width)
        f = 0
        i = 0
        while f < nchunks:
            n = min(per, nchunks - f)
            k = i % 2
            src = w[f * 128:(f + n) * 128, :].rearrange("(n p) d -> p n d", p=128)
            dst = self.stg[k][:, 0:n * width].rearrange("p (n d) -> p n d", d=width)
            P.op("sync", lambda e, dst=dst, src=src: e.dma_start(out=dst, in_=src), writes=[self.b_stg[k]], dma=True)
            self.cast(self.cast_eng(), self.wres[:, f * width:(f + n) * width], self.stg[k][:, 0:n * width],
                      [self.b_stg[k]], [self.b_wres[j] for j in range(f, f + n)])
            f += n
            i += 1

    def prep_gu_iters(self, wg, wu, scr, b_scr):
        P = self.P
        its = []
        for c in range(8):
            for gi, w in ((0, wg), (1, wu)):
                def it(c=c, gi=gi, w=w):
                    k = self.prep_i % 2
                    self.prep_i += 1
                    P.op("sync", lambda e, k=k, w=w, c=c: e.dma_start(out=self.stg[k][:], in_=w[c * 128:(c + 1) * 128, :]),
                         writes=[self.b_stg[k]], dma=True)
                    self.cast(self.cast_eng(), self.stgb[k][:], self.stg[k][:], [self.b_stg[k]], [self.b_stgb[k]])
                    dst = scr[:, :, gi, c, :].rearrange("f p j -> p f j")
                    src = self.stgb[k][:].rearrange("p (f j) -> p f j", j=128)
                    P.op("gpsimd", lambda e, dst=dst, src=src: e.dma_start(out=dst, in_=src), reads=[self.b_stgb[k]],
                         writes=[b_scr], dma=True)
                its.append(it)
        return its

    def prep_gu(self, wg, wu, scr, b_scr):
        for it in self.prep_gu_iters(wg, wu, scr, b_scr):
            it()

    def prep_for(self, pfx):
        wg = self.din(pfx + "_wg", [D, DFF], F32)
        wu = self.din(pfx + "_wu", [D, DFF], F32)
        nm = "gus%d" % (TB.gus_ctr % 2)
        TB.gus_ctr += 1
        scr = self.dscr(nm, [NF, 128, 2, 8, 128], BF16)
        b_scr = self.b_dram[nm]
        return scr, b_scr, self.prep_gu_iters(wg, wu, scr, b_scr)

    def load_ln(self, g, b):
        P = self.P
        P.op("sync", lambda e: e.dma_start(out=self.lng[:], in_=g.partition_broadcast(128)), writes=[self.b_lng], dma=True)
        P.op("sync", lambda e: e.dma_start(out=self.lnb[:], in_=b.partition_broadcast(128)), writes=[self.b_lnb], dma=True)

    def load_x(self, src, b_src, t, k):
        P = self.P
        s_ap = src[t * TT:(t + 1) * TT, :].rearrange("(s p) d -> p s d", p=128)
        P.op("sync", lambda e: e.dma_start(out=self.xt[k][:], in_=s_ap), reads=[b_src] if b_src else [], writes=self.b_xt[k], dma=True)

    def transposes(self, k):
        P = self.P
        for c in range(8):
            pk = c % 2
            for s in range(4):
                P.op("tensor", lambda e, c=c, s=s, pk=pk: e.transpose(out=self.ps[pk][:, s * 128:(s + 1) * 128],
                                                                      in_=self.xt[k][:, s, c * 128:(c + 1) * 128], identity=self.idt[:]),
                     reads=[self.b_xt[k][s], self.b_idt], writes=[self.b_ps[pk]], pe_accum=True)
            self.cast("scalar" if c % 2 else "vector", self.xT[:, c, :], self.ps[pk][:], [self.b_ps[pk]], [self.b_xT[c]])

    def gate_up(self, scr, b_scr, t):
        P = self.P
        for f in range(NF):
            slot = (t * NF + f) % 3
            P.op("gpsimd" if f % 2 else "sync", lambda e, f=f, slot=slot: e.dma_start(out=self.ring[:, slot], in_=scr[f]),
                 reads=[b_scr], writes=[self.b_ring[slot]], dma=True)
            pg = 2 + (f % 2) * 2
            pu = pg + 1
            for gi, pp in ((0, pg), (1, pu)):
                for c in range(8):
                    P.op("tensor", lambda e, c=c, gi=gi, pp=pp, slot=slot: e.matmul(self.ps[pp][:], lhsT=self.ring[:, slot, gi, c, :], rhs=self.xT[:, c, :],
                                                                                 start=(c == 0), stop=(c == 7)),
                         reads=[self.b_ring[slot], self.b_xT[c]], writes=[self.b_ps[pp]], pe_accum=True)
            k = f % 2
            P.op("scalar", lambda e, k=k, pg=pg: e.activation(out=self.sg[k][:], in_=self.ps[pg][:], func=AF.Silu),
                 reads=[self.b_ps[pg]], writes=[self.b_sg[k]])
            P.op("vector", lambda e, k=k, pu=pu, f=f: e.tensor_tensor(out=self.hT[:, f, :], in0=self.sg[k][:], in1=self.ps[pu][:], op=ALU.mult),
                 reads=[self.b_sg[k], self.b_ps[pu]], writes=[self.b_hT[f]])

    def down_ln(self, k, lhs, b_lhs, nch, width_w, ysc, dst, b_dst, t):
        P = self.P
        for s in range(4):
            P.op("gpsimd", lambda e, s=s: e.tensor_scalar(out=self.xt[k][:, s, :], in0=self.xt[k][:, s, :], scalar1=ALPHA, scalar2=None, op0=ALU.mult),
                 reads=[self.b_xt[k][s]], writes=[self.b_xt[k][s]])
        for s in range(4):
            for n in range(2):
                pp = 6 + (s * 2 + n) % 2
                for f in range(nch):
                    P.op("tensor", lambda e, s=s, n=n, f=f, pp=pp: e.matmul(self.ps[pp][:], lhsT=lhs[:, f, s * 128:(s + 1) * 128],
                                                                          rhs=self.wres[:, f * width_w + n * 512: f * width_w + (n + 1) * 512],
                                                                          start=(f == 0), stop=(f == nch - 1)),
                         reads=[b_lhs[f], self.b_wres[f]], writes=[self.b_ps[pp]], pe_accum=True)
                P.op("vector", lambda e, s=s, n=n, pp=pp: e.scalar_tensor_tensor(out=self.xt[k][:, s, n * 512:(n + 1) * 512], in0=self.ps[pp][:], scalar=ysc,
                                                                                in1=self.xt[k][:, s, n * 512:(n + 1) * 512], op0=ALU.mult, op1=ALU.add),
                     reads=[self.b_ps[pp], self.b_xt[k][s]], writes=[self.b_xt[k][s]])
            self.layernorm(k, s)
        d_ap = dst[t * TT:(t + 1) * TT, :].rearrange("(s p) d -> p s d", p=128)
        P.op("sync", lambda e: e.dma_start(out=d_ap, in_=self.xt[k][:]), reads=self.b_xt[k], writes=[b_dst], dma=True)

    def layernorm(self, k, s):
        P = self.P
        z = self.xt[k]
        for h in range(2):
            P.op("vector", lambda e, h=h: e.bn_stats(self.st[:, s, h, :], z[:, s, h * 512:(h + 1) * 512]),
                 reads=[self.b_xt[k][s]], writes=[self.b_st[s]])
        P.op("vector", lambda e: e.bn_aggr(self.mv[:, s, 0:2], self.st[:, s, :, :]), reads=[self.b_st[s]], writes=[self.b_mv[s]])
        P.op("vector", lambda e: e.tensor_scalar(out=self.mv[:, s, 2:3], in0=self.mv[:, s, 1:2], scalar1=LN_EPS, scalar2=None, op0=ALU.add),
             reads=[self.b_mv[s]], writes=[self.b_mv[s]])
        P.op("scalar", lambda e: e.sqrt(self.mv[:, s, 3:4], self.mv[:, s, 2:3]), reads=[self.b_mv[s]], writes=[self.b_mv[s]])
        P.op("vector", lambda e: e.reciprocal(self.mv[:, s, 2:3], self.mv[:, s, 3:4]), reads=[self.b_mv[s]], writes=[self.b_mv[s]])
        P.op("vector", lambda e: e.tensor_scalar(out=z[:, s, :], in0=z[:, s, :], scalar1=self.mv[:, s, 0:1], scalar2=self.mv[:, s, 2:3],
                                                 op0=ALU.subtract, op1=ALU.mult),
             reads=[self.b_xt[k][s], self.b_mv[s]], writes=[self.b_xt[k][s]])
        P.op("gpsimd", lambda e: e.tensor_tensor(out=z[:, s, :], in0=z[:, s, :], in1=self.lng[:], op=ALU.mult),
             reads=[self.b_xt[k][s], self.b_lng], writes=[self.b_xt[k][s]])
        P.op("gpsimd", lambda e: e.tensor_tensor(out=z[:, s, :], in0=z[:, s, :], in1=self.lnb[:], op=ALU.add),
             reads=[self.b_xt[k][s], self.b_lnb], writes=[self.b_xt[k][s]])

    def stage_ffn(self, pfx, src, b_src, dst, b_dst, tcount, prepped=None):
        wd = self.din(pfx + "_wd", [DFF, D], F32)
        g = self.din(pfx + "_lg", [D], F32)
        b = self.din(pfx + "_lb", [D], F32)
        if prepped is None:
            scr, b_scr, its = self.prep_for(pfx)
            for it in its:
                it()
        else:
            scr, b_scr = prepped
        self.load_ln(g, b)
        self.load_res(wd, NF, D)
        for t in range(NTILE):
            k = (tcount + t) % 2
            self.load_x(src, b_src, t, k)
            self.transposes(k)
            self.gate_up(scr, b_scr, t)
            self.down_ln(k, self.hT, self.b_hT, NF, D, 0.5, dst, b_dst, t)
        return tcount + NTILE

    def stage_mix(self, pfx, ym, b_ym, src, b_src, dst, b_dst, tcount, tile_hook=None):
        wo = self.din(pfx + "_w_out", [D, D], F32)
        g = self.din(pfx + "_ln2_g", [D], F32)
        b = self.din(pfx + "_ln2_b", [D], F32)
        self.load_ln(g, b)
        self.load_res(wo, 8, D)
        for t in range(NTILE):
            k = (tcount + t) % 2
            self.load_x(src, b_src, t, k)
            y_ap = ym[:, t * TT:(t + 1) * TT].rearrange("(c p) t -> p c t", p=128)
            self.P.op("gpsimd", lambda e, y_ap=y_ap: e.dma_start(out=self.hT[:, 0:8, :], in_=y_ap), reads=[b_ym], writes=self.b_hT[0:8], dma=True)
            self.down_ln(k, self.hT, self.b_hT, 8, D, 1.0, dst, b_dst, t)
            if tile_hook is not None:
                tile_hook(t)
        return tcount + NTILE

    def stage_proj(self, pfx, src, b_src, outs, tcount):
        P = self.P
        win = self.din(pfx + "_w_in", [D, DIN], F32)
        bgate = self.din(pfx + "_b_gate", [24], F32)
        qT, kT, uT, vs, vw, gates = (outs[k][0] for k in ("qT", "kT", "uT", "vs", "vw", "gates"))
        bd = {k: outs[k][1] for k in ("qT", "kT", "uT", "vs", "vw", "gates")}
        P.op("sync", lambda e: e.dma_start(out=self.bg[:], in_=bgate.partition_broadcast(128)), writes=[self.b_bg], dma=True)
        for c in range(8):
            k = c % 2
            P.op("sync", lambda e, k=k, c=c: e.dma_start(out=self.stg[k][:, 0:DIN], in_=win[c * 128:(c + 1) * 128, :]), writes=[self.b_stg[k]], dma=True)
            self.cast(self.cast_eng(), self.wres[:, c * DIN:(c + 1) * DIN], self.stg[k][:, 0:DIN], [self.b_stg[k]], [self.b_wres[c]])
        W = lambda c, a, b_: self.wres[:, c * DIN + a: c * DIN + b_]
        for t in range(NTILE):
            k = (tcount + t) % 2
            self.load_x(src, b_src, t, k)
            self.transposes(k)
            tok = slice(t * TT, (t + 1) * TT)
            for i, col in enumerate(TCOLS):
                pp = 2 + i % 4
                for c in range(8):
                    P.op("tensor", lambda e, c=c, col=col, pp=pp: e.matmul(self.ps[pp][0:96, :], lhsT=W(c, col, col + 96), rhs=self.xT[:, c, :],
                                                                         start=(c == 0), stop=(c == 7)),
                         reads=[self.b_wres[c], self.b_xT[c]], writes=[self.b_ps[pp]], pe_accum=True)
                if i < 8:
                    P.op("scalar", lambda e, i=i, pp=pp: e.mul(self.oT[0:96, i, :], self.ps[pp][0:96, :], QSCALE),
                         reads=[self.b_ps[pp]], writes=[self.b_oT[i]])
                else:
                    P.op("vector", lambda e, i=i, pp=pp: e.tensor_copy(out=self.oT[0:96, i, :], in_=self.ps[pp][0:96, :]),
                         reads=[self.b_ps[pp]], writes=[self.b_oT[i]])
            P.op("sync", lambda e, tok=tok: e.dma_start(out=qT[:, :, tok].rearrange("h d t -> d h t"), in_=self.oT[:, 0:8, :]),
                 reads=self.b_oT[0:8], writes=[bd["qT"]], dma=True)
            P.op("sync", lambda e, tok=tok: e.dma_start(out=kT[:, :, tok].rearrange("h d t -> d h t"), in_=self.oT[0:96, 8:16, :]),
                 reads=self.b_oT[8:16], writes=[bd["kT"]], dma=True)
            for j in range(2):
                pp = 6 + j
                for c in range(8):
                    P.op("tensor", lambda e, c=c, j=j, pp=pp: e.matmul(self.ps[pp][:], lhsT=W(c, j * 128, (j + 1) * 128), rhs=self.xT[:, c, :],
                                                                     start=(c == 0), stop=(c == 7)),
                         reads=[self.b_wres[c], self.b_xT[c]], writes=[self.b_ps[pp]], pe_accum=True)
                P.op("vector", lambda e, j=j, pp=pp: e.tensor_copy(out=self.ou[:, j, :], in_=self.ps[pp][:]), reads=[self.b_ps[pp]], writes=[self.b_ou[j]])
            for j2 in range(2):
                for half in range(2):
                    P.op("sync", lambda e, tok=tok, j2=j2, half=half: e.dma_start(out=uT[2 * j2 + half][:, tok], in_=self.ou[half * 64:(half + 1) * 64, j2, :]),
                         reads=[self.b_ou[j2]], writes=[bd["uT"]], dma=True)
            for s in range(4):
                pp = 2 + s
                for c in range(8):
                    P.op("tensor", lambda e, c=c, s=s, pp=pp: e.matmul(self.ps[pp][:, 0:192], lhsT=self.xT[:, c, s * 128:(s + 1) * 128], rhs=W(c, 1600, 1792),
                                                                     start=(c == 0), stop=(c == 7)),
                         reads=[self.b_wres[c], self.b_xT[c]], writes=[self.b_ps[pp]], pe_accum=True)
                for c in range(8):
                    P.op("tensor", lambda e, c=c, s=s, pp=pp: e.matmul(self.ps[pp][:, 256:472], lhsT=self.xT[:, c, s * 128:(s + 1) * 128], rhs=W(c, 1984, 2200),
                                                                     start=(c == 0), stop=(c == 7), skip_group_check=True),
                         reads=[self.b_wres[c], self.b_xT[c]], writes=[self.b_ps[pp]], pe_accum=True)
                P.op("vector", lambda e, s=s, pp=pp: e.tensor_copy(out=self.ov[:, s, 0:192], in_=self.ps[pp][:, 0:192]), reads=[self.b_ps[pp]], writes=[self.b_ov[s]])
                P.op("vector", lambda e, s=s, pp=pp: e.tensor_copy(out=self.ov[:, s, 192:384], in_=self.ps[pp][:, 256:448]), reads=[self.b_ps[pp]], writes=[self.b_ov[s]])
                P.op("vector", lambda e, s=s, pp=pp: e.tensor_tensor(out=self.og[:, s, :], in0=self.ps[pp][:, 448:472], in1=self.bg[:], op=ALU.add),
                     reads=[self.b_ps[pp], self.b_bg], writes=[self.b_og[s]])
                P.op("scalar", lambda e, s=s: e.activation(out=self.og[:, s, :], in_=self.og[:, s, :], func=AF.Sigmoid), reads=[self.b_og[s]], writes=[self.b_og[s]])
            for g_ in range(2):
                P.op("gpsimd", lambda e, tok=tok, g_=g_: e.dma_start(out=vs[g_, tok, :].rearrange("(s p) d -> p s d", p=128), in_=self.ov[:, :, 96 * g_:96 * g_ + 96]),
                     reads=self.b_ov, writes=[bd["vs"]], dma=True)
                P.op("gpsimd", lambda e, tok=tok, g_=g_: e.dma_start(out=vw[g_, tok, :].rearrange("(s p) d -> p s d", p=128), in_=self.ov[:, :, 192 + 96 * g_:192 + 96 * g_ + 96]),
                     reads=self.b_ov, writes=[bd["vw"]], dma=True)
            P.op("gpsimd", lambda e, tok=tok: e.dma_start(out=gates[tok, :].rearrange("(s p) d -> p s d", p=128), in_=self.og[:]),
                 reads=self.b_og, writes=[bd["gates"]], dma=True)
        return tcount + NTILE


S = 16384
NQT = S // 512
NEG = -30000.0
VP = 128
GELU_C = float(2.0 * (2.0 / np.pi) ** 0.5)


def b_consts():
    import ml_dtypes
    c = {}
    c["ident_f"] = np.eye(128, dtype=np.float32)
    c["ident_b"] = np.eye(128, dtype=np.float32).astype(ml_dtypes.bfloat16)
    k = np.arange(128)[:, None]
    t = np.arange(512)[None, :]
    masks = np.zeros((16, 128, 512), np.float32)
    for i in range(8):
        kp = 128 * i - 512 + k
        masks[i] = np.where((t - kp >= 0) & (t - kp < 512), 0.0, NEG)
    for r in range(4):
        masks[8 + r] = np.where(16 * k + 31 <= 512 * r + t, 0.0, NEG)
    for i in range(4):
        masks[12 + i] = np.where(128 * i + k <= t, 0.0, NEG)
    c["masks"] = masks.astype(ml_dtypes.bfloat16)
    kk = np.arange(2048)[None, :]
    r = np.arange(32)[:, None]
    c["kpat"] = (((kk // 64) % 32) == r).astype(np.float32).astype(ml_dtypes.bfloat16)
    W = np.zeros((1024, 256), np.float32)
    for j in range(256):
        for n, w in ((4 * j - 1, 0.5), (4 * j, 1.0), (4 * j + 1, 1.0), (4 * j + 2, 1.0), (4 * j + 3, 0.5)):
            if 0 <= n < 1023:
                W[n, j] = w
    c["wimp"] = W.reshape(8, 128, 256).astype(ml_dtypes.bfloat16)
    fp = np.zeros((128, 4), np.float32)
    fp[:64, 0] = 1000.0; fp[:64, 1] = 2000.0
    fp[64:, 1] = 2000.0; fp[64:, 2] = 3000.0
    c["fpat"] = fp
    return c


def pool_consts(j):
    w = 2 ** (j + 1)
    a = np.zeros((64, 4), np.float32)
    a[:, j] = 1.0 / w
    A = np.zeros((64, 4, 16), np.float32)
    tt = np.arange(16)
    A[:, j, :] = 1.0 / np.minimum(tt + 1, w)
    return a, A


def _V(x, e):
    return x(e) if callable(x) else x


class BB:
    def __init__(self, nc, P, dram):
        self.nc = nc
        self.P = P
        self.dram = dram

    def din(self, name, shape, dt):
        return self.dram.din(name, shape, dt)

    def build(self, pfx, src, ymc, b_ymc):
        P = self.P
        nc = self.nc
        op = P.op
        qT, b_qT = src["qT"]; gates, b_gates = src["gates"]
        kcT, b_kcT = src["kcT"]; vcT, b_vcT = src["vcT"]; ksT, b_ksT = src["ksT"]; kwT, b_kwT = src["kwT"]
        vs, b_vs = src["vs"]; vw, b_vw = src["vw"]; uT, b_uT = src["uT"]
        cw = {}
        for kv in ("k", "v"):
            cw[kv + "w1"] = self.din(pfx + "_cmp_%s_w1" % kv, [3072, 96], F32)
            cw[kv + "w2"] = self.din(pfx + "_cmp_%s_w2" % kv, [96, 96], F32)
            cw[kv + "pos"] = self.din(pfx + "_cmp_%s_pos" % kv, [32, 96], F32)
        pool_w = self.din(pfx + "_pool_w", [64, 64], F32)
        pool_sc = self.din(pfx + "_pool_sc", [64, 1], F32)
        pool_a = self.din("pool_a", [64, 4], F32)
        pool_A = self.din("pool_A", [64, 4, 16], F32)
        d_identf = self.din("ident", [128, 128], F32)
        d_identb = self.din("ident_b", [128, 128], BF16)
        d_masks = self.din("masks", [16, 128, 512], BF16)
        d_kpat = self.din("kpat", [32, 2048], BF16)
        d_wimp = self.din("wimp", [8, 128, 256], BF16)
        d_fpat = self.din("fpat", [128, 4], F32)
        d_gsel = self.din("gsel", [6, 24], F32)
        def ysl(r0, r1, t0, n):
            return ymc[t0 // 4096, r0:r1, (t0 % 4096):(t0 % 4096) + n]

        ps_s = [P.psum("ps_s%d" % i, [128, 512]) for i in range(2)]; b_ps_s = [P.pbuf() for i in range(2)]
        ps_o = [P.psum("ps_o%d" % i, [128, 512]) for i in range(2)]; b_ps_o = [P.pbuf() for i in range(2)]
        ps_i = [P.psum("ps_i%d" % i, [128, 512]) for i in range(2)]; b_ps_i = [P.pbuf() for i in range(2)]
        ps_t = [P.psum("ps_t%d" % i, [128, 512]) for i in range(2)]; b_ps_t = [P.pbuf() for i in range(2)]

        idf = P.sbuf("idf", [128, 128], F32); b_idf = P.buf()
        idb = P.sbuf("idb", [128, 128], BF16); b_idb = P.buf()
        masks = P.sbuf("masks", [128, 16, 512], BF16); b_masks = P.buf()
        wimp = P.sbuf("wimp", [128, 8, 256], BF16); b_wimp = P.buf()
        fpat = P.sbuf("fpat", [128, 4], F32); b_fpat = P.buf()
        ksa = P.sbuf("ksa", [128, S], BF16); b_ksa = P.buf()
        vsa = P.sbuf("vsa", [128, 128, VP], BF16); b_vsa = P.buf()
        kcmpT = P.sbuf("kcmpT", [96, 1024], BF16); b_kcmpT = P.buf()
        vcmpa = P.sbuf("vcmpa", [128, 8, VP], BF16); b_vcmpa = P.buf()
        op("sync", lambda e: e.dma_start(out=idf[:], in_=d_identf), writes=[b_idf], dma=True)
        op("sync", lambda e: e.dma_start(out=idb[:], in_=d_identb), writes=[b_idb], dma=True)
        op("sync", lambda e: e.dma_start(out=masks[:], in_=d_masks.rearrange("m p t -> p m t")), writes=[b_masks], dma=True)
        op("sync", lambda e: e.dma_start(out=wimp[:], in_=d_wimp.rearrange("c p j -> p c j")), writes=[b_wimp], dma=True)
        op("sync", lambda e: e.dma_start(out=fpat[:], in_=d_fpat), writes=[b_fpat], dma=True)
        gsel = P.sbuf("gsel", [128, 6, 24], F32); b_gsel = P.buf()
        op("sync", lambda e: e.dma_start(out=gsel[:].rearrange("p a b -> p (a b)"), in_=d_gsel.rearrange("a b -> (a b)").partition_broadcast(128)), writes=[b_gsel], dma=True)
        op("gpsimd", lambda e: e.memset(kcmpT[:], 0.0), writes=[b_kcmpT])
        op("gpsimd", lambda e: e.memset(vcmpa[:], 1.0), writes=[b_vcmpa])
        op("gpsimd", lambda e: e.memset(vsa[:], 1.0), writes=[b_vsa])

        q_sb = [P.sbuf("q_sb%d" % i, [96, 4, 512], BF16) for i in range(2)]; b_q = [P.buf() for i in range(2)]
        g_sb = [P.sbuf("g_sb%d" % i, [128, 4, 6], F32) for i in range(2)]; b_g = [P.buf() for i in range(2)]
        g_full = P.sbuf("g_full", [128, 4, 24], F32); b_gfull = P.buf()
        g_tmp = P.sbuf("g_tmp", [128, 4, 6, 24], F32); b_gtmp = P.buf()
        kw_sb = [P.sbuf("kw_sb%d" % i, [96, 1024], BF16) for i in range(2)]; b_kw = [P.buf() for i in range(2)]
        vwa = [P.sbuf("vwa%d" % i, [128, 8, VP], BF16) for i in range(2)]; b_vwa = [P.buf() for i in range(2)]
        e_all = [P.sbuf("e_all%d" % i, [128, 8, 512], BF16) for i in range(2)]; b_eall = [P.buf() for i in range(2)]
        e_sb = [P.sbuf("e_sb%d" % i, [128, 512], BF16) for i in range(3)]; b_e = [P.buf() for i in range(3)]
        oTs = [P.sbuf("oTs%d" % i, [97, 512], F32) for i in range(2)]; b_oTs = [P.buf() for i in range(2)]
        impacc = P.sbuf("impacc", [128, 4, 258], F32); b_imp = [P.buf() for i in range(4)]
        mr = P.sbuf("mr", [128, 256], F32); b_mr = P.buf()
        m8 = P.sbuf("m8", [128, 3, 8], F32); b_m8 = P.buf()
        selb = P.sbuf("selb", [128, 4, 352], F32); b_selb = [P.buf() for i in range(4)]
        qaug = P.sbuf("qaug", [128, 2, 8, 512], BF16); b_qaug = [[P.buf() for v in range(8)] for h in range(2)]
        oacc = [P.sbuf("oacc%d" % i, [128, 4, 2, 96], F32) for i in range(2)]; b_oacc = [P.buf() for i in range(2)]
        oTo = [P.sbuf("oTo%d" % i, [96, 2, 512], BF16) for i in range(2)]; b_oTo = [P.buf() for i in range(2)]
        rd = P.sbuf("rd", [128, 8, 4], F32); b_rd = P.buf()
        qa_f = qaug[:].rearrange("p a b c -> p (a b c)").bitcast(F32)
        ea0 = e_all[0][:].rearrange("p a b -> p (a b)")
        ea1_f = e_all[1][:].rearrange("p a b -> p (a b)").bitcast(F32)
        w1s = qa_f[0:96, 0:3072].rearrange("d (p e) -> d p e", e=96); b_w1s = P.buf()
        gx = qa_f[0:96, 3072:4096]; b_gx = P.buf()
        w1b = ea0[0:96, 0:3072].rearrange("d (p e) -> d p e", e=96); b_w1b = P.buf()
        gb = ea0[0:96, 3072:4096]; b_gb = P.buf()
        gy = ea1_f[0:96, 0:1024]; b_gy = P.buf()
        w2s = P.sbuf("w2s", [96, 96], F32); b_w2s = P.buf()
        w2b = P.sbuf("w2b", [96, 96], BF16); b_w2b = P.buf()
        poss = P.sbuf("poss", [32, 96], F32); b_poss = P.buf()
        posT = P.sbuf("posT", [96, 32], BF16); b_posT = P.buf()
        cb = P.sbuf("cb", [96, 1], F32); b_cb = P.buf()
        stage = getattr(self, 'stage', 99)
        for i in range(8):
            op("gpsimd", lambda e, i=i: e.dma_start(out=ksa[96:128, i * 2048:(i + 1) * 2048], in_=d_kpat), writes=[b_ksa], dma=True)
        for i in range(16 if stage >= 2 else 0):
            op("gpsimd", lambda e, i=i: e.dma_start(out=vsa[:, i * 8:(i + 1) * 8, 0:96],
                                                                        in_=_V(vs, e)[i * 1024:(i + 1) * 1024, :].rearrange("(k p) d -> p k d", p=128)),
               reads=[b_vs], writes=[b_vsa], dma=True)

        for kv, srcT, b_srcT in ((("k", kcT, b_kcT), ("v", vcT, b_vcT)) if stage >= 1 else ()):
            op("sync", lambda e, srcT=srcT: e.dma_start(out=ksa[0:96, :], in_=srcT), reads=[b_srcT], writes=[b_ksa], dma=True)
            op("sync", lambda e, kv=kv: e.dma_start(out=w1s, in_=cw[kv + "w1"].rearrange("(p d) e -> d p e", d=96)), writes=[b_w1s], dma=True)
            op("sync", lambda e, kv=kv: e.dma_start(out=w2s[:], in_=cw[kv + "w2"]), writes=[b_w2s], dma=True)
            op("sync", lambda e, kv=kv: e.dma_start(out=poss[:], in_=cw[kv + "pos"]), writes=[b_poss], dma=True)
            op("vector", lambda e: e.tensor_copy(out=w1b, in_=w1s), reads=[b_w1s], writes=[b_w1b])
            op("vector", lambda e: e.tensor_copy(out=w2b[:], in_=w2s[:]), reads=[b_w2s], writes=[b_w2b])
            op("tensor", lambda e: e.transpose(out=ps_t[0][0:96, 0:32], in_=poss[:], identity=idf[0:32, 0:32]), reads=[b_poss, b_idf], writes=[b_ps_t[0]], pe_accum=True)
            op("vector", lambda e: e.tensor_copy(out=posT[:], in_=ps_t[0][0:96, 0:32]), reads=[b_ps_t[0]], writes=[b_posT])
            for p in range(32):
                op("tensor", lambda e, p=p: e.matmul(ps_t[1][0:96, 0:1], lhsT=w1b[:, p, :], rhs=posT[:, p:p + 1], start=(p == 0), stop=(p == 31)),
                   reads=[b_w1b, b_posT], writes=[b_ps_t[1]], pe_accum=True)
            op("vector", lambda e: e.tensor_copy(out=cb[:], in_=ps_t[1][0:96, 0:1]), reads=[b_ps_t[1]], writes=[b_cb])
            if stage < 1.2:
                continue
            op("gpsimd", lambda e: e.memset(gb, 0.0), writes=[b_gb])
            kview = ksa[0:96, :].rearrange("d (n s) -> d n s", s=16)
            for half, (n0, N) in enumerate(((0, 512), (512, 511))):
                pp = ps_s[half]
                for p in range(32):
                    rhs = kview[:, n0:n0 + N, p] if p < 16 else kview[:, n0 + 1:n0 + 1 + N, p - 16]
                    op("tensor", lambda e, p=p, rhs=rhs, pp=pp, N=N: e.matmul(pp[0:96, 0:N], lhsT=w1b[:, p, :], rhs=rhs, start=(p == 0), stop=(p == 31)),
                       reads=[b_w1b, b_ksa], writes=[b_ps_s[half]], pe_accum=True)
                sl = slice(n0, n0 + N)
                op("vector", lambda e, pp=pp, N=N, sl=sl: e.tensor_scalar(out=gx[:, sl], in0=pp[0:96, 0:N], scalar1=cb[:, 0:1], scalar2=None, op0=ALU.add),
                   reads=[b_ps_s[half], b_cb], writes=[b_gx])
                op("vector", lambda e, sl=sl: e.tensor_tensor(out=gy[:, sl], in0=gx[:, sl], in1=gx[:, sl], op=ALU.mult), reads=[b_gx], writes=[b_gy])
                op("vector", lambda e, sl=sl: e.tensor_scalar(out=gy[:, sl], in0=gy[:, sl], scalar1=0.044715, scalar2=1.0, op0=ALU.mult, op1=ALU.add), reads=[b_gy], writes=[b_gy])
                op("vector", lambda e, sl=sl: e.tensor_tensor(out=gy[:, sl], in0=gy[:, sl], in1=gx[:, sl], op=ALU.mult), reads=[b_gy, b_gx], writes=[b_gy])
                op("scalar", lambda e, sl=sl: e.activation(out=gy[:, sl], in_=gy[:, sl], func=AF.Sigmoid, scale=GELU_C), reads=[b_gy], writes=[b_gy])
                op("vector", lambda e, sl=sl: e.tensor_tensor(out=gb[:, sl], in0=gy[:, sl], in1=gx[:, sl], op=ALU.mult), reads=[b_gy, b_gx], writes=[b_gb])
            if stage < 1.3:
                continue
            if kv == "k":
                for half, (n0, N) in enumerate(((0, 512), (512, 511))):
                    op("tensor", lambda e, half=half, n0=n0, N=N: e.matmul(ps_o[half][0:96, 0:N], lhsT=w2b[:], rhs=gb[:, n0:n0 + N], start=True, stop=True),
                       reads=[b_w2b, b_gb], writes=[b_ps_o[half]], pe_accum=True)
                    op("vector", lambda e, half=half, n0=n0, N=N: e.tensor_copy(out=kcmpT[:, n0:n0 + N], in_=ps_o[half][0:96, 0:N]), reads=[b_ps_o[half]], writes=[b_kcmpT])
            else:
                for c in range(8):
                    k2 = c % 2
                    op("tensor", lambda e, c=c, k2=k2: e.matmul(ps_o[k2][:, 0:96], lhsT=gb[:, c * 128:(c + 1) * 128], rhs=w2b[:], start=True, stop=True),
                       reads=[b_w2b, b_gb], writes=[b_ps_o[k2]], pe_accum=True)
                    op("vector", lambda e, c=c, k2=k2: e.tensor_copy(out=vcmpa[:, c, 0:96], in_=ps_o[k2][:, 0:96]), reads=[b_ps_o[k2]], writes=[b_vcmpa])

        stage = getattr(self, 'stage', 99)
        op("sync", lambda e: e.dma_start(out=ksa[0:96, :], in_=ksT), reads=[b_ksT], writes=[b_ksa], dma=True)
        CH = 512
        pw_s = P.sbuf("pw_s", [64, 64], F32); b_pw_s = P.buf()
        pw_b = P.sbuf("pw_b", [64, 64], BF16); b_pw_b = P.buf()
        psc = P.sbuf("psc", [64, 1], F32); b_psc = P.buf()
        pa = P.sbuf("pa", [64, 4], F32); b_pa = P.buf()
        pA = P.sbuf("pA", [64, 4, 16], F32); b_pA = P.buf()
        L_ = 16 + CH
        ub = [P.sbuf("ub%d" % i, [64, L_], F32)[:] for i in range(2)]; b_ub = [P.buf() for i in range(2)]
        sw = [P.sbuf("sw%d" % i, [64, L_], F32)[:] for i in range(4)]; b_sw = [P.buf() for i in range(4)]
        acc = P.sbuf("pacc", [64, CH], F32)[:]; b_acc = P.buf()
        a16 = P.sbuf("a16", [64, 2, 16], F32)[:]; b_a16 = P.buf()
        accb = P.sbuf("paccb", [64, CH], BF16)[:]; b_accb = P.buf()
        pout = [P.sbuf("pout%d" % i, [64, CH], BF16)[:] for i in range(2)]; b_pout = [P.buf() for i in range(2)]
        op("sync", lambda e: e.dma_start(out=pw_s[:], in_=pool_w), writes=[b_pw_s], dma=True)
        op("sync", lambda e: e.dma_start(out=psc[:], in_=pool_sc), writes=[b_psc], dma=True)
        op("sync", lambda e: e.dma_start(out=pa[:], in_=pool_a), writes=[b_pa], dma=True)
        op("sync", lambda e: e.dma_start(out=pA[:], in_=pool_A), writes=[b_pA], dma=True)
        op("vector", lambda e: e.tensor_copy(out=pw_b[:], in_=pw_s[:]), reads=[b_pw_s], writes=[b_pw_b])
        def pool_chunk(ci):
            k = ci % 2
            u = ub[k]
            if ci == 0:
                op("gpsimd", lambda e, u=u: e.memset(u[:, 0:16], 0.0), writes=[b_ub[k]])
                op("sync", lambda e, u=u: e.dma_start(out=u[:, 16:], in_=uT[:, 0:CH]), reads=[b_uT], writes=[b_ub[k]], dma=True)
            else:
                op("sync", lambda e, u=u, ci=ci: e.dma_start(out=u, in_=uT[:, ci * CH - 16:(ci + 1) * CH]), reads=[b_uT], writes=[b_ub[k]], dma=True)
            L = 16 + CH
            prev, b_prev = u, b_ub[k]
            for wi, sh in enumerate((1, 2, 4, 8)):
                lo = 2 * sh - 1
                dst = sw[wi]
                op("gpsimd", lambda e, dst=dst, prev=prev, lo=lo, sh=sh, L=L: e.tensor_tensor(out=dst[:, lo:L], in0=prev[:, lo:L], in1=prev[:, lo - sh:L - sh], op=ALU.add),
                   reads=[b_prev], writes=[b_sw[wi]])
                prev, b_prev = dst, b_sw[wi]
            op("vector", lambda e: e.tensor_scalar(out=acc, in0=sw[0][:, 16:], scalar1=pa[:, 0:1], scalar2=None, op0=ALU.mult), reads=[b_sw[0], b_pa], writes=[b_acc])
            for wi in range(1, 4):
                op("vector", lambda e, wi=wi: e.scalar_tensor_tensor(out=acc, in0=sw[wi][:, 16:], scalar=pa[:, wi:wi + 1], in1=acc, op0=ALU.mult, op1=ALU.add),
                   reads=[b_sw[wi], b_pa, b_acc], writes=[b_acc])
            if ci == 0:
                op("vector", lambda e: e.tensor_tensor(out=a16[:, 0, :], in0=sw[0][:, 16:32], in1=pA[:, 0, :], op=ALU.mult), reads=[b_sw[0], b_pA], writes=[b_a16])
                for wi in range(1, 4):
                    op("vector", lambda e, wi=wi: e.tensor_tensor(out=a16[:, 1, :], in0=sw[wi][:, 16:32], in1=pA[:, wi, :], op=ALU.mult), reads=[b_sw[wi], b_pA, b_a16], writes=[b_a16])
                    op("vector", lambda e: e.tensor_tensor(out=a16[:, 0, :], in0=a16[:, 0, :], in1=a16[:, 1, :], op=ALU.add), reads=[b_a16], writes=[b_a16])
                op("vector", lambda e: e.tensor_copy(out=acc[:, 0:16], in_=a16[:, 0, :]), reads=[b_a16, b_acc], writes=[b_acc])
            op("vector", lambda e, u=u: e.tensor_tensor(out=accb, in0=acc, in1=u[:, 16:], op=ALU.subtract), reads=[b_acc, b_ub[k]], writes=[b_accb])
            for hh in range(CH // 512):
                pbi = nxt("t", 2)
                op("tensor", lambda e, hh=hh, pbi=pbi: e.matmul(ps_t[pbi][0:64, :], lhsT=pw_b[:], rhs=accb[:, hh * 512:(hh + 1) * 512], start=True, stop=True),
                   reads=[b_pw_b, b_accb], writes=[b_ps_t[pbi]], pe_accum=True)
                op("vector", lambda e, hh=hh, k=k, pbi=pbi: e.tensor_scalar(out=pout[k][:, hh * 512:(hh + 1) * 512], in0=ps_t[pbi][0:64, :], scalar1=psc[:, 0:1], scalar2=None, op0=ALU.mult),
                   reads=[b_ps_t[pbi], b_psc], writes=[b_pout[k]])
            op("sync", lambda e, k=k, ci=ci: e.dma_start(out=ysl(0, 64, ci * CH, CH), in_=pout[k]), reads=[b_pout[k]], writes=[b_ymc], dma=True)


        P.barrier()

        for i in range(2):
            op("gpsimd", lambda e, i=i: e.memset(vwa[i][:], 1.0), writes=[b_vwa[i]])
        op("gpsimd", lambda e: e.memset(selb[:], 0.0), writes=b_selb)
        op("gpsimd", lambda e: e.memset(impacc[:], 0.0), writes=b_imp)

        cnt = {"s": 0, "o": 0, "e": 0, "t": 0, "rd": 0}

        def nxt(key, n):
            v = cnt[key] % n
            cnt[key] += 1
            return v

        def load_tile(qt):
            k = qt % 2
            tok = slice(qt * 512, (qt + 1) * 512)
            op("sync", lambda e: e.dma_start(out=q_sb[k][:], in_=qT[:, :, tok].rearrange("h d t -> d h t")), reads=[b_qT], writes=[b_q[k]], dma=True)
            op("sync", lambda e: e.dma_start(out=g_full[:], in_=gates[tok, :].rearrange("(s p) c -> p s c", p=128)), reads=[b_gates], writes=[b_gfull], dma=True)
            op("gpsimd", lambda e: e.tensor_tensor(out=g_tmp[:], in0=g_full[:].unsqueeze(2).to_broadcast([128, 4, 6, 24]), in1=gsel[:].unsqueeze(1).to_broadcast([128, 4, 6, 24]), op=ALU.mult),
               reads=[b_gfull, b_gsel], writes=[b_gtmp])
            op("vector", lambda e: e.tensor_reduce(out=g_sb[k][:], in_=g_tmp[:], axis=AX.X, op=ALU.add), reads=[b_gtmp], writes=[b_g[k]])
            if qt == 0:
                op("sync", lambda e: e.dma_start(out=kw_sb[k][:, 512:1024], in_=kwT[:, 0:512]), reads=[b_kwT], writes=[b_kw[k]], dma=True)
                op("sync", lambda e: e.dma_start(out=vwa[k][:, 4:8, 0:96], in_=_V(vw, e)[0:512, :].rearrange("(k p) d -> p k d", p=128)), reads=[b_vw], writes=[b_vwa[k]], dma=True)
            else:
                op("sync", lambda e: e.dma_start(out=kw_sb[k][:], in_=kwT[:, qt * 512 - 512: qt * 512 + 512]), reads=[b_kwT], writes=[b_kw[k]], dma=True)
                op("sync", lambda e: e.dma_start(out=vwa[k][:, :, 0:96], in_=_V(vw, e)[qt * 512 - 512: qt * 512 + 512, :].rearrange("(k p) d -> p k d", p=128)), reads=[b_vw], writes=[b_vwa[k]], dma=True)

        def finish_branch(qt, po, h_own, gcol, first, rd_out=None):
            k = qt % 2
            bi = nxt("t", 2)
            osb = oTs[bi]
            op("vector", lambda e: e.tensor_copy(out=osb[:], in_=ps_o[po][0:97, :]), reads=[b_ps_o[po]], writes=[b_oTs[bi]])
            pt = ps_t[bi]
            for sub in range(4):
                op("tensor", lambda e, sub=sub: e.transpose(out=pt[:, sub * 128: sub * 128 + 97], in_=osb[0:97, sub * 128:(sub + 1) * 128], identity=idf[0:97, 0:97]),
                   reads=[b_oTs[bi], b_idf], writes=[b_ps_t[bi]], pe_accum=True)
            ri = nxt("rd", 8)
            ptv = pt[:].rearrange("p (s c) -> p s c", c=128)
            op("vector", lambda e: e.tensor_scalar(out=rd[:, ri, :], in0=ptv[:, :, 96], scalar1=1e-30, scalar2=None, op0=ALU.max), reads=[b_ps_t[bi]], writes=[b_rd])
            op("vector", lambda e: e.reciprocal(rd[:, ri, :], rd[:, ri, :]), reads=[b_rd], writes=[b_rd])
            if h_own is not None:
                ci = nxt("rd", 8)
                op("vector", lambda e: e.tensor_tensor(out=rd[:, ci, :], in0=rd[:, ri, :], in1=g_sb[k][:, :, h_own * 3 + gcol], op=ALU.mult), reads=[b_rd, b_g[k]], writes=[b_rd])
                dbg = getattr(self, "dbg_branch", None)
                if dbg is not None and dbg != gcol:
                    op("vector", lambda e: e.memset(rd[:, ci, :], 0.0), reads=[b_rd], writes=[b_rd])
                for sub in range(4):
                    if first:
                        op("vector", lambda e, sub=sub: e.tensor_scalar(out=oacc[k][:, sub, h_own, :], in0=ptv[:, sub, 0:96], scalar1=rd[:, ci, sub:sub + 1], scalar2=None, op0=ALU.mult),
                           reads=[b_ps_t[bi], b_rd], writes=[b_oacc[k]])
                    else:
                        op("vector", lambda e, sub=sub: e.scalar_tensor_tensor(out=oacc[k][:, sub, h_own, :], in0=ptv[:, sub, 0:96], scalar=rd[:, ci, sub:sub + 1],
                                                                              in1=oacc[k][:, sub, h_own, :], op0=ALU.mult, op1=ALU.add),
                           reads=[b_ps_t[bi], b_rd, b_oacc[k]], writes=[b_oacc[k]])
            return ri

        def cmp_part(qt):
            k = qt % 2
            nct = min(8, qt // 4 + 1)
            for h in range(4):
                ea = e_all[h % 2]
                for c in range(nct):
                    si = nxt("s", 2)
                    partial = qt < 4 * c + 4
                    op("tensor", lambda e, c=c, h=h, si=si, partial=partial: e.matmul(ps_s[si][:], lhsT=kcmpT[:, c * 128:(c + 1) * 128], rhs=q_sb[k][:, h, :], start=True, stop=not partial),
                       reads=[b_kcmpT, b_q[k]], writes=[b_ps_s[si]], pe_accum=True)
                    if partial:
                        r = qt - 4 * c
                        op("tensor", lambda e, si=si, r=r: e.matmul(ps_s[si][:], lhsT=idb[:], rhs=masks[:, 8 + r, :], start=False, stop=True),
                           reads=[b_idb, b_masks], writes=[b_ps_s[si]], pe_accum=True)
                    op("scalar", lambda e, c=c, si=si, ea=ea: e.activation(out=ea[:, c, :], in_=ps_s[si][:], func=AF.Exp), reads=[b_ps_s[si]], writes=[b_eall[h % 2]])
                po = nxt("o", 2)
                for c in range(nct):
                    op("tensor", lambda e, c=c, ea=ea, po=po: e.matmul(ps_o[po][:, :], lhsT=vcmpa[:, c, 0:128], rhs=ea[:, c, :], start=(c == 0), stop=(c == nct - 1)),
                       reads=[b_vcmpa, b_eall[h % 2]], writes=[b_ps_o[po]], pe_accum=True)
                for sub in range(4):
                    pi = ps_i[sub // 2]
                    for c in range(nct):
                        op("tensor", lambda e, c=c, sub=sub, ea=ea, pi=pi: e.matmul(pi[:, (sub % 2) * 256:(sub % 2) * 256 + 256], lhsT=ea[:, c, sub * 128:(sub + 1) * 128], rhs=wimp[:, c, :],
                                                                               start=(c == 0), stop=(c == nct - 1)),
                           reads=[b_wimp, b_eall[h % 2]], writes=[b_ps_i[sub // 2]], pe_accum=True)
                ri = finish_branch(qt, po, h if h < 2 else None, 0, True)
                for sub in range(4):
                    pi = ps_i[sub // 2]
                    src = pi[:, (sub % 2) * 256:(sub % 2) * 256 + 256]
                    if h == 0:
                        op("vector", lambda e, sub=sub, src=src, ri=ri: e.tensor_scalar(out=impacc[:, sub, 1:257], in0=src, scalar1=rd[:, ri, sub:sub + 1], scalar2=None, op0=ALU.mult),
                           reads=[b_ps_i[sub // 2], b_rd], writes=[b_imp[sub]])
                    else:
                        op("vector", lambda e, sub=sub, src=src, ri=ri: e.scalar_tensor_tensor(out=impacc[:, sub, 1:257], in0=src, scalar=rd[:, ri, sub:sub + 1], in1=impacc[:, sub, 1:257],
                                                                                            op0=ALU.mult, op1=ALU.add),
                           reads=[b_ps_i[sub // 2], b_rd, b_imp[sub]], writes=[b_imp[sub]])

        def topk_part(qt):
            for sub in range(4):
                st = 4 * qt + sub
                sc = impacc[:, sub, 1:257]
                op("vector", lambda e, sub=sub, st=st: e.tensor_tensor(out=impacc[:, sub, 2 * st:2 * st + 3], in0=impacc[:, sub, 2 * st:2 * st + 3], in1=fpat[:, 0:3], op=ALU.add),
                   reads=[b_imp[sub], b_fpat], writes=[b_imp[sub]])
                op("vector", lambda e, sub=sub: e.tensor_scalar(out=impacc[:, sub, 1:2], in0=impacc[:, sub, 1:2], scalar1=4000.0, scalar2=None, op0=ALU.add), reads=[b_imp[sub]], writes=[b_imp[sub]])
                op("vector", lambda e, sc=sc: e.max(out=m8[:, 0, :], in_=sc), reads=[b_imp[sub]], writes=[b_m8])
                op("vector", lambda e, sc=sc: e.match_replace(out=mr[:], in_to_replace=m8[:, 0, :], in_values=sc, imm_value=-1e9), reads=[b_imp[sub], b_m8], writes=[b_mr])
                op("vector", lambda e: e.max(out=m8[:, 1, :], in_=mr[:]), reads=[b_mr], writes=[b_m8])
                op("vector", lambda e: e.tensor_reduce(out=m8[:, 2, 0:1], in_=m8[:, 1, :], axis=AX.X, op=ALU.min), reads=[b_m8], writes=[b_m8])
                op("vector", lambda e, sub=sub, sc=sc: e.tensor_scalar(out=selb[:, sub, 96:352], in0=sc, scalar1=m8[:, 2, 0:1], scalar2=NEG, op0=ALU.is_lt, op1=ALU.mult),
                   reads=[b_imp[sub], b_m8], writes=[b_selb[sub]])

        def qaug_part(qt):
            k = qt % 2
            nv = (qt + 1 + 3) // 4
            for v in range(nv):
                bi = nxt("t", 2)
                pt = ps_t[bi]
                for sub in range(4):
                    op("tensor", lambda e, sub=sub, v=v, pt=pt: e.transpose(out=pt[:, sub * 128:(sub + 1) * 128], in_=selb[:, sub, 32 * v:32 * v + 128], identity=idf[:]),
                       reads=[b_selb[sub], b_idf], writes=[b_ps_t[bi]], pe_accum=True)
                for h in range(2):
                    op("vector", lambda e, h=h, v=v, pt=pt: e.tensor_copy(out=qaug[96:128, h, v, :], in_=pt[96:128, :]),
                       reads=[b_ps_t[bi]], writes=[b_qaug[h][v]])
                    op("gpsimd", lambda e, h=h, v=v: e.tensor_copy(out=qaug[0:96, h, v, :], in_=q_sb[k][:, h, :]), reads=[b_q[k]], writes=[b_qaug[h][v]])

        def win_part(qt):
            k = qt % 2
            tiles = list(range(4, 8)) if qt == 0 else list(range(8))
            items = [(h, n, i) for h in range(2) for n, i in enumerate(tiles)]
            pos = {}
            pend = None

            def pv(item):
                h, n, i = item
                ei, po = pos[item]
                op("tensor", lambda e, i=i, ei=ei, po=po, n=n: e.matmul(ps_o[po][:, :], lhsT=vwa[k][:, i, 0:128], rhs=e_sb[ei][:], start=(n == 0), stop=(n == len(tiles) - 1)),
                   reads=[b_vwa[k], b_e[ei]], writes=[b_ps_o[po]], pe_accum=True)
                if n == len(tiles) - 1:
                    finish_branch(qt, po, h, 2, False)

            po_h = {}
            for item in items:
                h, n, i = item
                if n == 0:
                    po_h[h] = nxt("o", 2)
                si = nxt("s", 2)
                ei = nxt("e", 3)
                pos[item] = (ei, po_h[h])
                op("tensor", lambda e, i=i, si=si, h=h: e.matmul(ps_s[si][:], lhsT=kw_sb[k][:, i * 128:(i + 1) * 128], rhs=q_sb[k][:, h, :], start=True, stop=False),
                   reads=[b_kw[k], b_q[k]], writes=[b_ps_s[si]], pe_accum=True)
                op("tensor", lambda e, i=i, si=si: e.matmul(ps_s[si][:], lhsT=idb[:], rhs=masks[:, i, :], start=False, stop=True),
                   reads=[b_idb, b_masks], writes=[b_ps_s[si]], pe_accum=True)
                op("scalar", lambda e, si=si, ei=ei: e.activation(out=e_sb[ei][:], in_=ps_s[si][:], func=AF.Exp), reads=[b_ps_s[si]], writes=[b_e[ei]])
                if pend is not None:
                    pv(pend)
                pend = item
            pv(pend)

        def slc_part(qt):
            k = qt % 2
            nkt = 4 * (qt + 1)
            items = [(h, kt) for h in range(2) for kt in range(nkt)]
            pos = {}
            pend = None
            po_h = {}

            def pv(item):
                h, kt = item
                ei, po = pos[item]
                op("tensor", lambda e, kt=kt, ei=ei, po=po: e.matmul(ps_o[po][:, :], lhsT=vsa[:, kt, 0:128], rhs=e_sb[ei][:], start=(kt == 0), stop=(kt == nkt - 1)),
                   reads=[b_vsa, b_e[ei]], writes=[b_ps_o[po]], pe_accum=True)
                if kt == nkt - 1:
                    finish_branch(qt, po, h, 1, False)

            for item in items:
                h, kt = item
                if kt == 0:
                    po_h[h] = nxt("o", 2)
                si = nxt("s", 2)
                ei = nxt("e", 3)
                pos[item] = (ei, po_h[h])
                v = kt // 16
                diag = kt >= 4 * qt
                op("tensor", lambda e, kt=kt, si=si, v=v, diag=diag, h=h: e.matmul(ps_s[si][:], lhsT=ksa[:, kt * 128:(kt + 1) * 128], rhs=qaug[:, h, v, :], start=True, stop=not diag),
                   reads=[b_ksa, b_qaug[h][v]], writes=[b_ps_s[si]], pe_accum=True)
                if diag:
                    op("tensor", lambda e, kt=kt, si=si: e.matmul(ps_s[si][:], lhsT=idb[:], rhs=masks[:, 12 + kt - 4 * qt, :], start=False, stop=True),
                       reads=[b_idb, b_masks], writes=[b_ps_s[si]], pe_accum=True)
                op("scalar", lambda e, si=si, ei=ei: e.activation(out=e_sb[ei][:], in_=ps_s[si][:], func=AF.Exp), reads=[b_ps_s[si]], writes=[b_e[ei]])
                if pend is not None:
                    pv(pend)
                pend = item
            pv(pend)

        def out_part(qt):
            k = qt % 2
            tok = slice(qt * 512, (qt + 1) * 512)
            for h in range(2):
                bi = nxt("t", 2)
                pt = ps_t[bi]
                for sub in range(4):
                    op("tensor", lambda e, sub=sub, h=h, pt=pt: e.transpose(out=pt[0:96, sub * 128:(sub + 1) * 128], in_=oacc[k][:, sub, h, :], identity=idf[:]),
                       reads=[b_oacc[k], b_idf], writes=[b_ps_t[bi]], pe_accum=True)
                op("vector", lambda e, h=h, pt=pt: e.tensor_copy(out=oTo[k][:, h, :], in_=pt[0:96, :]), reads=[b_ps_t[bi]], writes=[b_oTo[k]])
            op("sync", lambda e: e.dma_start(out=ysl(64, 256, qt * 512, 512).rearrange("(h d) t -> d h t", d=96), in_=oTo[k][:]), reads=[b_oTo[k]], writes=[b_ymc], dma=True)

        nq = self.nqt if hasattr(self, "nqt") else NQT
        if stage < 4:
            nq = 0
        if nq > 0:
            load_tile(0)
            cmp_part(0)
            if stage >= 4.2:
                topk_part(0)
        for qt in range(nq):
            if stage >= 4.3:
                qaug_part(qt)
            if stage >= 4.4:
                win_part(qt)
            pool_chunk(qt)
            if qt + 1 < nq and stage >= 4.7:
                load_tile(qt + 1)
                cmp_part(qt + 1)
                topk_part(qt + 1)
            if stage >= 4.5:
                slc_part(qt)
            if stage >= 4.6:
                out_part(qt)
            if stage < 4.7:
                break
import ml_dtypes
from concourse.bass_utils import run_bass_kernel_spmd

_BF = ml_dtypes.bfloat16
_NCORE = 8
_GROUPS = [[0, 1, 2, 3], [4, 5, 6, 7]]
_DBG = {}
_CC_CHAIN = Buf("cc_chain")


def _reset_chain():
    _CC_CHAIN.w = None
    _CC_CHAIN.r = {}


def _gather(P, c_ap, b_c, g_ap, b_g):
    if _DBG.get("nocc"):
        rows = c_ap.shape[0]
        for r in range(4):
            P.op("gpsimd", lambda e, r=r: e.dma_start(out=g_ap[r * rows:(r + 1) * rows, :], in_=c_ap), reads=[b_c], writes=[b_g], dma=True)
        return
    P.op("gpsimd", lambda e: e.collective_compute("AllGather", ALU.bypass, replica_groups=_GROUPS, ins=[c_ap], outs=[g_ap]),
         reads=[b_c, _CC_CHAIN], writes=[b_g, _CC_CHAIN], cc=True)


def build_fused():
    nc = bass.Bass("TRN2", target_bir_lowering=False)
    _reset_chain()
    TB.gus_ctr = 0
    P = Prog(nc)
    dram = Dram(nc, P)
    dram.dump = tuple(_DBG.get("dump", ()))
    ident = dram.din("ident", [128, 128], F32)
    xin = dram.din("xin", [NTOK, D], F32)
    xout = dram.dout("xout", [NTOK, D], F32)
    op = P.op

    jcache = {}

    def jexpr(e):
        if "v" not in jcache:
            pid = e.partition_id()
            j = e.snap(pid % 4, min_val=0, max_val=3)
            g = e.snap(j // 2, min_val=0, max_val=1)
            jo = e.snap(g * 2 + (1 - (j % 2)), min_val=0, max_val=3)
            jcache["v"] = dict(j=j, g=g, jo=jo)
        return jcache["v"]

    cur, b_cur = xin, None
    tcount = 0
    ym_loc = b_ym_loc = None
    for l in range(3):
        P.push_scope()
        tb = TB(nc, P, ident, dram)
        if l > 0:
            pl = "l%d" % (l - 1)
            x2 = dram.dscr("x2", [NTOK, D], F32)
            scr2, b_scr2, its2 = tb.prep_for(pl + "_ffn2")

            def hook(t_, its2=its2):
                for it in its2[2 * t_:2 * t_ + 2]:
                    it()
            tcount = tb.stage_mix(pl, ym_loc, b_ym_loc, cur, b_cur, x2, dram.b["x2"], tcount, tile_hook=hook)
            if l == 2:
                tcount = tb.stage_ffn(pl + "_ffn2", x2, dram.b["x2"], xout, dram.b["xout"], tcount, prepped=(scr2, b_scr2))
                P.pop_scope()
                break
            x3 = dram.dscr("x3", [NTOK, D], F32)
            tcount = tb.stage_ffn(pl + "_ffn2", x2, dram.b["x2"], x3, dram.b["x3"], tcount, prepped=(scr2, b_scr2))
            cur, b_cur = x3, dram.b["x3"]
        ll = "l%d" % l
        x1 = dram.dscr("x1", [NTOK, D], F32)
        tcount = tb.stage_ffn(ll + "_ffn1", cur, b_cur, x1, dram.b["x1"], tcount)
        cur, b_cur = x1, dram.b["x1"]
        def ctensor(name, shp, dt):
            ap = dram.dscr(name, shp, dt)
            return ap, dram.b[name]
        def gtensor(name, shp, dt):
            if name not in dram.ap:
                dram.ap[name] = nc.dram_tensor(name, list(shp), dt).ap()
                dram.b[name] = P.buf(name)
            return dram.ap[name], dram.b[name]
        C_q, b_Cq = ctensor("c_q", [8, 128, NTOK], BF16)
        C_u, b_Cu = ctensor("c_u", [4, 64, NTOK], F32)
        C_k, b_Ck = ctensor("c_k", [8, 96, NTOK], BF16)
        C_vs, b_Cvs = ctensor("c_vs", [2, NTOK, 96], BF16)
        C_vw, b_Cvw = ctensor("c_vw", [2, NTOK, 96], BF16)
        C_g, b_Cg = ctensor("c_g", [NTOK, 24], F32)
        G_q, b_Gq = gtensor("g_q", [8, 4, 128, NTOK], BF16)
        G_u, b_Gu = gtensor("g_u", [4, 4, 64, NTOK], F32)
        G_k, b_Gk = gtensor("g_k", [8, 4, 96, NTOK], BF16)
        G_vs, b_Gvs = gtensor("g_vs", [2, 4, NTOK, 96], BF16)
        G_vw, b_Gvw = gtensor("g_vw", [2, 4, NTOK, 96], BF16)
        G_g, b_Gg = ctensor("g_g", [4 * NTOK, 24], F32)
        outs = {"gates": (C_g, b_Cg)}
        outs["qT"] = (C_q, b_Cq)
        outs["uT"] = ([C_u[p] for p in range(4)], b_Cu)
        outs["kT"] = (C_k, b_Ck)
        outs["vs"] = (C_vs, b_Cvs)
        outs["vw"] = (C_vw, b_Cvw)
        tcount = tb.stage_proj(ll, cur, b_cur, outs, tcount)
        P.pop_scope()
        if _DBG.get("stop3") and l == 1:
            P.wait_all("sync", list(dram.b.values()))
            P.finish()
            return nc
        if _DBG.get("dump") and l == 0:
            d0 = dram.dscr("dbg_ck0_pre", [96, NTOK], BF16)
            op("sync", lambda e: e.dma_start(out=d0, in_=C_k[0]), reads=[b_Ck], writes=[dram.b["dbg_ck0_pre"]], dma=True)
            dram.outs += [dram.b["dbg_ck0_pre"]]
            P.barrier()
        for i in range(8):
            _gather(P, C_q[i], b_Cq, G_q[i].rearrange("r d t -> (r d) t"), b_Gq)
        for i in range(8):
            _gather(P, C_k[i], b_Ck, G_k[i].rearrange("r d t -> (r d) t"), b_Gk)
        for i in range(4):
            _gather(P, C_u[i], b_Cu, G_u[i].rearrange("r c t -> (r c) t"), b_Gu)
        for i in range(2):
            _gather(P, C_vs[i], b_Cvs, G_vs[i].rearrange("r t d -> (r t) d"), b_Gvs)
            _gather(P, C_vw[i], b_Cvw, G_vw[i].rearrange("r t d -> (r t) d"), b_Gvw)
        _gather(P, C_g, b_Cg, G_g, b_Gg)
        P.barrier()
        if _DBG.get("dump") and l == 0:
            d1 = dram.dscr("dbg_ck0", [96, NTOK], BF16); d2 = dram.dscr("dbg_gk0", [4 * 96, NTOK], BF16)
            op("sync", lambda e: e.dma_start(out=d1, in_=C_k[0]), reads=[b_Ck], writes=[dram.b["dbg_ck0"]], dma=True)
            op("sync", lambda e: e.dma_start(out=d2, in_=G_k[0].rearrange("r d t -> (r d) t")), reads=[b_Gk], writes=[dram.b["dbg_gk0"]], dma=True)
            dram.outs += [dram.b["dbg_ck0"], dram.b["dbg_gk0"]]
        loc = {}

        def mk(name, shp, dt):
            ap = dram.dscr("l_" + name, shp, dt)
            loc[name] = (ap, dram.b["l_" + name])
            return ap, loc[name][1]

        l_qT, b_lq = mk("qT", [4, 96, S], BF16)
        kinds = ["kcT", "vcT", "ksT", "kwT"]
        for name in kinds:
            mk(name, [96, S], BF16)
        mk("uT", [64, S], F32); mk("vs", [S, 96], BF16); mk("vw", [S, 96], BF16)
        gk5 = G_k.rearrange("(k g) r d t -> k g r d t", g=2)
        gq5 = G_q.rearrange("(p h) r d t -> p h r d t", h=2)
        for h2 in range(2):
            def q_own(e, h2=h2):
                return e.dma_start(out=l_qT[h2].rearrange("d (r t) -> d r t", r=4),
                                   in_=gq5[bass.ds(jexpr(e)["j"], 1), h2, :, 0:96, :].rearrange("o r d t -> (o d) r t"))

            def q_oth(e, h2=h2):
                return e.dma_start(out=l_qT[2 + h2].rearrange("d (r t) -> d r t", r=4),
                                   in_=gq5[bass.ds(jexpr(e)["jo"], 1), h2, :, 0:96, :].rearrange("o r d t -> (o d) r t"))
            op("sync", q_own, reads=[b_Gq], writes=[b_lq], dma=True)
            op("sync", q_oth, reads=[b_Gq], writes=[b_lq], dma=True)

        def uf(e):
            return e.dma_start(out=loc["uT"][0].rearrange("c (r t) -> c r t", r=4),
                               in_=G_u[bass.ds(jexpr(e)["j"], 1), :, :, :].rearrange("o r c t -> (o c) r t"))
        op("sync", uf, reads=[b_Gu], writes=[loc["uT"][1]], dma=True)
        for ki, name in enumerate(kinds):
            def kf(e, name=name, ki=ki):
                return e.dma_start(out=loc[name][0].rearrange("d (r t) -> d r t", r=4),
                                   in_=gk5[ki, bass.ds(jexpr(e)["g"], 1), :, :, :].rearrange("o r d t -> (o d) r t"))
            op("sync", kf, reads=[b_Gk], writes=[loc[name][1]], dma=True)
        for name, Gv, b_Gv in (("vs", G_vs, b_Gvs), ("vw", G_vw, b_Gvw)):
            for r in range(4):
                rows = slice(r * NTOK, (r + 1) * NTOK)

                def vf(e, name=name, rows=rows, r=r, Gv=Gv):
                    return e.dma_start(out=loc[name][0][rows, :], in_=Gv[bass.ds(jexpr(e)["g"], 1), r, :, :].rearrange("o t d -> (o t) d"))
                op("sync", vf, reads=[b_Gv], writes=[loc[name][1]], dma=True)
        loc["gates"] = (G_g, b_Gg)
        if _DBG.get("stop1") is not None and _DBG.get("stop1") == l:
            P.wait_all("sync", list(dram.b.values()))
            P.finish()
            return nc
        ymc = dram.dscr("c_ym", [4, 256, NTOK], BF16)
        b_ymc = dram.b["c_ym"]
        P.push_scope()
        bb = BB(nc, P, dram)
        bb.build(ll, loc, ymc, b_ymc)
        P.pop_scope()
        g_ym, b_gym = gtensor("g_ym", [4, 2, 4, 128, NTOK], BF16)
        for tj in range(4):
            for hf in range(2):
                _gather(P, ymc[tj, hf * 128:(hf + 1) * 128, :], b_ymc, g_ym[tj, hf].rearrange("r i t -> (r i) t"), b_gym)
        P.barrier()
        ym_loc = dram.dscr("l_ym", [D, NTOK], BF16)
        b_ym_loc = dram.b["l_ym"]
        for hf in range(2):
            def yf(e, g_ym=g_ym, ym_loc=ym_loc, hf=hf):
                return e.dma_start(out=ym_loc[hf * 512:(hf + 1) * 512, :], in_=g_ym[bass.ds(jexpr(e)["j"], 1), hf, :, :, :].rearrange("o r i t -> (o r i) t"))
            op("sync", yf, reads=[b_gym], writes=[b_ym_loc], dma=True)
        if _DBG.get("stop2") is not None and _DBG.get("stop2") == l:
            P.wait_all("sync", list(dram.b.values()))
            P.finish()
            return nc
    P.wait_all("sync", dram.outs)
    P.finish()
    return nc


def _yperm():
    src_of = lambda r, row: (64 * r + row) if row < 64 else (256 + 192 * r + (row - 64))
    return np.array([src_of(r, hf * 128 + i) for hf in range(2) for r in range(4) for i in range(128)])


_YPERM = _yperm()


def kernel(**inputs):
    inputs = {k: np.asarray(v) for k, v in inputs.items()}
    x = inputs["x"].reshape(2 * S, D).astype(np.float32, copy=False)
    nc = build_fused()
    C = b_consts()
    maps = []
    for c in range(_NCORE):
        b, j = divmod(c, 4)
        m = {"ident": C["ident_f"], "ident_b": C["ident_b"], "masks": C["masks"], "kpat": C["kpat"], "wimp": C["wimp"], "fpat": C["fpat"]}
        a, A = pool_consts(j)
        m["pool_a"] = a
        gs = np.zeros((6, 24), np.float32)
        gs[np.arange(6), 6 * j + np.arange(6)] = 1.0
        m["gsel"] = gs
        m["pool_A"] = A
        m["xin"] = x[c * NTOK:(c + 1) * NTOK]
        for l in range(2):
            p = "l%d" % l
            m[p + "_ffn1_wg"] = inputs["ffn1_w_gate"][l]; m[p + "_ffn1_wu"] = inputs["ffn1_w_up"][l]; m[p + "_ffn1_wd"] = inputs["ffn1_w_down"][l]
            m[p + "_ffn1_lg"] = inputs["ln1_g"][l]; m[p + "_ffn1_lb"] = inputs["ln1_b"][l]
            m[p + "_ffn2_wg"] = inputs["ffn2_w_gate"][l]; m[p + "_ffn2_wu"] = inputs["ffn2_w_up"][l]; m[p + "_ffn2_wd"] = inputs["ffn2_w_down"][l]
            m[p + "_ffn2_lg"] = inputs["ln3_g"][l]; m[p + "_ffn2_lb"] = inputs["ln3_b"][l]
            m[p + "_w_in"] = inputs["w_in"][l]; m[p + "_b_gate"] = inputs["b_gate"][l]
            m[p + "_w_out"] = inputs["w_out"][l][_YPERM]
            m[p + "_ln2_g"] = inputs["ln2_g"][l]; m[p + "_ln2_b"] = inputs["ln2_b"][l]
            m[p + "_cmp_k_w1"] = inputs["cmp_k_w1"][l]; m[p + "_cmp_k_w2"] = inputs["cmp_k_w2"][l]; m[p + "_cmp_k_pos"] = inputs["cmp_pos_k"][l]
            m[p + "_cmp_v_w1"] = inputs["cmp_v_w1"][l]; m[p + "_cmp_v_w2"] = inputs["cmp_v_w2"][l]; m[p + "_cmp_v_pos"] = inputs["cmp_pos_v"][l]
            m[p + "_pool_w"] = inputs["pool_w"][l][j]
            m[p + "_pool_sc"] = inputs["pool_scale"][l][64 * j:64 * j + 64].reshape(64, 1)
        maps.append({k: np.ascontiguousarray(v) for k, v in m.items()})
    res = run_bass_kernel_spmd(nc, maps, core_ids=list(range(_NCORE))).results
    _DBG["res"] = res if _DBG.get("dump") else None
    out = np.concatenate([r["xout"] for r in res], axis=0)
    return out.reshape(2, S, D).astype(np.float32, copy=False)
```

```python
import numpy as np
import concourse.bass as bass
import concourse.mybir as mybir
from contextlib import ExitStack

F32 = mybir.dt.float32
BF16 = mybir.dt.bfloat16
AF = mybir.ActivationFunctionType
ALU = mybir.AluOpType
AX = mybir.AxisListType

ENGS = ("sync", "tensor", "vector", "scalar", "gpsimd")
N_DMA_SEMS = 12


class Buf:
    __slots__ = ("name", "w", "r", "psum")

    def __init__(self, name, psum=False):
        self.name = name
        self.psum = psum
        self.w = None
        self.r = {}


class Prog:
    def __init__(self, nc):
        self.nc = nc
        self.es = ExitStack()
        self.streams = {e: [] for e in ENGS}
        self.sem = {}
        self.cnt = {}
        for e in ENGS:
            self.sem[e] = nc.alloc_semaphore("c_" + e)
            self.cnt[e] = 0
        self.dsem = {}
        self.dcnt = {}
        self.drr = {}
        for e in ("sync", "gpsimd", "scalar"):
            for i in range(N_DMA_SEMS):
                k = "d_%s_%d" % (e, i)
                self.sem[k] = nc.alloc_semaphore(k)
                self.cnt[k] = 0
            self.drr[e] = 0
        self.sem["cc"] = nc.alloc_semaphore("cc_sem")
        self.cnt["cc"] = 0
        self.seen = {e: {} for e in ENGS}
        self.nbuf = 0
        self.block_hooks = []
        self.scopes = []
        self.scope_ctr = 0
        self.scope_id = 0

    def sbuf(self, name, shape, dtype):
        t = self.es.enter_context(self.nc.sbuf_tensor("sb%d_%s" % (self.scope_id, name), list(shape), dtype))
        return t

    def psum(self, name, shape, dtype=F32):
        t = self.es.enter_context(self.nc.psum_tensor("pp%d_%s" % (self.scope_id, name), list(shape), dtype))
        return t

    def push_scope(self):
        self.scopes.append(self.es)
        self.es = ExitStack()
        self.scope_ctr += 1
        self.scope_id = self.scope_ctr

    def pop_scope(self):
        self.barrier()
        self.es.close()
        self.es = self.scopes.pop()

    def buf(self, name=None, psum=False):
        self.nbuf += 1
        return Buf(name or ("b%d" % self.nbuf), psum)

    def pbuf(self):
        return self.buf(psum=True)

    def _need(self, eng, deps):
        out = []
        seen = self.seen[eng]
        for k, v in deps.items():
            if seen.get(k, 0) < v:
                seen[k] = v
                out.append((k, v))
        return out

    def op(self, eng, fn, reads=(), writes=(), dma=False, pe_accum=False, cc=False):
        deps = {}

        def add(tok):
            if tok is None:
                return
            k, v = tok
            if deps.get(k, 0) < v:
                deps[k] = v

        for b in reads:
            add(b.w)
            if b.psum:
                for k, v in b.r.items():
                    if k != eng:
                        add((k, v))
        for b in writes:
            if not (pe_accum and b.w is not None and b.w[0] == "tensor"):
                add(b.w)
            for k, v in b.r.items():
                add((k, v))
        if dma:
            rr = self.drr[eng]
            self.drr[eng] = (rr + 1) % N_DMA_SEMS
            dk = "d_%s_%d" % (eng, rr)
            if self.cnt[dk] > 0:
                add((dk, self.cnt[dk]))
        if eng == "tensor":
            deps.pop("tensor", None)
        waits = self._need(eng, deps)
        if cc:
            self.cnt["cc"] += 1
            tok = ("cc", self.cnt["cc"])
            inc = 1
        elif dma:
            self.cnt[dk] += 16
            tok = (dk, self.cnt[dk])
            inc = 16
        else:
            self.cnt[eng] += 1
            tok = (eng, self.cnt[eng])
            inc = 1
        semh = self.sem[tok[0]]
        wl = [(self.sem[k], v) for k, v in waits]

        def emit(e, fn=fn, wl=wl, semh=semh, inc=inc):
            for s, v in wl:
                e.wait_ge(s, v)
            fn(e).then_inc(semh, inc)

        self.streams[eng].append(emit)
        for b in writes:
            b.w = tok
            b.r = {}
        for b in reads:
            if b.r.get(tok[0], 0) < tok[1]:
                b.r[tok[0]] = tok[1]
        return tok

    def wait_all(self, eng, bufs):
        deps = {}
        for b in bufs:
            if b.w is not None:
                k, v = b.w
                if deps.get(k, 0) < v:
                    deps[k] = v
        wl = [(self.sem[k], v) for k, v in self._need(eng, deps)]

        def emit(e, wl=wl):
            for s, v in wl:
                e.wait_ge(s, v)

        self.streams[eng].append(emit)

    def barrier(self):
        allc = {k: v for k, v in self.cnt.items() if v > 0}
        for eng in ENGS:
            wl = [(self.sem[k], v) for k, v in self._need(eng, dict(allc))]

            def emit(e, wl=wl):
                for s, v in wl:
                    e.wait_ge(s, v)

            self.streams[eng].append(emit)

    def block_break(self):
        self.barrier()
        for e in ENGS:
            self.streams[e].append(None)

    def finish(self):
        nc = self.nc
        segs = {e: [[]] for e in ENGS}
        for e in ENGS:
            for f in self.streams[e]:
                if f is None:
                    segs[e].append([])
                else:
                    segs[e][-1].append(f)
        nseg = len(segs["sync"])
        for i in range(nseg):
            for hook in self.block_hooks:
                hook()
            with nc.Block() as block:
                @block.sync
                def _(e):
                    for f in segs["sync"][i]:
                        f(e)

                @block.tensor
                def _(e):
                    for f in segs["tensor"][i]:
                        f(e)

                @block.vector
                def _(e):
                    for f in segs["vector"][i]:
                        f(e)

                @block.scalar
                def _(e):
                    for f in segs["scalar"][i]:
                        f(e)

                @block.gpsimd
                def _(e):
                    for f in segs["gpsimd"][i]:
                        f(e)
        self.es.close()


class Dram:
    def __init__(self, nc, P):
        self.nc = nc
        self.P = P
        self.ap = {}
        self.b = {}
        self.outs = []
        self.dump = ()
        self.arena = {}
        self.ARENA_ELEMS = {"bf16": 120 * 1024 * 1024, "f32": 60 * 1024 * 1024}

    def din(self, name, shape, dt):
        if name not in self.ap:
            self.ap[name] = self.nc.dram_tensor(name, list(shape), dt, kind="ExternalInput").ap()
        return self.ap[name]

    def dout(self, name, shape, dt):
        self.ap[name] = self.nc.dram_tensor(name, list(shape), dt, kind="ExternalOutput").ap()
        self.b[name] = self.P.buf(name)
        self.outs.append(self.b[name])
        return self.ap[name]

    def dscr(self, name, shape, dt):
        if name not in self.ap:
            if name in self.dump:
                self.ap[name] = self.nc.dram_tensor(name, list(shape), dt, kind="ExternalOutput").ap()
            else:
                n = 1
                for d in shape:
                    n *= int(d)
                key = "bf16" if dt == BF16 else "f32"
                if key not in self.arena:
                    cap = self.ARENA_ELEMS[key]
                    self.arena[key] = [self.nc.dram_tensor("arena_" + key, [cap], dt).ap(), 0, cap]
                ar = self.arena[key]
                off = ar[1]
                n_al = (n + 2047) // 2048 * 2048
                assert off + n_al <= ar[2], ("arena overflow", key, name)
                ar[1] = off + n_al
                flat = ar[0][off:off + n]
                if len(shape) == 1:
                    self.ap[name] = flat
                else:
                    names = ["a%d" % i for i in range(len(shape))]
                    pat = "(" + " ".join(names) + ") -> " + " ".join(names)
                    kw = {nm: int(d) for nm, d in zip(names[1:], shape[1:])}
                    self.ap[name] = flat.rearrange(pat, **kw)
            self.b[name] = self.P.buf(name)
        return self.ap[name]

D = 1024
DFF = 2816
NF = DFF // 128
DIN = 2200
NTOK = 4096
TT = 512
NTILE = NTOK // TT
ALPHA = float((2.0 * 2) ** 0.25)
LN_EPS = 1e-5
QSCALE = float(96 ** -0.5)
TCOLS = [256 + 96 * h for h in range(8)] + [1024, 1120, 1216, 1312, 1408, 1504, 1792, 1888]


class TB:
    gus_ctr = 0

    def __init__(self, nc, P, ident, dram):
        self.nc = nc
        self.P = P
        self.dram = dram
        self.ident = ident
        self.b_dram = dram.b
        self.idt = P.sbuf("idt", [128, 128], F32); self.b_idt = P.buf()
        self.xt = [P.sbuf("xt%d" % i, [128, 4, D], F32) for i in range(2)]
        self.b_xt = [[P.buf() for s in range(4)] for i in range(2)]
        self.xT = P.sbuf("xT", [128, 8, TT], BF16); self.b_xT = [P.buf() for c in range(8)]
        self.hT = P.sbuf("hT", [128, NF, TT], BF16); self.b_hT = [P.buf() for f in range(NF)]
        self.wres = P.sbuf("wres", [128, NF * D], BF16); self.b_wres = [P.buf() for f in range(NF)]
        self.ring = P.sbuf("ring", [128, 3, 2, 8, 128], BF16); self.b_ring = [P.buf() for i in range(3)]
        self.stg = [P.sbuf("stg%d" % i, [128, DFF], F32) for i in range(2)]; self.b_stg = [P.buf() for i in range(2)]
        self.stgb = [P.sbuf("stgb%d" % i, [128, DFF], BF16) for i in range(2)]; self.b_stgb = [P.buf() for i in range(2)]
        self.lng = P.sbuf("lng", [128, D], F32); self.b_lng = P.buf()
        self.lnb = P.sbuf("lnb", [128, D], F32); self.b_lnb = P.buf()
        self.sg = [P.sbuf("sg%d" % i, [128, TT], F32) for i in range(2)]; self.b_sg = [P.buf() for i in range(2)]
        self.st = P.sbuf("st", [128, 4, 2, 6], F32); self.b_st = [P.buf() for s in range(4)]
        self.mv = P.sbuf("mv", [128, 4, 4], F32); self.b_mv = [P.buf() for s in range(4)]
        self.oT = P.sbuf("oT", [128, 16, TT], BF16); self.b_oT = [P.buf() for i in range(16)]
        P.op("gpsimd", lambda e: e.memset(self.oT[96:128, :, :], 0.0), writes=self.b_oT)
        self.ou = P.sbuf("ou", [128, 2, TT], F32); self.b_ou = [P.buf() for i in range(2)]
        self.ov = P.sbuf("ov", [128, 4, 384], BF16); self.b_ov = [P.buf() for i in range(4)]
        self.og = P.sbuf("og", [128, 4, 24], F32); self.b_og = [P.buf() for i in range(4)]
        self.bg = P.sbuf("bg", [128, 24], F32); self.b_bg = P.buf()
        self.ps = [P.psum("ps%d" % i, [128, 512], F32) for i in range(8)]
        self.b_ps = [P.pbuf() for i in range(8)]
        self.rr = 0
        self.prep_i = 0
        self.prep_cast_eng = None
        P.op("sync", lambda e: e.dma_start(out=self.idt[:], in_=self.ident), writes=[self.b_idt], dma=True)

    def din(self, name, shape, dt):
        return self.dram.din(name, shape, dt)

    def dscr(self, name, shape, dt):
        return self.dram.dscr(name, shape, dt)

    def cast_eng(self):
        self.rr += 1
        return ("vector", "gpsimd", "scalar")[self.rr % 3]

    def cast(self, eng, out, in_, reads, writes):
        if eng == "scalar":
            self.P.op("scalar", lambda e: e.activation(out=out, in_=in_, func=AF.Copy), reads=reads, writes=writes)
        else:
            self.P.op(eng, lambda e: e.tensor_copy(out=out, in_=in_), reads=reads, writes=writes)

    def load_res(self, w, nchunks, width):
        P = self.P
        per = max(1, DFF // width)
        f = 0
        i = 0
        while f < nchunks:
            n = min(per, nchunks - f)
            k = i % 2
            src = w[f * 128:(f + n) * 128, :].rearrange("(n p) d -> p n d", p=128)
            dst = self.stg[k][:, 0:n * width].rearrange("p (n d) -> p n d", d=width)
            P.op("sync", lambda e, dst=dst, src=src: e.dma_start(out=dst, in_=src), writes=[self.b_stg[k]], dma=True)
            self.cast(self.cast_eng(), self.wres[:, f * width:(f + n) * width], self.stg[k][:, 0:n * width],
                      [self.b_stg[k]], [self.b_wres[j] for j in range(f, f + n)])
            f += n
            i += 1

    def prep_gu_iters(self, wg, wu, scr, b_scr):
        P = self.P
        its = []
        for c in range(8):
            for gi, w in ((0, wg), (1, wu)):
                def it(c=c, gi=gi, w=w):
                    k = self.prep_i % 2
                    self.prep_i += 1
                    P.op("sync", lambda e, k=k, w=w, c=c: e.dma_start(out=self.stg[k][:], in_=w[c * 128:(c + 1) * 128, :]),
                         writes=[self.b_stg[k]], dma=True)
                    self.cast(self.prep_cast_eng or self.cast_eng(), self.stgb[k][:], self.stg[k][:], [self.b_stg[k]], [self.b_stgb[k]])
                    dst = scr[:, :, gi, c, :].rearrange("f p j -> p f j")
                    src = self.stgb[k][:].rearrange("p (f j) -> p f j", j=128)
                    P.op("gpsimd", lambda e, dst=dst, src=src: e.dma_start(out=dst, in_=src), reads=[self.b_stgb[k]],
                         writes=[b_scr], dma=True)
                its.append(it)
        return its

    def prep_gu(self, wg, wu, scr, b_scr):
        for it in self.prep_gu_iters(wg, wu, scr, b_scr):
            it()

    def prep_for(self, pfx):
        wg = self.din(pfx + "_wg", [D, DFF], F32)
        wu = self.din(pfx + "_wu", [D, DFF], F32)
        nm = "gus%d" % (TB.gus_ctr % 2)
        TB.gus_ctr += 1
        scr = self.dscr(nm, [NF, 128, 2, 8, 128], BF16)
        b_scr = self.b_dram[nm]
        return scr, b_scr, self.prep_gu_iters(wg, wu, scr, b_scr)

    def load_ln(self, g, b):
        P = self.P
        P.op("sync", lambda e: e.dma_start(out=self.lng[:], in_=g.partition_broadcast(128)), writes=[self.b_lng], dma=True)
        P.op("sync", lambda e: e.dma_start(out=self.lnb[:], in_=b.partition_broadcast(128)), writes=[self.b_lnb], dma=True)

    def load_x(self, src, b_src, t, k):
        P = self.P
        s_ap = src[t * TT:(t + 1) * TT, :].rearrange("(s p) d -> p s d", p=128)
        P.op("sync", lambda e: e.dma_start(out=self.xt[k][:], in_=s_ap), reads=[b_src] if b_src else [], writes=self.b_xt[k], dma=True)

    def transposes(self, k):
        P = self.P
        for c in range(8):
            pk = c % 2
            for s in range(4):
                P.op("tensor", lambda e, c=c, s=s, pk=pk: e.transpose(out=self.ps[pk][:, s * 128:(s + 1) * 128],
                                                                      in_=self.xt[k][:, s, c * 128:(c + 1) * 128], identity=self.idt[:]),
                     reads=[self.b_xt[k][s], self.b_idt], writes=[self.b_ps[pk]], pe_accum=True)
            self.cast("scalar" if c % 2 else "vector", self.xT[:, c, :], self.ps[pk][:], [self.b_ps[pk]], [self.b_xT[c]])

    def gate_up(self, scr, b_scr, t):
        P = self.P
        for f in range(NF):
            slot = (t * NF + f) % 3
            P.op("gpsimd" if f % 2 else "sync", lambda e, f=f, slot=slot: e.dma_start(out=self.ring[:, slot], in_=scr[f]),
                 reads=[b_scr], writes=[self.b_ring[slot]], dma=True)
            pg = 2 + (f % 2) * 2
            pu = pg + 1
            for gi, pp in ((0, pg), (1, pu)):
                for c in range(8):
                    P.op("tensor", lambda e, c=c, gi=gi, pp=pp, slot=slot: e.matmul(self.ps[pp][:], lhsT=self.ring[:, slot, gi, c, :], rhs=self.xT[:, c, :],
                                                                                 start=(c == 0), stop=(c == 7)),
                         reads=[self.b_ring[slot], self.b_xT[c]], writes=[self.b_ps[pp]], pe_accum=True)
            k = f % 2
            P.op("scalar", lambda e, k=k, pg=pg: e.activation(out=self.sg[k][:], in_=self.ps[pg][:], func=AF.Silu),
                 reads=[self.b_ps[pg]], writes=[self.b_sg[k]])
            P.op("vector", lambda e, k=k, pu=pu, f=f: e.tensor_tensor(out=self.hT[:, f, :], in0=self.sg[k][:], in1=self.ps[pu][:], op=ALU.mult),
                 reads=[self.b_sg[k], self.b_ps[pu]], writes=[self.b_hT[f]])

    def down_ln(self, k, lhs, b_lhs, nch, width_w, ysc, dst, b_dst, t):
        P = self.P
        for s in range(4):
            P.op("gpsimd", lambda e, s=s: e.tensor_scalar(out=self.xt[k][:, s, :], in0=self.xt[k][:, s, :], scalar1=ALPHA, scalar2=None, op0=ALU.mult),
                 reads=[self.b_xt[k][s]], writes=[self.b_xt[k][s]])
        for s in range(4):
            for n in range(2):
                pp = 6 + (s * 2 + n) % 2
                for f in range(nch):
                    P.op("tensor", lambda e, s=s, n=n, f=f, pp=pp: e.matmul(self.ps[pp][:], lhsT=lhs[:, f, s * 128:(s + 1) * 128],
                                                                          rhs=self.wres[:, f * width_w + n * 512: f * width_w + (n + 1) * 512],
                                                                          start=(f == 0), stop=(f == nch - 1)),
                         reads=[b_lhs[f], self.b_wres[f]], writes=[self.b_ps[pp]], pe_accum=True)
                P.op("vector", lambda e, s=s, n=n, pp=pp: e.scalar_tensor_tensor(out=self.xt[k][:, s, n * 512:(n + 1) * 512], in0=self.ps[pp][:], scalar=ysc,
                                                                                in1=self.xt[k][:, s, n * 512:(n + 1) * 512], op0=ALU.mult, op1=ALU.add),
                     reads=[self.b_ps[pp], self.b_xt[k][s]], writes=[self.b_xt[k][s]])
            self.layernorm(k, s)
        d_ap = dst[t * TT:(t + 1) * TT, :].rearrange("(s p) d -> p s d", p=128)
        P.op("sync", lambda e: e.dma_start(out=d_ap, in_=self.xt[k][:]), reads=self.b_xt[k], writes=[b_dst], dma=True)

    def layernorm(self, k, s):
        P = self.P
        z = self.xt[k]
        for h in range(2):
            P.op("vector", lambda e, h=h: e.bn_stats(self.st[:, s, h, :], z[:, s, h * 512:(h + 1) * 512]),
                 reads=[self.b_xt[k][s]], writes=[self.b_st[s]])
        P.op("vector", lambda e: e.bn_aggr(self.mv[:, s, 0:2], self.st[:, s, :, :]), reads=[self.b_st[s]], writes=[self.b_mv[s]])
        P.op("vector", lambda e: e.tensor_scalar(out=self.mv[:, s, 2:3], in0=self.mv[:, s, 1:2], scalar1=LN_EPS, scalar2=None, op0=ALU.add),
             reads=[self.b_mv[s]], writes=[self.b_mv[s]])
        P.op("scalar", lambda e: e.sqrt(self.mv[:, s, 3:4], self.mv[:, s, 2:3]), reads=[self.b_mv[s]], writes=[self.b_mv[s]])
        P.op("vector", lambda e: e.reciprocal(self.mv[:, s, 2:3], self.mv[:, s, 3:4]), reads=[self.b_mv[s]], writes=[self.b_mv[s]])
        P.op("vector", lambda e: e.tensor_scalar(out=z[:, s, :], in0=z[:, s, :], scalar1=self.mv[:, s, 0:1], scalar2=self.mv[:, s, 2:3],
                                                 op0=ALU.subtract, op1=ALU.mult),
             reads=[self.b_xt[k][s], self.b_mv[s]], writes=[self.b_xt[k][s]])
        P.op("gpsimd", lambda e: e.tensor_tensor(out=z[:, s, :], in0=z[:, s, :], in1=self.lng[:], op=ALU.mult),
             reads=[self.b_xt[k][s], self.b_lng], writes=[self.b_xt[k][s]])
        P.op("gpsimd", lambda e: e.tensor_tensor(out=z[:, s, :], in0=z[:, s, :], in1=self.lnb[:], op=ALU.add),
             reads=[self.b_xt[k][s], self.b_lnb], writes=[self.b_xt[k][s]])

    def stage_ffn(self, pfx, src, b_src, dst, b_dst, tcount, prepped=None, tile_hook=None):
        wd = self.din(pfx + "_wd", [DFF, D], F32)
        g = self.din(pfx + "_lg", [D], F32)
        b = self.din(pfx + "_lb", [D], F32)
        if prepped is None:
            scr, b_scr, its = self.prep_for(pfx)
            for it in its:
                it()
        else:
            scr, b_scr = prepped
        self.load_ln(g, b)
        self.load_res(wd, NF, D)
        for t in range(NTILE):
            k = (tcount + t) % 2
            self.load_x(src, b_src, t, k)
            self.transposes(k)
            self.gate_up(scr, b_scr, t)
            self.down_ln(k, self.hT, self.b_hT, NF, D, 0.5, dst, b_dst, t)
            if tile_hook is not None:
                tile_hook(t)
        return tcount + NTILE

    def stage_mix(self, pfx, ym, b_ym, src, b_src, dst, b_dst, tcount, tile_hook=None):
        wo = self.din(pfx + "_w_out", [D, D], F32)
        g = self.din(pfx + "_ln2_g", [D], F32)
        b = self.din(pfx + "_ln2_b", [D], F32)
        self.load_ln(g, b)
        self.load_res(wo, 8, D)
        for t in range(NTILE):
            k = (tcount + t) % 2
            self.load_x(src, b_src, t, k)
            y_ap = ym[:, t * TT:(t + 1) * TT].rearrange("(c p) t -> p c t", p=128)
            self.P.op("gpsimd", lambda e, y_ap=y_ap: e.dma_start(out=self.hT[:, 0:8, :], in_=y_ap), reads=[b_ym], writes=self.b_hT[0:8], dma=True)
            self.down_ln(k, self.hT, self.b_hT, 8, D, 1.0, dst, b_dst, t)
            if tile_hook is not None:
                tile_hook(t)
        return tcount + NTILE

    def stage_proj(self, pfx, src, b_src, outs, tcount):
        P = self.P
        win = self.din(pfx + "_w_in", [D, DIN], F32)
        bgate = self.din(pfx + "_b_gate", [24], F32)
        qT, kT, uT, vs, vw, gates = (outs[k][0] for k in ("qT", "kT", "uT", "vs", "vw", "gates"))
        bd = {k: outs[k][1] for k in ("qT", "kT", "uT", "vs", "vw", "gates")}
        P.op("sync", lambda e: e.dma_start(out=self.bg[:], in_=bgate.partition_broadcast(128)), writes=[self.b_bg], dma=True)
        for c in range(8):
            k = c % 2
            P.op("sync", lambda e, k=k, c=c: e.dma_start(out=self.stg[k][:, 0:DIN], in_=win[c * 128:(c + 1) * 128, :]), writes=[self.b_stg[k]], dma=True)
            self.cast(self.cast_eng(), self.wres[:, c * DIN:(c + 1) * DIN], self.stg[k][:, 0:DIN], [self.b_stg[k]], [self.b_wres[c]])
        W = lambda c, a, b_: self.wres[:, c * DIN + a: c * DIN + b_]
        for t in range(NTILE):
            k = (tcount + t) % 2
            self.load_x(src, b_src, t, k)
            self.transposes(k)
            tok = slice(t * TT, (t + 1) * TT)
            for i, col in enumerate(TCOLS):
                pp = 2 + i % 4
                for c in range(8):
                    P.op("tensor", lambda e, c=c, col=col, pp=pp: e.matmul(self.ps[pp][0:96, :], lhsT=W(c, col, col + 96), rhs=self.xT[:, c, :],
                                                                         start=(c == 0), stop=(c == 7)),
                         reads=[self.b_wres[c], self.b_xT[c]], writes=[self.b_ps[pp]], pe_accum=True)
                if i < 8:
                    P.op("scalar", lambda e, i=i, pp=pp: e.mul(self.oT[0:96, i, :], self.ps[pp][0:96, :], QSCALE),
                         reads=[self.b_ps[pp]], writes=[self.b_oT[i]])
                else:
                    P.op("vector", lambda e, i=i, pp=pp: e.tensor_copy(out=self.oT[0:96, i, :], in_=self.ps[pp][0:96, :]),
                         reads=[self.b_ps[pp]], writes=[self.b_oT[i]])
            P.op("sync", lambda e, tok=tok: e.dma_start(out=qT[:, :, tok].rearrange("h d t -> d h t"), in_=self.oT[:, 0:8, :]),
                 reads=self.b_oT[0:8], writes=[bd["qT"]], dma=True)
            P.op("sync", lambda e, tok=tok: e.dma_start(out=kT[:, :, tok].rearrange("h d t -> d h t"), in_=self.oT[0:96, 8:16, :]),
                 reads=self.b_oT[8:16], writes=[bd["kT"]], dma=True)
            for j in range(2):
                pp = 6 + j
                for c in range(8):
                    P.op("tensor", lambda e, c=c, j=j, pp=pp: e.matmul(self.ps[pp][:], lhsT=W(c, j * 128, (j + 1) * 128), rhs=self.xT[:, c, :],
                                                                     start=(c == 0), stop=(c == 7)),
                         reads=[self.b_wres[c], self.b_xT[c]], writes=[self.b_ps[pp]], pe_accum=True)
                P.op("vector", lambda e, j=j, pp=pp: e.tensor_copy(out=self.ou[:, j, :], in_=self.ps[pp][:]), reads=[self.b_ps[pp]], writes=[self.b_ou[j]])
            for j2 in range(2):
                for half in range(2):
                    P.op("sync", lambda e, tok=tok, j2=j2, half=half: e.dma_start(out=uT[2 * j2 + half][:, tok], in_=self.ou[half * 64:(half + 1) * 64, j2, :]),
                         reads=[self.b_ou[j2]], writes=[bd["uT"]], dma=True)
            for s in range(4):
                pp = 2 + s
                for c in range(8):
                    P.op("tensor", lambda e, c=c, s=s, pp=pp: e.matmul(self.ps[pp][:, 0:192], lhsT=self.xT[:, c, s * 128:(s + 1) * 128], rhs=W(c, 1600, 1792),
                                                                     start=(c == 0), stop=(c == 7)),
                         reads=[self.b_wres[c], self.b_xT[c]], writes=[self.b_ps[pp]], pe_accum=True)
                for c in range(8):
                    P.op("tensor", lambda e, c=c, s=s, pp=pp: e.matmul(self.ps[pp][:, 256:472], lhsT=self.xT[:, c, s * 128:(s + 1) * 128], rhs=W(c, 1984, 2200),
                                                                     start=(c == 0), stop=(c == 7), skip_group_check=True),
                         reads=[self.b_wres[c], self.b_xT[c]], writes=[self.b_ps[pp]], pe_accum=True)
                P.op("vector", lambda e, s=s, pp=pp: e.tensor_copy(out=self.ov[:, s, 0:192], in_=self.ps[pp][:, 0:192]), reads=[self.b_ps[pp]], writes=[self.b_ov[s]])
                P.op("vector", lambda e, s=s, pp=pp: e.tensor_copy(out=self.ov[:, s, 192:384], in_=self.ps[pp][:, 256:448]), reads=[self.b_ps[pp]], writes=[self.b_ov[s]])
                P.op("vector", lambda e, s=s, pp=pp: e.tensor_tensor(out=self.og[:, s, :], in0=self.ps[pp][:, 448:472], in1=self.bg[:], op=ALU.add),
                     reads=[self.b_ps[pp], self.b_bg], writes=[self.b_og[s]])
                P.op("scalar", lambda e, s=s: e.activation(out=self.og[:, s, :], in_=self.og[:, s, :], func=AF.Sigmoid), reads=[self.b_og[s]], writes=[self.b_og[s]])
            for g_ in range(2):
                P.op("gpsimd", lambda e, tok=tok, g_=g_: e.dma_start(out=vs[g_, tok, :].rearrange("(s p) d -> p s d", p=128), in_=self.ov[:, :, 96 * g_:96 * g_ + 96]),
                     reads=self.b_ov, writes=[bd["vs"]], dma=True)
                P.op("gpsimd", lambda e, tok=tok, g_=g_: e.dma_start(out=vw[g_, tok, :].rearrange("(s p) d -> p s d", p=128), in_=self.ov[:, :, 192 + 96 * g_:192 + 96 * g_ + 96]),
                     reads=self.b_ov, writes=[bd["vw"]], dma=True)
            P.op("gpsimd", lambda e, tok=tok: e.dma_start(out=gates[tok, :].rearrange("(s p) d -> p s d", p=128), in_=self.og[:]),
                 reads=self.b_og, writes=[bd["gates"]], dma=True)
        return tcount + NTILE


S = 16384
NQT = S // 512
NEG = -30000.0
VP = 128
GELU_C = float(2.0 * (2.0 / np.pi) ** 0.5)


def b_consts():
    import ml_dtypes
    c = {}
    c["ident_f"] = np.eye(128, dtype=np.float32)
    c["ident_b"] = np.eye(128, dtype=np.float32).astype(ml_dtypes.bfloat16)
    k = np.arange(128)[:, None]
    t = np.arange(512)[None, :]
    masks = np.zeros((16, 128, 512), np.float32)
    for i in range(8):
        kp = 128 * i - 512 + k
        masks[i] = np.where((t - kp >= 0) & (t - kp < 512), 0.0, NEG)
    for r in range(4):
        masks[8 + r] = np.where(16 * k + 31 <= 512 * r + t, 0.0, NEG)
    for i in range(4):
        masks[12 + i] = np.where(128 * i + k <= t, 0.0, NEG)
    c["masks"] = masks.astype(ml_dtypes.bfloat16)
    kk = np.arange(2048)[None, :]
    r = np.arange(32)[:, None]
    c["kpat"] = (((kk // 64) % 32) == r).astype(np.float32).astype(ml_dtypes.bfloat16)
    W = np.zeros((1024, 256), np.float32)
    for j in range(256):
        for n, w in ((4 * j - 1, 0.5), (4 * j, 1.0), (4 * j + 1, 1.0), (4 * j + 2, 1.0), (4 * j + 3, 0.5)):
            if 0 <= n < 1023:
                W[n, j] = w
    c["wimp"] = W.reshape(8, 128, 256).astype(ml_dtypes.bfloat16)
    fp = np.zeros((128, 4), np.float32)
    fp[:64, 0] = 1000.0; fp[:64, 1] = 2000.0
    fp[64:, 1] = 2000.0; fp[64:, 2] = 3000.0
    c["fpat"] = fp
    return c


def pool_consts(j):
    w = 2 ** (j + 1)
    a = np.zeros((64, 4), np.float32)
    a[:, j] = 1.0 / w
    A = np.zeros((64, 4, 16), np.float32)
    tt = np.arange(16)
    A[:, j, :] = 1.0 / np.minimum(tt + 1, w)
    return a, A


def _V(x, e):
    return x(e) if callable(x) else x


class BB:
    def __init__(self, nc, P, dram):
        self.nc = nc
        self.P = P
        self.dram = dram

    def din(self, name, shape, dt):
        return self.dram.din(name, shape, dt)

    def build(self, pfx, src, ymc, b_ymc):
        P = self.P
        nc = self.nc
        op = P.op
        qT, b_qT = src["qT"]; gates, b_gates = src["gates"]
        kcT, b_kcT = src["kcT"]; vcT, b_vcT = src["vcT"]; ksT, b_ksT = src["ksT"]; kwT, b_kwT = src["kwT"]
        vs, b_vs = src["vs"]; vw, b_vw = src["vw"]; uT, b_uT = src["uT"]
        cw = {}
        for kv in ("k", "v"):
            cw[kv + "w1"] = self.din(pfx + "_cmp_%s_w1" % kv, [3072, 96], F32)
            cw[kv + "w2"] = self.din(pfx + "_cmp_%s_w2" % kv, [96, 96], F32)
            cw[kv + "pos"] = self.din(pfx + "_cmp_%s_pos" % kv, [32, 96], F32)
        pool_w = self.din(pfx + "_pool_w", [64, 64], F32)
        pool_sc = self.din(pfx + "_pool_sc", [64, 1], F32)
        pool_a = self.din("pool_a", [64, 4], F32)
        pool_A = self.din("pool_A", [64, 4, 16], F32)
        d_identf = self.din("ident", [128, 128], F32)
        d_identb = self.din("ident_b", [128, 128], BF16)
        d_masks = self.din("masks", [16, 128, 512], BF16)
        d_kpat = self.din("kpat", [32, 2048], BF16)
        d_wimp = self.din("wimp", [8, 128, 256], BF16)
        d_fpat = self.din("fpat", [128, 4], F32)
        d_gsel = self.din("gsel", [6, 24], F32)
        def ysl(r0, r1, t0, n):
            return ymc[t0 // 4096, r0:r1, (t0 % 4096):(t0 % 4096) + n]

        ps_s = [P.psum("ps_s%d" % i, [128, 512]) for i in range(2)]; b_ps_s = [P.pbuf() for i in range(2)]
        ps_o = [P.psum("ps_o%d" % i, [128, 512]) for i in range(2)]; b_ps_o = [P.pbuf() for i in range(2)]
        ps_i = [P.psum("ps_i%d" % i, [128, 512]) for i in range(2)]; b_ps_i = [P.pbuf() for i in range(2)]
        ps_t = [P.psum("ps_t%d" % i, [128, 512]) for i in range(2)]; b_ps_t = [P.pbuf() for i in range(2)]

        idf = P.sbuf("idf", [128, 128], F32); b_idf = P.buf()
        idb = P.sbuf("idb", [128, 128], BF16); b_idb = P.buf()
        masks = P.sbuf("masks", [128, 16, 512], BF16); b_masks = P.buf()
        wimp = P.sbuf("wimp", [128, 8, 256], BF16); b_wimp = P.buf()
        fpat = P.sbuf("fpat", [128, 4], F32); b_fpat = P.buf()
        ksa = P.sbuf("ksa", [128, S], BF16); b_ksa = P.buf()
        vsa = P.sbuf("vsa", [128, 128, VP], BF16); b_vsa = P.buf()
        kcmpT = P.sbuf("kcmpT", [96, 1024], BF16); b_kcmpT = P.buf()
        vcmpa = P.sbuf("vcmpa", [128, 8, VP], BF16); b_vcmpa = P.buf()
        op("sync", lambda e: e.dma_start(out=idf[:], in_=d_identf), writes=[b_idf], dma=True)
        op("sync", lambda e: e.dma_start(out=idb[:], in_=d_identb), writes=[b_idb], dma=True)
        op("sync", lambda e: e.dma_start(out=masks[:], in_=d_masks.rearrange("m p t -> p m t")), writes=[b_masks], dma=True)
        op("sync", lambda e: e.dma_start(out=wimp[:], in_=d_wimp.rearrange("c p j -> p c j")), writes=[b_wimp], dma=True)
        op("sync", lambda e: e.dma_start(out=fpat[:], in_=d_fpat), writes=[b_fpat], dma=True)
        gsel = P.sbuf("gsel", [128, 6, 24], F32); b_gsel = P.buf()
        op("sync", lambda e: e.dma_start(out=gsel[:].rearrange("p a b -> p (a b)"), in_=d_gsel.rearrange("a b -> (a b)").partition_broadcast(128)), writes=[b_gsel], dma=True)
        op("gpsimd", lambda e: e.memset(kcmpT[:], 0.0), writes=[b_kcmpT])
        op("gpsimd", lambda e: e.memset(vcmpa[:], 1.0), writes=[b_vcmpa])
        op("gpsimd", lambda e: e.memset(vsa[:], 1.0), writes=[b_vsa])

        q_sb = [P.sbuf("q_sb%d" % i, [96, 4, 512], BF16) for i in range(2)]; b_q = [P.buf() for i in range(2)]
        g_sb = [P.sbuf("g_sb%d" % i, [128, 4, 6], F32) for i in range(2)]; b_g = [P.buf() for i in range(2)]
        g_full = P.sbuf("g_full", [128, 4, 24], F32); b_gfull = P.buf()
        g_tmp = P.sbuf("g_tmp", [128, 4, 6, 24], F32); b_gtmp = P.buf()
        kw_sb = [P.sbuf("kw_sb%d" % i, [96, 1024], BF16) for i in range(2)]; b_kw = [P.buf() for i in range(2)]
        vwa = [P.sbuf("vwa%d" % i, [128, 8, VP], BF16) for i in range(2)]; b_vwa = [P.buf() for i in range(2)]
        e_all = [P.sbuf("e_all%d" % i, [128, 8, 512], BF16) for i in range(2)]; b_eall = [P.buf() for i in range(2)]
        e_sb = [P.sbuf("e_sb%d" % i, [128, 512], BF16) for i in range(3)]; b_e = [P.buf() for i in range(3)]
        oTs = [P.sbuf("oTs%d" % i, [97, 512], F32) for i in range(2)]; b_oTs = [P.buf() for i in range(2)]
        impacc = P.sbuf("impacc", [128, 4, 258], F32); b_imp = [P.buf() for i in range(4)]
        mr = P.sbuf("mr", [128, 256], F32); b_mr = P.buf()
        m8 = P.sbuf("m8", [128, 3, 8], F32); b_m8 = P.buf()
        selb = P.sbuf("selb", [128, 4, 352], F32); b_selb = [P.buf() for i in range(4)]
        qaug = P.sbuf("qaug", [128, 2, 8, 512], BF16); b_qaug = [[P.buf() for v in range(8)] for h in range(2)]
        oacc = [P.sbuf("oacc%d" % i, [128, 4, 2, 96], F32) for i in range(2)]; b_oacc = [P.buf() for i in range(2)]
        oTo = [P.sbuf("oTo%d" % i, [96, 2, 512], BF16) for i in range(2)]; b_oTo = [P.buf() for i in range(2)]
        rd = P.sbuf("rd", [128, 8, 4], F32); b_rd = P.buf()
        qa_f = qaug[:].rearrange("p a b c -> p (a b c)").bitcast(F32)
        ea0 = e_all[0][:].rearrange("p a b -> p (a b)")
        ea1_f = e_all[1][:].rearrange("p a b -> p (a b)").bitcast(F32)
        w1s = qa_f[0:96, 0:3072].rearrange("d (p e) -> d p e", e=96); b_w1s = P.buf()
        gx = qa_f[0:96, 3072:4096]; b_gx = P.buf()
        w1b = ea0[0:96, 0:3072].rearrange("d (p e) -> d p e", e=96); b_w1b = P.buf()
        gb = ea0[0:96, 3072:4096]; b_gb = P.buf()
        gy = ea1_f[0:96, 0:1024]; b_gy = P.buf()
        w2s = P.sbuf("w2s", [96, 96], F32); b_w2s = P.buf()
        w2b = P.sbuf("w2b", [96, 96], BF16); b_w2b = P.buf()
        poss = P.sbuf("poss", [32, 96], F32); b_poss = P.buf()
        posT = P.sbuf("posT", [96, 32], BF16); b_posT = P.buf()
        cb = P.sbuf("cb", [96, 1], F32); b_cb = P.buf()
        stage = getattr(self, 'stage', 99)
        for i in range(8):
            op("gpsimd", lambda e, i=i: e.dma_start(out=ksa[96:128, i * 2048:(i + 1) * 2048], in_=d_kpat), writes=[b_ksa], dma=True)
        for i in range(16 if stage >= 2 else 0):
            op("gpsimd", lambda e, i=i: e.dma_start(out=vsa[:, i * 8:(i + 1) * 8, 0:96],
                                                                        in_=_V(vs, e)[i * 1024:(i + 1) * 1024, :].rearrange("(k p) d -> p k d", p=128)),
               reads=[b_vs], writes=[b_vsa], dma=True)

        for kv, srcT, b_srcT in ((("k", kcT, b_kcT), ("v", vcT, b_vcT)) if stage >= 1 else ()):
            op("sync", lambda e, srcT=srcT: e.dma_start(out=ksa[0:96, :], in_=srcT), reads=[b_srcT], writes=[b_ksa], dma=True)
            op("sync", lambda e, kv=kv: e.dma_start(out=w1s, in_=cw[kv + "w1"].rearrange("(p d) e -> d p e", d=96)), writes=[b_w1s], dma=True)
            op("sync", lambda e, kv=kv: e.dma_start(out=w2s[:], in_=cw[kv + "w2"]), writes=[b_w2s], dma=True)
            op("sync", lambda e, kv=kv: e.dma_start(out=poss[:], in_=cw[kv + "pos"]), writes=[b_poss], dma=True)
            op("vector", lambda e: e.tensor_copy(out=w1b, in_=w1s), reads=[b_w1s], writes=[b_w1b])
            op("vector", lambda e: e.tensor_copy(out=w2b[:], in_=w2s[:]), reads=[b_w2s], writes=[b_w2b])
            op("tensor", lambda e: e.transpose(out=ps_t[0][0:96, 0:32], in_=poss[:], identity=idf[0:32, 0:32]), reads=[b_poss, b_idf], writes=[b_ps_t[0]], pe_accum=True)
            op("vector", lambda e: e.tensor_copy(out=posT[:], in_=ps_t[0][0:96, 0:32]), reads=[b_ps_t[0]], writes=[b_posT])
            for p in range(32):
                op("tensor", lambda e, p=p: e.matmul(ps_t[1][0:96, 0:1], lhsT=w1b[:, p, :], rhs=posT[:, p:p + 1], start=(p == 0), stop=(p == 31)),
                   reads=[b_w1b, b_posT], writes=[b_ps_t[1]], pe_accum=True)
            op("vector", lambda e: e.tensor_copy(out=cb[:], in_=ps_t[1][0:96, 0:1]), reads=[b_ps_t[1]], writes=[b_cb])
            if stage < 1.2:
                continue
            op("gpsimd", lambda e: e.memset(gb, 0.0), writes=[b_gb])
            kview = ksa[0:96, :].rearrange("d (n s) -> d n s", s=16)
            for half, (n0, N) in enumerate(((0, 512), (512, 511))):
                pp = ps_s[half]
                for p in range(32):
                    rhs = kview[:, n0:n0 + N, p] if p < 16 else kview[:, n0 + 1:n0 + 1 + N, p - 16]
                    op("tensor", lambda e, p=p, rhs=rhs, pp=pp, N=N: e.matmul(pp[0:96, 0:N], lhsT=w1b[:, p, :], rhs=rhs, start=(p == 0), stop=(p == 31)),
                       reads=[b_w1b, b_ksa], writes=[b_ps_s[half]], pe_accum=True)
                sl = slice(n0, n0 + N)
                op("vector", lambda e, pp=pp, N=N, sl=sl: e.tensor_scalar(out=gx[:, sl], in0=pp[0:96, 0:N], scalar1=cb[:, 0:1], scalar2=None, op0=ALU.add),
                   reads=[b_ps_s[half], b_cb], writes=[b_gx])
                op("vector", lambda e, sl=sl: e.tensor_tensor(out=gy[:, sl], in0=gx[:, sl], in1=gx[:, sl], op=ALU.mult), reads=[b_gx], writes=[b_gy])
                op("vector", lambda e, sl=sl: e.tensor_scalar(out=gy[:, sl], in0=gy[:, sl], scalar1=0.044715, scalar2=1.0, op0=ALU.mult, op1=ALU.add), reads=[b_gy], writes=[b_gy])
                op("vector", lambda e, sl=sl: e.tensor_tensor(out=gy[:, sl], in0=gy[:, sl], in1=gx[:, sl], op=ALU.mult), reads=[b_gy, b_gx], writes=[b_gy])
                op("scalar", lambda e, sl=sl: e.activation(out=gy[:, sl], in_=gy[:, sl], func=AF.Sigmoid, scale=GELU_C), reads=[b_gy], writes=[b_gy])
                op("vector", lambda e, sl=sl: e.tensor_tensor(out=gb[:, sl], in0=gy[:, sl], in1=gx[:, sl], op=ALU.mult), reads=[b_gy, b_gx], writes=[b_gb])
            if stage < 1.3:
                continue
            if kv == "k":
                for half, (n0, N) in enumerate(((0, 512), (512, 511))):
                    op("tensor", lambda e, half=half, n0=n0, N=N: e.matmul(ps_o[half][0:96, 0:N], lhsT=w2b[:], rhs=gb[:, n0:n0 + N], start=True, stop=True),
                       reads=[b_w2b, b_gb], writes=[b_ps_o[half]], pe_accum=True)
                    op("vector", lambda e, half=half, n0=n0, N=N: e.tensor_copy(out=kcmpT[:, n0:n0 + N], in_=ps_o[half][0:96, 0:N]), reads=[b_ps_o[half]], writes=[b_kcmpT])
            else:
                for c in range(8):
                    k2 = c % 2
                    op("tensor", lambda e, c=c, k2=k2: e.matmul(ps_o[k2][:, 0:96], lhsT=gb[:, c * 128:(c + 1) * 128], rhs=w2b[:], start=True, stop=True),
                       reads=[b_w2b, b_gb], writes=[b_ps_o[k2]], pe_accum=True)
                    op("vector", lambda e, c=c, k2=k2: e.tensor_copy(out=vcmpa[:, c, 0:96], in_=ps_o[k2][:, 0:96]), reads=[b_ps_o[k2]], writes=[b_vcmpa])

        stage = getattr(self, 'stage', 99)
        op("sync", lambda e: e.dma_start(out=ksa[0:96, :], in_=ksT), reads=[b_ksT], writes=[b_ksa], dma=True)
        CH = 512
        pw_s = P.sbuf("pw_s", [64, 64], F32); b_pw_s = P.buf()
        pw_b = P.sbuf("pw_b", [64, 64], BF16); b_pw_b = P.buf()
        psc = P.sbuf("psc", [64, 1], F32); b_psc = P.buf()
        pa = P.sbuf("pa", [64, 4], F32); b_pa = P.buf()
        pA = P.sbuf("pA", [64, 4, 16], F32); b_pA = P.buf()
        L_ = 16 + CH
        ub = [P.sbuf("ub%d" % i, [64, L_], F32)[:] for i in range(2)]; b_ub = [P.buf() for i in range(2)]
        sw = [P.sbuf("sw%d" % i, [64, L_], F32)[:] for i in range(4)]; b_sw = [P.buf() for i in range(4)]
        acc = P.sbuf("pacc", [64, CH], F32)[:]; b_acc = P.buf()
        a16 = P.sbuf("a16", [64, 2, 16], F32)[:]; b_a16 = P.buf()
        accb = P.sbuf("paccb", [64, CH], BF16)[:]; b_accb = P.buf()
        pout = [P.sbuf("pout%d" % i, [64, CH], BF16)[:] for i in range(2)]; b_pout = [P.buf() for i in range(2)]
        op("sync", lambda e: e.dma_start(out=pw_s[:], in_=pool_w), writes=[b_pw_s], dma=True)
        op("sync", lambda e: e.dma_start(out=psc[:], in_=pool_sc), writes=[b_psc], dma=True)
        op("sync", lambda e: e.dma_start(out=pa[:], in_=pool_a), writes=[b_pa], dma=True)
        op("sync", lambda e: e.dma_start(out=pA[:], in_=pool_A), writes=[b_pA], dma=True)
        op("vector", lambda e: e.tensor_copy(out=pw_b[:], in_=pw_s[:]), reads=[b_pw_s], writes=[b_pw_b])
        def pool_chunk(ci):
            k = ci % 2
            u = ub[k]
            if ci == 0:
                op("gpsimd", lambda e, u=u: e.memset(u[:, 0:16], 0.0), writes=[b_ub[k]])
                op("sync", lambda e, u=u: e.dma_start(out=u[:, 16:], in_=uT[:, 0:CH]), reads=[b_uT], writes=[b_ub[k]], dma=True)
            else:
                op("sync", lambda e, u=u, ci=ci: e.dma_start(out=u, in_=uT[:, ci * CH - 16:(ci + 1) * CH]), reads=[b_uT], writes=[b_ub[k]], dma=True)
            L = 16 + CH
            prev, b_prev = u, b_ub[k]
            for wi, sh in enumerate((1, 2, 4, 8)):
                lo = 2 * sh - 1
                dst = sw[wi]
                op("gpsimd", lambda e, dst=dst, prev=prev, lo=lo, sh=sh, L=L: e.tensor_tensor(out=dst[:, lo:L], in0=prev[:, lo:L], in1=prev[:, lo - sh:L - sh], op=ALU.add),
                   reads=[b_prev], writes=[b_sw[wi]])
                prev, b_prev = dst, b_sw[wi]
            op("vector", lambda e: e.tensor_scalar(out=acc, in0=sw[0][:, 16:], scalar1=pa[:, 0:1], scalar2=None, op0=ALU.mult), reads=[b_sw[0], b_pa], writes=[b_acc])
            for wi in range(1, 4):
                op("vector", lambda e, wi=wi: e.scalar_tensor_tensor(out=acc, in0=sw[wi][:, 16:], scalar=pa[:, wi:wi + 1], in1=acc, op0=ALU.mult, op1=ALU.add),
                   reads=[b_sw[wi], b_pa, b_acc], writes=[b_acc])
            if ci == 0:
                op("vector", lambda e: e.tensor_tensor(out=a16[:, 0, :], in0=sw[0][:, 16:32], in1=pA[:, 0, :], op=ALU.mult), reads=[b_sw[0], b_pA], writes=[b_a16])
                for wi in range(1, 4):
                    op("vector", lambda e, wi=wi: e.tensor_tensor(out=a16[:, 1, :], in0=sw[wi][:, 16:32], in1=pA[:, wi, :], op=ALU.mult), reads=[b_sw[wi], b_pA, b_a16], writes=[b_a16])
                    op("vector", lambda e: e.tensor_tensor(out=a16[:, 0, :], in0=a16[:, 0, :], in1=a16[:, 1, :], op=ALU.add), reads=[b_a16], writes=[b_a16])
                op("vector", lambda e: e.tensor_copy(out=acc[:, 0:16], in_=a16[:, 0, :]), reads=[b_a16, b_acc], writes=[b_acc])
            op("vector", lambda e, u=u: e.tensor_tensor(out=accb, in0=acc, in1=u[:, 16:], op=ALU.subtract), reads=[b_acc, b_ub[k]], writes=[b_accb])
            for hh in range(CH // 512):
                pbi = nxt("t", 2)
                op("tensor", lambda e, hh=hh, pbi=pbi: e.matmul(ps_t[pbi][0:64, :], lhsT=pw_b[:], rhs=accb[:, hh * 512:(hh + 1) * 512], start=True, stop=True),
                   reads=[b_pw_b, b_accb], writes=[b_ps_t[pbi]], pe_accum=True)
                op("vector", lambda e, hh=hh, k=k, pbi=pbi: e.tensor_scalar(out=pout[k][:, hh * 512:(hh + 1) * 512], in0=ps_t[pbi][0:64, :], scalar1=psc[:, 0:1], scalar2=None, op0=ALU.mult),
                   reads=[b_ps_t[pbi], b_psc], writes=[b_pout[k]])
            op("sync", lambda e, k=k, ci=ci: e.dma_start(out=ysl(0, 64, ci * CH, CH), in_=pout[k]), reads=[b_pout[k]], writes=[b_ymc], dma=True)


        P.barrier()

        for i in range(2):
            op("gpsimd", lambda e, i=i: e.memset(vwa[i][:], 1.0), writes=[b_vwa[i]])
        op("gpsimd", lambda e: e.memset(selb[:], 0.0), writes=b_selb)
        op("gpsimd", lambda e: e.memset(impacc[:], 0.0), writes=b_imp)

        cnt = {"s": 0, "o": 0, "e": 0, "t": 0, "rd": 0}

        def nxt(key, n):
            v = cnt[key] % n
            cnt[key] += 1
            return v

        def load_tile(qt):
            k = qt % 2
            tok = slice(qt * 512, (qt + 1) * 512)
            op("sync", lambda e: e.dma_start(out=q_sb[k][:], in_=qT[:, :, tok].rearrange("h d t -> d h t")), reads=[b_qT], writes=[b_q[k]], dma=True)
            op("sync", lambda e: e.dma_start(out=g_full[:], in_=gates[tok, :].rearrange("(s p) c -> p s c", p=128)), reads=[b_gates], writes=[b_gfull], dma=True)
            op("gpsimd", lambda e: e.tensor_tensor(out=g_tmp[:], in0=g_full[:].unsqueeze(2).to_broadcast([128, 4, 6, 24]), in1=gsel[:].unsqueeze(1).to_broadcast([128, 4, 6, 24]), op=ALU.mult),
               reads=[b_gfull, b_gsel], writes=[b_gtmp])
            op("vector", lambda e: e.tensor_reduce(out=g_sb[k][:], in_=g_tmp[:], axis=AX.X, op=ALU.add), reads=[b_gtmp], writes=[b_g[k]])
            if qt == 0:
                op("sync", lambda e: e.dma_start(out=kw_sb[k][:, 512:1024], in_=kwT[:, 0:512]), reads=[b_kwT], writes=[b_kw[k]], dma=True)
                op("sync", lambda e: e.dma_start(out=vwa[k][:, 4:8, 0:96], in_=_V(vw, e)[0:512, :].rearrange("(k p) d -> p k d", p=128)), reads=[b_vw], writes=[b_vwa[k]], dma=True)
            else:
                op("sync", lambda e: e.dma_start(out=kw_sb[k][:], in_=kwT[:, qt * 512 - 512: qt * 512 + 512]), reads=[b_kwT], writes=[b_kw[k]], dma=True)
                op("sync", lambda e: e.dma_start(out=vwa[k][:, :, 0:96], in_=_V(vw, e)[qt * 512 - 512: qt * 512 + 512, :].rearrange("(k p) d -> p k d", p=128)), reads=[b_vw], writes=[b_vwa[k]], dma=True)

        def finish_branch(qt, po, h_own, gcol, first, rd_out=None):
            k = qt % 2
            bi = nxt("t", 2)
            osb = oTs[bi]
            op("vector", lambda e: e.tensor_copy(out=osb[:], in_=ps_o[po][0:97, :]), reads=[b_ps_o[po]], writes=[b_oTs[bi]])
            pt = ps_t[bi]
            for sub in range(4):
                op("tensor", lambda e, sub=sub: e.transpose(out=pt[:, sub * 128: sub * 128 + 97], in_=osb[0:97, sub * 128:(sub + 1) * 128], identity=idf[0:97, 0:97]),
                   reads=[b_oTs[bi], b_idf], writes=[b_ps_t[bi]], pe_accum=True)
            ri = nxt("rd", 8)
            ptv = pt[:].rearrange("p (s c) -> p s c", c=128)
            op("vector", lambda e: e.tensor_scalar(out=rd[:, ri, :], in0=ptv[:, :, 96], scalar1=1e-30, scalar2=None, op0=ALU.max), reads=[b_ps_t[bi]], writes=[b_rd])
            op("vector", lambda e: e.reciprocal(rd[:, ri, :], rd[:, ri, :]), reads=[b_rd], writes=[b_rd])
            if h_own is not None:
                ci = nxt("rd", 8)
                op("vector", lambda e: e.tensor_tensor(out=rd[:, ci, :], in0=rd[:, ri, :], in1=g_sb[k][:, :, h_own * 3 + gcol], op=ALU.mult), reads=[b_rd, b_g[k]], writes=[b_rd])
                dbg = getattr(self, "dbg_branch", None)
                if dbg is not None and dbg != gcol:
                    op("vector", lambda e: e.memset(rd[:, ci, :], 0.0), reads=[b_rd], writes=[b_rd])
                for sub in range(4):
                    if first:
                        op("vector", lambda e, sub=sub: e.tensor_scalar(out=oacc[k][:, sub, h_own, :], in0=ptv[:, sub, 0:96], scalar1=rd[:, ci, sub:sub + 1], scalar2=None, op0=ALU.mult),
                           reads=[b_ps_t[bi], b_rd], writes=[b_oacc[k]])
                    else:
                        op("vector", lambda e, sub=sub: e.scalar_tensor_tensor(out=oacc[k][:, sub, h_own, :], in0=ptv[:, sub, 0:96], scalar=rd[:, ci, sub:sub + 1],
                                                                              in1=oacc[k][:, sub, h_own, :], op0=ALU.mult, op1=ALU.add),
                           reads=[b_ps_t[bi], b_rd, b_oacc[k]], writes=[b_oacc[k]])
            return ri

        def cmp_part(qt):
            k = qt % 2
            nct = min(8, qt // 4 + 1)
            for h in range(4):
                ea = e_all[h % 2]
                for c in range(nct):
                    si = nxt("s", 2)
                    partial = qt < 4 * c + 4
                    op("tensor", lambda e, c=c, h=h, si=si, partial=partial: e.matmul(ps_s[si][:], lhsT=kcmpT[:, c * 128:(c + 1) * 128], rhs=q_sb[k][:, h, :], start=True, stop=not partial),
                       reads=[b_kcmpT, b_q[k]], writes=[b_ps_s[si]], pe_accum=True)
                    if partial:
                        r = qt - 4 * c
                        op("tensor", lambda e, si=si, r=r: e.matmul(ps_s[si][:], lhsT=idb[:], rhs=masks[:, 8 + r, :], start=False, stop=True),
                           reads=[b_idb, b_masks], writes=[b_ps_s[si]], pe_accum=True)
                    op("scalar", lambda e, c=c, si=si, ea=ea: e.activation(out=ea[:, c, :], in_=ps_s[si][:], func=AF.Exp), reads=[b_ps_s[si]], writes=[b_eall[h % 2]])
                po = nxt("o", 2)
                for c in range(nct):
                    op("tensor", lambda e, c=c, ea=ea, po=po: e.matmul(ps_o[po][:, :], lhsT=vcmpa[:, c, 0:128], rhs=ea[:, c, :], start=(c == 0), stop=(c == nct - 1)),
                       reads=[b_vcmpa, b_eall[h % 2]], writes=[b_ps_o[po]], pe_accum=True)
                for sub in range(4):
                    pi = ps_i[sub // 2]
                    for c in range(nct):
                        op("tensor", lambda e, c=c, sub=sub, ea=ea, pi=pi: e.matmul(pi[:, (sub % 2) * 256:(sub % 2) * 256 + 256], lhsT=ea[:, c, sub * 128:(sub + 1) * 128], rhs=wimp[:, c, :],
                                                                               start=(c == 0), stop=(c == nct - 1)),
                           reads=[b_wimp, b_eall[h % 2]], writes=[b_ps_i[sub // 2]], pe_accum=True)
                ri = finish_branch(qt, po, h if h < 2 else None, 0, True)
                for sub in range(4):
                    pi = ps_i[sub // 2]
                    src = pi[:, (sub % 2) * 256:(sub % 2) * 256 + 256]
                    if h == 0:
                        op("vector", lambda e, sub=sub, src=src, ri=ri: e.tensor_scalar(out=impacc[:, sub, 1:257], in0=src, scalar1=rd[:, ri, sub:sub + 1], scalar2=None, op0=ALU.mult),
                           reads=[b_ps_i[sub // 2], b_rd], writes=[b_imp[sub]])
                    else:
                        op("vector", lambda e, sub=sub, src=src, ri=ri: e.scalar_tensor_tensor(out=impacc[:, sub, 1:257], in0=src, scalar=rd[:, ri, sub:sub + 1], in1=impacc[:, sub, 1:257],
                                                                                            op0=ALU.mult, op1=ALU.add),
                           reads=[b_ps_i[sub // 2], b_rd, b_imp[sub]], writes=[b_imp[sub]])

        def topk_part(qt):
            for sub in range(4):
                st = 4 * qt + sub
                sc = impacc[:, sub, 1:257]
                op("vector", lambda e, sub=sub, st=st: e.tensor_tensor(out=impacc[:, sub, 2 * st:2 * st + 3], in0=impacc[:, sub, 2 * st:2 * st + 3], in1=fpat[:, 0:3], op=ALU.add),
                   reads=[b_imp[sub], b_fpat], writes=[b_imp[sub]])
                op("vector", lambda e, sub=sub: e.tensor_scalar(out=impacc[:, sub, 1:2], in0=impacc[:, sub, 1:2], scalar1=4000.0, scalar2=None, op0=ALU.add), reads=[b_imp[sub]], writes=[b_imp[sub]])
                op("vector", lambda e, sc=sc: e.max(out=m8[:, 0, :], in_=sc), reads=[b_imp[sub]], writes=[b_m8])
                op("vector", lambda e, sc=sc: e.match_replace(out=mr[:], in_to_replace=m8[:, 0, :], in_values=sc, imm_value=-1e9), reads=[b_imp[sub], b_m8], writes=[b_mr])
                op("vector", lambda e: e.max(out=m8[:, 1, :], in_=mr[:]), reads=[b_mr], writes=[b_m8])
                op("vector", lambda e: e.tensor_reduce(out=m8[:, 2, 0:1], in_=m8[:, 1, :], axis=AX.X, op=ALU.min), reads=[b_m8], writes=[b_m8])
                op("vector", lambda e, sub=sub, sc=sc: e.tensor_scalar(out=selb[:, sub, 96:352], in0=sc, scalar1=m8[:, 2, 0:1], scalar2=NEG, op0=ALU.is_lt, op1=ALU.mult),
                   reads=[b_imp[sub], b_m8], writes=[b_selb[sub]])

        def qaug_part(qt):
            k = qt % 2
            nv = (qt + 1 + 3) // 4
            for v in range(nv):
                bi = nxt("t", 2)
                pt = ps_t[bi]
                for sub in range(4):
                    op("tensor", lambda e, sub=sub, v=v, pt=pt: e.transpose(out=pt[:, sub * 128:(sub + 1) * 128], in_=selb[:, sub, 32 * v:32 * v + 128], identity=idf[:]),
                       reads=[b_selb[sub], b_idf], writes=[b_ps_t[bi]], pe_accum=True)
                for h in range(2):
                    op("vector", lambda e, h=h, v=v, pt=pt: e.tensor_copy(out=qaug[96:128, h, v, :], in_=pt[96:128, :]),
                       reads=[b_ps_t[bi]], writes=[b_qaug[h][v]])
                    op("gpsimd", lambda e, h=h, v=v: e.tensor_copy(out=qaug[0:96, h, v, :], in_=q_sb[k][:, h, :]), reads=[b_q[k]], writes=[b_qaug[h][v]])

        def win_part(qt):
            k = qt % 2
            tiles = list(range(4, 8)) if qt == 0 else list(range(8))
            items = [(h, n, i) for h in range(2) for n, i in enumerate(tiles)]
            pos = {}
            pend = None

            def pv(item):
                h, n, i = item
                ei, po = pos[item]
                op("tensor", lambda e, i=i, ei=ei, po=po, n=n: e.matmul(ps_o[po][:, :], lhsT=vwa[k][:, i, 0:128], rhs=e_sb[ei][:], start=(n == 0), stop=(n == len(tiles) - 1)),
                   reads=[b_vwa[k], b_e[ei]], writes=[b_ps_o[po]], pe_accum=True)
                if n == len(tiles) - 1:
                    finish_branch(qt, po, h, 2, False)

            po_h = {}
            for item in items:
                h, n, i = item
                if n == 0:
                    po_h[h] = nxt("o", 2)
                si = nxt("s", 2)
                ei = nxt("e", 3)
                pos[item] = (ei, po_h[h])
                op("tensor", lambda e, i=i, si=si, h=h: e.matmul(ps_s[si][:], lhsT=kw_sb[k][:, i * 128:(i + 1) * 128], rhs=q_sb[k][:, h, :], start=True, stop=False),
                   reads=[b_kw[k], b_q[k]], writes=[b_ps_s[si]], pe_accum=True)
                op("tensor", lambda e, i=i, si=si: e.matmul(ps_s[si][:], lhsT=idb[:], rhs=masks[:, i, :], start=False, stop=True),
                   reads=[b_idb, b_masks], writes=[b_ps_s[si]], pe_accum=True)
                op("scalar", lambda e, si=si, ei=ei: e.activation(out=e_sb[ei][:], in_=ps_s[si][:], func=AF.Exp), reads=[b_ps_s[si]], writes=[b_e[ei]])
                if pend is not None:
                    pv(pend)
                pend = item
            pv(pend)

        def slc_part(qt):
            k = qt % 2
            nkt = 4 * (qt + 1)
            items = [(h, kt) for h in range(2) for kt in range(nkt)]
            pos = {}
            pend = None
            po_h = {}

            def pv(item):
                h, kt = item
                ei, po = pos[item]
                op("tensor", lambda e, kt=kt, ei=ei, po=po: e.matmul(ps_o[po][:, :], lhsT=vsa[:, kt, 0:128], rhs=e_sb[ei][:], start=(kt == 0), stop=(kt == nkt - 1)),
                   reads=[b_vsa, b_e[ei]], writes=[b_ps_o[po]], pe_accum=True)
                if kt == nkt - 1:
                    finish_branch(qt, po, h, 1, False)

            for item in items:
                h, kt = item
                if kt == 0:
                    po_h[h] = nxt("o", 2)
                si = nxt("s", 2)
                ei = nxt("e", 3)
                pos[item] = (ei, po_h[h])
                v = kt // 16
                diag = kt >= 4 * qt
                op("tensor", lambda e, kt=kt, si=si, v=v, diag=diag, h=h: e.matmul(ps_s[si][:], lhsT=ksa[:, kt * 128:(kt + 1) * 128], rhs=qaug[:, h, v, :], start=True, stop=not diag),
                   reads=[b_ksa, b_qaug[h][v]], writes=[b_ps_s[si]], pe_accum=True)
                if diag:
                    op("tensor", lambda e, kt=kt, si=si: e.matmul(ps_s[si][:], lhsT=idb[:], rhs=masks[:, 12 + kt - 4 * qt, :], start=False, stop=True),
                       reads=[b_idb, b_masks], writes=[b_ps_s[si]], pe_accum=True)
                op("scalar", lambda e, si=si, ei=ei: e.activation(out=e_sb[ei][:], in_=ps_s[si][:], func=AF.Exp), reads=[b_ps_s[si]], writes=[b_e[ei]])
                if pend is not None:
                    pv(pend)
                pend = item
            pv(pend)

        def out_part(qt):
            k = qt % 2
            tok = slice(qt * 512, (qt + 1) * 512)
            for h in range(2):
                bi = nxt("t", 2)
                pt = ps_t[bi]
                for sub in range(4):
                    op("tensor", lambda e, sub=sub, h=h, pt=pt: e.transpose(out=pt[0:96, sub * 128:(sub + 1) * 128], in_=oacc[k][:, sub, h, :], identity=idf[:]),
                       reads=[b_oacc[k], b_idf], writes=[b_ps_t[bi]], pe_accum=True)
                op("vector", lambda e, h=h, pt=pt: e.tensor_copy(out=oTo[k][:, h, :], in_=pt[0:96, :]), reads=[b_ps_t[bi]], writes=[b_oTo[k]])
            op("sync", lambda e: e.dma_start(out=ysl(64, 256, qt * 512, 512).rearrange("(h d) t -> d h t", d=96), in_=oTo[k][:]), reads=[b_oTo[k]], writes=[b_ymc], dma=True)

        nq = self.nqt if hasattr(self, "nqt") else NQT
        if stage < 4:
            nq = 0
        if nq > 0:
            load_tile(0)
            cmp_part(0)
            if stage >= 4.2:
                topk_part(0)
        for qt in range(nq):
            if stage >= 4.3:
                qaug_part(qt)
            if stage >= 4.4:
                win_part(qt)
            pool_chunk(qt)
            if qt + 1 < nq and stage >= 4.7:
                load_tile(qt + 1)
                cmp_part(qt + 1)
                topk_part(qt + 1)
            if stage >= 4.5:
                slc_part(qt)
            if stage >= 4.6:
                out_part(qt)
            if stage < 4.7:
                break
import ml_dtypes
from concourse.bass_utils import run_bass_kernel_spmd

_BF = ml_dtypes.bfloat16
_NCORE = 8
_GROUPS = [[0, 1, 2, 3], [4, 5, 6, 7]]
_DBG = {}
_CC_CHAIN = Buf("cc_chain")


def _reset_chain():
    _CC_CHAIN.w = None
    _CC_CHAIN.r = {}


def _gather(P, c_ap, b_c, g_ap, b_g):
    if _DBG.get("nocc"):
        rows = c_ap.shape[0]
        for r in range(4):
            P.op("gpsimd", lambda e, r=r: e.dma_start(out=g_ap[r * rows:(r + 1) * rows, :], in_=c_ap), reads=[b_c], writes=[b_g], dma=True)
        return
    P.op("gpsimd", lambda e: e.collective_compute("AllGather", ALU.bypass, replica_groups=_GROUPS, ins=[c_ap], outs=[g_ap]),
         reads=[b_c, _CC_CHAIN], writes=[b_g, _CC_CHAIN], cc=True)


def build_fused():
    nc = bass.Bass("TRN2", target_bir_lowering=False)
    _reset_chain()
    TB.gus_ctr = 0
    P = Prog(nc)
    dram = Dram(nc, P)
    dram.dump = tuple(_DBG.get("dump", ()))
    ident = dram.din("ident", [128, 128], F32)
    xin = dram.din("xin", [NTOK, D], F32)
    xout = dram.dout("xout", [NTOK, D], F32)
    op = P.op

    jcache = {}

    def jexpr(e):
        if "v" not in jcache:
            pid = e.partition_id()
            j = e.snap(pid % 4, min_val=0, max_val=3)
            g = e.snap(j // 2, min_val=0, max_val=1)
            jo = e.snap(g * 2 + (1 - (j % 2)), min_val=0, max_val=3)
            jcache["v"] = dict(j=j, g=g, jo=jo)
        return jcache["v"]

    cur, b_cur = xin, None
    tcount = 0
    ym_loc = b_ym_loc = None
    for l in range(3):
        P.push_scope()
        tb = TB(nc, P, ident, dram)
        if l > 0:
            pl = "l%d" % (l - 1)
            x2 = dram.dscr("x2", [NTOK, D], F32)
            scr2, b_scr2, its2 = tb.prep_for(pl + "_ffn2")

            def hook(t_, its2=its2):
                for it in its2[2 * t_:2 * t_ + 2]:
                    it()
            tcount = tb.stage_mix(pl, ym_loc, b_ym_loc, cur, b_cur, x2, dram.b["x2"], tcount, tile_hook=hook)
            if l == 2:
                tcount = tb.stage_ffn(pl + "_ffn2", x2, dram.b["x2"], xout, dram.b["xout"], tcount, prepped=(scr2, b_scr2))
                P.pop_scope()
                break
            x3 = dram.dscr("x3", [NTOK, D], F32)
            scr1, b_scr1, its1 = tb.prep_for("l%d_ffn1" % l)

            def hook1(t_, its1=its1):
                tb.prep_cast_eng = "gpsimd"
                for it in its1[2 * t_:2 * t_ + 2]:
                    it()
                tb.prep_cast_eng = None
            tcount = tb.stage_ffn(pl + "_ffn2", x2, dram.b["x2"], x3, dram.b["x3"], tcount, prepped=(scr2, b_scr2), tile_hook=hook1)
            cur, b_cur = x3, dram.b["x3"]
            pre1 = (scr1, b_scr1)
        ll = "l%d" % l
        x1 = dram.dscr("x1", [NTOK, D], F32)
        tcount = tb.stage_ffn(ll + "_ffn1", cur, b_cur, x1, dram.b["x1"], tcount, prepped=(pre1 if l > 0 else None))
        cur, b_cur = x1, dram.b["x1"]
        def ctensor(name, shp, dt):
            ap = dram.dscr(name, shp, dt)
            return ap, dram.b[name]
        def gtensor(name, shp, dt):
            if name not in dram.ap:
                dram.ap[name] = nc.dram_tensor(name, list(shp), dt).ap()
                dram.b[name] = P.buf(name)
            return dram.ap[name], dram.b[name]
        C_q, b_Cq = ctensor("c_q", [8, 128, NTOK], BF16)
        C_u, b_Cu = ctensor("c_u", [4, 64, NTOK], F32)
        C_k, b_Ck = ctensor("c_k", [8, 96, NTOK], BF16)
        C_vs, b_Cvs = ctensor("c_vs", [2, NTOK, 96], BF16)
        C_vw, b_Cvw = ctensor("c_vw", [2, NTOK, 96], BF16)
        C_g, b_Cg = ctensor("c_g", [NTOK, 24], F32)
        G_q, b_Gq = gtensor("g_q", [8, 4, 128, NTOK], BF16)
        G_u, b_Gu = gtensor("g_u", [4, 4, 64, NTOK], F32)
        G_k, b_Gk = gtensor("g_k", [8, 4, 96, NTOK], BF16)
        G_vs, b_Gvs = gtensor("g_vs", [2, 4, NTOK, 96], BF16)
        G_vw, b_Gvw = gtensor("g_vw", [2, 4, NTOK, 96], BF16)
        G_g, b_Gg = ctensor("g_g", [4 * NTOK, 24], F32)
        outs = {"gates": (C_g, b_Cg)}
        outs["qT"] = (C_q, b_Cq)
        outs["uT"] = ([C_u[p] for p in range(4)], b_Cu)
        outs["kT"] = (C_k, b_Ck)
        outs["vs"] = (C_vs, b_Cvs)
        outs["vw"] = (C_vw, b_Cvw)
        tcount = tb.stage_proj(ll, cur, b_cur, outs, tcount)
        P.pop_scope()
        if _DBG.get("stop3") and l == 1:
            P.wait_all("sync", list(dram.b.values()))
            P.finish()
            return nc
        if _DBG.get("dump") and l == 0:
            d0 = dram.dscr("dbg_ck0_pre", [96, NTOK], BF16)
            op("sync", lambda e: e.dma_start(out=d0, in_=C_k[0]), reads=[b_Ck], writes=[dram.b["dbg_ck0_pre"]], dma=True)
            dram.outs += [dram.b["dbg_ck0_pre"]]
            P.barrier()
        for i in range(8):
            _gather(P, C_q[i], b_Cq, G_q[i].rearrange("r d t -> (r d) t"), b_Gq)
        for i in range(8):
            _gather(P, C_k[i], b_Ck, G_k[i].rearrange("r d t -> (r d) t"), b_Gk)
        for i in range(4):
            _gather(P, C_u[i], b_Cu, G_u[i].rearrange("r c t -> (r c) t"), b_Gu)
        for i in range(2):
            _gather(P, C_vs[i], b_Cvs, G_vs[i].rearrange("r t d -> (r t) d"), b_Gvs)
            _gather(P, C_vw[i], b_Cvw, G_vw[i].rearrange("r t d -> (r t) d"), b_Gvw)
        _gather(P, C_g, b_Cg, G_g, b_Gg)
        P.barrier()
        if _DBG.get("dump") and l == 0:
            d1 = dram.dscr("dbg_ck0", [96, NTOK], BF16); d2 = dram.dscr("dbg_gk0", [4 * 96, NTOK], BF16)
            op("sync", lambda e: e.dma_start(out=d1, in_=C_k[0]), reads=[b_Ck], writes=[dram.b["dbg_ck0"]], dma=True)
            op("sync", lambda e: e.dma_start(out=d2, in_=G_k[0].rearrange("r d t -> (r d) t")), reads=[b_Gk], writes=[dram.b["dbg_gk0"]], dma=True)
            dram.outs += [dram.b["dbg_ck0"], dram.b["dbg_gk0"]]
        loc = {}

        def mk(name, shp, dt):
            ap = dram.dscr("l_" + name, shp, dt)
            loc[name] = (ap, dram.b["l_" + name])
            return ap, loc[name][1]

        l_qT, b_lq = mk("qT", [4, 96, S], BF16)
        kinds = ["kcT", "vcT", "ksT", "kwT"]
        for name in kinds:
            mk(name, [96, S], BF16)
        mk("uT", [64, S], F32); mk("vs", [S, 96], BF16); mk("vw", [S, 96], BF16)
        gk5 = G_k.rearrange("(k g) r d t -> k g r d t", g=2)
        gq5 = G_q.rearrange("(p h) r d t -> p h r d t", h=2)
        for h2 in range(2):
            def q_own(e, h2=h2):
                return e.dma_start(out=l_qT[h2].rearrange("d (r t) -> d r t", r=4),
                                   in_=gq5[bass.ds(jexpr(e)["j"], 1), h2, :, 0:96, :].rearrange("o r d t -> (o d) r t"))

            def q_oth(e, h2=h2):
                return e.dma_start(out=l_qT[2 + h2].rearrange("d (r t) -> d r t", r=4),
                                   in_=gq5[bass.ds(jexpr(e)["jo"], 1), h2, :, 0:96, :].rearrange("o r d t -> (o d) r t"))
            op("sync", q_own, reads=[b_Gq], writes=[b_lq], dma=True)
            op("sync", q_oth, reads=[b_Gq], writes=[b_lq], dma=True)

        def uf(e):
            return e.dma_start(out=loc["uT"][0].rearrange("c (r t) -> c r t", r=4),
                               in_=G_u[bass.ds(jexpr(e)["j"], 1), :, :, :].rearrange("o r c t -> (o c) r t"))
        op("sync", uf, reads=[b_Gu], writes=[loc["uT"][1]], dma=True)
        for ki, name in enumerate(kinds):
            def kf(e, name=name, ki=ki):
                return e.dma_start(out=loc[name][0].rearrange("d (r t) -> d r t", r=4),
                                   in_=gk5[ki, bass.ds(jexpr(e)["g"], 1), :, :, :].rearrange("o r d t -> (o d) r t"))
            op("sync", kf, reads=[b_Gk], writes=[loc[name][1]], dma=True)
        for name, Gv, b_Gv in (("vs", G_vs, b_Gvs), ("vw", G_vw, b_Gvw)):
            for r in range(4):
                rows = slice(r * NTOK, (r + 1) * NTOK)

                def vf(e, name=name, rows=rows, r=r, Gv=Gv):
                    return e.dma_start(out=loc[name][0][rows, :], in_=Gv[bass.ds(jexpr(e)["g"], 1), r, :, :].rearrange("o t d -> (o t) d"))
                op("sync", vf, reads=[b_Gv], writes=[loc[name][1]], dma=True)
        loc["gates"] = (G_g, b_Gg)
        if _DBG.get("stop1") is not None and _DBG.get("stop1") == l:
            P.wait_all("sync", list(dram.b.values()))
            P.finish()
            return nc
        ymc = dram.dscr("c_ym", [4, 256, NTOK], BF16)
        b_ymc = dram.b["c_ym"]
        P.push_scope()
        bb = BB(nc, P, dram)
        bb.build(ll, loc, ymc, b_ymc)
        P.pop_scope()
        g_ym, b_gym = gtensor("g_ym", [4, 2, 4, 128, NTOK], BF16)
        for tj in range(4):
            for hf in range(2):
                _gather(P, ymc[tj, hf * 128:(hf + 1) * 128, :], b_ymc, g_ym[tj, hf].rearrange("r i t -> (r i) t"), b_gym)
        P.barrier()
        ym_loc = dram.dscr("l_ym", [D, NTOK], BF16)
        b_ym_loc = dram.b["l_ym"]
        for hf in range(2):
            def yf(e, g_ym=g_ym, ym_loc=ym_loc, hf=hf):
                return e.dma_start(out=ym_loc[hf * 512:(hf + 1) * 512, :], in_=g_ym[bass.ds(jexpr(e)["j"], 1), hf, :, :, :].rearrange("o r i t -> (o r i) t"))
            op("sync", yf, reads=[b_gym], writes=[b_ym_loc], dma=True)
        if _DBG.get("stop2") is not None and _DBG.get("stop2") == l:
            P.wait_all("sync", list(dram.b.values()))
            P.finish()
            return nc
    P.wait_all("sync", dram.outs)
    P.finish()
    return nc


def _yperm():
    src_of = lambda r, row: (64 * r + row) if row < 64 else (256 + 192 * r + (row - 64))
    return np.array([src_of(r, hf * 128 + i) for hf in range(2) for r in range(4) for i in range(128)])


_YPERM = _yperm()


def kernel(**inputs):
    inputs = {k: np.asarray(v) for k, v in inputs.items()}
    x = inputs["x"].reshape(2 * S, D).astype(np.float32, copy=False)
    nc = build_fused()
    C = b_consts()
    maps = []
    for c in range(_NCORE):
        b, j = divmod(c, 4)
        m = {"ident": C["ident_f"], "ident_b": C["ident_b"], "masks": C["masks"], "kpat": C["kpat"], "wimp": C["wimp"], "fpat": C["fpat"]}
        a, A = pool_consts(j)
        m["pool_a"] = a
        gs = np.zeros((6, 24), np.float32)
        gs[np.arange(6), 6 * j + np.arange(6)] = 1.0
        m["gsel"] = gs
        m["pool_A"] = A
        m["xin"] = x[c * NTOK:(c + 1) * NTOK]
        for l in range(2):
            p = "l%d" % l
            m[p + "_ffn1_wg"] = inputs["ffn1_w_gate"][l]; m[p + "_ffn1_wu"] = inputs["ffn1_w_up"][l]; m[p + "_ffn1_wd"] = inputs["ffn1_w_down"][l]
            m[p + "_ffn1_lg"] = inputs["ln1_g"][l]; m[p + "_ffn1_lb"] = inputs["ln1_b"][l]
            m[p + "_ffn2_wg"] = inputs["ffn2_w_gate"][l]; m[p + "_ffn2_wu"] = inputs["ffn2_w_up"][l]; m[p + "_ffn2_wd"] = inputs["ffn2_w_down"][l]
            m[p + "_ffn2_lg"] = inputs["ln3_g"][l]; m[p + "_ffn2_lb"] = inputs["ln3_b"][l]
            m[p + "_w_in"] = inputs["w_in"][l]; m[p + "_b_gate"] = inputs["b_gate"][l]
            m[p + "_w_out"] = inputs["w_out"][l][_YPERM]
            m[p + "_ln2_g"] = inputs["ln2_g"][l]; m[p + "_ln2_b"] = inputs["ln2_b"][l]
            m[p + "_cmp_k_w1"] = inputs["cmp_k_w1"][l]; m[p + "_cmp_k_w2"] = inputs["cmp_k_w2"][l]; m[p + "_cmp_k_pos"] = inputs["cmp_pos_k"][l]
            m[p + "_cmp_v_w1"] = inputs["cmp_v_w1"][l]; m[p + "_cmp_v_w2"] = inputs["cmp_v_w2"][l]; m[p + "_cmp_v_pos"] = inputs["cmp_pos_v"][l]
            m[p + "_pool_w"] = inputs["pool_w"][l][j]
            m[p + "_pool_sc"] = inputs["pool_scale"][l][64 * j:64 * j + 64].reshape(64, 1)
        maps.append({k: np.ascontiguousarray(v) for k, v in m.items()})
    res = run_bass_kernel_spmd(nc, maps, core_ids=list(range(_NCORE))).results
    _DBG["res"] = res if _DBG.get("dump") else None
    out = np.concatenate([r["xout"] for r in res], axis=0)
    return out.reshape(2, S, D).astype(np.float32, copy=False)
```

```python
import numpy as np
import concourse.bass as bass
import concourse.mybir as mybir
from contextlib import ExitStack

F32 = mybir.dt.float32
BF16 = mybir.dt.bfloat16
AF = mybir.ActivationFunctionType
ALU = mybir.AluOpType
AX = mybir.AxisListType

ENGS = ("sync", "tensor", "vector", "scalar", "gpsimd")
N_DMA_SEMS = 12


class Buf:
    __slots__ = ("name", "w", "r", "psum")

    def __init__(self, name, psum=False):
        self.name = name
        self.psum = psum
        self.w = None
        self.r = {}


class Prog:
    def __init__(self, nc):
        self.nc = nc
        self.es = ExitStack()
        self.streams = {e: [] for e in ENGS}
        self.sem = {}
        self.cnt = {}
        for e in ENGS:
            self.sem[e] = nc.alloc_semaphore("c_" + e)
            self.cnt[e] = 0
        self.dsem = {}
        self.dcnt = {}
        self.drr = {}
        for e in ("sync", "gpsimd", "scalar"):
            for i in range(N_DMA_SEMS):
                k = "d_%s_%d" % (e, i)
                self.sem[k] = nc.alloc_semaphore(k)
                self.cnt[k] = 0
            self.drr[e] = 0
        self.sem["cc"] = nc.alloc_semaphore("cc_sem")
        self.cnt["cc"] = 0
        self.seen = {e: {} for e in ENGS}
        self.nbuf = 0
        self.block_hooks = []
        self.scopes = []
        self.scope_ctr = 0
        self.scope_id = 0

    def sbuf(self, name, shape, dtype):
        t = self.es.enter_context(self.nc.sbuf_tensor("sb%d_%s" % (self.scope_id, name), list(shape), dtype))
        return t

    def psum(self, name, shape, dtype=F32):
        t = self.es.enter_context(self.nc.psum_tensor("pp%d_%s" % (self.scope_id, name), list(shape), dtype))
        return t

    def push_scope(self):
        self.scopes.append(self.es)
        self.es = ExitStack()
        self.scope_ctr += 1
        self.scope_id = self.scope_ctr

    def pop_scope(self):
        self.barrier()
        self.es.close()
        self.es = self.scopes.pop()

    def buf(self, name=None, psum=False):
        self.nbuf += 1
        return Buf(name or ("b%d" % self.nbuf), psum)

    def pbuf(self):
        return self.buf(psum=True)

    def _need(self, eng, deps):
        out = []
        seen = self.seen[eng]
        for k, v in deps.items():
            if seen.get(k, 0) < v:
                seen[k] = v
                out.append((k, v))
        return out

    def op(self, eng, fn, reads=(), writes=(), dma=False, pe_accum=False, cc=False):
        deps = {}

        def add(tok):
            if tok is None:
                return
            k, v = tok
            if deps.get(k, 0) < v:
                deps[k] = v

        for b in reads:
            add(b.w)
            if b.psum:
                for k, v in b.r.items():
                    if k != eng:
                        add((k, v))
        for b in writes:
            if not (pe_accum and b.w is not None and b.w[0] == "tensor"):
                add(b.w)
            for k, v in b.r.items():
                add((k, v))
        if dma:
            rr = self.drr[eng]
            self.drr[eng] = (rr + 1) % N_DMA_SEMS
            dk = "d_%s_%d" % (eng, rr)
            if self.cnt[dk] > 0:
                add((dk, self.cnt[dk]))
        if eng == "tensor":
            deps.pop("tensor", None)
        waits = self._need(eng, deps)
        if cc:
            self.cnt["cc"] += 1
            tok = ("cc", self.cnt["cc"])
            inc = 1
        elif dma:
            self.cnt[dk] += 16
            tok = (dk, self.cnt[dk])
            inc = 16
        else:
            self.cnt[eng] += 1
            tok = (eng, self.cnt[eng])
            inc = 1
        semh = self.sem[tok[0]]
        wl = [(self.sem[k], v) for k, v in waits]

        def emit(e, fn=fn, wl=wl, semh=semh, inc=inc):
            for s, v in wl:
                e.wait_ge(s, v)
            fn(e).then_inc(semh, inc)

        self.streams[eng].append(emit)
        for b in writes:
            b.w = tok
            b.r = {}
        for b in reads:
            if b.r.get(tok[0], 0) < tok[1]:
                b.r[tok[0]] = tok[1]
        return tok

    def wait_all(self, eng, bufs):
        deps = {}
        for b in bufs:
            if b.w is not None:
                k, v = b.w
                if deps.get(k, 0) < v:
                    deps[k] = v
        wl = [(self.sem[k], v) for k, v in self._need(eng, deps)]

        def emit(e, wl=wl):
            for s, v in wl:
                e.wait_ge(s, v)

        self.streams[eng].append(emit)

    def barrier(self):
        allc = {k: v for k, v in self.cnt.items() if v > 0}
        for eng in ENGS:
            wl = [(self.sem[k], v) for k, v in self._need(eng, dict(allc))]

            def emit(e, wl=wl):
                for s, v in wl:
                    e.wait_ge(s, v)

            self.streams[eng].append(emit)

    def block_break(self):
        self.barrier()
        for e in ENGS:
            self.streams[e].append(None)

    def finish(self):
        nc = self.nc
        segs = {e: [[]] for e in ENGS}
        for e in ENGS:
            for f in self.streams[e]:
                if f is None:
                    segs[e].append([])
                else:
                    segs[e][-1].append(f)
        nseg = len(segs["sync"])
        for i in range(nseg):
            for hook in self.block_hooks:
                hook()
            with nc.Block() as block:
                @block.sync
                def _(e):
                    for f in segs["sync"][i]:
                        f(e)

                @block.tensor
                def _(e):
                    for f in segs["tensor"][i]:
                        f(e)

                @block.vector
                def _(e):
                    for f in segs["vector"][i]:
                        f(e)

                @block.scalar
                def _(e):
                    for f in segs["scalar"][i]:
                        f(e)

                @block.gpsimd
                def _(e):
                    for f in segs["gpsimd"][i]:
                        f(e)
        self.es.close()


class Dram:
    def __init__(self, nc, P):
        self.nc = nc
        self.P = P
        self.ap = {}
        self.b = {}
        self.outs = []
        self.dump = ()
        self.arena = {}
        self.ARENA_ELEMS = {"bf16": 120 * 1024 * 1024, "f32": 60 * 1024 * 1024}

    def din(self, name, shape, dt):
        if name not in self.ap:
            self.ap[name] = self.nc.dram_tensor(name, list(shape), dt, kind="ExternalInput").ap()
        return self.ap[name]

    def dout(self, name, shape, dt):
        self.ap[name] = self.nc.dram_tensor(name, list(shape), dt, kind="ExternalOutput").ap()
        self.b[name] = self.P.buf(name)
        self.outs.append(self.b[name])
        return self.ap[name]

    def dscr(self, name, shape, dt):
        if name not in self.ap:
            if name in self.dump:
                self.ap[name] = self.nc.dram_tensor(name, list(shape), dt, kind="ExternalOutput").ap()
            else:
                n = 1
                for d in shape:
                    n *= int(d)
                key = "bf16" if dt == BF16 else "f32"
                if key not in self.arena:
                    cap = self.ARENA_ELEMS[key]
                    self.arena[key] = [self.nc.dram_tensor("arena_" + key, [cap], dt).ap(), 0, cap]
                ar = self.arena[key]
                off = ar[1]
                n_al = (n + 2047) // 2048 * 2048
                assert off + n_al <= ar[2], ("arena overflow", key, name)
                ar[1] = off + n_al
                flat = ar[0][off:off + n]
                if len(shape) == 1:
                    self.ap[name] = flat
                else:
                    names = ["a%d" % i for i in range(len(shape))]
                    pat = "(" + " ".join(names) + ") -> " + " ".join(names)
                    kw = {nm: int(d) for nm, d in zip(names[1:], shape[1:])}
                    self.ap[name] = flat.rearrange(pat, **kw)
            self.b[name] = self.P.buf(name)
        return self.ap[name]

D = 1024
DFF = 2816
NF = DFF // 128
DIN = 2200
NTOK = 4096
TT = 512
NTILE = NTOK // TT
ALPHA = float((2.0 * 2) ** 0.25)
LN_EPS = 1e-5
QSCALE = float(96 ** -0.5)
TCOLS = [256 + 96 * h for h in range(8)] + [1024, 1120, 1216, 1312, 1408, 1504, 1792, 1888]


class TB:
    gus_ctr = 0

    def __init__(self, nc, P, ident, dram):
        self.nc = nc
        self.P = P
        self.dram = dram
        self.ident = ident
        self.b_dram = dram.b
        self.idt = P.sbuf("idt", [128, 128], F32); self.b_idt = P.buf()
        self.xt = [P.sbuf("xt%d" % i, [128, 4, D], F32) for i in range(2)]
        self.b_xt = [[P.buf() for s in range(4)] for i in range(2)]
        self.xT = P.sbuf("xT", [128, 8, TT], BF16); self.b_xT = [P.buf() for c in range(8)]
        self.hT = P.sbuf("hT", [128, NF, TT], BF16); self.b_hT = [P.buf() for f in range(NF)]
        self.wres = P.sbuf("wres", [128, NF * D], BF16); self.b_wres = [P.buf() for f in range(NF)]
        self.ring = P.sbuf("ring", [128, 3, 2, 8, 128], BF16); self.b_ring = [P.buf() for i in range(3)]
        self.stg = [P.sbuf("stg%d" % i, [128, DFF], F32) for i in range(2)]; self.b_stg = [P.buf() for i in range(2)]
        self.stgb = [P.sbuf("stgb%d" % i, [128, DFF], BF16) for i in range(2)]; self.b_stgb = [P.buf() for i in range(2)]
        self.lng = P.sbuf("lng", [128, D], F32); self.b_lng = P.buf()
        self.lnb = P.sbuf("lnb", [128, D], F32); self.b_lnb = P.buf()
        self.sg = [P.sbuf("sg%d" % i, [128, TT], F32) for i in range(2)]; self.b_sg = [P.buf() for i in range(2)]
        self.st = P.sbuf("st", [128, 4, 2, 6], F32); self.b_st = [P.buf() for s in range(4)]
        self.mv = P.sbuf("mv", [128, 4, 4], F32); self.b_mv = [P.buf() for s in range(4)]
        self.oT = P.sbuf("oT", [128, 16, TT], BF16); self.b_oT = [P.buf() for i in range(16)]
        P.op("gpsimd", lambda e: e.memset(self.oT[96:128, :, :], 0.0), writes=self.b_oT)
        self.ou = P.sbuf("ou", [128, 2, TT], F32); self.b_ou = [P.buf() for i in range(2)]
        self.ov = P.sbuf("ov", [128, 4, 384], BF16); self.b_ov = [P.buf() for i in range(4)]
        self.og = P.sbuf("og", [128, 4, 24], F32); self.b_og = [P.buf() for i in range(4)]
        self.bg = P.sbuf("bg", [128, 24], F32); self.b_bg = P.buf()
        self.ps = [P.psum("ps%d" % i, [128, 512], F32) for i in range(8)]
        self.b_ps = [P.pbuf() for i in range(8)]
        self.rr = 0
        self.prep_i = 0
        self.prep_cast_eng = None
        P.op("sync", lambda e: e.dma_start(out=self.idt[:], in_=self.ident), writes=[self.b_idt], dma=True)

    def din(self, name, shape, dt):
        return self.dram.din(name, shape, dt)

    def dscr(self, name, shape, dt):
        return self.dram.dscr(name, shape, dt)

    def cast_eng(self):
        self.rr += 1
        return ("vector", "gpsimd", "scalar")[self.rr % 3]

    def cast(self, eng, out, in_, reads, writes):
        if eng == "scalar":
            self.P.op("scalar", lambda e: e.activation(out=out, in_=in_, func=AF.Copy), reads=reads, writes=writes)
        else:
            self.P.op(eng, lambda e: e.tensor_copy(out=out, in_=in_), reads=reads, writes=writes)

    def load_res(self, w, nchunks, width):
        P = self.P
        per = max(1, DFF // width)
        f = 0
        i = 0
        while f < nchunks:
            n = min(per, nchunks - f)
            k = i % 2
            src = w[f * 128:(f + n) * 128, :].rearrange("(n p) d -> p n d", p=128)
            dst = self.stg[k][:, 0:n * width].rearrange("p (n d) -> p n d", d=width)
            P.op("sync", lambda e, dst=dst, src=src: e.dma_start(out=dst, in_=src), writes=[self.b_stg[k]], dma=True)
            self.cast(self.cast_eng(), self.wres[:, f * width:(f + n) * width], self.stg[k][:, 0:n * width],
                      [self.b_stg[k]], [self.b_wres[j] for j in range(f, f + n)])
            f += n
            i += 1

    def prep_gu_iters(self, wg, wu, scr, b_scr):
        P = self.P
        its = []
        for c in range(8):
            for gi, w in ((0, wg), (1, wu)):
                def it(c=c, gi=gi, w=w):
                    k = self.prep_i % 2
                    self.prep_i += 1
                    P.op("sync", lambda e, k=k, w=w, c=c: e.dma_start(out=self.stg[k][:], in_=w[c * 128:(c + 1) * 128, :]),
                         writes=[self.b_stg[k]], dma=True)
                    self.cast(self.prep_cast_eng or self.cast_eng(), self.stgb[k][:], self.stg[k][:], [self.b_stg[k]], [self.b_stgb[k]])
                    dst = scr[:, :, gi, c, :].rearrange("f p j -> p f j")
                    src = self.stgb[k][:].rearrange("p (f j) -> p f j", j=128)
                    P.op("gpsimd", lambda e, dst=dst, src=src: e.dma_start(out=dst, in_=src), reads=[self.b_stgb[k]],
                         writes=[b_scr], dma=True)
                its.append(it)
        return its

    def prep_gu(self, wg, wu, scr, b_scr):
        for it in self.prep_gu_iters(wg, wu, scr, b_scr):
            it()

    def prep_for(self, pfx):
        wg = self.din(pfx + "_wg", [D, DFF], F32)
        wu = self.din(pfx + "_wu", [D, DFF], F32)
        nm = "gus%d" % (TB.gus_ctr % 2)
        TB.gus_ctr += 1
        scr = self.dscr(nm, [NF, 128, 2, 8, 128], BF16)
        b_scr = self.b_dram[nm]
        return scr, b_scr, self.prep_gu_iters(wg, wu, scr, b_scr)

    def load_ln(self, g, b):
        P = self.P
        P.op("sync", lambda e: e.dma_start(out=self.lng[:], in_=g.partition_broadcast(128)), writes=[self.b_lng], dma=True)
        P.op("sync", lambda e: e.dma_start(out=self.lnb[:], in_=b.partition_broadcast(128)), writes=[self.b_lnb], dma=True)

    def load_x(self, src, b_src, t, k):
        P = self.P
        s_ap = src[t * TT:(t + 1) * TT, :].rearrange("(s p) d -> p s d", p=128)
        P.op("sync", lambda e: e.dma_start(out=self.xt[k][:], in_=s_ap), reads=[b_src] if b_src else [], writes=self.b_xt[k], dma=True)

    def transposes(self, k):
        P = self.P
        for c in range(8):
            pk = c % 2
            for s in range(4):
                P.op("tensor", lambda e, c=c, s=s, pk=pk: e.transpose(out=self.ps[pk][:, s * 128:(s + 1) * 128],
                                                                      in_=self.xt[k][:, s, c * 128:(c + 1) * 128], identity=self.idt[:]),
                     reads=[self.b_xt[k][s], self.b_idt], writes=[self.b_ps[pk]], pe_accum=True)
            self.cast("scalar" if c % 2 else "vector", self.xT[:, c, :], self.ps[pk][:], [self.b_ps[pk]], [self.b_xT[c]])

    def gate_up(self, scr, b_scr, t):
        P = self.P
        for f in range(NF):
            slot = (t * NF + f) % 3
            P.op("gpsimd" if f % 2 else "sync", lambda e, f=f, slot=slot: e.dma_start(out=self.ring[:, slot], in_=scr[f]),
                 reads=[b_scr], writes=[self.b_ring[slot]], dma=True)
            pg = 2 + (f % 2) * 2
            pu = pg + 1
            for gi, pp in ((0, pg), (1, pu)):
                for c in range(8):
                    P.op("tensor", lambda e, c=c, gi=gi, pp=pp, slot=slot: e.matmul(self.ps[pp][:], lhsT=self.ring[:, slot, gi, c, :], rhs=self.xT[:, c, :],
                                                                                 start=(c == 0), stop=(c == 7)),
                         reads=[self.b_ring[slot], self.b_xT[c]], writes=[self.b_ps[pp]], pe_accum=True)
            k = f % 2
            P.op("scalar", lambda e, k=k, pg=pg: e.activation(out=self.sg[k][:], in_=self.ps[pg][:], func=AF.Silu),
                 reads=[self.b_ps[pg]], writes=[self.b_sg[k]])
            P.op("vector", lambda e, k=k, pu=pu, f=f: e.tensor_tensor(out=self.hT[:, f, :], in0=self.sg[k][:], in1=self.ps[pu][:], op=ALU.mult),
                 reads=[self.b_sg[k], self.b_ps[pu]], writes=[self.b_hT[f]])

    def down_ln(self, k, lhs, b_lhs, nch, width_w, ysc, dst, b_dst, t):
        P = self.P
        for s in range(4):
            P.op("gpsimd", lambda e, s=s: e.tensor_scalar(out=self.xt[k][:, s, :], in0=self.xt[k][:, s, :], scalar1=ALPHA, scalar2=None, op0=ALU.mult),
                 reads=[self.b_xt[k][s]], writes=[self.b_xt[k][s]])
        for s in range(4):
            for n in range(2):
                pp = 6 + (s * 2 + n) % 2
                for f in range(nch):
                    P.op("tensor", lambda e, s=s, n=n, f=f, pp=pp: e.matmul(self.ps[pp][:], lhsT=lhs[:, f, s * 128:(s + 1) * 128],
                                                                          rhs=self.wres[:, f * width_w + n * 512: f * width_w + (n + 1) * 512],
                                                                          start=(f == 0), stop=(f == nch - 1)),
                         reads=[b_lhs[f], self.b_wres[f]], writes=[self.b_ps[pp]], pe_accum=True)
                P.op("vector", lambda e, s=s, n=n, pp=pp: e.scalar_tensor_tensor(out=self.xt[k][:, s, n * 512:(n + 1) * 512], in0=self.ps[pp][:], scalar=ysc,
                                                                                in1=self.xt[k][:, s, n * 512:(n + 1) * 512], op0=ALU.mult, op1=ALU.add),
                     reads=[self.b_ps[pp], self.b_xt[k][s]], writes=[self.b_xt[k][s]])
            self.layernorm(k, s)
        d_ap = dst[t * TT:(t + 1) * TT, :].rearrange("(s p) d -> p s d", p=128)
        P.op("sync", lambda e: e.dma_start(out=d_ap, in_=self.xt[k][:]), reads=self.b_xt[k], writes=[b_dst], dma=True)

    def layernorm(self, k, s):
        P = self.P
        z = self.xt[k]
        for h in range(2):
            P.op("vector", lambda e, h=h: e.bn_stats(self.st[:, s, h, :], z[:, s, h * 512:(h + 1) * 512]),
                 reads=[self.b_xt[k][s]], writes=[self.b_st[s]])
        P.op("vector", lambda e: e.bn_aggr(self.mv[:, s, 0:2], self.st[:, s, :, :]), reads=[self.b_st[s]], writes=[self.b_mv[s]])
        P.op("vector", lambda e: e.tensor_scalar(out=self.mv[:, s, 2:3], in0=self.mv[:, s, 1:2], scalar1=LN_EPS, scalar2=None, op0=ALU.add),
             reads=[self.b_mv[s]], writes=[self.b_mv[s]])
        P.op("scalar", lambda e: e.sqrt(self.mv[:, s, 3:4], self.mv[:, s, 2:3]), reads=[self.b_mv[s]], writes=[self.b_mv[s]])
        P.op("vector", lambda e: e.reciprocal(self.mv[:, s, 2:3], self.mv[:, s, 3:4]), reads=[self.b_mv[s]], writes=[self.b_mv[s]])
        P.op("vector", lambda e: e.tensor_scalar(out=z[:, s, :], in0=z[:, s, :], scalar1=self.mv[:, s, 0:1], scalar2=self.mv[:, s, 2:3],
                                                 op0=ALU.subtract, op1=ALU.mult),
             reads=[self.b_xt[k][s], self.b_mv[s]], writes=[self.b_xt[k][s]])
        P.op("gpsimd", lambda e: e.tensor_tensor(out=z[:, s, :], in0=z[:, s, :], in1=self.lng[:], op=ALU.mult),
             reads=[self.b_xt[k][s], self.b_lng], writes=[self.b_xt[k][s]])
        P.op("gpsimd", lambda e: e.tensor_tensor(out=z[:, s, :], in0=z[:, s, :], in1=self.lnb[:], op=ALU.add),
             reads=[self.b_xt[k][s], self.b_lnb], writes=[self.b_xt[k][s]])

    def stage_ffn(self, pfx, src, b_src, dst, b_dst, tcount, prepped=None, tile_hook=None):
        wd = self.din(pfx + "_wd", [DFF, D], F32)
        g = self.din(pfx + "_lg", [D], F32)
        b = self.din(pfx + "_lb", [D], F32)
        if prepped is None:
            scr, b_scr, its = self.prep_for(pfx)
            for it in its:
                it()
        else:
            scr, b_scr = prepped
        self.load_ln(g, b)
        self.load_res(wd, NF, D)
        for t in range(NTILE):
            k = (tcount + t) % 2
            self.load_x(src, b_src, t, k)
            self.transposes(k)
            self.gate_up(scr, b_scr, t)
            self.down_ln(k, self.hT, self.b_hT, NF, D, 0.5, dst, b_dst, t)
            if tile_hook is not None:
                tile_hook(t)
        return tcount + NTILE

    def stage_mix(self, pfx, ym, b_ym, src, b_src, dst, b_dst, tcount, tile_hook=None):
        wo = self.din(pfx + "_w_out", [D, D], F32)
        g = self.din(pfx + "_ln2_g", [D], F32)
        b = self.din(pfx + "_ln2_b", [D], F32)
        self.load_ln(g, b)
        self.load_res(wo, 8, D)
        for t in range(NTILE):
            k = (tcount + t) % 2
            self.load_x(src, b_src, t, k)
            y_ap = ym[:, t * TT:(t + 1) * TT].rearrange("(c p) t -> p c t", p=128)
            self.P.op("gpsimd", lambda e, y_ap=y_ap: e.dma_start(out=self.hT[:, 0:8, :], in_=y_ap), reads=[b_ym], writes=self.b_hT[0:8], dma=True)
            self.down_ln(k, self.hT, self.b_hT, 8, D, 1.0, dst, b_dst, t)
            if tile_hook is not None:
                tile_hook(t)
        return tcount + NTILE

    def stage_proj(self, pfx, src, b_src, outs, tcount):
        P = self.P
        win = self.din(pfx + "_w_in", [D, DIN], F32)
        bgate = self.din(pfx + "_b_gate", [24], F32)
        qT, kT, uT, vs, vw, gates = (outs[k][0] for k in ("qT", "kT", "uT", "vs", "vw", "gates"))
        bd = {k: outs[k][1] for k in ("qT", "kT", "uT", "vs", "vw", "gates")}
        P.op("sync", lambda e: e.dma_start(out=self.bg[:], in_=bgate.partition_broadcast(128)), writes=[self.b_bg], dma=True)
        for c in range(8):
            k = c % 2
            P.op("sync", lambda e, k=k, c=c: e.dma_start(out=self.stg[k][:, 0:DIN], in_=win[c * 128:(c + 1) * 128, :]), writes=[self.b_stg[k]], dma=True)
            self.cast(self.cast_eng(), self.wres[:, c * DIN:(c + 1) * DIN], self.stg[k][:, 0:DIN], [self.b_stg[k]], [self.b_wres[c]])
        W = lambda c, a, b_: self.wres[:, c * DIN + a: c * DIN + b_]
        for t in range(NTILE):
            k = (tcount + t) % 2
            self.load_x(src, b_src, t, k)
            self.transposes(k)
            tok = slice(t * TT, (t + 1) * TT)
            for i, col in enumerate(TCOLS):
                pp = 2 + i % 4
                for c in range(8):
                    P.op("tensor", lambda e, c=c, col=col, pp=pp: e.matmul(self.ps[pp][:, :], lhsT=W(c, col, col + 128), rhs=self.xT[:, c, :],
                                                                         start=(c == 0), stop=(c == 7)),
                         reads=[self.b_wres[c], self.b_xT[c]], writes=[self.b_ps[pp]], pe_accum=True)
                if i < 8:
                    P.op("scalar", lambda e, i=i, pp=pp: e.mul(self.oT[0:96, i, :], self.ps[pp][0:96, :], QSCALE),
                         reads=[self.b_ps[pp]], writes=[self.b_oT[i]])
                else:
                    P.op("vector", lambda e, i=i, pp=pp: e.tensor_copy(out=self.oT[0:96, i, :], in_=self.ps[pp][0:96, :]),
                         reads=[self.b_ps[pp]], writes=[self.b_oT[i]])
            P.op("sync", lambda e, tok=tok: e.dma_start(out=qT[:, :, tok].rearrange("h d t -> d h t"), in_=self.oT[:, 0:8, :]),
                 reads=self.b_oT[0:8], writes=[bd["qT"]], dma=True)
            P.op("sync", lambda e, tok=tok: e.dma_start(out=kT[:, :, tok].rearrange("h d t -> d h t"), in_=self.oT[0:96, 8:16, :]),
                 reads=self.b_oT[8:16], writes=[bd["kT"]], dma=True)
            for j in range(2):
                pp = 6 + j
                for c in range(8):
                    P.op("tensor", lambda e, c=c, j=j, pp=pp: e.matmul(self.ps[pp][:], lhsT=W(c, j * 128, (j + 1) * 128), rhs=self.xT[:, c, :],
                                                                     start=(c == 0), stop=(c == 7)),
                         reads=[self.b_wres[c], self.b_xT[c]], writes=[self.b_ps[pp]], pe_accum=True)
                P.op("vector", lambda e, j=j, pp=pp: e.tensor_copy(out=self.ou[:, j, :], in_=self.ps[pp][:]), reads=[self.b_ps[pp]], writes=[self.b_ou[j]])
            for j2 in range(2):
                for half in range(2):
                    P.op("sync", lambda e, tok=tok, j2=j2, half=half: e.dma_start(out=uT[2 * j2 + half][:, tok], in_=self.ou[half * 64:(half + 1) * 64, j2, :]),
                         reads=[self.b_ou[j2]], writes=[bd["uT"]], dma=True)
            for s in range(4):
                pp = 2 + s
                for c in range(8):
                    P.op("tensor", lambda e, c=c, s=s, pp=pp: e.matmul(self.ps[pp][:, 0:192], lhsT=self.xT[:, c, s * 128:(s + 1) * 128], rhs=W(c, 1600, 1792),
                                                                     start=(c == 0), stop=(c == 7)),
                         reads=[self.b_wres[c], self.b_xT[c]], writes=[self.b_ps[pp]], pe_accum=True)
                for c in range(8):
                    P.op("tensor", lambda e, c=c, s=s, pp=pp: e.matmul(self.ps[pp][:, 256:472], lhsT=self.xT[:, c, s * 128:(s + 1) * 128], rhs=W(c, 1984, 2200),
                                                                     start=(c == 0), stop=(c == 7), skip_group_check=True),
                         reads=[self.b_wres[c], self.b_xT[c]], writes=[self.b_ps[pp]], pe_accum=True)
                P.op("vector", lambda e, s=s, pp=pp: e.tensor_copy(out=self.ov[:, s, 0:192], in_=self.ps[pp][:, 0:192]), reads=[self.b_ps[pp]], writes=[self.b_ov[s]])
                P.op("vector", lambda e, s=s, pp=pp: e.tensor_copy(out=self.ov[:, s, 192:384], in_=self.ps[pp][:, 256:448]), reads=[self.b_ps[pp]], writes=[self.b_ov[s]])
                P.op("vector", lambda e, s=s, pp=pp: e.tensor_tensor(out=self.og[:, s, :], in0=self.ps[pp][:, 448:472], in1=self.bg[:], op=ALU.add),
                     reads=[self.b_ps[pp], self.b_bg], writes=[self.b_og[s]])
                P.op("scalar", lambda e, s=s: e.activation(out=self.og[:, s, :], in_=self.og[:, s, :], func=AF.Sigmoid), reads=[self.b_og[s]], writes=[self.b_og[s]])
            for g_ in range(2):
                P.op("gpsimd", lambda e, tok=tok, g_=g_: e.dma_start(out=vs[g_, tok, :].rearrange("(s p) d -> p s d", p=128), in_=self.ov[:, :, 96 * g_:96 * g_ + 96]),
                     reads=self.b_ov, writes=[bd["vs"]], dma=True)
                P.op("gpsimd", lambda e, tok=tok, g_=g_: e.dma_start(out=vw[g_, tok, :].rearrange("(s p) d -> p s d", p=128), in_=self.ov[:, :, 192 + 96 * g_:192 + 96 * g_ + 96]),
                     reads=self.b_ov, writes=[bd["vw"]], dma=True)
            P.op("gpsimd", lambda e, tok=tok: e.dma_start(out=gates[tok, :].rearrange("(s p) d -> p s d", p=128), in_=self.og[:]),
                 reads=self.b_og, writes=[bd["gates"]], dma=True)
        return tcount + NTILE


S = 16384
NQT = S // 512
NEG = -30000.0
VP = 128
GELU_C = float(2.0 * (2.0 / np.pi) ** 0.5)


def b_consts():
    import ml_dtypes
    c = {}
    c["ident_f"] = np.eye(128, dtype=np.float32)
    c["ident_b"] = np.eye(128, dtype=np.float32).astype(ml_dtypes.bfloat16)
    k = np.arange(128)[:, None]
    t = np.arange(512)[None, :]
    masks = np.zeros((16, 128, 512), np.float32)
    for i in range(8):
        kp = 128 * i - 512 + k
        masks[i] = np.where((t - kp >= 0) & (t - kp < 512), 0.0, NEG)
    for r in range(4):
        masks[8 + r] = np.where(16 * k + 31 <= 512 * r + t, 0.0, NEG)
    for i in range(4):
        masks[12 + i] = np.where(128 * i + k <= t, 0.0, NEG)
    c["masks"] = masks.astype(ml_dtypes.bfloat16)
    kk = np.arange(2048)[None, :]
    r = np.arange(32)[:, None]
    c["kpat"] = (((kk // 64) % 32) == r).astype(np.float32).astype(ml_dtypes.bfloat16)
    W = np.zeros((1024, 256), np.float32)
    for j in range(256):
        for n, w in ((4 * j - 1, 0.5), (4 * j, 1.0), (4 * j + 1, 1.0), (4 * j + 2, 1.0), (4 * j + 3, 0.5)):
            if 0 <= n < 1023:
                W[n, j] = w
    c["wimp"] = W.reshape(8, 128, 256).astype(ml_dtypes.bfloat16)
    fp = np.zeros((128, 4), np.float32)
    fp[:64, 0] = 1000.0; fp[:64, 1] = 2000.0
    fp[64:, 1] = 2000.0; fp[64:, 2] = 3000.0
    c["fpat"] = fp
    return c


def pool_consts(j):
    w = 2 ** (j + 1)
    a = np.zeros((64, 4), np.float32)
    a[:, j] = 1.0 / w
    A = np.zeros((64, 4, 16), np.float32)
    tt = np.arange(16)
    A[:, j, :] = 1.0 / np.minimum(tt + 1, w)
    return a, A


def _V(x, e):
    return x(e) if callable(x) else x


class BB:
    def __init__(self, nc, P, dram):
        self.nc = nc
        self.P = P
        self.dram = dram

    def din(self, name, shape, dt):
        return self.dram.din(name, shape, dt)

    def build(self, pfx, src, ymc, b_ymc):
        P = self.P
        nc = self.nc
        op = P.op
        qT, b_qT = src["qT"]; gates, b_gates = src["gates"]
        kcT, b_kcT = src["kcT"]; vcT, b_vcT = src["vcT"]; ksT, b_ksT = src["ksT"]; kwT, b_kwT = src["kwT"]
        vs, b_vs = src["vs"]; vw, b_vw = src["vw"]; uT, b_uT = src["uT"]
        cw = {}
        for kv in ("k", "v"):
            cw[kv + "w1"] = self.din(pfx + "_cmp_%s_w1" % kv, [3072, 96], F32)
            cw[kv + "w2"] = self.din(pfx + "_cmp_%s_w2" % kv, [96, 96], F32)
            cw[kv + "pos"] = self.din(pfx + "_cmp_%s_pos" % kv, [32, 96], F32)
        pool_w = self.din(pfx + "_pool_w", [64, 64], F32)
        pool_sc = self.din(pfx + "_pool_sc", [64, 1], F32)
        pool_a = self.din("pool_a", [64, 4], F32)
        pool_A = self.din("pool_A", [64, 4, 16], F32)
        d_identf = self.din("ident", [128, 128], F32)
        d_identb = self.din("ident_b", [128, 128], BF16)
        d_masks = self.din("masks", [16, 128, 512], BF16)
        d_kpat = self.din("kpat", [32, 2048], BF16)
        d_wimp = self.din("wimp", [8, 128, 256], BF16)
        d_fpat = self.din("fpat", [128, 4], F32)
        d_gsel = self.din("gsel", [6, 24], F32)
        def ysl(r0, r1, t0, n):
            return ymc[t0 // 4096, r0:r1, (t0 % 4096):(t0 % 4096) + n]

        ps_s = [P.psum("ps_s%d" % i, [128, 512]) for i in range(2)]; b_ps_s = [P.pbuf() for i in range(2)]
        ps_o = [P.psum("ps_o%d" % i, [128, 512]) for i in range(2)]; b_ps_o = [P.pbuf() for i in range(2)]
        ps_i = [P.psum("ps_i%d" % i, [128, 512]) for i in range(2)]; b_ps_i = [P.pbuf() for i in range(2)]
        ps_t = [P.psum("ps_t%d" % i, [128, 512]) for i in range(2)]; b_ps_t = [P.pbuf() for i in range(2)]

        idf = P.sbuf("idf", [128, 128], F32); b_idf = P.buf()
        idb = P.sbuf("idb", [128, 128], BF16); b_idb = P.buf()
        masks = P.sbuf("masks", [128, 16, 512], BF16); b_masks = P.buf()
        wimp = P.sbuf("wimp", [128, 8, 256], BF16); b_wimp = P.buf()
        fpat = P.sbuf("fpat", [128, 4], F32); b_fpat = P.buf()
        ksa = P.sbuf("ksa", [128, S], BF16); b_ksa = P.buf()
        vsa = P.sbuf("vsa", [128, 128, VP], BF16); b_vsa = P.buf()
        kcmpT = P.sbuf("kcmpT", [96, 1024], BF16); b_kcmpT = P.buf()
        vcmpa = P.sbuf("vcmpa", [128, 8, VP], BF16); b_vcmpa = P.buf()
        op("sync", lambda e: e.dma_start(out=idf[:], in_=d_identf), writes=[b_idf], dma=True)
        op("sync", lambda e: e.dma_start(out=idb[:], in_=d_identb), writes=[b_idb], dma=True)
        op("sync", lambda e: e.dma_start(out=masks[:], in_=d_masks.rearrange("m p t -> p m t")), writes=[b_masks], dma=True)
        op("sync", lambda e: e.dma_start(out=wimp[:], in_=d_wimp.rearrange("c p j -> p c j")), writes=[b_wimp], dma=True)
        op("sync", lambda e: e.dma_start(out=fpat[:], in_=d_fpat), writes=[b_fpat], dma=True)
        gsel = P.sbuf("gsel", [128, 6, 24], F32); b_gsel = P.buf()
        op("sync", lambda e: e.dma_start(out=gsel[:].rearrange("p a b -> p (a b)"), in_=d_gsel.rearrange("a b -> (a b)").partition_broadcast(128)), writes=[b_gsel], dma=True)
        op("gpsimd", lambda e: e.memset(kcmpT[:], 0.0), writes=[b_kcmpT])
        op("gpsimd", lambda e: e.memset(vcmpa[:], 1.0), writes=[b_vcmpa])
        op("gpsimd", lambda e: e.memset(vsa[:], 1.0), writes=[b_vsa])

        q_sb = [P.sbuf("q_sb%d" % i, [96, 4, 512], BF16) for i in range(2)]; b_q = [P.buf() for i in range(2)]
        g_sb = [P.sbuf("g_sb%d" % i, [128, 4, 6], F32) for i in range(2)]; b_g = [P.buf() for i in range(2)]
        g_full = P.sbuf("g_full", [128, 4, 24], F32); b_gfull = P.buf()
        g_tmp = P.sbuf("g_tmp", [128, 4, 6, 24], F32); b_gtmp = P.buf()
        kw_sb = [P.sbuf("kw_sb%d" % i, [96, 1024], BF16) for i in range(2)]; b_kw = [P.buf() for i in range(2)]
        vwa = [P.sbuf("vwa%d" % i, [128, 8, VP], BF16) for i in range(2)]; b_vwa = [P.buf() for i in range(2)]
        e_all = [P.sbuf("e_all%d" % i, [128, 8, 512], BF16) for i in range(2)]; b_eall = [P.buf() for i in range(2)]
        e_sb = [P.sbuf("e_sb%d" % i, [128, 512], BF16) for i in range(3)]; b_e = [P.buf() for i in range(3)]
        oTs = [P.sbuf("oTs%d" % i, [97, 512], F32) for i in range(2)]; b_oTs = [P.buf() for i in range(2)]
        impacc = P.sbuf("impacc", [128, 4, 258], F32); b_imp = [P.buf() for i in range(4)]
        mr = P.sbuf("mr", [128, 256], F32); b_mr = P.buf()
        m8 = P.sbuf("m8", [128, 3, 8], F32); b_m8 = P.buf()
        selb = P.sbuf("selb", [128, 4, 352], F32); b_selb = [P.buf() for i in range(4)]
        qaug = P.sbuf("qaug", [128, 2, 8, 512], BF16); b_qaug = [[P.buf() for v in range(8)] for h in range(2)]
        oacc = [P.sbuf("oacc%d" % i, [128, 4, 2, 96], F32) for i in range(2)]; b_oacc = [P.buf() for i in range(2)]
        oTo = [P.sbuf("oTo%d" % i, [96, 2, 512], BF16) for i in range(2)]; b_oTo = [P.buf() for i in range(2)]
        rd = P.sbuf("rd", [128, 8, 4], F32); b_rd = P.buf()
        qa_f = qaug[:].rearrange("p a b c -> p (a b c)").bitcast(F32)
        ea0 = e_all[0][:].rearrange("p a b -> p (a b)")
        ea1_f = e_all[1][:].rearrange("p a b -> p (a b)").bitcast(F32)
        w1s = qa_f[0:96, 0:3072].rearrange("d (p e) -> d p e", e=96); b_w1s = P.buf()
        gx = qa_f[0:96, 3072:4096]; b_gx = P.buf()
        w1b = ea0[0:96, 0:3072].rearrange("d (p e) -> d p e", e=96); b_w1b = P.buf()
        gb = ea0[0:96, 3072:4096]; b_gb = P.buf()
        gy = ea1_f[0:96, 0:1024]; b_gy = P.buf()
        w2s = P.sbuf("w2s", [96, 96], F32); b_w2s = P.buf()
        w2b = P.sbuf("w2b", [96, 96], BF16); b_w2b = P.buf()
        poss = P.sbuf("poss", [32, 96], F32); b_poss = P.buf()
        posT = P.sbuf("posT", [96, 32], BF16); b_posT = P.buf()
        cb = P.sbuf("cb", [96, 1], F32); b_cb = P.buf()
        stage = getattr(self, 'stage', 99)
        for i in range(8):
            op("gpsimd", lambda e, i=i: e.dma_start(out=ksa[96:128, i * 2048:(i + 1) * 2048], in_=d_kpat), writes=[b_ksa], dma=True)
        for i in range(16 if stage >= 2 else 0):
            op("gpsimd", lambda e, i=i: e.dma_start(out=vsa[:, i * 8:(i + 1) * 8, 0:96],
                                                                        in_=_V(vs, e)[i * 1024:(i + 1) * 1024, :].rearrange("(k p) d -> p k d", p=128)),
               reads=[b_vs], writes=[b_vsa], dma=True)

        for kv, srcT, b_srcT in ((("k", kcT, b_kcT), ("v", vcT, b_vcT)) if stage >= 1 else ()):
            op("sync", lambda e, srcT=srcT: e.dma_start(out=ksa[0:96, :], in_=srcT), reads=[b_srcT], writes=[b_ksa], dma=True)
            op("sync", lambda e, kv=kv: e.dma_start(out=w1s, in_=cw[kv + "w1"].rearrange("(p d) e -> d p e", d=96)), writes=[b_w1s], dma=True)
            op("sync", lambda e, kv=kv: e.dma_start(out=w2s[:], in_=cw[kv + "w2"]), writes=[b_w2s], dma=True)
            op("sync", lambda e, kv=kv: e.dma_start(out=poss[:], in_=cw[kv + "pos"]), writes=[b_poss], dma=True)
            op("vector", lambda e: e.tensor_copy(out=w1b, in_=w1s), reads=[b_w1s], writes=[b_w1b])
            op("vector", lambda e: e.tensor_copy(out=w2b[:], in_=w2s[:]), reads=[b_w2s], writes=[b_w2b])
            op("tensor", lambda e: e.transpose(out=ps_t[0][0:96, 0:32], in_=poss[:], identity=idf[0:32, 0:32]), reads=[b_poss, b_idf], writes=[b_ps_t[0]], pe_accum=True)
            op("vector", lambda e: e.tensor_copy(out=posT[:], in_=ps_t[0][0:96, 0:32]), reads=[b_ps_t[0]], writes=[b_posT])
            for p in range(32):
                op("tensor", lambda e, p=p: e.matmul(ps_t[1][0:96, 0:1], lhsT=w1b[:, p, :], rhs=posT[:, p:p + 1], start=(p == 0), stop=(p == 31)),
                   reads=[b_w1b, b_posT], writes=[b_ps_t[1]], pe_accum=True)
            op("vector", lambda e: e.tensor_copy(out=cb[:], in_=ps_t[1][0:96, 0:1]), reads=[b_ps_t[1]], writes=[b_cb])
            if stage < 1.2:
                continue
            op("gpsimd", lambda e: e.memset(gb, 0.0), writes=[b_gb])
            kview = ksa[0:96, :].rearrange("d (n s) -> d n s", s=16)
            for half, (n0, N) in enumerate(((0, 512), (512, 511))):
                pp = ps_s[half]
                for p in range(32):
                    rhs = kview[:, n0:n0 + N, p] if p < 16 else kview[:, n0 + 1:n0 + 1 + N, p - 16]
                    op("tensor", lambda e, p=p, rhs=rhs, pp=pp, N=N: e.matmul(pp[0:96, 0:N], lhsT=w1b[:, p, :], rhs=rhs, start=(p == 0), stop=(p == 31)),
                       reads=[b_w1b, b_ksa], writes=[b_ps_s[half]], pe_accum=True)
                sl = slice(n0, n0 + N)
                op("vector", lambda e, pp=pp, N=N, sl=sl: e.tensor_scalar(out=gx[:, sl], in0=pp[0:96, 0:N], scalar1=cb[:, 0:1], scalar2=None, op0=ALU.add),
                   reads=[b_ps_s[half], b_cb], writes=[b_gx])
                op("vector", lambda e, sl=sl: e.tensor_tensor(out=gy[:, sl], in0=gx[:, sl], in1=gx[:, sl], op=ALU.mult), reads=[b_gx], writes=[b_gy])
                op("vector", lambda e, sl=sl: e.tensor_scalar(out=gy[:, sl], in0=gy[:, sl], scalar1=0.044715, scalar2=1.0, op0=ALU.mult, op1=ALU.add), reads=[b_gy], writes=[b_gy])
                op("vector", lambda e, sl=sl: e.tensor_tensor(out=gy[:, sl], in0=gy[:, sl], in1=gx[:, sl], op=ALU.mult), reads=[b_gy, b_gx], writes=[b_gy])
                op("scalar", lambda e, sl=sl: e.activation(out=gy[:, sl], in_=gy[:, sl], func=AF.Sigmoid, scale=GELU_C), reads=[b_gy], writes=[b_gy])
                op("vector", lambda e, sl=sl: e.tensor_tensor(out=gb[:, sl], in0=gy[:, sl], in1=gx[:, sl], op=ALU.mult), reads=[b_gy, b_gx], writes=[b_gb])
            if stage < 1.3:
                continue
            if kv == "k":
                for half, (n0, N) in enumerate(((0, 512), (512, 511))):
                    op("tensor", lambda e, half=half, n0=n0, N=N: e.matmul(ps_o[half][0:96, 0:N], lhsT=w2b[:], rhs=gb[:, n0:n0 + N], start=True, stop=True),
                       reads=[b_w2b, b_gb], writes=[b_ps_o[half]], pe_accum=True)
                    op("vector", lambda e, half=half, n0=n0, N=N: e.tensor_copy(out=kcmpT[:, n0:n0 + N], in_=ps_o[half][0:96, 0:N]), reads=[b_ps_o[half]], writes=[b_kcmpT])
            else:
                for c in range(8):
                    k2 = c % 2
                    op("tensor", lambda e, c=c, k2=k2: e.matmul(ps_o[k2][:, 0:96], lhsT=gb[:, c * 128:(c + 1) * 128], rhs=w2b[:], start=True, stop=True),
                       reads=[b_w2b, b_gb], writes=[b_ps_o[k2]], pe_accum=True)
                    op("vector", lambda e, c=c, k2=k2: e.tensor_copy(out=vcmpa[:, c, 0:96], in_=ps_o[k2][:, 0:96]), reads=[b_ps_o[k2]], writes=[b_vcmpa])

        stage = getattr(self, 'stage', 99)
        op("sync", lambda e: e.dma_start(out=ksa[0:96, :], in_=ksT), reads=[b_ksT], writes=[b_ksa], dma=True)
        CH = 512
        pw_s = P.sbuf("pw_s", [64, 64], F32); b_pw_s = P.buf()
        pw_b = P.sbuf("pw_b", [64, 64], BF16); b_pw_b = P.buf()
        psc = P.sbuf("psc", [64, 1], F32); b_psc = P.buf()
        pa = P.sbuf("pa", [64, 4], F32); b_pa = P.buf()
        pA = P.sbuf("pA", [64, 4, 16], F32); b_pA = P.buf()
        L_ = 16 + CH
        ub = [P.sbuf("ub%d" % i, [64, L_], F32)[:] for i in range(2)]; b_ub = [P.buf() for i in range(2)]
        sw = [P.sbuf("sw%d" % i, [64, L_], F32)[:] for i in range(4)]; b_sw = [P.buf() for i in range(4)]
        acc = P.sbuf("pacc", [64, CH], F32)[:]; b_acc = P.buf()
        a16 = P.sbuf("a16", [64, 2, 16], F32)[:]; b_a16 = P.buf()
        accb = P.sbuf("paccb", [64, CH], BF16)[:]; b_accb = P.buf()
        pout = [P.sbuf("pout%d" % i, [64, CH], BF16)[:] for i in range(2)]; b_pout = [P.buf() for i in range(2)]
        op("sync", lambda e: e.dma_start(out=pw_s[:], in_=pool_w), writes=[b_pw_s], dma=True)
        op("sync", lambda e: e.dma_start(out=psc[:], in_=pool_sc), writes=[b_psc], dma=True)
        op("sync", lambda e: e.dma_start(out=pa[:], in_=pool_a), writes=[b_pa], dma=True)
        op("sync", lambda e: e.dma_start(out=pA[:], in_=pool_A), writes=[b_pA], dma=True)
        op("vector", lambda e: e.tensor_copy(out=pw_b[:], in_=pw_s[:]), reads=[b_pw_s], writes=[b_pw_b])
        def pool_chunk(ci):
            k = ci % 2
            u = ub[k]
            if ci == 0:
                op("gpsimd", lambda e, u=u: e.memset(u[:, 0:16], 0.0), writes=[b_ub[k]])
                op("sync", lambda e, u=u: e.dma_start(out=u[:, 16:], in_=uT[:, 0:CH]), reads=[b_uT], writes=[b_ub[k]], dma=True)
            else:
                op("sync", lambda e, u=u, ci=ci: e.dma_start(out=u, in_=uT[:, ci * CH - 16:(ci + 1) * CH]), reads=[b_uT], writes=[b_ub[k]], dma=True)
            L = 16 + CH
            prev, b_prev = u, b_ub[k]
            for wi, sh in enumerate((1, 2, 4, 8)):
                lo = 2 * sh - 1
                dst = sw[wi]
                op("gpsimd", lambda e, dst=dst, prev=prev, lo=lo, sh=sh, L=L: e.tensor_tensor(out=dst[:, lo:L], in0=prev[:, lo:L], in1=prev[:, lo - sh:L - sh], op=ALU.add),
                   reads=[b_prev], writes=[b_sw[wi]])
                prev, b_prev = dst, b_sw[wi]
            op("vector", lambda e: e.tensor_scalar(out=acc, in0=sw[0][:, 16:], scalar1=pa[:, 0:1], scalar2=None, op0=ALU.mult), reads=[b_sw[0], b_pa], writes=[b_acc])
            for wi in range(1, 4):
                op("vector", lambda e, wi=wi: e.scalar_tensor_tensor(out=acc, in0=sw[wi][:, 16:], scalar=pa[:, wi:wi + 1], in1=acc, op0=ALU.mult, op1=ALU.add),
                   reads=[b_sw[wi], b_pa, b_acc], writes=[b_acc])
            if ci == 0:
                op("vector", lambda e: e.tensor_tensor(out=a16[:, 0, :], in0=sw[0][:, 16:32], in1=pA[:, 0, :], op=ALU.mult), reads=[b_sw[0], b_pA], writes=[b_a16])
                for wi in range(1, 4):
                    op("vector", lambda e, wi=wi: e.tensor_tensor(out=a16[:, 1, :], in0=sw[wi][:, 16:32], in1=pA[:, wi, :], op=ALU.mult), reads=[b_sw[wi], b_pA, b_a16], writes=[b_a16])
                    op("vector", lambda e: e.tensor_tensor(out=a16[:, 0, :], in0=a16[:, 0, :], in1=a16[:, 1, :], op=ALU.add), reads=[b_a16], writes=[b_a16])
                op("vector", lambda e: e.tensor_copy(out=acc[:, 0:16], in_=a16[:, 0, :]), reads=[b_a16, b_acc], writes=[b_acc])
            op("vector", lambda e, u=u: e.tensor_tensor(out=accb, in0=acc, in1=u[:, 16:], op=ALU.subtract), reads=[b_acc, b_ub[k]], writes=[b_accb])
            for hh in range(CH // 512):
                pbi = nxt("t", 2)
                op("tensor", lambda e, hh=hh, pbi=pbi: e.matmul(ps_t[pbi][0:64, :], lhsT=pw_b[:], rhs=accb[:, hh * 512:(hh + 1) * 512], start=True, stop=True),
                   reads=[b_pw_b, b_accb], writes=[b_ps_t[pbi]], pe_accum=True)
                op("vector", lambda e, hh=hh, k=k, pbi=pbi: e.tensor_scalar(out=pout[k][:, hh * 512:(hh + 1) * 512], in0=ps_t[pbi][0:64, :], scalar1=psc[:, 0:1], scalar2=None, op0=ALU.mult),
                   reads=[b_ps_t[pbi], b_psc], writes=[b_pout[k]])
            op("sync", lambda e, k=k, ci=ci: e.dma_start(out=ysl(0, 64, ci * CH, CH), in_=pout[k]), reads=[b_pout[k]], writes=[b_ymc], dma=True)


        P.barrier()

        for i in range(2):
            op("gpsimd", lambda e, i=i: e.memset(vwa[i][:], 1.0), writes=[b_vwa[i]])
        op("gpsimd", lambda e: e.memset(selb[:], 0.0), writes=b_selb)
        op("gpsimd", lambda e: e.memset(impacc[:], 0.0), writes=b_imp)

        cnt = {"s": 0, "o": 0, "e": 0, "t": 0, "rd": 0}

        def nxt(key, n):
            v = cnt[key] % n
            cnt[key] += 1
            return v

        def load_tile(qt):
            k = qt % 2
            tok = slice(qt * 512, (qt + 1) * 512)
            op("sync", lambda e: e.dma_start(out=q_sb[k][:], in_=qT[:, :, tok].rearrange("h d t -> d h t")), reads=[b_qT], writes=[b_q[k]], dma=True)
            op("sync", lambda e: e.dma_start(out=g_full[:], in_=gates[tok, :].rearrange("(s p) c -> p s c", p=128)), reads=[b_gates], writes=[b_gfull], dma=True)
            op("gpsimd", lambda e: e.tensor_tensor(out=g_tmp[:], in0=g_full[:].unsqueeze(2).to_broadcast([128, 4, 6, 24]), in1=gsel[:].unsqueeze(1).to_broadcast([128, 4, 6, 24]), op=ALU.mult),
               reads=[b_gfull, b_gsel], writes=[b_gtmp])
            op("vector", lambda e: e.tensor_reduce(out=g_sb[k][:], in_=g_tmp[:], axis=AX.X, op=ALU.add), reads=[b_gtmp], writes=[b_g[k]])
            if qt == 0:
                op("sync", lambda e: e.dma_start(out=kw_sb[k][:, 512:1024], in_=kwT[:, 0:512]), reads=[b_kwT], writes=[b_kw[k]], dma=True)
                op("sync", lambda e: e.dma_start(out=vwa[k][:, 4:8, 0:96], in_=_V(vw, e)[0:512, :].rearrange("(k p) d -> p k d", p=128)), reads=[b_vw], writes=[b_vwa[k]], dma=True)
            else:
                op("sync", lambda e: e.dma_start(out=kw_sb[k][:], in_=kwT[:, qt * 512 - 512: qt * 512 + 512]), reads=[b_kwT], writes=[b_kw[k]], dma=True)
                op("sync", lambda e: e.dma_start(out=vwa[k][:, :, 0:96], in_=_V(vw, e)[qt * 512 - 512: qt * 512 + 512, :].rearrange("(k p) d -> p k d", p=128)), reads=[b_vw], writes=[b_vwa[k]], dma=True)

        def finish_branch(qt, po, h_own, gcol, first, rd_out=None):
            k = qt % 2
            bi = nxt("t", 2)
            osb = oTs[bi]
            op("vector", lambda e: e.tensor_copy(out=osb[:], in_=ps_o[po][0:97, :]), reads=[b_ps_o[po]], writes=[b_oTs[bi]])
            pt = ps_t[bi]
            for sub in range(4):
                op("tensor", lambda e, sub=sub: e.transpose(out=pt[:, sub * 128: sub * 128 + 97], in_=osb[0:97, sub * 128:(sub + 1) * 128], identity=idf[0:97, 0:97]),
                   reads=[b_oTs[bi], b_idf], writes=[b_ps_t[bi]], pe_accum=True)
            ri = nxt("rd", 8)
            ptv = pt[:].rearrange("p (s c) -> p s c", c=128)
            op("vector", lambda e: e.tensor_scalar(out=rd[:, ri, :], in0=ptv[:, :, 96], scalar1=1e-30, scalar2=None, op0=ALU.max), reads=[b_ps_t[bi]], writes=[b_rd])
            op("vector", lambda e: e.reciprocal(rd[:, ri, :], rd[:, ri, :]), reads=[b_rd], writes=[b_rd])
            if h_own is not None:
                ci = nxt("rd", 8)
                op("vector", lambda e: e.tensor_tensor(out=rd[:, ci, :], in0=rd[:, ri, :], in1=g_sb[k][:, :, h_own * 3 + gcol], op=ALU.mult), reads=[b_rd, b_g[k]], writes=[b_rd])
                dbg = getattr(self, "dbg_branch", None)
                if dbg is not None and dbg != gcol:
                    op("vector", lambda e: e.memset(rd[:, ci, :], 0.0), reads=[b_rd], writes=[b_rd])
                for sub in range(4):
                    if first:
                        op("vector", lambda e, sub=sub: e.tensor_scalar(out=oacc[k][:, sub, h_own, :], in0=ptv[:, sub, 0:96], scalar1=rd[:, ci, sub:sub + 1], scalar2=None, op0=ALU.mult),
                           reads=[b_ps_t[bi], b_rd], writes=[b_oacc[k]])
                    else:
                        op("vector", lambda e, sub=sub: e.scalar_tensor_tensor(out=oacc[k][:, sub, h_own, :], in0=ptv[:, sub, 0:96], scalar=rd[:, ci, sub:sub + 1],
                                                                              in1=oacc[k][:, sub, h_own, :], op0=ALU.mult, op1=ALU.add),
                           reads=[b_ps_t[bi], b_rd, b_oacc[k]], writes=[b_oacc[k]])
            return ri

        def cmp_part(qt):
            k = qt % 2
            nct = min(8, qt // 4 + 1)
            for h in range(4):
                ea = e_all[h % 2]
                for c in range(nct):
                    si = nxt("s", 2)
                    partial = qt < 4 * c + 4
                    op("tensor", lambda e, c=c, h=h, si=si, partial=partial: e.matmul(ps_s[si][:], lhsT=kcmpT[:, c * 128:(c + 1) * 128], rhs=q_sb[k][:, h, :], start=True, stop=not partial),
                       reads=[b_kcmpT, b_q[k]], writes=[b_ps_s[si]], pe_accum=True)
                    if partial:
                        r = qt - 4 * c
                        op("tensor", lambda e, si=si, r=r: e.matmul(ps_s[si][:], lhsT=idb[:], rhs=masks[:, 8 + r, :], start=False, stop=True),
                           reads=[b_idb, b_masks], writes=[b_ps_s[si]], pe_accum=True)
                    op("scalar", lambda e, c=c, si=si, ea=ea: e.activation(out=ea[:, c, :], in_=ps_s[si][:], func=AF.Exp), reads=[b_ps_s[si]], writes=[b_eall[h % 2]])
                po = nxt("o", 2)
                for c in range(nct):
                    op("tensor", lambda e, c=c, ea=ea, po=po: e.matmul(ps_o[po][:, :], lhsT=vcmpa[:, c, 0:128], rhs=ea[:, c, :], start=(c == 0), stop=(c == nct - 1)),
                       reads=[b_vcmpa, b_eall[h % 2]], writes=[b_ps_o[po]], pe_accum=True)
                for sub in range(4):
                    pi = ps_i[sub // 2]
                    for c in range(nct):
                        op("tensor", lambda e, c=c, sub=sub, ea=ea, pi=pi: e.matmul(pi[:, (sub % 2) * 256:(sub % 2) * 256 + 256], lhsT=ea[:, c, sub * 128:(sub + 1) * 128], rhs=wimp[:, c, :],
                                                                               start=(c == 0), stop=(c == nct - 1)),
                           reads=[b_wimp, b_eall[h % 2]], writes=[b_ps_i[sub // 2]], pe_accum=True)
                ri = finish_branch(qt, po, h if h < 2 else None, 0, True)
                for sub in range(4):
                    pi = ps_i[sub // 2]
                    src = pi[:, (sub % 2) * 256:(sub % 2) * 256 + 256]
                    if h == 0:
                        op("vector", lambda e, sub=sub, src=src, ri=ri: e.tensor_scalar(out=impacc[:, sub, 1:257], in0=src, scalar1=rd[:, ri, sub:sub + 1], scalar2=None, op0=ALU.mult),
                           reads=[b_ps_i[sub // 2], b_rd], writes=[b_imp[sub]])
                    else:
                        op("vector", lambda e, sub=sub, src=src, ri=ri: e.scalar_tensor_tensor(out=impacc[:, sub, 1:257], in0=src, scalar=rd[:, ri, sub:sub + 1], in1=impacc[:, sub, 1:257],
                                                                                            op0=ALU.mult, op1=ALU.add),
                           reads=[b_ps_i[sub // 2], b_rd, b_imp[sub]], writes=[b_imp[sub]])

        def topk_part(qt):
            for sub in range(4):
                st = 4 * qt + sub
                sc = impacc[:, sub, 1:257]
                op("vector", lambda e, sub=sub, st=st: e.tensor_tensor(out=impacc[:, sub, 2 * st:2 * st + 3], in0=impacc[:, sub, 2 * st:2 * st + 3], in1=fpat[:, 0:3], op=ALU.add),
                   reads=[b_imp[sub], b_fpat], writes=[b_imp[sub]])
                op("vector", lambda e, sub=sub: e.tensor_scalar(out=impacc[:, sub, 1:2], in0=impacc[:, sub, 1:2], scalar1=4000.0, scalar2=None, op0=ALU.add), reads=[b_imp[sub]], writes=[b_imp[sub]])
                op("vector", lambda e, sc=sc: e.max(out=m8[:, 0, :], in_=sc), reads=[b_imp[sub]], writes=[b_m8])
                op("vector", lambda e, sc=sc: e.match_replace(out=mr[:], in_to_replace=m8[:, 0, :], in_values=sc, imm_value=-1e9), reads=[b_imp[sub], b_m8], writes=[b_mr])
                op("vector", lambda e: e.max(out=m8[:, 1, :], in_=mr[:]), reads=[b_mr], writes=[b_m8])
                op("vector", lambda e: e.tensor_reduce(out=m8[:, 2, 0:1], in_=m8[:, 1, :], axis=AX.X, op=ALU.min), reads=[b_m8], writes=[b_m8])
                op("vector", lambda e, sub=sub, sc=sc: e.tensor_scalar(out=selb[:, sub, 96:352], in0=sc, scalar1=m8[:, 2, 0:1], scalar2=NEG, op0=ALU.is_lt, op1=ALU.mult),
                   reads=[b_imp[sub], b_m8], writes=[b_selb[sub]])

        def qaug_part(qt):
            k = qt % 2
            nv = (qt + 1 + 3) // 4
            for v in range(nv):
                bi = nxt("t", 2)
                pt = ps_t[bi]
                for sub in range(4):
                    op("tensor", lambda e, sub=sub, v=v, pt=pt: e.transpose(out=pt[:, sub * 128:(sub + 1) * 128], in_=selb[:, sub, 32 * v:32 * v + 128], identity=idf[:]),
                       reads=[b_selb[sub], b_idf], writes=[b_ps_t[bi]], pe_accum=True)
                for h in range(2):
                    op("vector", lambda e, h=h, v=v, pt=pt: e.tensor_copy(out=qaug[96:128, h, v, :], in_=pt[96:128, :]),
                       reads=[b_ps_t[bi]], writes=[b_qaug[h][v]])
                    op("gpsimd", lambda e, h=h, v=v: e.tensor_copy(out=qaug[0:96, h, v, :], in_=q_sb[k][:, h, :]), reads=[b_q[k]], writes=[b_qaug[h][v]])

        def win_part(qt):
            k = qt % 2
            tiles = list(range(4, 8)) if qt == 0 else list(range(8))
            items = [(h, n, i) for h in range(2) for n, i in enumerate(tiles)]
            pos = {}
            pend = None

            def pv(item):
                h, n, i = item
                ei, po = pos[item]
                op("tensor", lambda e, i=i, ei=ei, po=po, n=n: e.matmul(ps_o[po][:, :], lhsT=vwa[k][:, i, 0:128], rhs=e_sb[ei][:], start=(n == 0), stop=(n == len(tiles) - 1)),
                   reads=[b_vwa[k], b_e[ei]], writes=[b_ps_o[po]], pe_accum=True)
                if n == len(tiles) - 1:
                    finish_branch(qt, po, h, 2, False)

            po_h = {}
            for item in items:
                h, n, i = item
                if n == 0:
                    po_h[h] = nxt("o", 2)
                si = nxt("s", 2)
                ei = nxt("e", 3)
                pos[item] = (ei, po_h[h])
                op("tensor", lambda e, i=i, si=si, h=h: e.matmul(ps_s[si][:], lhsT=kw_sb[k][:, i * 128:(i + 1) * 128], rhs=q_sb[k][:, h, :], start=True, stop=False),
                   reads=[b_kw[k], b_q[k]], writes=[b_ps_s[si]], pe_accum=True)
                op("tensor", lambda e, i=i, si=si: e.matmul(ps_s[si][:], lhsT=idb[:], rhs=masks[:, i, :], start=False, stop=True),
                   reads=[b_idb, b_masks], writes=[b_ps_s[si]], pe_accum=True)
                op("scalar", lambda e, si=si, ei=ei: e.activation(out=e_sb[ei][:], in_=ps_s[si][:], func=AF.Exp), reads=[b_ps_s[si]], writes=[b_e[ei]])
                if pend is not None:
                    pv(pend)
                pend = item
            pv(pend)

        def slc_part(qt):
            k = qt % 2
            nkt = 4 * (qt + 1)
            items = [(h, kt) for h in range(2) for kt in range(nkt)]
            pos = {}
            pend = None
            po_h = {}

            def pv(item):
                h, kt = item
                ei, po = pos[item]
                op("tensor", lambda e, kt=kt, ei=ei, po=po: e.matmul(ps_o[po][:, :], lhsT=vsa[:, kt, 0:128], rhs=e_sb[ei][:], start=(kt == 0), stop=(kt == nkt - 1)),
                   reads=[b_vsa, b_e[ei]], writes=[b_ps_o[po]], pe_accum=True)
                if kt == nkt - 1:
                    finish_branch(qt, po, h, 1, False)

            for item in items:
                h, kt = item
                if kt == 0:
                    po_h[h] = nxt("o", 2)
                si = nxt("s", 2)
                ei = nxt("e", 3)
                pos[item] = (ei, po_h[h])
                v = kt // 16
                diag = kt >= 4 * qt
                op("tensor", lambda e, kt=kt, si=si, v=v, diag=diag, h=h: e.matmul(ps_s[si][:], lhsT=ksa[:, kt * 128:(kt + 1) * 128], rhs=qaug[:, h, v, :], start=True, stop=not diag),
                   reads=[b_ksa, b_qaug[h][v]], writes=[b_ps_s[si]], pe_accum=True)
                if diag:
                    op("tensor", lambda e, kt=kt, si=si: e.matmul(ps_s[si][:], lhsT=idb[:], rhs=masks[:, 12 + kt - 4 * qt, :], start=False, stop=True),
                       reads=[b_idb, b_masks], writes=[b_ps_s[si]], pe_accum=True)
                op("scalar", lambda e, si=si, ei=ei: e.activation(out=e_sb[ei][:], in_=ps_s[si][:], func=AF.Exp), reads=[b_ps_s[si]], writes=[b_e[ei]])
                if pend is not None:
                    pv(pend)
                pend = item
            pv(pend)

        def out_part(qt):
            k = qt % 2
            tok = slice(qt * 512, (qt + 1) * 512)
            for h in range(2):
                bi = nxt("t", 2)
                pt = ps_t[bi]
                for sub in range(4):
                    op("tensor", lambda e, sub=sub, h=h, pt=pt: e.transpose(out=pt[0:96, sub * 128:(sub + 1) * 128], in_=oacc[k][:, sub, h, :], identity=idf[:]),
                       reads=[b_oacc[k], b_idf], writes=[b_ps_t[bi]], pe_accum=True)
                op("vector", lambda e, h=h, pt=pt: e.tensor_copy(out=oTo[k][:, h, :], in_=pt[0:96, :]), reads=[b_ps_t[bi]], writes=[b_oTo[k]])
            op("sync", lambda e: e.dma_start(out=ysl(64, 256, qt * 512, 512).rearrange("(h d) t -> d h t", d=96), in_=oTo[k][:]), reads=[b_oTo[k]], writes=[b_ymc], dma=True)

        nq = self.nqt if hasattr(self, "nqt") else NQT
        if stage < 4:
            nq = 0
        if nq > 0:
            load_tile(0)
            cmp_part(0)
            if stage >= 4.2:
                topk_part(0)
        for qt in range(nq):
            if stage >= 4.3:
                qaug_part(qt)
            if stage >= 4.4:
                win_part(qt)
            pool_chunk(qt)
            if qt + 1 < nq and stage >= 4.7:
                load_tile(qt + 1)
                cmp_part(qt + 1)
                topk_part(qt + 1)
            if stage >= 4.5:
                slc_part(qt)
            if stage >= 4.6:
                out_part(qt)
            if stage < 4.7:
                break
import ml_dtypes
from concourse.bass_utils import run_bass_kernel_spmd

_BF = ml_dtypes.bfloat16
_NCORE = 8
_GROUPS = [[0, 1, 2, 3], [4, 5, 6, 7]]
_DBG = {}
_CC_CHAIN = Buf("cc_chain")


def _reset_chain():
    _CC_CHAIN.w = None
    _CC_CHAIN.r = {}


def _gather(P, c_ap, b_c, g_ap, b_g):
    if _DBG.get("nocc"):
        rows = c_ap.shape[0]
        for r in range(4):
            P.op("gpsimd", lambda e, r=r: e.dma_start(out=g_ap[r * rows:(r + 1) * rows, :], in_=c_ap), reads=[b_c], writes=[b_g], dma=True)
        return
    P.op("gpsimd", lambda e: e.collective_compute("AllGather", ALU.bypass, replica_groups=_GROUPS, ins=[c_ap], outs=[g_ap]),
         reads=[b_c, _CC_CHAIN], writes=[b_g, _CC_CHAIN], cc=True)


def build_fused():
    nc = bass.Bass("TRN2", target_bir_lowering=False)
    _reset_chain()
    TB.gus_ctr = 0
    P = Prog(nc)
    dram = Dram(nc, P)
    dram.dump = tuple(_DBG.get("dump", ()))
    ident = dram.din("ident", [128, 128], F32)
    xin = dram.din("xin", [NTOK, D], F32)
    xout = dram.dout("xout", [NTOK, D], F32)
    op = P.op

    jcache = {}

    def jexpr(e):
        if "v" not in jcache:
            pid = e.partition_id()
            j = e.snap(pid % 4, min_val=0, max_val=3)
            g = e.snap(j // 2, min_val=0, max_val=1)
            jo = e.snap(g * 2 + (1 - (j % 2)), min_val=0, max_val=3)
            jcache["v"] = dict(j=j, g=g, jo=jo)
        return jcache["v"]

    cur, b_cur = xin, None
    tcount = 0
    ym_loc = b_ym_loc = None
    for l in range(3):
        P.push_scope()
        tb = TB(nc, P, ident, dram)
        if l > 0:
            pl = "l%d" % (l - 1)
            x2 = dram.dscr("x2", [NTOK, D], F32)
            scr2, b_scr2, its2 = tb.prep_for(pl + "_ffn2")

            def hook(t_, its2=its2):
                for it in its2[2 * t_:2 * t_ + 2]:
                    it()
            tcount = tb.stage_mix(pl, ym_loc, b_ym_loc, cur, b_cur, x2, dram.b["x2"], tcount, tile_hook=hook)
            if l == 2:
                tcount = tb.stage_ffn(pl + "_ffn2", x2, dram.b["x2"], xout, dram.b["xout"], tcount, prepped=(scr2, b_scr2))
                P.pop_scope()
                break
            x3 = dram.dscr("x3", [NTOK, D], F32)
            scr1, b_scr1, its1 = tb.prep_for("l%d_ffn1" % l)

            def hook1(t_, its1=its1):
                tb.prep_cast_eng = "gpsimd"
                for it in its1[2 * t_:2 * t_ + 2]:
                    it()
                tb.prep_cast_eng = None
            tcount = tb.stage_ffn(pl + "_ffn2", x2, dram.b["x2"], x3, dram.b["x3"], tcount, prepped=(scr2, b_scr2), tile_hook=hook1)
            cur, b_cur = x3, dram.b["x3"]
            pre1 = (scr1, b_scr1)
        ll = "l%d" % l
        x1 = dram.dscr("x1", [NTOK, D], F32)
        tcount = tb.stage_ffn(ll + "_ffn1", cur, b_cur, x1, dram.b["x1"], tcount, prepped=(pre1 if l > 0 else None))
        cur, b_cur = x1, dram.b["x1"]
        def ctensor(name, shp, dt):
            ap = dram.dscr(name, shp, dt)
            return ap, dram.b[name]
        def gtensor(name, shp, dt):
            if name not in dram.ap:
                dram.ap[name] = nc.dram_tensor(name, list(shp), dt).ap()
                dram.b[name] = P.buf(name)
            return dram.ap[name], dram.b[name]
        C_q, b_Cq = ctensor("c_q", [8, 128, NTOK], BF16)
        C_u, b_Cu = ctensor("c_u", [4, 64, NTOK], F32)
        C_k, b_Ck = ctensor("c_k", [8, 96, NTOK], BF16)
        C_vs, b_Cvs = ctensor("c_vs", [2, NTOK, 96], BF16)
        C_vw, b_Cvw = ctensor("c_vw", [2, NTOK, 96], BF16)
        C_g, b_Cg = ctensor("c_g", [NTOK, 24], F32)
        G_q, b_Gq = gtensor("g_q", [8, 4, 128, NTOK], BF16)
        G_u, b_Gu = gtensor("g_u", [4, 4, 64, NTOK], F32)
        G_k, b_Gk = gtensor("g_k", [8, 4, 96, NTOK], BF16)
        G_vs, b_Gvs = gtensor("g_vs", [2, 4, NTOK, 96], BF16)
        G_vw, b_Gvw = gtensor("g_vw", [2, 4, NTOK, 96], BF16)
        G_g, b_Gg = ctensor("g_g", [4 * NTOK, 24], F32)
        outs = {"gates": (C_g, b_Cg)}
        outs["qT"] = (C_q, b_Cq)
        outs["uT"] = ([C_u[p] for p in range(4)], b_Cu)
        outs["kT"] = (C_k, b_Ck)
        outs["vs"] = (C_vs, b_Cvs)
        outs["vw"] = (C_vw, b_Cvw)
        tcount = tb.stage_proj(ll, cur, b_cur, outs, tcount)
        P.pop_scope()
        if _DBG.get("stop3") and l == 1:
            P.wait_all("sync", list(dram.b.values()))
            P.finish()
            return nc
        if _DBG.get("dump") and l == 0:
            d0 = dram.dscr("dbg_ck0_pre", [96, NTOK], BF16)
            op("sync", lambda e: e.dma_start(out=d0, in_=C_k[0]), reads=[b_Ck], writes=[dram.b["dbg_ck0_pre"]], dma=True)
            dram.outs += [dram.b["dbg_ck0_pre"]]
            P.barrier()
        for i in range(8):
            _gather(P, C_q[i], b_Cq, G_q[i].rearrange("r d t -> (r d) t"), b_Gq)
        for i in range(8):
            _gather(P, C_k[i], b_Ck, G_k[i].rearrange("r d t -> (r d) t"), b_Gk)
        for i in range(4):
            _gather(P, C_u[i], b_Cu, G_u[i].rearrange("r c t -> (r c) t"), b_Gu)
        for i in range(2):
            _gather(P, C_vs[i], b_Cvs, G_vs[i].rearrange("r t d -> (r t) d"), b_Gvs)
            _gather(P, C_vw[i], b_Cvw, G_vw[i].rearrange("r t d -> (r t) d"), b_Gvw)
        _gather(P, C_g, b_Cg, G_g, b_Gg)
        P.barrier()
        if _DBG.get("dump") and l == 0:
            d1 = dram.dscr("dbg_ck0", [96, NTOK], BF16); d2 = dram.dscr("dbg_gk0", [4 * 96, NTOK], BF16)
            op("sync", lambda e: e.dma_start(out=d1, in_=C_k[0]), reads=[b_Ck], writes=[dram.b["dbg_ck0"]], dma=True)
            op("sync", lambda e: e.dma_start(out=d2, in_=G_k[0].rearrange("r d t -> (r d) t")), reads=[b_Gk], writes=[dram.b["dbg_gk0"]], dma=True)
            dram.outs += [dram.b["dbg_ck0"], dram.b["dbg_gk0"]]
        loc = {}

        def mk(name, shp, dt):
            ap = dram.dscr("l_" + name, shp, dt)
            loc[name] = (ap, dram.b["l_" + name])
            return ap, loc[name][1]

        l_qT, b_lq = mk("qT", [4, 96, S], BF16)
        kinds = ["kcT", "vcT", "ksT", "kwT"]
        for name in kinds:
            mk(name, [96, S], BF16)
        mk("uT", [64, S], F32); mk("vs", [S, 96], BF16); mk("vw", [S, 96], BF16)
        gk5 = G_k.rearrange("(k g) r d t -> k g r d t", g=2)
        gq5 = G_q.rearrange("(p h) r d t -> p h r d t", h=2)
        for h2 in range(2):
            def q_own(e, h2=h2):
                return e.dma_start(out=l_qT[h2].rearrange("d (r t) -> d r t", r=4),
                                   in_=gq5[bass.ds(jexpr(e)["j"], 1), h2, :, 0:96, :].rearrange("o r d t -> (o d) r t"))

            def q_oth(e, h2=h2):
                return e.dma_start(out=l_qT[2 + h2].rearrange("d (r t) -> d r t", r=4),
                                   in_=gq5[bass.ds(jexpr(e)["jo"], 1), h2, :, 0:96, :].rearrange("o r d t -> (o d) r t"))
            op("sync", q_own, reads=[b_Gq], writes=[b_lq], dma=True)
            op("sync", q_oth, reads=[b_Gq], writes=[b_lq], dma=True)

        def uf(e):
            return e.dma_start(out=loc["uT"][0].rearrange("c (r t) -> c r t", r=4),
                               in_=G_u[bass.ds(jexpr(e)["j"], 1), :, :, :].rearrange("o r c t -> (o c) r t"))
        op("sync", uf, reads=[b_Gu], writes=[loc["uT"][1]], dma=True)
        for ki, name in enumerate(kinds):
            def kf(e, name=name, ki=ki):
                return e.dma_start(out=loc[name][0].rearrange("d (r t) -> d r t", r=4),
                                   in_=gk5[ki, bass.ds(jexpr(e)["g"], 1), :, :, :].rearrange("o r d t -> (o d) r t"))
            op("sync", kf, reads=[b_Gk], writes=[loc[name][1]], dma=True)
        for name, Gv, b_Gv in (("vs", G_vs, b_Gvs), ("vw", G_vw, b_Gvw)):
            for r in range(4):
                rows = slice(r * NTOK, (r + 1) * NTOK)

                def vf(e, name=name, rows=rows, r=r, Gv=Gv):
                    return e.dma_start(out=loc[name][0][rows, :], in_=Gv[bass.ds(jexpr(e)["g"], 1), r, :, :].rearrange("o t d -> (o t) d"))
                op("sync", vf, reads=[b_Gv], writes=[loc[name][1]], dma=True)
        loc["gates"] = (G_g, b_Gg)
        if _DBG.get("stop1") is not None and _DBG.get("stop1") == l:
            P.wait_all("sync", list(dram.b.values()))
            P.finish()
            return nc
        ymc = dram.dscr("c_ym", [4, 256, NTOK], BF16)
        b_ymc = dram.b["c_ym"]
        P.push_scope()
        bb = BB(nc, P, dram)
        bb.build(ll, loc, ymc, b_ymc)
        P.pop_scope()
        g_ym, b_gym = gtensor("g_ym", [4, 2, 4, 128, NTOK], BF16)
        for tj in range(4):
            for hf in range(2):
                _gather(P, ymc[tj, hf * 128:(hf + 1) * 128, :], b_ymc, g_ym[tj, hf].rearrange("r i t -> (r i) t"), b_gym)
        P.barrier()
        ym_loc = dram.dscr("l_ym", [D, NTOK], BF16)
        b_ym_loc = dram.b["l_ym"]
        for hf in range(2):
            def yf(e, g_ym=g_ym, ym_loc=ym_loc, hf=hf):
                return e.dma_start(out=ym_loc[hf * 512:(hf + 1) * 512, :], in_=g_ym[bass.ds(jexpr(e)["j"], 1), hf, :, :, :].rearrange("o r i t -> (o r i) t"))
            op("sync", yf, reads=[b_gym], writes=[b_ym_loc], dma=True)
        if _DBG.get("stop2") is not None and _DBG.get("stop2") == l:
            P.wait_all("sync", list(dram.b.values()))
            P.finish()
            return nc
    P.wait_all("sync", dram.outs)
    P.finish()
    return nc


def _yperm():
    src_of = lambda r, row: (64 * r + row) if row < 64 else (256 + 192 * r + (row - 64))
    return np.array([src_of(r, hf * 128 + i) for hf in range(2) for r in range(4) for i in range(128)])


_YPERM = _yperm()


def kernel(**inputs):
    inputs = {k: np.asarray(v) for k, v in inputs.items()}
    x = inputs["x"].reshape(2 * S, D).astype(np.float32, copy=False)
    nc = build_fused()
    C = b_consts()
    maps = []
    for c in range(_NCORE):
        b, j = divmod(c, 4)
        m = {"ident": C["ident_f"], "ident_b": C["ident_b"], "masks": C["masks"], "kpat": C["kpat"], "wimp": C["wimp"], "fpat": C["fpat"]}
        a, A = pool_consts(j)
        m["pool_a"] = a
        gs = np.zeros((6, 24), np.float32)
        gs[np.arange(6), 6 * j + np.arange(6)] = 1.0
        m["gsel"] = gs
        m["pool_A"] = A
        m["xin"] = x[c * NTOK:(c + 1) * NTOK]
        for l in range(2):
            p = "l%d" % l
            m[p + "_ffn1_wg"] = inputs["ffn1_w_gate"][l]; m[p + "_ffn1_wu"] = inputs["ffn1_w_up"][l]; m[p + "_ffn1_wd"] = inputs["ffn1_w_down"][l]
            m[p + "_ffn1_lg"] = inputs["ln1_g"][l]; m[p + "_ffn1_lb"] = inputs["ln1_b"][l]
            m[p + "_ffn2_wg"] = inputs["ffn2_w_gate"][l]; m[p + "_ffn2_wu"] = inputs["ffn2_w_up"][l]; m[p + "_ffn2_wd"] = inputs["ffn2_w_down"][l]
            m[p + "_ffn2_lg"] = inputs["ln3_g"][l]; m[p + "_ffn2_lb"] = inputs["ln3_b"][l]
            m[p + "_w_in"] = inputs["w_in"][l]; m[p + "_b_gate"] = inputs["b_gate"][l]
            m[p + "_w_out"] = inputs["w_out"][l][_YPERM]
            m[p + "_ln2_g"] = inputs["ln2_g"][l]; m[p + "_ln2_b"] = inputs["ln2_b"][l]
            m[p + "_cmp_k_w1"] = inputs["cmp_k_w1"][l]; m[p + "_cmp_k_w2"] = inputs["cmp_k_w2"][l]; m[p + "_cmp_k_pos"] = inputs["cmp_pos_k"][l]
            m[p + "_cmp_v_w1"] = inputs["cmp_v_w1"][l]; m[p + "_cmp_v_w2"] = inputs["cmp_v_w2"][l]; m[p + "_cmp_v_pos"] = inputs["cmp_pos_v"][l]
            m[p + "_pool_w"] = inputs["pool_w"][l][j]
            m[p + "_pool_sc"] = inputs["pool_scale"][l][64 * j:64 * j + 64].reshape(64, 1)
        maps.append({k: np.ascontiguousarray(v) for k, v in m.items()})
    res = run_bass_kernel_spmd(nc, maps, core_ids=list(range(_NCORE))).results
    _DBG["res"] = res if _DBG.get("dump") else None
    out = np.concatenate([r["xout"] for r in res], axis=0)
    return out.reshape(2, S, D).astype(np.float32, copy=False)
```

```python
import numpy as np
import concourse.bass as bass
import concourse.mybir as mybir
from contextlib import ExitStack

F32 = mybir.dt.float32
BF16 = mybir.dt.bfloat16
AF = mybir.ActivationFunctionType
ALU = mybir.AluOpType
AX = mybir.AxisListType

ENGS = ("sync", "tensor", "vector", "scalar", "gpsimd")
N_DMA_SEMS = 12


class Buf:
    __slots__ = ("name", "w", "r", "psum")

    def __init__(self, name, psum=False):
        self.name = name
        self.psum = psum
        self.w = None
        self.r = {}


class Prog:
    def __init__(self, nc):
        self.nc = nc
        self.es = ExitStack()
        self.streams = {e: [] for e in ENGS}
        self.sem = {}
        self.cnt = {}
        for e in ENGS:
            self.sem[e] = nc.alloc_semaphore("c_" + e)
            self.cnt[e] = 0
        self.dsem = {}
        self.dcnt = {}
        self.drr = {}
        for e in ("sync", "gpsimd", "scalar"):
            for i in range(N_DMA_SEMS):
                k = "d_%s_%d" % (e, i)
                self.sem[k] = nc.alloc_semaphore(k)
                self.cnt[k] = 0
            self.drr[e] = 0
        self.sem["cc"] = nc.alloc_semaphore("cc_sem")
        self.cnt["cc"] = 0
        self.seen = {e: {} for e in ENGS}
        self.nbuf = 0
        self.block_hooks = []
        self.scopes = []
        self.scope_ctr = 0
        self.scope_id = 0

    def sbuf(self, name, shape, dtype):
        t = self.es.enter_context(self.nc.sbuf_tensor("sb%d_%s" % (self.scope_id, name), list(shape), dtype))
        return t

    def psum(self, name, shape, dtype=F32):
        t = self.es.enter_context(self.nc.psum_tensor("pp%d_%s" % (self.scope_id, name), list(shape), dtype))
        return t

    def push_scope(self):
        self.scopes.append(self.es)
        self.es = ExitStack()
        self.scope_ctr += 1
        self.scope_id = self.scope_ctr

    def pop_scope(self):
        self.barrier()
        self.es.close()
        self.es = self.scopes.pop()

    def buf(self, name=None, psum=False):
        self.nbuf += 1
        return Buf(name or ("b%d" % self.nbuf), psum)

    def pbuf(self):
        return self.buf(psum=True)

    def _need(self, eng, deps):
        out = []
        seen = self.seen[eng]
        for k, v in deps.items():
            if seen.get(k, 0) < v:
                seen[k] = v
                out.append((k, v))
        return out

    def op(self, eng, fn, reads=(), writes=(), dma=False, pe_accum=False, cc=False):
        deps = {}

        def add(tok):
            if tok is None:
                return
            k, v = tok
            if deps.get(k, 0) < v:
                deps[k] = v

        for b in reads:
            add(b.w)
            if b.psum:
                for k, v in b.r.items():
                    if k != eng:
                        add((k, v))
        for b in writes:
            if not (pe_accum and b.w is not None and b.w[0] == "tensor"):
                add(b.w)
            for k, v in b.r.items():
                add((k, v))
        if dma:
            rr = self.drr[eng]
            self.drr[eng] = (rr + 1) % N_DMA_SEMS
            dk = "d_%s_%d" % (eng, rr)
            if self.cnt[dk] > 0:
                add((dk, self.cnt[dk]))
        if eng == "tensor":
            deps.pop("tensor", None)
        waits = self._need(eng, deps)
        if cc:
            self.cnt["cc"] += 1
            tok = ("cc", self.cnt["cc"])
            inc = 1
        elif dma:
            self.cnt[dk] += 16
            tok = (dk, self.cnt[dk])
            inc = 16
        else:
            self.cnt[eng] += 1
            tok = (eng, self.cnt[eng])
            inc = 1
        semh = self.sem[tok[0]]
        wl = [(self.sem[k], v) for k, v in waits]

        def emit(e, fn=fn, wl=wl, semh=semh, inc=inc):
            for s, v in wl:
                e.wait_ge(s, v)
            fn(e).then_inc(semh, inc)

        self.streams[eng].append(emit)
        for b in writes:
            b.w = tok
            b.r = {}
        for b in reads:
            if b.r.get(tok[0], 0) < tok[1]:
                b.r[tok[0]] = tok[1]
        return tok

    def wait_all(self, eng, bufs):
        deps = {}
        for b in bufs:
            if b.w is not None:
                k, v = b.w
                if deps.get(k, 0) < v:
                    deps[k] = v
        wl = [(self.sem[k], v) for k, v in self._need(eng, deps)]

        def emit(e, wl=wl):
            for s, v in wl:
                e.wait_ge(s, v)

        self.streams[eng].append(emit)

    def barrier(self):
        allc = {k: v for k, v in self.cnt.items() if v > 0}
        for eng in ENGS:
            wl = [(self.sem[k], v) for k, v in self._need(eng, dict(allc))]

            def emit(e, wl=wl):
                for s, v in wl:
                    e.wait_ge(s, v)

            self.streams[eng].append(emit)

    def block_break(self):
        self.barrier()
        for e in ENGS:
            self.streams[e].append(None)

    def finish(self):
        nc = self.nc
        segs = {e: [[]] for e in ENGS}
        for e in ENGS:
            for f in self.streams[e]:
                if f is None:
                    segs[e].append([])
                else:
                    segs[e][-1].append(f)
        nseg = len(segs["sync"])
        for i in range(nseg):
            for hook in self.block_hooks:
                hook()
            with nc.Block() as block:
                @block.sync
                def _(e):
                    for f in segs["sync"][i]:
                        f(e)

                @block.tensor
                def _(e):
                    for f in segs["tensor"][i]:
                        f(e)

                @block.vector
                def _(e):
                    for f in segs["vector"][i]:
                        f(e)

                @block.scalar
                def _(e):
                    for f in segs["scalar"][i]:
                        f(e)

                @block.gpsimd
                def _(e):
                    for f in segs["gpsimd"][i]:
                        f(e)
        self.es.close()


class Dram:
    def __init__(self, nc, P):
        self.nc = nc
        self.P = P
        self.ap = {}
        self.b = {}
        self.outs = []
        self.dump = ()
        self.arena = {}
        self.ARENA_ELEMS = {"bf16": 120 * 1024 * 1024, "f32": 60 * 1024 * 1024}

    def din(self, name, shape, dt):
        if name not in self.ap:
            self.ap[name] = self.nc.dram_tensor(name, list(shape), dt, kind="ExternalInput").ap()
        return self.ap[name]

    def dout(self, name, shape, dt):
        self.ap[name] = self.nc.dram_tensor(name, list(shape), dt, kind="ExternalOutput").ap()
        self.b[name] = self.P.buf(name)
        self.outs.append(self.b[name])
        return self.ap[name]

    def dscr(self, name, shape, dt):
        if name not in self.ap:
            if name in self.dump:
                self.ap[name] = self.nc.dram_tensor(name, list(shape), dt, kind="ExternalOutput").ap()
            else:
                n = 1
                for d in shape:
                    n *= int(d)
                key = "bf16" if dt == BF16 else "f32"
                if key not in self.arena:
                    cap = self.ARENA_ELEMS[key]
                    self.arena[key] = [self.nc.dram_tensor("arena_" + key, [cap], dt).ap(), 0, cap]
                ar = self.arena[key]
                off = ar[1]
                n_al = (n + 2047) // 2048 * 2048
                assert off + n_al <= ar[2], ("arena overflow", key, name)
                ar[1] = off + n_al
                flat = ar[0][off:off + n]
                if len(shape) == 1:
                    self.ap[name] = flat
                else:
                    names = ["a%d" % i for i in range(len(shape))]
                    pat = "(" + " ".join(names) + ") -> " + " ".join(names)
                    kw = {nm: int(d) for nm, d in zip(names[1:], shape[1:])}
                    self.ap[name] = flat.rearrange(pat, **kw)
            self.b[name] = self.P.buf(name)
        return self.ap[name]

D = 1024
DFF = 2816
NF = DFF // 128
DIN = 2200
NTOK = 4096
TT = 512
NTILE = NTOK // TT
ALPHA = float((2.0 * 2) ** 0.25)
LN_EPS = 1e-5
QSCALE = float(96 ** -0.5)
TCOLS = [256 + 96 * h for h in range(8)] + [1024, 1120, 1216, 1312, 1408, 1504, 1792, 1888]


class TB:
    gus_ctr = 0

    def __init__(self, nc, P, ident, dram):
        self.nc = nc
        self.P = P
        self.dram = dram
        self.ident = ident
        self.b_dram = dram.b
        self.idt = P.sbuf("idt", [128, 128], F32); self.b_idt = P.buf()
        self.xt = [P.sbuf("xt%d" % i, [128, 4, D], F32) for i in range(2)]
        self.b_xt = [[P.buf() for s in range(4)] for i in range(2)]
        self.xT = P.sbuf("xT", [128, 8, TT], BF16); self.b_xT = [P.buf() for c in range(8)]
        self.hT = P.sbuf("hT", [128, NF, TT], BF16); self.b_hT = [P.buf() for f in range(NF)]
        self.wres = P.sbuf("wres", [128, NF * D], BF16); self.b_wres = [P.buf() for f in range(NF)]
        self.ring = P.sbuf("ring", [128, 3, 2, 8, 128], BF16); self.b_ring = [P.buf() for i in range(3)]
        self.stg = [P.sbuf("stg%d" % i, [128, DFF], F32) for i in range(2)]; self.b_stg = [P.buf() for i in range(2)]
        self.stgb = [P.sbuf("stgb%d" % i, [128, DFF], BF16) for i in range(2)]; self.b_stgb = [P.buf() for i in range(2)]
        self.lng = P.sbuf("lng", [128, D], F32); self.b_lng = P.buf()
        self.lnb = P.sbuf("lnb", [128, D], F32); self.b_lnb = P.buf()
        self.sg = [P.sbuf("sg%d" % i, [128, TT], F32) for i in range(2)]; self.b_sg = [P.buf() for i in range(2)]
        self.st = P.sbuf("st", [128, 4, 2, 6], F32); self.b_st = [P.buf() for s in range(4)]
        self.mv = P.sbuf("mv", [128, 4, 4], F32); self.b_mv = [P.buf() for s in range(4)]
        self.oT = P.sbuf("oT", [128, 16, TT], BF16); self.b_oT = [P.buf() for i in range(16)]
        P.op("gpsimd", lambda e: e.memset(self.oT[96:128, :, :], 0.0), writes=self.b_oT)
        self.ou = P.sbuf("ou", [128, 2, TT], F32); self.b_ou = [P.buf() for i in range(2)]
        self.ov = P.sbuf("ov", [128, 4, 384], BF16); self.b_ov = [P.buf() for i in range(4)]
        self.og = P.sbuf("og", [128, 4, 24], F32); self.b_og = [P.buf() for i in range(4)]
        self.bg = P.sbuf("bg", [128, 24], F32); self.b_bg = P.buf()
        self.ps = [P.psum("ps%d" % i, [128, 512], F32) for i in range(8)]
        self.b_ps = [P.pbuf() for i in range(8)]
        self.rr = 0
        self.prep_i = 0
        self.prep_cast_eng = None
        P.op("sync", lambda e: e.dma_start(out=self.idt[:], in_=self.ident), writes=[self.b_idt], dma=True)

    def din(self, name, shape, dt):
        return self.dram.din(name, shape, dt)

    def dscr(self, name, shape, dt):
        return self.dram.dscr(name, shape, dt)

    def cast_eng(self):
        self.rr += 1
        return ("vector", "gpsimd", "scalar")[self.rr % 3]

    def cast(self, eng, out, in_, reads, writes):
        if eng == "scalar":
            self.P.op("scalar", lambda e: e.activation(out=out, in_=in_, func=AF.Copy), reads=reads, writes=writes)
        else:
            self.P.op(eng, lambda e: e.tensor_copy(out=out, in_=in_), reads=reads, writes=writes)

    def load_res(self, w, nchunks, width):
        P = self.P
        per = max(1, DFF // width)
        f = 0
        i = 0
        while f < nchunks:
            n = min(per, nchunks - f)
            k = i % 2
            src = w[f * 128:(f + n) * 128, :].rearrange("(n p) d -> p n d", p=128)
            dst = self.stg[k][:, 0:n * width].rearrange("p (n d) -> p n d", d=width)
            P.op("sync", lambda e, dst=dst, src=src: e.dma_start(out=dst, in_=src), writes=[self.b_stg[k]], dma=True)
            self.cast(self.cast_eng(), self.wres[:, f * width:(f + n) * width], self.stg[k][:, 0:n * width],
                      [self.b_stg[k]], [self.b_wres[j] for j in range(f, f + n)])
            f += n
            i += 1

    def prep_gu_iters(self, wg, wu, scr, b_scr):
        P = self.P
        its = []
        for c in range(8):
            for gi, w in ((0, wg), (1, wu)):
                def it(c=c, gi=gi, w=w):
                    k = self.prep_i % 2
                    self.prep_i += 1
                    P.op("sync", lambda e, k=k, w=w, c=c: e.dma_start(out=self.stg[k][:], in_=w[c * 128:(c + 1) * 128, :]),
                         writes=[self.b_stg[k]], dma=True)
                    self.cast(self.prep_cast_eng or self.cast_eng(), self.stgb[k][:], self.stg[k][:], [self.b_stg[k]], [self.b_stgb[k]])
                    dst = scr[:, :, gi, c, :].rearrange("f p j -> p f j")
                    src = self.stgb[k][:].rearrange("p (f j) -> p f j", j=128)
                    P.op("gpsimd", lambda e, dst=dst, src=src: e.dma_start(out=dst, in_=src), reads=[self.b_stgb[k]],
                         writes=[b_scr], dma=True)
                its.append(it)
        return its

    def prep_gu(self, wg, wu, scr, b_scr):
        for it in self.prep_gu_iters(wg, wu, scr, b_scr):
            it()

    def prep_for(self, pfx):
        wg = self.din(pfx + "_wg", [D, DFF], F32)
        wu = self.din(pfx + "_wu", [D, DFF], F32)
        nm = "gus%d" % (TB.gus_ctr % 2)
        TB.gus_ctr += 1
        scr = self.dscr(nm, [NF, 128, 2, 8, 128], BF16)
        b_scr = self.b_dram[nm]
        return scr, b_scr, self.prep_gu_iters(wg, wu, scr, b_scr)

    def load_ln(self, g, b):
        P = self.P
        P.op("sync", lambda e: e.dma_start(out=self.lng[:], in_=g.partition_broadcast(128)), writes=[self.b_lng], dma=True)
        P.op("sync", lambda e: e.dma_start(out=self.lnb[:], in_=b.partition_broadcast(128)), writes=[self.b_lnb], dma=True)

    def load_x(self, src, b_src, t, k):
        P = self.P
        s_ap = src[t * TT:(t + 1) * TT, :].rearrange("(s p) d -> p s d", p=128)
        P.op("sync", lambda e: e.dma_start(out=self.xt[k][:], in_=s_ap), reads=[b_src] if b_src else [], writes=self.b_xt[k], dma=True)

    def transposes(self, k):
        P = self.P
        for c in range(8):
            pk = c % 2
            for s in range(4):
                P.op("tensor", lambda e, c=c, s=s, pk=pk: e.transpose(out=self.ps[pk][:, s * 128:(s + 1) * 128],
                                                                      in_=self.xt[k][:, s, c * 128:(c + 1) * 128], identity=self.idt[:]),
                     reads=[self.b_xt[k][s], self.b_idt], writes=[self.b_ps[pk]], pe_accum=True)
            self.cast("scalar" if c % 2 else "vector", self.xT[:, c, :], self.ps[pk][:], [self.b_ps[pk]], [self.b_xT[c]])

    def gate_up(self, scr, b_scr, t):
        P = self.P
        for f in range(NF):
            slot = (t * NF + f) % 3
            P.op("gpsimd" if f % 2 else "sync", lambda e, f=f, slot=slot: e.dma_start(out=self.ring[:, slot], in_=scr[f]),
                 reads=[b_scr], writes=[self.b_ring[slot]], dma=True)
            pg = 2 + (f % 2) * 2
            pu = pg + 1
            for gi, pp in ((0, pg), (1, pu)):
                for c in range(8):
                    P.op("tensor", lambda e, c=c, gi=gi, pp=pp, slot=slot: e.matmul(self.ps[pp][:], lhsT=self.ring[:, slot, gi, c, :], rhs=self.xT[:, c, :],
                                                                                 start=(c == 0), stop=(c == 7)),
                         reads=[self.b_ring[slot], self.b_xT[c]], writes=[self.b_ps[pp]], pe_accum=True)
            k = f % 2
            P.op("scalar", lambda e, k=k, pg=pg: e.activation(out=self.sg[k][:], in_=self.ps[pg][:], func=AF.Silu),
                 reads=[self.b_ps[pg]], writes=[self.b_sg[k]])
            P.op("vector", lambda e, k=k, pu=pu, f=f: e.tensor_tensor(out=self.hT[:, f, :], in0=self.sg[k][:], in1=self.ps[pu][:], op=ALU.mult),
                 reads=[self.b_sg[k], self.b_ps[pu]], writes=[self.b_hT[f]])

    def down_ln(self, k, lhs, b_lhs, nch, width_w, ysc, dst, b_dst, t):
        P = self.P
        for s in range(4):
            P.op("gpsimd", lambda e, s=s: e.tensor_scalar(out=self.xt[k][:, s, :], in0=self.xt[k][:, s, :], scalar1=ALPHA, scalar2=None, op0=ALU.mult),
                 reads=[self.b_xt[k][s]], writes=[self.b_xt[k][s]])
        for s in range(4):
            for n in range(2):
                pp = 6 + (s * 2 + n) % 2
                for f in range(nch):
                    P.op("tensor", lambda e, s=s, n=n, f=f, pp=pp: e.matmul(self.ps[pp][:], lhsT=lhs[:, f, s * 128:(s + 1) * 128],
                                                                          rhs=self.wres[:, f * width_w + n * 512: f * width_w + (n + 1) * 512],
                                                                          start=(f == 0), stop=(f == nch - 1)),
                         reads=[b_lhs[f], self.b_wres[f]], writes=[self.b_ps[pp]], pe_accum=True)
                P.op("vector", lambda e, s=s, n=n, pp=pp: e.scalar_tensor_tensor(out=self.xt[k][:, s, n * 512:(n + 1) * 512], in0=self.ps[pp][:], scalar=ysc,
                                                                                in1=self.xt[k][:, s, n * 512:(n + 1) * 512], op0=ALU.mult, op1=ALU.add),
                     reads=[self.b_ps[pp], self.b_xt[k][s]], writes=[self.b_xt[k][s]])
            self.layernorm(k, s)
        d_ap = dst[t * TT:(t + 1) * TT, :].rearrange("(s p) d -> p s d", p=128)
        P.op("sync", lambda e: e.dma_start(out=d_ap, in_=self.xt[k][:]), reads=self.b_xt[k], writes=[b_dst], dma=True)

    def layernorm(self, k, s):
        P = self.P
        z = self.xt[k]
        for h in range(2):
            P.op("vector", lambda e, h=h: e.bn_stats(self.st[:, s, h, :], z[:, s, h * 512:(h + 1) * 512]),
                 reads=[self.b_xt[k][s]], writes=[self.b_st[s]])
        P.op("vector", lambda e: e.bn_aggr(self.mv[:, s, 0:2], self.st[:, s, :, :]), reads=[self.b_st[s]], writes=[self.b_mv[s]])
        P.op("vector", lambda e: e.tensor_scalar(out=self.mv[:, s, 2:3], in0=self.mv[:, s, 1:2], scalar1=LN_EPS, scalar2=None, op0=ALU.add),
             reads=[self.b_mv[s]], writes=[self.b_mv[s]])
        P.op("scalar", lambda e: e.sqrt(self.mv[:, s, 3:4], self.mv[:, s, 2:3]), reads=[self.b_mv[s]], writes=[self.b_mv[s]])
        P.op("vector", lambda e: e.reciprocal(self.mv[:, s, 2:3], self.mv[:, s, 3:4]), reads=[self.b_mv[s]], writes=[self.b_mv[s]])
        P.op("vector", lambda e: e.tensor_scalar(out=z[:, s, :], in0=z[:, s, :], scalar1=self.mv[:, s, 0:1], scalar2=self.mv[:, s, 2:3],
                                                 op0=ALU.subtract, op1=ALU.mult),
             reads=[self.b_xt[k][s], self.b_mv[s]], writes=[self.b_xt[k][s]])
        P.op("gpsimd", lambda e: e.tensor_tensor(out=z[:, s, :], in0=z[:, s, :], in1=self.lng[:], op=ALU.mult),
             reads=[self.b_xt[k][s], self.b_lng], writes=[self.b_xt[k][s]])
        P.op("gpsimd", lambda e: e.tensor_tensor(out=z[:, s, :], in0=z[:, s, :], in1=self.lnb[:], op=ALU.add),
             reads=[self.b_xt[k][s], self.b_lnb], writes=[self.b_xt[k][s]])

    def stage_ffn(self, pfx, src, b_src, dst, b_dst, tcount, prepped=None, tile_hook=None):
        wd = self.din(pfx + "_wd", [DFF, D], F32)
        g = self.din(pfx + "_lg", [D], F32)
        b = self.din(pfx + "_lb", [D], F32)
        if prepped is None:
            scr, b_scr, its = self.prep_for(pfx)
            for it in its:
                it()
        else:
            scr, b_scr = prepped
        self.load_ln(g, b)
        self.load_res(wd, NF, D)
        self.load_x(src, b_src, 0, tcount % 2)
        for t in range(NTILE):
            k = (tcount + t) % 2
            self.transposes(k)
            self.gate_up(scr, b_scr, t)
            if t + 1 < NTILE:
                self.load_x(src, b_src, t + 1, (tcount + t + 1) % 2)
            self.down_ln(k, self.hT, self.b_hT, NF, D, 0.5, dst, b_dst, t)
            if tile_hook is not None:
                tile_hook(t)
        return tcount + NTILE

    def stage_mix(self, pfx, ym, b_ym, src, b_src, dst, b_dst, tcount, tile_hook=None):
        wo = self.din(pfx + "_w_out", [D, D], F32)
        g = self.din(pfx + "_ln2_g", [D], F32)
        b = self.din(pfx + "_ln2_b", [D], F32)
        self.load_ln(g, b)
        self.load_res(wo, 8, D)
        self.load_x(src, b_src, 0, tcount % 2)
        for t in range(NTILE):
            k = (tcount + t) % 2
            if t + 1 < NTILE:
                self.load_x(src, b_src, t + 1, (tcount + t + 1) % 2)
            y_ap = ym[:, t * TT:(t + 1) * TT].rearrange("(c p) t -> p c t", p=128)
            self.P.op("gpsimd", lambda e, y_ap=y_ap: e.dma_start(out=self.hT[:, 0:8, :], in_=y_ap), reads=[b_ym], writes=self.b_hT[0:8], dma=True)
            self.down_ln(k, self.hT, self.b_hT, 8, D, 1.0, dst, b_dst, t)
            if tile_hook is not None:
                tile_hook(t)
        return tcount + NTILE

    def stage_proj(self, pfx, src, b_src, outs, tcount):
        P = self.P
        win = self.din(pfx + "_w_in", [D, DIN], F32)
        bgate = self.din(pfx + "_b_gate", [24], F32)
        qT, kT, uT, vs, vw, gates = (outs[k][0] for k in ("qT", "kT", "uT", "vs", "vw", "gates"))
        bd = {k: outs[k][1] for k in ("qT", "kT", "uT", "vs", "vw", "gates")}
        P.op("sync", lambda e: e.dma_start(out=self.bg[:], in_=bgate.partition_broadcast(128)), writes=[self.b_bg], dma=True)
        for c in range(8):
            k = c % 2
            P.op("sync", lambda e, k=k, c=c: e.dma_start(out=self.stg[k][:, 0:DIN], in_=win[c * 128:(c + 1) * 128, :]), writes=[self.b_stg[k]], dma=True)
            self.cast(self.cast_eng(), self.wres[:, c * DIN:(c + 1) * DIN], self.stg[k][:, 0:DIN], [self.b_stg[k]], [self.b_wres[c]])
        W = lambda c, a, b_: self.wres[:, c * DIN + a: c * DIN + b_]
        for t in range(NTILE):
            k = (tcount + t) % 2
            self.load_x(src, b_src, t, k)
            self.transposes(k)
            tok = slice(t * TT, (t + 1) * TT)
            for i, col in enumerate(TCOLS):
                pp = 2 + i % 4
                for c in range(8):
                    P.op("tensor", lambda e, c=c, col=col, pp=pp: e.matmul(self.ps[pp][:, :], lhsT=W(c, col, col + 128), rhs=self.xT[:, c, :],
                                                                         start=(c == 0), stop=(c == 7)),
                         reads=[self.b_wres[c], self.b_xT[c]], writes=[self.b_ps[pp]], pe_accum=True)
                if i < 8:
                    P.op("scalar", lambda e, i=i, pp=pp: e.mul(self.oT[0:96, i, :], self.ps[pp][0:96, :], QSCALE),
                         reads=[self.b_ps[pp]], writes=[self.b_oT[i]])
                else:
                    P.op("vector", lambda e, i=i, pp=pp: e.tensor_copy(out=self.oT[0:96, i, :], in_=self.ps[pp][0:96, :]),
                         reads=[self.b_ps[pp]], writes=[self.b_oT[i]])
            P.op("sync", lambda e, tok=tok: e.dma_start(out=qT[:, :, tok].rearrange("h d t -> d h t"), in_=self.oT[:, 0:8, :]),
                 reads=self.b_oT[0:8], writes=[bd["qT"]], dma=True)
            P.op("sync", lambda e, tok=tok: e.dma_start(out=kT[:, :, tok].rearrange("h d t -> d h t"), in_=self.oT[0:96, 8:16, :]),
                 reads=self.b_oT[8:16], writes=[bd["kT"]], dma=True)
            for j in range(2):
                pp = 6 + j
                for c in range(8):
                    P.op("tensor", lambda e, c=c, j=j, pp=pp: e.matmul(self.ps[pp][:], lhsT=W(c, j * 128, (j + 1) * 128), rhs=self.xT[:, c, :],
                                                                     start=(c == 0), stop=(c == 7)),
                         reads=[self.b_wres[c], self.b_xT[c]], writes=[self.b_ps[pp]], pe_accum=True)
                P.op("vector", lambda e, j=j, pp=pp: e.tensor_copy(out=self.ou[:, j, :], in_=self.ps[pp][:]), reads=[self.b_ps[pp]], writes=[self.b_ou[j]])
            for j2 in range(2):
                for half in range(2):
                    P.op("sync", lambda e, tok=tok, j2=j2, half=half: e.dma_start(out=uT[2 * j2 + half][:, tok], in_=self.ou[half * 64:(half + 1) * 64, j2, :]),
                         reads=[self.b_ou[j2]], writes=[bd["uT"]], dma=True)
            for s in range(4):
                pp = 2 + s
                for c in range(8):
                    P.op("tensor", lambda e, c=c, s=s, pp=pp: e.matmul(self.ps[pp][:, 0:192], lhsT=self.xT[:, c, s * 128:(s + 1) * 128], rhs=W(c, 1600, 1792),
                                                                     start=(c == 0), stop=(c == 7)),
                         reads=[self.b_wres[c], self.b_xT[c]], writes=[self.b_ps[pp]], pe_accum=True)
                for c in range(8):
                    P.op("tensor", lambda e, c=c, s=s, pp=pp: e.matmul(self.ps[pp][:, 256:472], lhsT=self.xT[:, c, s * 128:(s + 1) * 128], rhs=W(c, 1984, 2200),
                                                                     start=(c == 0), stop=(c == 7), skip_group_check=True),
                         reads=[self.b_wres[c], self.b_xT[c]], writes=[self.b_ps[pp]], pe_accum=True)
                P.op("vector", lambda e, s=s, pp=pp: e.tensor_copy(out=self.ov[:, s, 0:192], in_=self.ps[pp][:, 0:192]), reads=[self.b_ps[pp]], writes=[self.b_ov[s]])
                P.op("vector", lambda e, s=s, pp=pp: e.tensor_copy(out=self.ov[:, s, 192:384], in_=self.ps[pp][:, 256:448]), reads=[self.b_ps[pp]], writes=[self.b_ov[s]])
                P.op("vector", lambda e, s=s, pp=pp: e.tensor_tensor(out=self.og[:, s, :], in0=self.ps[pp][:, 448:472], in1=self.bg[:], op=ALU.add),
                     reads=[self.b_ps[pp], self.b_bg], writes=[self.b_og[s]])
                P.op("scalar", lambda e, s=s: e.activation(out=self.og[:, s, :], in_=self.og[:, s, :], func=AF.Sigmoid), reads=[self.b_og[s]], writes=[self.b_og[s]])
            for g_ in range(2):
                P.op("gpsimd", lambda e, tok=tok, g_=g_: e.dma_start(out=vs[g_, tok, :].rearrange("(s p) d -> p s d", p=128), in_=self.ov[:, :, 96 * g_:96 * g_ + 96]),
                     reads=self.b_ov, writes=[bd["vs"]], dma=True)
                P.op("gpsimd", lambda e, tok=tok, g_=g_: e.dma_start(out=vw[g_, tok, :].rearrange("(s p) d -> p s d", p=128), in_=self.ov[:, :, 192 + 96 * g_:192 + 96 * g_ + 96]),
                     reads=self.b_ov, writes=[bd["vw"]], dma=True)
            P.op("gpsimd", lambda e, tok=tok: e.dma_start(out=gates[tok, :].rearrange("(s p) d -> p s d", p=128), in_=self.og[:]),
                 reads=self.b_og, writes=[bd["gates"]], dma=True)
        return tcount + NTILE


S = 16384
NQT = S // 512
NEG = -30000.0
VP = 128
GELU_C = float(2.0 * (2.0 / np.pi) ** 0.5)


def b_consts():
    import ml_dtypes
    c = {}
    c["ident_f"] = np.eye(128, dtype=np.float32)
    c["ident_b"] = np.eye(128, dtype=np.float32).astype(ml_dtypes.bfloat16)
    k = np.arange(128)[:, None]
    t = np.arange(512)[None, :]
    masks = np.zeros((16, 128, 512), np.float32)
    for i in range(8):
        kp = 128 * i - 512 + k
        masks[i] = np.where((t - kp >= 0) & (t - kp < 512), 0.0, NEG)
    for r in range(4):
        masks[8 + r] = np.where(16 * k + 31 <= 512 * r + t, 0.0, NEG)
    for i in range(4):
        masks[12 + i] = np.where(128 * i + k <= t, 0.0, NEG)
    c["masks"] = masks.astype(ml_dtypes.bfloat16)
    kk = np.arange(2048)[None, :]
    r = np.arange(32)[:, None]
    c["kpat"] = (((kk // 64) % 32) == r).astype(np.float32).astype(ml_dtypes.bfloat16)
    W = np.zeros((1024, 256), np.float32)
    for j in range(256):
        for n, w in ((4 * j - 1, 0.5), (4 * j, 1.0), (4 * j + 1, 1.0), (4 * j + 2, 1.0), (4 * j + 3, 0.5)):
            if 0 <= n < 1023:
                W[n, j] = w
    c["wimp"] = W.reshape(8, 128, 256).astype(ml_dtypes.bfloat16)
    fp = np.zeros((128, 4), np.float32)
    fp[:64, 0] = 1000.0; fp[:64, 1] = 2000.0
    fp[64:, 1] = 2000.0; fp[64:, 2] = 3000.0
    c["fpat"] = fp
    return c


def pool_consts(j):
    w = 2 ** (j + 1)
    a = np.zeros((64, 4), np.float32)
    a[:, j] = 1.0 / w
    A = np.zeros((64, 4, 16), np.float32)
    tt = np.arange(16)
    A[:, j, :] = 1.0 / np.minimum(tt + 1, w)
    return a, A


def _V(x, e):
    return x(e) if callable(x) else x


class BB:
    def __init__(self, nc, P, dram):
        self.nc = nc
        self.P = P
        self.dram = dram

    def din(self, name, shape, dt):
        return self.dram.din(name, shape, dt)

    def build(self, pfx, src, ymc, b_ymc):
        P = self.P
        nc = self.nc
        op = P.op
        qT, b_qT = src["qT"]; gates, b_gates = src["gates"]
        kcT, b_kcT = src["kcT"]; vcT, b_vcT = src["vcT"]; ksT, b_ksT = src["ksT"]; kwT, b_kwT = src["kwT"]
        vs, b_vs = src["vs"]; vw, b_vw = src["vw"]; uT, b_uT = src["uT"]
        cw = {}
        for kv in ("k", "v"):
            cw[kv + "w1"] = self.din(pfx + "_cmp_%s_w1" % kv, [3072, 96], F32)
            cw[kv + "w2"] = self.din(pfx + "_cmp_%s_w2" % kv, [96, 96], F32)
            cw[kv + "pos"] = self.din(pfx + "_cmp_%s_pos" % kv, [32, 96], F32)
        pool_w = self.din(pfx + "_pool_w", [64, 64], F32)
        pool_sc = self.din(pfx + "_pool_sc", [64, 1], F32)
        pool_a = self.din("pool_a", [64, 4], F32)
        pool_A = self.din("pool_A", [64, 4, 16], F32)
        d_identf = self.din("ident", [128, 128], F32)
        d_identb = self.din("ident_b", [128, 128], BF16)
        d_masks = self.din("masks", [16, 128, 512], BF16)
        d_kpat = self.din("kpat", [32, 2048], BF16)
        d_wimp = self.din("wimp", [8, 128, 256], BF16)
        d_fpat = self.din("fpat", [128, 4], F32)
        d_gsel = self.din("gsel", [6, 24], F32)
        def ysl(r0, r1, t0, n):
            return ymc[t0 // 4096, r0:r1, (t0 % 4096):(t0 % 4096) + n]

        ps_s = [P.psum("ps_s%d" % i, [128, 512]) for i in range(2)]; b_ps_s = [P.pbuf() for i in range(2)]
        ps_o = [P.psum("ps_o%d" % i, [128, 512]) for i in range(2)]; b_ps_o = [P.pbuf() for i in range(2)]
        ps_i = [P.psum("ps_i%d" % i, [128, 512]) for i in range(2)]; b_ps_i = [P.pbuf() for i in range(2)]
        ps_t = [P.psum("ps_t%d" % i, [128, 512]) for i in range(2)]; b_ps_t = [P.pbuf() for i in range(2)]

        idf = P.sbuf("idf", [128, 128], F32); b_idf = P.buf()
        idb = P.sbuf("idb", [128, 128], BF16); b_idb = P.buf()
        masks = P.sbuf("masks", [128, 16, 512], BF16); b_masks = P.buf()
        wimp = P.sbuf("wimp", [128, 8, 256], BF16); b_wimp = P.buf()
        fpat = P.sbuf("fpat", [128, 4], F32); b_fpat = P.buf()
        ksa = P.sbuf("ksa", [128, S], BF16); b_ksa = P.buf()
        vsa = P.sbuf("vsa", [128, 128, VP], BF16); b_vsa = P.buf()
        kcmpT = P.sbuf("kcmpT", [96, 1024], BF16); b_kcmpT = P.buf()
        vcmpa = P.sbuf("vcmpa", [128, 8, VP], BF16); b_vcmpa = P.buf()
        op("sync", lambda e: e.dma_start(out=idf[:], in_=d_identf), writes=[b_idf], dma=True)
        op("sync", lambda e: e.dma_start(out=idb[:], in_=d_identb), writes=[b_idb], dma=True)
        op("sync", lambda e: e.dma_start(out=masks[:], in_=d_masks.rearrange("m p t -> p m t")), writes=[b_masks], dma=True)
        op("sync", lambda e: e.dma_start(out=wimp[:], in_=d_wimp.rearrange("c p j -> p c j")), writes=[b_wimp], dma=True)
        op("sync", lambda e: e.dma_start(out=fpat[:], in_=d_fpat), writes=[b_fpat], dma=True)
        gsel = P.sbuf("gsel", [128, 6, 24], F32); b_gsel = P.buf()
        op("sync", lambda e: e.dma_start(out=gsel[:].rearrange("p a b -> p (a b)"), in_=d_gsel.rearrange("a b -> (a b)").partition_broadcast(128)), writes=[b_gsel], dma=True)
        op("gpsimd", lambda e: e.memset(kcmpT[:], 0.0), writes=[b_kcmpT])
        op("gpsimd", lambda e: e.memset(vcmpa[:], 1.0), writes=[b_vcmpa])
        op("gpsimd", lambda e: e.memset(vsa[:], 1.0), writes=[b_vsa])

        q_sb = [P.sbuf("q_sb%d" % i, [96, 4, 512], BF16) for i in range(2)]; b_q = [P.buf() for i in range(2)]
        g_sb = [P.sbuf("g_sb%d" % i, [128, 4, 6], F32) for i in range(2)]; b_g = [P.buf() for i in range(2)]
        g_full = P.sbuf("g_full", [128, 4, 24], F32); b_gfull = P.buf()
        g_tmp = P.sbuf("g_tmp", [128, 4, 6, 24], F32); b_gtmp = P.buf()
        kw_sb = [P.sbuf("kw_sb%d" % i, [96, 1024], BF16) for i in range(2)]; b_kw = [P.buf() for i in range(2)]
        vwa = [P.sbuf("vwa%d" % i, [128, 8, VP], BF16) for i in range(2)]; b_vwa = [P.buf() for i in range(2)]
        e_all = [P.sbuf("e_all%d" % i, [128, 8, 512], BF16) for i in range(2)]; b_eall = [P.buf() for i in range(2)]
        e_sb = [P.sbuf("e_sb%d" % i, [128, 512], BF16) for i in range(3)]; b_e = [P.buf() for i in range(3)]
        oTs = [P.sbuf("oTs%d" % i, [97, 512], F32) for i in range(2)]; b_oTs = [P.buf() for i in range(2)]
        impacc = P.sbuf("impacc", [128, 4, 258], F32); b_imp = [P.buf() for i in range(4)]
        mr = P.sbuf("mr", [128, 256], F32); b_mr = P.buf()
        m8 = P.sbuf("m8", [128, 3, 8], F32); b_m8 = P.buf()
        selb = P.sbuf("selb", [128, 4, 352], F32); b_selb = [P.buf() for i in range(4)]
        qaug = P.sbuf("qaug", [128, 2, 8, 512], BF16); b_qaug = [[P.buf() for v in range(8)] for h in range(2)]
        oacc = [P.sbuf("oacc%d" % i, [128, 4, 2, 96], F32) for i in range(2)]; b_oacc = [P.buf() for i in range(2)]
        oTo = [P.sbuf("oTo%d" % i, [96, 2, 512], BF16) for i in range(2)]; b_oTo = [P.buf() for i in range(2)]
        rd = P.sbuf("rd", [128, 8, 4], F32); b_rd = P.buf()
        qa_f = qaug[:].rearrange("p a b c -> p (a b c)").bitcast(F32)
        ea0 = e_all[0][:].rearrange("p a b -> p (a b)")
        ea1_f = e_all[1][:].rearrange("p a b -> p (a b)").bitcast(F32)
        w1s = qa_f[0:96, 0:3072].rearrange("d (p e) -> d p e", e=96); b_w1s = P.buf()
        gx = qa_f[0:96, 3072:4096]; b_gx = P.buf()
        w1b = ea0[0:96, 0:3072].rearrange("d (p e) -> d p e", e=96); b_w1b = P.buf()
        gb = ea0[0:96, 3072:4096]; b_gb = P.buf()
        gy = ea1_f[0:96, 0:1024]; b_gy = P.buf()
        w2s = P.sbuf("w2s", [96, 96], F32); b_w2s = P.buf()
        w2b = P.sbuf("w2b", [96, 96], BF16); b_w2b = P.buf()
        poss = P.sbuf("poss", [32, 96], F32); b_poss = P.buf()
        posT = P.sbuf("posT", [96, 32], BF16); b_posT = P.buf()
        cb = P.sbuf("cb", [96, 1], F32); b_cb = P.buf()
        stage = getattr(self, 'stage', 99)
        for i in range(8):
            op("gpsimd", lambda e, i=i: e.dma_start(out=ksa[96:128, i * 2048:(i + 1) * 2048], in_=d_kpat), writes=[b_ksa], dma=True)
        for i in range(16 if stage >= 2 else 0):
            op("gpsimd", lambda e, i=i: e.dma_start(out=vsa[:, i * 8:(i + 1) * 8, 0:96],
                                                                        in_=_V(vs, e)[i * 1024:(i + 1) * 1024, :].rearrange("(k p) d -> p k d", p=128)),
               reads=[b_vs], writes=[b_vsa], dma=True)

        for kv, srcT, b_srcT in ((("k", kcT, b_kcT), ("v", vcT, b_vcT)) if stage >= 1 else ()):
            op("sync", lambda e, srcT=srcT: e.dma_start(out=ksa[0:96, :], in_=srcT), reads=[b_srcT], writes=[b_ksa], dma=True)
            op("sync", lambda e, kv=kv: e.dma_start(out=w1s, in_=cw[kv + "w1"].rearrange("(p d) e -> d p e", d=96)), writes=[b_w1s], dma=True)
            op("sync", lambda e, kv=kv: e.dma_start(out=w2s[:], in_=cw[kv + "w2"]), writes=[b_w2s], dma=True)
            op("sync", lambda e, kv=kv: e.dma_start(out=poss[:], in_=cw[kv + "pos"]), writes=[b_poss], dma=True)
            op("vector", lambda e: e.tensor_copy(out=w1b, in_=w1s), reads=[b_w1s], writes=[b_w1b])
            op("vector", lambda e: e.tensor_copy(out=w2b[:], in_=w2s[:]), reads=[b_w2s], writes=[b_w2b])
            op("tensor", lambda e: e.transpose(out=ps_t[0][0:96, 0:32], in_=poss[:], identity=idf[0:32, 0:32]), reads=[b_poss, b_idf], writes=[b_ps_t[0]], pe_accum=True)
            op("vector", lambda e: e.tensor_copy(out=posT[:], in_=ps_t[0][0:96, 0:32]), reads=[b_ps_t[0]], writes=[b_posT])
            for p in range(32):
                op("tensor", lambda e, p=p: e.matmul(ps_t[1][0:96, 0:1], lhsT=w1b[:, p, :], rhs=posT[:, p:p + 1], start=(p == 0), stop=(p == 31)),
                   reads=[b_w1b, b_posT], writes=[b_ps_t[1]], pe_accum=True)
            op("vector", lambda e: e.tensor_copy(out=cb[:], in_=ps_t[1][0:96, 0:1]), reads=[b_ps_t[1]], writes=[b_cb])
            if stage < 1.2:
                continue
            op("gpsimd", lambda e: e.memset(gb, 0.0), writes=[b_gb])
            kview = ksa[0:96, :].rearrange("d (n s) -> d n s", s=16)
            for half, (n0, N) in enumerate(((0, 512), (512, 511))):
                pp = ps_s[half]
                for p in range(32):
                    rhs = kview[:, n0:n0 + N, p] if p < 16 else kview[:, n0 + 1:n0 + 1 + N, p - 16]
                    op("tensor", lambda e, p=p, rhs=rhs, pp=pp, N=N: e.matmul(pp[0:96, 0:N], lhsT=w1b[:, p, :], rhs=rhs, start=(p == 0), stop=(p == 31)),
                       reads=[b_w1b, b_ksa], writes=[b_ps_s[half]], pe_accum=True)
                sl = slice(n0, n0 + N)
                op("vector", lambda e, pp=pp, N=N, sl=sl: e.tensor_scalar(out=gx[:, sl], in0=pp[0:96, 0:N], scalar1=cb[:, 0:1], scalar2=None, op0=ALU.add),
                   reads=[b_ps_s[half], b_cb], writes=[b_gx])
                op("vector", lambda e, sl=sl: e.tensor_tensor(out=gy[:, sl], in0=gx[:, sl], in1=gx[:, sl], op=ALU.mult), reads=[b_gx], writes=[b_gy])
                op("vector", lambda e, sl=sl: e.tensor_scalar(out=gy[:, sl], in0=gy[:, sl], scalar1=0.044715, scalar2=1.0, op0=ALU.mult, op1=ALU.add), reads=[b_gy], writes=[b_gy])
                op("vector", lambda e, sl=sl: e.tensor_tensor(out=gy[:, sl], in0=gy[:, sl], in1=gx[:, sl], op=ALU.mult), reads=[b_gy, b_gx], writes=[b_gy])
                op("scalar", lambda e, sl=sl: e.activation(out=gy[:, sl], in_=gy[:, sl], func=AF.Sigmoid, scale=GELU_C), reads=[b_gy], writes=[b_gy])
                op("vector", lambda e, sl=sl: e.tensor_tensor(out=gb[:, sl], in0=gy[:, sl], in1=gx[:, sl], op=ALU.mult), reads=[b_gy, b_gx], writes=[b_gb])
            if stage < 1.3:
                continue
            if kv == "k":
                for half, (n0, N) in enumerate(((0, 512), (512, 511))):
                    op("tensor", lambda e, half=half, n0=n0, N=N: e.matmul(ps_o[half][0:96, 0:N], lhsT=w2b[:], rhs=gb[:, n0:n0 + N], start=True, stop=True),
                       reads=[b_w2b, b_gb], writes=[b_ps_o[half]], pe_accum=True)
                    op("vector", lambda e, half=half, n0=n0, N=N: e.tensor_copy(out=kcmpT[:, n0:n0 + N], in_=ps_o[half][0:96, 0:N]), reads=[b_ps_o[half]], writes=[b_kcmpT])
            else:
                for c in range(8):
                    k2 = c % 2
                    op("tensor", lambda e, c=c, k2=k2: e.matmul(ps_o[k2][:, 0:96], lhsT=gb[:, c * 128:(c + 1) * 128], rhs=w2b[:], start=True, stop=True),
                       reads=[b_w2b, b_gb], writes=[b_ps_o[k2]], pe_accum=True)
                    op("vector", lambda e, c=c, k2=k2: e.tensor_copy(out=vcmpa[:, c, 0:96], in_=ps_o[k2][:, 0:96]), reads=[b_ps_o[k2]], writes=[b_vcmpa])

        stage = getattr(self, 'stage', 99)
        op("sync", lambda e: e.dma_start(out=ksa[0:96, :], in_=ksT), reads=[b_ksT], writes=[b_ksa], dma=True)
        CH = 512
        pw_s = P.sbuf("pw_s", [64, 64], F32); b_pw_s = P.buf()
        pw_b = P.sbuf("pw_b", [64, 64], BF16); b_pw_b = P.buf()
        psc = P.sbuf("psc", [64, 1], F32); b_psc = P.buf()
        pa = P.sbuf("pa", [64, 4], F32); b_pa = P.buf()
        pA = P.sbuf("pA", [64, 4, 16], F32); b_pA = P.buf()
        L_ = 16 + CH
        ub = [P.sbuf("ub%d" % i, [64, L_], F32)[:] for i in range(2)]; b_ub = [P.buf() for i in range(2)]
        sw = [P.sbuf("sw%d" % i, [64, L_], F32)[:] for i in range(4)]; b_sw = [P.buf() for i in range(4)]
        acc = P.sbuf("pacc", [64, CH], F32)[:]; b_acc = P.buf()
        a16 = P.sbuf("a16", [64, 2, 16], F32)[:]; b_a16 = P.buf()
        accb = P.sbuf("paccb", [64, CH], BF16)[:]; b_accb = P.buf()
        pout = [P.sbuf("pout%d" % i, [64, CH], BF16)[:] for i in range(2)]; b_pout = [P.buf() for i in range(2)]
        op("sync", lambda e: e.dma_start(out=pw_s[:], in_=pool_w), writes=[b_pw_s], dma=True)
        op("sync", lambda e: e.dma_start(out=psc[:], in_=pool_sc), writes=[b_psc], dma=True)
        op("sync", lambda e: e.dma_start(out=pa[:], in_=pool_a), writes=[b_pa], dma=True)
        op("sync", lambda e: e.dma_start(out=pA[:], in_=pool_A), writes=[b_pA], dma=True)
        op("vector", lambda e: e.tensor_copy(out=pw_b[:], in_=pw_s[:]), reads=[b_pw_s], writes=[b_pw_b])
        def pool_chunk(ci):
            k = ci % 2
            u = ub[k]
            if ci == 0:
                op("gpsimd", lambda e, u=u: e.memset(u[:, 0:16], 0.0), writes=[b_ub[k]])
                op("sync", lambda e, u=u: e.dma_start(out=u[:, 16:], in_=uT[:, 0:CH]), reads=[b_uT], writes=[b_ub[k]], dma=True)
            else:
                op("sync", lambda e, u=u, ci=ci: e.dma_start(out=u, in_=uT[:, ci * CH - 16:(ci + 1) * CH]), reads=[b_uT], writes=[b_ub[k]], dma=True)
            L = 16 + CH
            prev, b_prev = u, b_ub[k]
            for wi, sh in enumerate((1, 2, 4, 8)):
                lo = 2 * sh - 1
                dst = sw[wi]
                op("gpsimd", lambda e, dst=dst, prev=prev, lo=lo, sh=sh, L=L: e.tensor_tensor(out=dst[:, lo:L], in0=prev[:, lo:L], in1=prev[:, lo - sh:L - sh], op=ALU.add),
                   reads=[b_prev], writes=[b_sw[wi]])
                prev, b_prev = dst, b_sw[wi]
            op("vector", lambda e: e.tensor_scalar(out=acc, in0=sw[0][:, 16:], scalar1=pa[:, 0:1], scalar2=None, op0=ALU.mult), reads=[b_sw[0], b_pa], writes=[b_acc])
            for wi in range(1, 4):
                op("vector", lambda e, wi=wi: e.scalar_tensor_tensor(out=acc, in0=sw[wi][:, 16:], scalar=pa[:, wi:wi + 1], in1=acc, op0=ALU.mult, op1=ALU.add),
                   reads=[b_sw[wi], b_pa, b_acc], writes=[b_acc])
            if ci == 0:
                op("vector", lambda e: e.tensor_tensor(out=a16[:, 0, :], in0=sw[0][:, 16:32], in1=pA[:, 0, :], op=ALU.mult), reads=[b_sw[0], b_pA], writes=[b_a16])
                for wi in range(1, 4):
                    op("vector", lambda e, wi=wi: e.tensor_tensor(out=a16[:, 1, :], in0=sw[wi][:, 16:32], in1=pA[:, wi, :], op=ALU.mult), reads=[b_sw[wi], b_pA, b_a16], writes=[b_a16])
                    op("vector", lambda e: e.tensor_tensor(out=a16[:, 0, :], in0=a16[:, 0, :], in1=a16[:, 1, :], op=ALU.add), reads=[b_a16], writes=[b_a16])
                op("vector", lambda e: e.tensor_copy(out=acc[:, 0:16], in_=a16[:, 0, :]), reads=[b_a16, b_acc], writes=[b_acc])
            op("vector", lambda e, u=u: e.tensor_tensor(out=accb, in0=acc, in1=u[:, 16:], op=ALU.subtract), reads=[b_acc, b_ub[k]], writes=[b_accb])
            for hh in range(CH // 512):
                pbi = nxt("t", 2)
                op("tensor", lambda e, hh=hh, pbi=pbi: e.matmul(ps_t[pbi][0:64, :], lhsT=pw_b[:], rhs=accb[:, hh * 512:(hh + 1) * 512], start=True, stop=True),
                   reads=[b_pw_b, b_accb], writes=[b_ps_t[pbi]], pe_accum=True)
                op("vector", lambda e, hh=hh, k=k, pbi=pbi: e.tensor_scalar(out=pout[k][:, hh * 512:(hh + 1) * 512], in0=ps_t[pbi][0:64, :], scalar1=psc[:, 0:1], scalar2=None, op0=ALU.mult),
                   reads=[b_ps_t[pbi], b_psc], writes=[b_pout[k]])
            op("sync", lambda e, k=k, ci=ci: e.dma_start(out=ysl(0, 64, ci * CH, CH), in_=pout[k]), reads=[b_pout[k]], writes=[b_ymc], dma=True)


        P.barrier()

        for i in range(2):
            op("gpsimd", lambda e, i=i: e.memset(vwa[i][:], 1.0), writes=[b_vwa[i]])
        op("gpsimd", lambda e: e.memset(selb[:], 0.0), writes=b_selb)
        op("gpsimd", lambda e: e.memset(impacc[:], 0.0), writes=b_imp)

        cnt = {"s": 0, "o": 0, "e": 0, "t": 0, "rd": 0}

        def nxt(key, n):
            v = cnt[key] % n
            cnt[key] += 1
            return v

        def load_tile(qt):
            k = qt % 2
            tok = slice(qt * 512, (qt + 1) * 512)
            op("sync", lambda e: e.dma_start(out=q_sb[k][:], in_=qT[:, :, tok].rearrange("h d t -> d h t")), reads=[b_qT], writes=[b_q[k]], dma=True)
            op("sync", lambda e: e.dma_start(out=g_full[:], in_=gates[tok, :].rearrange("(s p) c -> p s c", p=128)), reads=[b_gates], writes=[b_gfull], dma=True)
            op("gpsimd", lambda e: e.tensor_tensor(out=g_tmp[:], in0=g_full[:].unsqueeze(2).to_broadcast([128, 4, 6, 24]), in1=gsel[:].unsqueeze(1).to_broadcast([128, 4, 6, 24]), op=ALU.mult),
               reads=[b_gfull, b_gsel], writes=[b_gtmp])
            op("vector", lambda e: e.tensor_reduce(out=g_sb[k][:], in_=g_tmp[:], axis=AX.X, op=ALU.add), reads=[b_gtmp], writes=[b_g[k]])
            if qt == 0:
                op("sync", lambda e: e.dma_start(out=kw_sb[k][:, 512:1024], in_=kwT[:, 0:512]), reads=[b_kwT], writes=[b_kw[k]], dma=True)
                op("sync", lambda e: e.dma_start(out=vwa[k][:, 4:8, 0:96], in_=_V(vw, e)[0:512, :].rearrange("(k p) d -> p k d", p=128)), reads=[b_vw], writes=[b_vwa[k]], dma=True)
            else:
                op("sync", lambda e: e.dma_start(out=kw_sb[k][:], in_=kwT[:, qt * 512 - 512: qt * 512 + 512]), reads=[b_kwT], writes=[b_kw[k]], dma=True)
                op("sync", lambda e: e.dma_start(out=vwa[k][:, :, 0:96], in_=_V(vw, e)[qt * 512 - 512: qt * 512 + 512, :].rearrange("(k p) d -> p k d", p=128)), reads=[b_vw], writes=[b_vwa[k]], dma=True)

        def finish_branch(qt, po, h_own, gcol, first, rd_out=None):
            k = qt % 2
            bi = nxt("t", 2)
            osb = oTs[bi]
            op("vector", lambda e: e.tensor_copy(out=osb[:], in_=ps_o[po][0:97, :]), reads=[b_ps_o[po]], writes=[b_oTs[bi]])
            pt = ps_t[bi]
            for sub in range(4):
                op("tensor", lambda e, sub=sub: e.transpose(out=pt[:, sub * 128: sub * 128 + 97], in_=osb[0:97, sub * 128:(sub + 1) * 128], identity=idf[0:97, 0:97]),
                   reads=[b_oTs[bi], b_idf], writes=[b_ps_t[bi]], pe_accum=True)
            ri = nxt("rd", 8)
            ptv = pt[:].rearrange("p (s c) -> p s c", c=128)
            op("vector", lambda e: e.tensor_scalar(out=rd[:, ri, :], in0=ptv[:, :, 96], scalar1=1e-30, scalar2=None, op0=ALU.max), reads=[b_ps_t[bi]], writes=[b_rd])
            op("vector", lambda e: e.reciprocal(rd[:, ri, :], rd[:, ri, :]), reads=[b_rd], writes=[b_rd])
            if h_own is not None:
                ci = nxt("rd", 8)
                op("vector", lambda e: e.tensor_tensor(out=rd[:, ci, :], in0=rd[:, ri, :], in1=g_sb[k][:, :, h_own * 3 + gcol], op=ALU.mult), reads=[b_rd, b_g[k]], writes=[b_rd])
                dbg = getattr(self, "dbg_branch", None)
                if dbg is not None and dbg != gcol:
                    op("vector", lambda e: e.memset(rd[:, ci, :], 0.0), reads=[b_rd], writes=[b_rd])
                for sub in range(4):
                    if first:
                        op("vector", lambda e, sub=sub: e.tensor_scalar(out=oacc[k][:, sub, h_own, :], in0=ptv[:, sub, 0:96], scalar1=rd[:, ci, sub:sub + 1], scalar2=None, op0=ALU.mult),
                           reads=[b_ps_t[bi], b_rd], writes=[b_oacc[k]])
                    else:
                        op("vector", lambda e, sub=sub: e.scalar_tensor_tensor(out=oacc[k][:, sub, h_own, :], in0=ptv[:, sub, 0:96], scalar=rd[:, ci, sub:sub + 1],
                                                                              in1=oacc[k][:, sub, h_own, :], op0=ALU.mult, op1=ALU.add),
                           reads=[b_ps_t[bi], b_rd, b_oacc[k]], writes=[b_oacc[k]])
            return ri

        def cmp_part(qt):
            k = qt % 2
            nct = min(8, qt // 4 + 1)
            for h in range(4):
                ea = e_all[h % 2]
                for c in range(nct):
                    si = nxt("s", 2)
                    partial = qt < 4 * c + 4
                    op("tensor", lambda e, c=c, h=h, si=si, partial=partial: e.matmul(ps_s[si][:], lhsT=kcmpT[:, c * 128:(c + 1) * 128], rhs=q_sb[k][:, h, :], start=True, stop=not partial),
                       reads=[b_kcmpT, b_q[k]], writes=[b_ps_s[si]], pe_accum=True)
                    if partial:
                        r = qt - 4 * c
                        op("tensor", lambda e, si=si, r=r: e.matmul(ps_s[si][:], lhsT=idb[:], rhs=masks[:, 8 + r, :], start=False, stop=True),
                           reads=[b_idb, b_masks], writes=[b_ps_s[si]], pe_accum=True)
                    op("scalar", lambda e, c=c, si=si, ea=ea: e.activation(out=ea[:, c, :], in_=ps_s[si][:], func=AF.Exp), reads=[b_ps_s[si]], writes=[b_eall[h % 2]])
                po = nxt("o", 2)
                for c in range(nct):
                    op("tensor", lambda e, c=c, ea=ea, po=po: e.matmul(ps_o[po][:, :], lhsT=vcmpa[:, c, 0:128], rhs=ea[:, c, :], start=(c == 0), stop=(c == nct - 1)),
                       reads=[b_vcmpa, b_eall[h % 2]], writes=[b_ps_o[po]], pe_accum=True)
                for sub in range(4):
                    pi = ps_i[sub // 2]
                    for c in range(nct):
                        op("tensor", lambda e, c=c, sub=sub, ea=ea, pi=pi: e.matmul(pi[:, (sub % 2) * 256:(sub % 2) * 256 + 256], lhsT=ea[:, c, sub * 128:(sub + 1) * 128], rhs=wimp[:, c, :],
                                                                               start=(c == 0), stop=(c == nct - 1)),
                           reads=[b_wimp, b_eall[h % 2]], writes=[b_ps_i[sub // 2]], pe_accum=True)
                ri = finish_branch(qt, po, h if h < 2 else None, 0, True)
                for sub in range(4):
                    pi = ps_i[sub // 2]
                    src = pi[:, (sub % 2) * 256:(sub % 2) * 256 + 256]
                    if h == 0:
                        op("vector", lambda e, sub=sub, src=src, ri=ri: e.tensor_scalar(out=impacc[:, sub, 1:257], in0=src, scalar1=rd[:, ri, sub:sub + 1], scalar2=None, op0=ALU.mult),
                           reads=[b_ps_i[sub // 2], b_rd], writes=[b_imp[sub]])
                    else:
                        op("vector", lambda e, sub=sub, src=src, ri=ri: e.scalar_tensor_tensor(out=impacc[:, sub, 1:257], in0=src, scalar=rd[:, ri, sub:sub + 1], in1=impacc[:, sub, 1:257],
                                                                                            op0=ALU.mult, op1=ALU.add),
                           reads=[b_ps_i[sub // 2], b_rd, b_imp[sub]], writes=[b_imp[sub]])

        def topk_part(qt):
            for sub in range(4):
                st = 4 * qt + sub
                sc = impacc[:, sub, 1:257]
                op("vector", lambda e, sub=sub, st=st: e.tensor_tensor(out=impacc[:, sub, 2 * st:2 * st + 3], in0=impacc[:, sub, 2 * st:2 * st + 3], in1=fpat[:, 0:3], op=ALU.add),
                   reads=[b_imp[sub], b_fpat], writes=[b_imp[sub]])
                op("vector", lambda e, sub=sub: e.tensor_scalar(out=impacc[:, sub, 1:2], in0=impacc[:, sub, 1:2], scalar1=4000.0, scalar2=None, op0=ALU.add), reads=[b_imp[sub]], writes=[b_imp[sub]])
                op("vector", lambda e, sc=sc: e.max(out=m8[:, 0, :], in_=sc), reads=[b_imp[sub]], writes=[b_m8])
                op("vector", lambda e, sc=sc: e.match_replace(out=mr[:], in_to_replace=m8[:, 0, :], in_values=sc, imm_value=-1e9), reads=[b_imp[sub], b_m8], writes=[b_mr])
                op("vector", lambda e: e.max(out=m8[:, 1, :], in_=mr[:]), reads=[b_mr], writes=[b_m8])
                op("vector", lambda e: e.tensor_reduce(out=m8[:, 2, 0:1], in_=m8[:, 1, :], axis=AX.X, op=ALU.min), reads=[b_m8], writes=[b_m8])
                op("vector", lambda e, sub=sub, sc=sc: e.tensor_scalar(out=selb[:, sub, 96:352], in0=sc, scalar1=m8[:, 2, 0:1], scalar2=NEG, op0=ALU.is_lt, op1=ALU.mult),
                   reads=[b_imp[sub], b_m8], writes=[b_selb[sub]])

        def qaug_part(qt):
            k = qt % 2
            nv = (qt + 1 + 3) // 4
            for v in range(nv):
                bi = nxt("t", 2)
                pt = ps_t[bi]
                for sub in range(4):
                    op("tensor", lambda e, sub=sub, v=v, pt=pt: e.transpose(out=pt[:, sub * 128:(sub + 1) * 128], in_=selb[:, sub, 32 * v:32 * v + 128], identity=idf[:]),
                       reads=[b_selb[sub], b_idf], writes=[b_ps_t[bi]], pe_accum=True)
                for h in range(2):
                    op("vector", lambda e, h=h, v=v, pt=pt: e.tensor_copy(out=qaug[96:128, h, v, :], in_=pt[96:128, :]),
                       reads=[b_ps_t[bi]], writes=[b_qaug[h][v]])
                    op("gpsimd", lambda e, h=h, v=v: e.tensor_copy(out=qaug[0:96, h, v, :], in_=q_sb[k][:, h, :]), reads=[b_q[k]], writes=[b_qaug[h][v]])

        def win_part(qt):
            k = qt % 2
            tiles = list(range(4, 8)) if qt == 0 else list(range(8))
            items = [(h, n, i) for h in range(2) for n, i in enumerate(tiles)]
            pos = {}
            pend = None

            def pv(item):
                h, n, i = item
                ei, po = pos[item]
                op("tensor", lambda e, i=i, ei=ei, po=po, n=n: e.matmul(ps_o[po][:, :], lhsT=vwa[k][:, i, 0:128], rhs=e_sb[ei][:], start=(n == 0), stop=(n == len(tiles) - 1)),
                   reads=[b_vwa[k], b_e[ei]], writes=[b_ps_o[po]], pe_accum=True)
                if n == len(tiles) - 1:
                    finish_branch(qt, po, h, 2, False)

            po_h = {}
            for item in items:
                h, n, i = item
                if n == 0:
                    po_h[h] = nxt("o", 2)
                si = nxt("s", 2)
                ei = nxt("e", 3)
                pos[item] = (ei, po_h[h])
                op("tensor", lambda e, i=i, si=si, h=h: e.matmul(ps_s[si][:], lhsT=kw_sb[k][:, i * 128:(i + 1) * 128], rhs=q_sb[k][:, h, :], start=True, stop=False),
                   reads=[b_kw[k], b_q[k]], writes=[b_ps_s[si]], pe_accum=True)
                op("tensor", lambda e, i=i, si=si: e.matmul(ps_s[si][:], lhsT=idb[:], rhs=masks[:, i, :], start=False, stop=True),
                   reads=[b_idb, b_masks], writes=[b_ps_s[si]], pe_accum=True)
                op("scalar", lambda e, si=si, ei=ei: e.activation(out=e_sb[ei][:], in_=ps_s[si][:], func=AF.Exp), reads=[b_ps_s[si]], writes=[b_e[ei]])
                if pend is not None:
                    pv(pend)
                pend = item
            pv(pend)

        def slc_part(qt):
            k = qt % 2
            nkt = 4 * (qt + 1)
            items = [(h, kt) for h in range(2) for kt in range(nkt)]
            pos = {}
            pend = None
            po_h = {}

            def pv(item):
                h, kt = item
                ei, po = pos[item]
                op("tensor", lambda e, kt=kt, ei=ei, po=po: e.matmul(ps_o[po][:, :], lhsT=vsa[:, kt, 0:128], rhs=e_sb[ei][:], start=(kt == 0), stop=(kt == nkt - 1)),
                   reads=[b_vsa, b_e[ei]], writes=[b_ps_o[po]], pe_accum=True)
                if kt == nkt - 1:
                    finish_branch(qt, po, h, 1, False)

            for item in items:
                h, kt = item
                if kt == 0:
                    po_h[h] = nxt("o", 2)
                si = nxt("s", 2)
                ei = nxt("e", 3)
                pos[item] = (ei, po_h[h])
                v = kt // 16
                diag = kt >= 4 * qt
                op("tensor", lambda e, kt=kt, si=si, v=v, diag=diag, h=h: e.matmul(ps_s[si][:], lhsT=ksa[:, kt * 128:(kt + 1) * 128], rhs=qaug[:, h, v, :], start=True, stop=not diag),
                   reads=[b_ksa, b_qaug[h][v]], writes=[b_ps_s[si]], pe_accum=True)
                if diag:
                    op("tensor", lambda e, kt=kt, si=si: e.matmul(ps_s[si][:], lhsT=idb[:], rhs=masks[:, 12 + kt - 4 * qt, :], start=False, stop=True),
                       reads=[b_idb, b_masks], writes=[b_ps_s[si]], pe_accum=True)
                op("scalar", lambda e, si=si, ei=ei: e.activation(out=e_sb[ei][:], in_=ps_s[si][:], func=AF.Exp), reads=[b_ps_s[si]], writes=[b_e[ei]])
                if pend is not None:
                    pv(pend)
                pend = item
            pv(pend)

        def out_part(qt):
            k = qt % 2
            tok = slice(qt * 512, (qt + 1) * 512)
            for h in range(2):
                bi = nxt("t", 2)
                pt = ps_t[bi]
                for sub in range(4):
                    op("tensor", lambda e, sub=sub, h=h, pt=pt: e.transpose(out=pt[0:96, sub * 128:(sub + 1) * 128], in_=oacc[k][:, sub, h, :], identity=idf[:]),
                       reads=[b_oacc[k], b_idf], writes=[b_ps_t[bi]], pe_accum=True)
                op("vector", lambda e, h=h, pt=pt: e.tensor_copy(out=oTo[k][:, h, :], in_=pt[0:96, :]), reads=[b_ps_t[bi]], writes=[b_oTo[k]])
            op("sync", lambda e: e.dma_start(out=ysl(64, 256, qt * 512, 512).rearrange("(h d) t -> d h t", d=96), in_=oTo[k][:]), reads=[b_oTo[k]], writes=[b_ymc], dma=True)

        nq = self.nqt if hasattr(self, "nqt") else NQT
        if stage < 4:
            nq = 0
        if nq > 0:
            load_tile(0)
            cmp_part(0)
            if stage >= 4.2:
                topk_part(0)
        for qt in range(nq):
            if stage >= 4.3:
                qaug_part(qt)
            if stage >= 4.4:
                win_part(qt)
            pool_chunk(qt)
            if qt + 1 < nq and stage >= 4.7:
                load_tile(qt + 1)
                cmp_part(qt + 1)
                topk_part(qt + 1)
            if stage >= 4.5:
                slc_part(qt)
            if stage >= 4.6:
                out_part(qt)
            if stage < 4.7:
                break
import ml_dtypes
from concourse.bass_utils import run_bass_kernel_spmd

_BF = ml_dtypes.bfloat16
_NCORE = 8
_GROUPS = [[0, 1, 2, 3], [4, 5, 6, 7]]
_DBG = {}
_CC_CHAIN = Buf("cc_chain")


def _reset_chain():
    _CC_CHAIN.w = None
    _CC_CHAIN.r = {}


def _gather(P, c_ap, b_c, g_ap, b_g):
    if _DBG.get("nocc"):
        rows = c_ap.shape[0]
        for r in range(4):
            P.op("gpsimd", lambda e, r=r: e.dma_start(out=g_ap[r * rows:(r + 1) * rows, :], in_=c_ap), reads=[b_c], writes=[b_g], dma=True)
        return
    P.op("gpsimd", lambda e: e.collective_compute("AllGather", ALU.bypass, replica_groups=_GROUPS, ins=[c_ap], outs=[g_ap]),
         reads=[b_c, _CC_CHAIN], writes=[b_g, _CC_CHAIN], cc=True)


def build_fused():
    nc = bass.Bass("TRN2", target_bir_lowering=False)
    _reset_chain()
    TB.gus_ctr = 0
    P = Prog(nc)
    dram = Dram(nc, P)
    dram.dump = tuple(_DBG.get("dump", ()))
    ident = dram.din("ident", [128, 128], F32)
    xin = dram.din("xin", [NTOK, D], F32)
    xout = dram.dout("xout", [NTOK, D], F32)
    op = P.op

    jcache = {}

    def jexpr(e):
        if "v" not in jcache:
            pid = e.partition_id()
            j = e.snap(pid % 4, min_val=0, max_val=3)
            g = e.snap(j // 2, min_val=0, max_val=1)
            jo = e.snap(g * 2 + (1 - (j % 2)), min_val=0, max_val=3)
            jcache["v"] = dict(j=j, g=g, jo=jo)
        return jcache["v"]

    cur, b_cur = xin, None
    tcount = 0
    ym_loc = b_ym_loc = None
    for l in range(3):
        P.push_scope()
        tb = TB(nc, P, ident, dram)
        if l > 0:
            pl = "l%d" % (l - 1)
            x2 = dram.dscr("x2", [NTOK, D], F32)
            scr2, b_scr2, its2 = tb.prep_for(pl + "_ffn2")

            def hook(t_, its2=its2):
                for it in its2[2 * t_:2 * t_ + 2]:
                    it()
            tcount = tb.stage_mix(pl, ym_loc, b_ym_loc, cur, b_cur, x2, dram.b["x2"], tcount, tile_hook=hook)
            if l == 2:
                tcount = tb.stage_ffn(pl + "_ffn2", x2, dram.b["x2"], xout, dram.b["xout"], tcount, prepped=(scr2, b_scr2))
                P.pop_scope()
                break
            x3 = dram.dscr("x3", [NTOK, D], F32)
            scr1, b_scr1, its1 = tb.prep_for("l%d_ffn1" % l)

            def hook1(t_, its1=its1):
                tb.prep_cast_eng = "gpsimd"
                for it in its1[2 * t_:2 * t_ + 2]:
                    it()
                tb.prep_cast_eng = None
            tcount = tb.stage_ffn(pl + "_ffn2", x2, dram.b["x2"], x3, dram.b["x3"], tcount, prepped=(scr2, b_scr2), tile_hook=hook1)
            cur, b_cur = x3, dram.b["x3"]
            pre1 = (scr1, b_scr1)
        ll = "l%d" % l
        x1 = dram.dscr("x1", [NTOK, D], F32)
        tcount = tb.stage_ffn(ll + "_ffn1", cur, b_cur, x1, dram.b["x1"], tcount, prepped=(pre1 if l > 0 else None))
        cur, b_cur = x1, dram.b["x1"]
        def ctensor(name, shp, dt):
            ap = dram.dscr(name, shp, dt)
            return ap, dram.b[name]
        def gtensor(name, shp, dt):
            if name not in dram.ap:
                dram.ap[name] = nc.dram_tensor(name, list(shp), dt).ap()
                dram.b[name] = P.buf(name)
            return dram.ap[name], dram.b[name]
        C_q, b_Cq = ctensor("c_q", [8, 128, NTOK], BF16)
        C_u, b_Cu = ctensor("c_u", [4, 64, NTOK], F32)
        C_k, b_Ck = ctensor("c_k", [8, 96, NTOK], BF16)
        C_vs, b_Cvs = ctensor("c_vs", [2, NTOK, 96], BF16)
        C_vw, b_Cvw = ctensor("c_vw", [2, NTOK, 96], BF16)
        C_g, b_Cg = ctensor("c_g", [NTOK, 24], F32)
        G_q, b_Gq = gtensor("g_q", [8, 4, 128, NTOK], BF16)
        G_u, b_Gu = gtensor("g_u", [4, 4, 64, NTOK], F32)
        G_k, b_Gk = gtensor("g_k", [8, 4, 96, NTOK], BF16)
        G_vs, b_Gvs = gtensor("g_vs", [2, 4, NTOK, 96], BF16)
        G_vw, b_Gvw = gtensor("g_vw", [2, 4, NTOK, 96], BF16)
        G_g, b_Gg = ctensor("g_g", [4 * NTOK, 24], F32)
        outs = {"gates": (C_g, b_Cg)}
        outs["qT"] = (C_q, b_Cq)
        outs["uT"] = ([C_u[p] for p in range(4)], b_Cu)
        outs["kT"] = (C_k, b_Ck)
        outs["vs"] = (C_vs, b_Cvs)
        outs["vw"] = (C_vw, b_Cvw)
        tcount = tb.stage_proj(ll, cur, b_cur, outs, tcount)
        P.pop_scope()
        if _DBG.get("stop3") and l == 1:
            P.wait_all("sync", list(dram.b.values()))
            P.finish()
            return nc
        if _DBG.get("dump") and l == 0:
            d0 = dram.dscr("dbg_ck0_pre", [96, NTOK], BF16)
            op("sync", lambda e: e.dma_start(out=d0, in_=C_k[0]), reads=[b_Ck], writes=[dram.b["dbg_ck0_pre"]], dma=True)
            dram.outs += [dram.b["dbg_ck0_pre"]]
            P.barrier()
        for i in range(8):
            _gather(P, C_q[i], b_Cq, G_q[i].rearrange("r d t -> (r d) t"), b_Gq)
        for i in range(8):
            _gather(P, C_k[i], b_Ck, G_k[i].rearrange("r d t -> (r d) t"), b_Gk)
        for i in range(4):
            _gather(P, C_u[i], b_Cu, G_u[i].rearrange("r c t -> (r c) t"), b_Gu)
        for i in range(2):
            _gather(P, C_vs[i], b_Cvs, G_vs[i].rearrange("r t d -> (r t) d"), b_Gvs)
            _gather(P, C_vw[i], b_Cvw, G_vw[i].rearrange("r t d -> (r t) d"), b_Gvw)
        _gather(P, C_g, b_Cg, G_g, b_Gg)
        P.barrier()
        if _DBG.get("dump") and l == 0:
            d1 = dram.dscr("dbg_ck0", [96, NTOK], BF16); d2 = dram.dscr("dbg_gk0", [4 * 96, NTOK], BF16)
            op("sync", lambda e: e.dma_start(out=d1, in_=C_k[0]), reads=[b_Ck], writes=[dram.b["dbg_ck0"]], dma=True)
            op("sync", lambda e: e.dma_start(out=d2, in_=G_k[0].rearrange("r d t -> (r d) t")), reads=[b_Gk], writes=[dram.b["dbg_gk0"]], dma=True)
            dram.outs += [dram.b["dbg_ck0"], dram.b["dbg_gk0"]]
        loc = {}

        def mk(name, shp, dt):
            ap = dram.dscr("l_" + name, shp, dt)
            loc[name] = (ap, dram.b["l_" + name])
            return ap, loc[name][1]

        l_qT, b_lq = mk("qT", [4, 96, S], BF16)
        kinds = ["kcT", "vcT", "ksT", "kwT"]
        for name in kinds:
            mk(name, [96, S], BF16)
        mk("uT", [64, S], F32); mk("vs", [S, 96], BF16); mk("vw", [S, 96], BF16)
        gk5 = G_k.rearrange("(k g) r d t -> k g r d t", g=2)
        gq5 = G_q.rearrange("(p h) r d t -> p h r d t", h=2)
        for h2 in range(2):
            def q_own(e, h2=h2):
                return e.dma_start(out=l_qT[h2].rearrange("d (r t) -> d r t", r=4),
                                   in_=gq5[bass.ds(jexpr(e)["j"], 1), h2, :, 0:96, :].rearrange("o r d t -> (o d) r t"))

            def q_oth(e, h2=h2):
                return e.dma_start(out=l_qT[2 + h2].rearrange("d (r t) -> d r t", r=4),
                                   in_=gq5[bass.ds(jexpr(e)["jo"], 1), h2, :, 0:96, :].rearrange("o r d t -> (o d) r t"))
            op("sync", q_own, reads=[b_Gq], writes=[b_lq], dma=True)
            op("sync", q_oth, reads=[b_Gq], writes=[b_lq], dma=True)

        def uf(e):
            return e.dma_start(out=loc["uT"][0].rearrange("c (r t) -> c r t", r=4),
                               in_=G_u[bass.ds(jexpr(e)["j"], 1), :, :, :].rearrange("o r c t -> (o c) r t"))
        op("sync", uf, reads=[b_Gu], writes=[loc["uT"][1]], dma=True)
        for ki, name in enumerate(kinds):
            def kf(e, name=name, ki=ki):
                return e.dma_start(out=loc[name][0].rearrange("d (r t) -> d r t", r=4),
                                   in_=gk5[ki, bass.ds(jexpr(e)["g"], 1), :, :, :].rearrange("o r d t -> (o d) r t"))
            op("sync", kf, reads=[b_Gk], writes=[loc[name][1]], dma=True)
        for name, Gv, b_Gv in (("vs", G_vs, b_Gvs), ("vw", G_vw, b_Gvw)):
            for r in range(4):
                rows = slice(r * NTOK, (r + 1) * NTOK)

                def vf(e, name=name, rows=rows, r=r, Gv=Gv):
                    return e.dma_start(out=loc[name][0][rows, :], in_=Gv[bass.ds(jexpr(e)["g"], 1), r, :, :].rearrange("o t d -> (o t) d"))
                op("sync", vf, reads=[b_Gv], writes=[loc[name][1]], dma=True)
        loc["gates"] = (G_g, b_Gg)
        if _DBG.get("stop1") is not None and _DBG.get("stop1") == l:
            P.wait_all("sync", list(dram.b.values()))
            P.finish()
            return nc
        ymc = dram.dscr("c_ym", [4, 256, NTOK], BF16)
        b_ymc = dram.b["c_ym"]
        P.push_scope()
        bb = BB(nc, P, dram)
        bb.build(ll, loc, ymc, b_ymc)
        P.pop_scope()
        g_ym, b_gym = gtensor("g_ym", [4, 2, 4, 128, NTOK], BF16)
        for tj in range(4):
            for hf in range(2):
                _gather(P, ymc[tj, hf * 128:(hf + 1) * 128, :], b_ymc, g_ym[tj, hf].rearrange("r i t -> (r i) t"), b_gym)
        P.barrier()
        ym_loc = dram.dscr("l_ym", [D, NTOK], BF16)
        b_ym_loc = dram.b["l_ym"]
        for hf in range(2):
            def yf(e, g_ym=g_ym, ym_loc=ym_loc, hf=hf):
                return e.dma_start(out=ym_loc[hf * 512:(hf + 1) * 512, :], in_=g_ym[bass.ds(jexpr(e)["j"], 1), hf, :, :, :].rearrange("o r i t -> (o r i) t"))
            op("sync", yf, reads=[b_gym], writes=[b_ym_loc], dma=True)
        if _DBG.get("stop2") is not None and _DBG.get("stop2") == l:
            P.wait_all("sync", list(dram.b.values()))
            P.finish()
            return nc
    P.wait_all("sync", dram.outs)
    P.finish()
    return nc


def _yperm():
    src_of = lambda r, row: (64 * r + row) if row < 64 else (256 + 192 * r + (row - 64))
    return np.array([src_of(r, hf * 128 + i) for hf in range(2) for r in range(4) for i in range(128)])


_YPERM = _yperm()


def kernel(**inputs):
    inputs = {k: np.asarray(v) for k, v in inputs.items()}
    x = inputs["x"].reshape(2 * S, D).astype(np.float32, copy=False)
    nc = build_fused()
    C = b_consts()
    maps = []
    for c in range(_NCORE):
        b, j = divmod(c, 4)
        m = {"ident": C["ident_f"], "ident_b": C["ident_b"], "masks": C["masks"], "kpat": C["kpat"], "wimp": C["wimp"], "fpat": C["fpat"]}
        a, A = pool_consts(j)
        m["pool_a"] = a
        gs = np.zeros((6, 24), np.float32)
        gs[np.arange(6), 6 * j + np.arange(6)] = 1.0
        m["gsel"] = gs
        m["pool_A"] = A
        m["xin"] = x[c * NTOK:(c + 1) * NTOK]
        for l in range(2):
            p = "l%d" % l
            m[p + "_ffn1_wg"] = inputs["ffn1_w_gate"][l]; m[p + "_ffn1_wu"] = inputs["ffn1_w_up"][l]; m[p + "_ffn1_wd"] = inputs["ffn1_w_down"][l]
            m[p + "_ffn1_lg"] = inputs["ln1_g"][l]; m[p + "_ffn1_lb"] = inputs["ln1_b"][l]
            m[p + "_ffn2_wg"] = inputs["ffn2_w_gate"][l]; m[p + "_ffn2_wu"] = inputs["ffn2_w_up"][l]; m[p + "_ffn2_wd"] = inputs["ffn2_w_down"][l]
            m[p + "_ffn2_lg"] = inputs["ln3_g"][l]; m[p + "_ffn2_lb"] = inputs["ln3_b"][l]
            m[p + "_w_in"] = inputs["w_in"][l]; m[p + "_b_gate"] = inputs["b_gate"][l]
            m[p + "_w_out"] = inputs["w_out"][l][_YPERM]
            m[p + "_ln2_g"] = inputs["ln2_g"][l]; m[p + "_ln2_b"] = inputs["ln2_b"][l]
            m[p + "_cmp_k_w1"] = inputs["cmp_k_w1"][l]; m[p + "_cmp_k_w2"] = inputs["cmp_k_w2"][l]; m[p + "_cmp_k_pos"] = inputs["cmp_pos_k"][l]
            m[p + "_cmp_v_w1"] = inputs["cmp_v_w1"][l]; m[p + "_cmp_v_w2"] = inputs["cmp_v_w2"][l]; m[p + "_cmp_v_pos"] = inputs["cmp_pos_v"][l]
            m[p + "_pool_w"] = inputs["pool_w"][l][j]
            m[p + "_pool_sc"] = inputs["pool_scale"][l][64 * j:64 * j + 64].reshape(64, 1)
        maps.append({k: np.ascontiguousarray(v) for k, v in m.items()})
    res = run_bass_kernel_spmd(nc, maps, core_ids=list(range(_NCORE))).results
    _DBG["res"] = res if _DBG.get("dump") else None
    out = np.concatenate([r["xout"] for r in res], axis=0)
    return out.reshape(2, S, D).astype(np.float32, copy=False)
```

```python
import numpy as np
import concourse.bass as bass
import concourse.mybir as mybir
from contextlib import ExitStack

F32 = mybir.dt.float32
BF16 = mybir.dt.bfloat16
AF = mybir.ActivationFunctionType
ALU = mybir.AluOpType
AX = mybir.AxisListType

ENGS = ("sync", "tensor", "vector", "scalar", "gpsimd")
N_DMA_SEMS = 12


class Buf:
    __slots__ = ("name", "w", "r", "psum")

    def __init__(self, name, psum=False):
        self.name = name
        self.psum = psum
        self.w = None
        self.r = {}


class Prog:
    def __init__(self, nc):
        self.nc = nc
        self.es = ExitStack()
        self.streams = {e: [] for e in ENGS}
        self.sem = {}
        self.cnt = {}
        for e in ENGS:
            self.sem[e] = nc.alloc_semaphore("c_" + e)
            self.cnt[e] = 0
        self.dsem = {}
        self.dcnt = {}
        self.drr = {}
        for e in ("sync", "gpsimd", "scalar"):
            for i in range(N_DMA_SEMS):
                k = "d_%s_%d" % (e, i)
                self.sem[k] = nc.alloc_semaphore(k)
                self.cnt[k] = 0
            self.drr[e] = 0
        self.sem["cc"] = nc.alloc_semaphore("cc_sem")
        self.cnt["cc"] = 0
        self.seen = {e: {} for e in ENGS}
        self.nbuf = 0
        self.block_hooks = []
        self.scopes = []
        self.scope_ctr = 0
        self.scope_id = 0

    def sbuf(self, name, shape, dtype):
        t = self.es.enter_context(self.nc.sbuf_tensor("sb%d_%s" % (self.scope_id, name), list(shape), dtype))
        return t

    def psum(self, name, shape, dtype=F32):
        t = self.es.enter_context(self.nc.psum_tensor("pp%d_%s" % (self.scope_id, name), list(shape), dtype))
        return t

    def push_scope(self):
        self.scopes.append(self.es)
        self.es = ExitStack()
        self.scope_ctr += 1
        self.scope_id = self.scope_ctr

    def pop_scope(self):
        self.barrier()
        self.es.close()
        self.es = self.scopes.pop()

    def buf(self, name=None, psum=False):
        self.nbuf += 1
        return Buf(name or ("b%d" % self.nbuf), psum)

    def pbuf(self):
        return self.buf(psum=True)

    def _need(self, eng, deps):
        out = []
        seen = self.seen[eng]
        for k, v in deps.items():
            if seen.get(k, 0) < v:
                seen[k] = v
                out.append((k, v))
        return out

    def op(self, eng, fn, reads=(), writes=(), dma=False, pe_accum=False, cc=False):
        deps = {}

        def add(tok):
            if tok is None:
                return
            k, v = tok
            if deps.get(k, 0) < v:
                deps[k] = v

        for b in reads:
            add(b.w)
            if b.psum:
                for k, v in b.r.items():
                    if k != eng:
                        add((k, v))
        for b in writes:
            if not (pe_accum and b.w is not None and b.w[0] == "tensor"):
                add(b.w)
            for k, v in b.r.items():
                add((k, v))
        if dma:
            rr = self.drr[eng]
            self.drr[eng] = (rr + 1) % N_DMA_SEMS
            dk = "d_%s_%d" % (eng, rr)
            if self.cnt[dk] > 0:
                add((dk, self.cnt[dk]))
        if eng == "tensor":
            deps.pop("tensor", None)
        waits = self._need(eng, deps)
        if cc:
            self.cnt["cc"] += 1
            tok = ("cc", self.cnt["cc"])
            inc = 1
        elif dma:
            self.cnt[dk] += 16
            tok = (dk, self.cnt[dk])
            inc = 16
        else:
            self.cnt[eng] += 1
            tok = (eng, self.cnt[eng])
            inc = 1
        semh = self.sem[tok[0]]
        wl = [(self.sem[k], v) for k, v in waits]

        def emit(e, fn=fn, wl=wl, semh=semh, inc=inc):
            for s, v in wl:
                e.wait_ge(s, v)
            fn(e).then_inc(semh, inc)

        self.streams[eng].append(emit)
        for b in writes:
            b.w = tok
            b.r = {}
        for b in reads:
            if b.r.get(tok[0], 0) < tok[1]:
                b.r[tok[0]] = tok[1]
        return tok

    def wait_all(self, eng, bufs):
        deps = {}
        for b in bufs:
            if b.w is not None:
                k, v = b.w
                if deps.get(k, 0) < v:
                    deps[k] = v
        wl = [(self.sem[k], v) for k, v in self._need(eng, deps)]

        def emit(e, wl=wl):
            for s, v in wl:
                e.wait_ge(s, v)

        self.streams[eng].append(emit)

    def barrier(self):
        allc = {k: v for k, v in self.cnt.items() if v > 0}
        for eng in ENGS:
            wl = [(self.sem[k], v) for k, v in self._need(eng, dict(allc))]

            def emit(e, wl=wl):
                for s, v in wl:
                    e.wait_ge(s, v)

            self.streams[eng].append(emit)

    def block_break(self):
        self.barrier()
        for e in ENGS:
            self.streams[e].append(None)

    def finish(self):
        nc = self.nc
        segs = {e: [[]] for e in ENGS}
        for e in ENGS:
            for f in self.streams[e]:
                if f is None:
                    segs[e].append([])
                else:
                    segs[e][-1].append(f)
        nseg = len(segs["sync"])
        for i in range(nseg):
            for hook in self.block_hooks:
                hook()
            with nc.Block() as block:
                @block.sync
                def _(e):
                    for f in segs["sync"][i]:
                        f(e)

                @block.tensor
                def _(e):
                    for f in segs["tensor"][i]:
                        f(e)

                @block.vector
                def _(e):
                    for f in segs["vector"][i]:
                        f(e)

                @block.scalar
                def _(e):
                    for f in segs["scalar"][i]:
                        f(e)

                @block.gpsimd
                def _(e):
                    for f in segs["gpsimd"][i]:
                        f(e)
        self.es.close()


class Dram:
    def __init__(self, nc, P):
        self.nc = nc
        self.P = P
        self.ap = {}
        self.b = {}
        self.outs = []
        self.dump = ()
        self.arena = {}
        self.ARENA_ELEMS = {"bf16": 120 * 1024 * 1024, "f32": 60 * 1024 * 1024}

    def din(self, name, shape, dt):
        if name not in self.ap:
            self.ap[name] = self.nc.dram_tensor(name, list(shape), dt, kind="ExternalInput").ap()
        return self.ap[name]

    def dout(self, name, shape, dt):
        self.ap[name] = self.nc.dram_tensor(name, list(shape), dt, kind="ExternalOutput").ap()
        self.b[name] = self.P.buf(name)
        self.outs.append(self.b[name])
        return self.ap[name]

    def dscr(self, name, shape, dt):
        if name not in self.ap:
            if name in self.dump:
                self.ap[name] = self.nc.dram_tensor(name, list(shape), dt, kind="ExternalOutput").ap()
            else:
                n = 1
                for d in shape:
                    n *= int(d)
                key = "bf16" if dt == BF16 else "f32"
                if key not in self.arena:
                    cap = self.ARENA_ELEMS[key]
                    self.arena[key] = [self.nc.dram_tensor("arena_" + key, [cap], dt).ap(), 0, cap]
                ar = self.arena[key]
                off = ar[1]
                n_al = (n + 2047) // 2048 * 2048
                assert off + n_al <= ar[2], ("arena overflow", key, name)
                ar[1] = off + n_al
                flat = ar[0][off:off + n]
                if len(shape) == 1:
                    self.ap[name] = flat
                else:
                    names = ["a%d" % i for i in range(len(shape))]
                    pat = "(" + " ".join(names) + ") -> " + " ".join(names)
                    kw = {nm: int(d) for nm, d in zip(names[1:], shape[1:])}
                    self.ap[name] = flat.rearrange(pat, **kw)
            self.b[name] = self.P.buf(name)
        return self.ap[name]

D = 1024
DFF = 2816
NF = DFF // 128
DIN = 2200
NTOK = 4096
TT = 512
NTILE = NTOK // TT
ALPHA = float((2.0 * 2) ** 0.25)
LN_EPS = 1e-5
QSCALE = float(96 ** -0.5)
TCOLS = [256 + 96 * h for h in range(8)] + [1024, 1120, 1216, 1312, 1408, 1504, 1792, 1888]


class TB:
    gus_ctr = 0

    def __init__(self, nc, P, ident, dram):
        self.nc = nc
        self.P = P
        self.dram = dram
        self.ident = ident
        self.b_dram = dram.b
        self.idt = P.sbuf("idt", [128, 128], F32); self.b_idt = P.buf()
        self.xt = [P.sbuf("xt%d" % i, [128, 4, D], F32) for i in range(2)]
        self.b_xt = [[P.buf() for s in range(4)] for i in range(2)]
        self.xT = P.sbuf("xT", [128, 8, TT], BF16); self.b_xT = [P.buf() for c in range(8)]
        self.hT = P.sbuf("hT", [128, NF, TT], BF16); self.b_hT = [P.buf() for f in range(NF)]
        self.wres = P.sbuf("wres", [128, NF * D], BF16); self.b_wres = [P.buf() for f in range(NF)]
        self.ring = P.sbuf("ring", [128, 3, 2, 8, 128], BF16); self.b_ring = [P.buf() for i in range(3)]
        self.stg = [P.sbuf("stg%d" % i, [128, DFF], F32) for i in range(2)]; self.b_stg = [P.buf() for i in range(2)]
        self.stgb = [P.sbuf("stgb%d" % i, [128, DFF], BF16) for i in range(2)]; self.b_stgb = [P.buf() for i in range(2)]
        self.lng = P.sbuf("lng", [128, D], F32); self.b_lng = P.buf()
        self.lnb = P.sbuf("lnb", [128, D], F32); self.b_lnb = P.buf()
        self.sg = [P.sbuf("sg%d" % i, [128, TT], F32) for i in range(2)]; self.b_sg = [P.buf() for i in range(2)]
        self.st = P.sbuf("st", [128, 4, 2, 6], F32); self.b_st = [P.buf() for s in range(4)]
        self.mv = P.sbuf("mv", [128, 4, 4], F32); self.b_mv = [P.buf() for s in range(4)]
        self.oT = P.sbuf("oT", [128, 16, TT], BF16); self.b_oT = [P.buf() for i in range(16)]
        P.op("gpsimd", lambda e: e.memset(self.oT[96:128, :, :], 0.0), writes=self.b_oT)
        self.ou = P.sbuf("ou", [128, 2, TT], F32); self.b_ou = [P.buf() for i in range(2)]
        self.ov = P.sbuf("ov", [128, 4, 384], BF16); self.b_ov = [P.buf() for i in range(4)]
        self.og = P.sbuf("og", [128, 4, 24], F32); self.b_og = [P.buf() for i in range(4)]
        self.bg = P.sbuf("bg", [128, 24], F32); self.b_bg = P.buf()
        self.ps = [P.psum("ps%d" % i, [128, 512], F32) for i in range(8)]
        self.b_ps = [P.pbuf() for i in range(8)]
        self.rr = 0
        self.prep_i = 0
        self.prep_cast_eng = None
        P.op("sync", lambda e: e.dma_start(out=self.idt[:], in_=self.ident), writes=[self.b_idt], dma=True)

    def din(self, name, shape, dt):
        return self.dram.din(name, shape, dt)

    def dscr(self, name, shape, dt):
        return self.dram.dscr(name, shape, dt)

    def cast_eng(self):
        self.rr += 1
        return ("vector", "gpsimd", "scalar")[self.rr % 3]

    def cast(self, eng, out, in_, reads, writes):
        if eng == "scalar":
            self.P.op("scalar", lambda e: e.activation(out=out, in_=in_, func=AF.Copy), reads=reads, writes=writes)
        else:
            self.P.op(eng, lambda e: e.tensor_copy(out=out, in_=in_), reads=reads, writes=writes)

    def load_res(self, w, nchunks, width):
        P = self.P
        per = max(1, DFF // width)
        f = 0
        i = 0
        while f < nchunks:
            n = min(per, nchunks - f)
            k = i % 2
            src = w[f * 128:(f + n) * 128, :].rearrange("(n p) d -> p n d", p=128)
            dst = self.stg[k][:, 0:n * width].rearrange("p (n d) -> p n d", d=width)
            P.op("sync", lambda e, dst=dst, src=src: e.dma_start(out=dst, in_=src), writes=[self.b_stg[k]], dma=True)
            self.cast(self.cast_eng(), self.wres[:, f * width:(f + n) * width], self.stg[k][:, 0:n * width],
                      [self.b_stg[k]], [self.b_wres[j] for j in range(f, f + n)])
            f += n
            i += 1

    def prep_gu_iters(self, wg, wu, scr, b_scr):
        P = self.P
        its = []
        for c in range(8):
            for gi, w in ((0, wg), (1, wu)):
                def it(c=c, gi=gi, w=w):
                    k = self.prep_i % 2
                    self.prep_i += 1
                    P.op("sync", lambda e, k=k, w=w, c=c: e.dma_start(out=self.stg[k][:], in_=w[c * 128:(c + 1) * 128, :]),
                         writes=[self.b_stg[k]], dma=True)
                    self.cast(self.prep_cast_eng or self.cast_eng(), self.stgb[k][:], self.stg[k][:], [self.b_stg[k]], [self.b_stgb[k]])
                    dst = scr[:, :, gi, c, :].rearrange("f p j -> p f j")
                    src = self.stgb[k][:].rearrange("p (f j) -> p f j", j=128)
                    P.op("gpsimd", lambda e, dst=dst, src=src: e.dma_start(out=dst, in_=src), reads=[self.b_stgb[k]],
                         writes=[b_scr], dma=True)
                its.append(it)
        return its

    def prep_gu(self, wg, wu, scr, b_scr):
        for it in self.prep_gu_iters(wg, wu, scr, b_scr):
            it()

    def prep_for(self, pfx):
        wg = self.din(pfx + "_wg", [D, DFF], F32)
        wu = self.din(pfx + "_wu", [D, DFF], F32)
        nm = "gus%d" % (TB.gus_ctr % 2)
        TB.gus_ctr += 1
        scr = self.dscr(nm, [NF, 128, 2, 8, 128], BF16)
        b_scr = self.b_dram[nm]
        return scr, b_scr, self.prep_gu_iters(wg, wu, scr, b_scr)

    def load_ln(self, g, b):
        P = self.P
        P.op("sync", lambda e: e.dma_start(out=self.lng[:], in_=g.partition_broadcast(128)), writes=[self.b_lng], dma=True)
        P.op("sync", lambda e: e.dma_start(out=self.lnb[:], in_=b.partition_broadcast(128)), writes=[self.b_lnb], dma=True)

    def load_x(self, src, b_src, t, k):
        P = self.P
        s_ap = src[t * TT:(t + 1) * TT, :].rearrange("(s p) d -> p s d", p=128)
        P.op("sync", lambda e: e.dma_start(out=self.xt[k][:], in_=s_ap), reads=[b_src] if b_src else [], writes=self.b_xt[k], dma=True)

    def transposes(self, k):
        P = self.P
        for c in range(8):
            pk = c % 2
            for s in range(4):
                P.op("tensor", lambda e, c=c, s=s, pk=pk: e.transpose(out=self.ps[pk][:, s * 128:(s + 1) * 128],
                                                                      in_=self.xt[k][:, s, c * 128:(c + 1) * 128], identity=self.idt[:]),
                     reads=[self.b_xt[k][s], self.b_idt], writes=[self.b_ps[pk]], pe_accum=True)
            self.cast("scalar" if c % 2 else "vector", self.xT[:, c, :], self.ps[pk][:], [self.b_ps[pk]], [self.b_xT[c]])

    def gate_up(self, scr, b_scr, t):
        P = self.P
        for f in range(NF):
            slot = (t * NF + f) % 3
            P.op("gpsimd" if f % 2 else "sync", lambda e, f=f, slot=slot: e.dma_start(out=self.ring[:, slot], in_=scr[f]),
                 reads=[b_scr], writes=[self.b_ring[slot]], dma=True)
            pg = 2 + (f % 2) * 2
            pu = pg + 1
            for gi, pp in ((0, pg), (1, pu)):
                for c in range(8):
                    P.op("tensor", lambda e, c=c, gi=gi, pp=pp, slot=slot: e.matmul(self.ps[pp][:], lhsT=self.ring[:, slot, gi, c, :], rhs=self.xT[:, c, :],
                                                                                 start=(c == 0), stop=(c == 7)),
                         reads=[self.b_ring[slot], self.b_xT[c]], writes=[self.b_ps[pp]], pe_accum=True)
            k = f % 2
            P.op("scalar", lambda e, k=k, pg=pg: e.activation(out=self.sg[k][:], in_=self.ps[pg][:], func=AF.Silu),
                 reads=[self.b_ps[pg]], writes=[self.b_sg[k]])
            P.op("vector", lambda e, k=k, pu=pu, f=f: e.tensor_tensor(out=self.hT[:, f, :], in0=self.sg[k][:], in1=self.ps[pu][:], op=ALU.mult),
                 reads=[self.b_sg[k], self.b_ps[pu]], writes=[self.b_hT[f]])

    def down_ln(self, k, lhs, b_lhs, nch, width_w, ysc, dst, b_dst, t):
        P = self.P
        for s in range(4):
            P.op("gpsimd", lambda e, s=s: e.tensor_scalar(out=self.xt[k][:, s, :], in0=self.xt[k][:, s, :], scalar1=ALPHA, scalar2=None, op0=ALU.mult),
                 reads=[self.b_xt[k][s]], writes=[self.b_xt[k][s]])
        for s in range(4):
            for n in range(2):
                pp = 6 + (s * 2 + n) % 2
                for f in range(nch):
                    P.op("tensor", lambda e, s=s, n=n, f=f, pp=pp: e.matmul(self.ps[pp][:], lhsT=lhs[:, f, s * 128:(s + 1) * 128],
                                                                          rhs=self.wres[:, f * width_w + n * 512: f * width_w + (n + 1) * 512],
                                                                          start=(f == 0), stop=(f == nch - 1)),
                         reads=[b_lhs[f], self.b_wres[f]], writes=[self.b_ps[pp]], pe_accum=True)
                P.op("vector", lambda e, s=s, n=n, pp=pp: e.scalar_tensor_tensor(out=self.xt[k][:, s, n * 512:(n + 1) * 512], in0=self.ps[pp][:], scalar=ysc,
                                                                                in1=self.xt[k][:, s, n * 512:(n + 1) * 512], op0=ALU.mult, op1=ALU.add),
                     reads=[self.b_ps[pp], self.b_xt[k][s]], writes=[self.b_xt[k][s]])
            self.layernorm(k, s)
        d_ap = dst[t * TT:(t + 1) * TT, :].rearrange("(s p) d -> p s d", p=128)
        P.op("sync", lambda e: e.dma_start(out=d_ap, in_=self.xt[k][:]), reads=self.b_xt[k], writes=[b_dst], dma=True)

    def layernorm(self, k, s):
        P = self.P
        z = self.xt[k]
        for h in range(2):
            P.op("vector", lambda e, h=h: e.bn_stats(self.st[:, s, h, :], z[:, s, h * 512:(h + 1) * 512]),
                 reads=[self.b_xt[k][s]], writes=[self.b_st[s]])
        P.op("vector", lambda e: e.bn_aggr(self.mv[:, s, 0:2], self.st[:, s, :, :]), reads=[self.b_st[s]], writes=[self.b_mv[s]])
        P.op("vector", lambda e: e.tensor_scalar(out=self.mv[:, s, 2:3], in0=self.mv[:, s, 1:2], scalar1=LN_EPS, scalar2=None, op0=ALU.add),
             reads=[self.b_mv[s]], writes=[self.b_mv[s]])
        P.op("scalar", lambda e: e.sqrt(self.mv[:, s, 3:4], self.mv[:, s, 2:3]), reads=[self.b_mv[s]], writes=[self.b_mv[s]])
        P.op("vector", lambda e: e.reciprocal(self.mv[:, s, 2:3], self.mv[:, s, 3:4]), reads=[self.b_mv[s]], writes=[self.b_mv[s]])
        P.op("vector", lambda e: e.tensor_scalar(out=z[:, s, :], in0=z[:, s, :], scalar1=self.mv[:, s, 0:1], scalar2=self.mv[:, s, 2:3],
                                                 op0=ALU.subtract, op1=ALU.mult),
             reads=[self.b_xt[k][s], self.b_mv[s]], writes=[self.b_xt[k][s]])
        P.op("gpsimd", lambda e: e.tensor_tensor(out=z[:, s, :], in0=z[:, s, :], in1=self.lng[:], op=ALU.mult),
             reads=[self.b_xt[k][s], self.b_lng], writes=[self.b_xt[k][s]])
        P.op("gpsimd", lambda e: e.tensor_tensor(out=z[:, s, :], in0=z[:, s, :], in1=self.lnb[:], op=ALU.add),
             reads=[self.b_xt[k][s], self.b_lnb], writes=[self.b_xt[k][s]])

    def stage_ffn(self, pfx, src, b_src, dst, b_dst, tcount, prepped=None, tile_hook=None):
        wd = self.din(pfx + "_wd", [DFF, D], F32)
        g = self.din(pfx + "_lg", [D], F32)
        b = self.din(pfx + "_lb", [D], F32)
        if prepped is None:
            scr, b_scr, its = self.prep_for(pfx)
            for it in its:
                it()
        else:
            scr, b_scr = prepped
        self.load_ln(g, b)
        self.load_res(wd, NF, D)
        self.load_x(src, b_src, 0, tcount % 2)
        for t in range(NTILE):
            k = (tcount + t) % 2
            self.transposes(k)
            self.gate_up(scr, b_scr, t)
            if t + 1 < NTILE:
                self.load_x(src, b_src, t + 1, (tcount + t + 1) % 2)
            self.down_ln(k, self.hT, self.b_hT, NF, D, 0.5, dst, b_dst, t)
            if tile_hook is not None:
                tile_hook(t)
        return tcount + NTILE

    def stage_mix(self, pfx, ym, b_ym, src, b_src, dst, b_dst, tcount, tile_hook=None):
        wo = self.din(pfx + "_w_out", [D, D], F32)
        g = self.din(pfx + "_ln2_g", [D], F32)
        b = self.din(pfx + "_ln2_b", [D], F32)
        self.load_ln(g, b)
        self.load_res(wo, 8, D)
        self.load_x(src, b_src, 0, tcount % 2)
        for t in range(NTILE):
            k = (tcount + t) % 2
            if t + 1 < NTILE:
                self.load_x(src, b_src, t + 1, (tcount + t + 1) % 2)
            y_ap = ym[:, t * TT:(t + 1) * TT].rearrange("(c p) t -> p c t", p=128)
            self.P.op("gpsimd", lambda e, y_ap=y_ap: e.dma_start(out=self.hT[:, 0:8, :], in_=y_ap), reads=[b_ym], writes=self.b_hT[0:8], dma=True)
            self.down_ln(k, self.hT, self.b_hT, 8, D, 1.0, dst, b_dst, t)
            if tile_hook is not None:
                tile_hook(t)
        return tcount + NTILE

    def stage_proj(self, pfx, src, b_src, outs, tcount):
        P = self.P
        win = self.din(pfx + "_w_in", [D, DIN], F32)
        bgate = self.din(pfx + "_b_gate", [24], F32)
        qT, kT, uT, vs, vw, gates = (outs[k][0] for k in ("qT", "kT", "uT", "vs", "vw", "gates"))
        bd = {k: outs[k][1] for k in ("qT", "kT", "uT", "vs", "vw", "gates")}
        P.op("sync", lambda e: e.dma_start(out=self.bg[:], in_=bgate.partition_broadcast(128)), writes=[self.b_bg], dma=True)
        for c in range(8):
            k = c % 2
            P.op("sync", lambda e, k=k, c=c: e.dma_start(out=self.stg[k][:, 0:DIN], in_=win[c * 128:(c + 1) * 128, :]), writes=[self.b_stg[k]], dma=True)
            self.cast(self.cast_eng(), self.wres[:, c * DIN:(c + 1) * DIN], self.stg[k][:, 0:DIN], [self.b_stg[k]], [self.b_wres[c]])
        W = lambda c, a, b_: self.wres[:, c * DIN + a: c * DIN + b_]
        self.load_x(src, b_src, 0, tcount % 2)
        for t in range(NTILE):
            k = (tcount + t) % 2
            self.transposes(k)
            if t + 1 < NTILE:
                self.load_x(src, b_src, t + 1, (tcount + t + 1) % 2)
            tok = slice(t * TT, (t + 1) * TT)
            for i, col in enumerate(TCOLS):
                pp = 2 + i % 4
                for c in range(8):
                    P.op("tensor", lambda e, c=c, col=col, pp=pp: e.matmul(self.ps[pp][:, :], lhsT=W(c, col, col + 128), rhs=self.xT[:, c, :],
                                                                         start=(c == 0), stop=(c == 7)),
                         reads=[self.b_wres[c], self.b_xT[c]], writes=[self.b_ps[pp]], pe_accum=True)
                if i < 8:
                    P.op("scalar", lambda e, i=i, pp=pp: e.mul(self.oT[0:96, i, :], self.ps[pp][0:96, :], QSCALE),
                         reads=[self.b_ps[pp]], writes=[self.b_oT[i]])
                else:
                    P.op("vector", lambda e, i=i, pp=pp: e.tensor_copy(out=self.oT[0:96, i, :], in_=self.ps[pp][0:96, :]),
                         reads=[self.b_ps[pp]], writes=[self.b_oT[i]])
            P.op("sync", lambda e, tok=tok: e.dma_start(out=qT[:, :, tok].rearrange("h d t -> d h t"), in_=self.oT[:, 0:8, :]),
                 reads=self.b_oT[0:8], writes=[bd["qT"]], dma=True)
            P.op("sync", lambda e, tok=tok: e.dma_start(out=kT[:, :, tok].rearrange("h d t -> d h t"), in_=self.oT[0:96, 8:16, :]),
                 reads=self.b_oT[8:16], writes=[bd["kT"]], dma=True)
            for j in range(2):
                pp = 6 + j
                for c in range(8):
                    P.op("tensor", lambda e, c=c, j=j, pp=pp: e.matmul(self.ps[pp][:], lhsT=W(c, j * 128, (j + 1) * 128), rhs=self.xT[:, c, :],
                                                                     start=(c == 0), stop=(c == 7)),
                         reads=[self.b_wres[c], self.b_xT[c]], writes=[self.b_ps[pp]], pe_accum=True)
                P.op("vector", lambda e, j=j, pp=pp: e.tensor_copy(out=self.ou[:, j, :], in_=self.ps[pp][:]), reads=[self.b_ps[pp]], writes=[self.b_ou[j]])
            for j2 in range(2):
                for half in range(2):
                    P.op("sync", lambda e, tok=tok, j2=j2, half=half: e.dma_start(out=uT[2 * j2 + half][:, tok], in_=self.ou[half * 64:(half + 1) * 64, j2, :]),
                         reads=[self.b_ou[j2]], writes=[bd["uT"]], dma=True)
            for s in range(4):
                pp = 2 + s
                for c in range(8):
                    P.op("tensor", lambda e, c=c, s=s, pp=pp: e.matmul(self.ps[pp][:, 0:192], lhsT=self.xT[:, c, s * 128:(s + 1) * 128], rhs=W(c, 1600, 1792),
                                                                     start=(c == 0), stop=(c == 7)),
                         reads=[self.b_wres[c], self.b_xT[c]], writes=[self.b_ps[pp]], pe_accum=True)
                for c in range(8):
                    P.op("tensor", lambda e, c=c, s=s, pp=pp: e.matmul(self.ps[pp][:, 256:472], lhsT=self.xT[:, c, s * 128:(s + 1) * 128], rhs=W(c, 1984, 2200),
                                                                     start=(c == 0), stop=(c == 7), skip_group_check=True),
                         reads=[self.b_wres[c], self.b_xT[c]], writes=[self.b_ps[pp]], pe_accum=True)
                P.op("vector", lambda e, s=s, pp=pp: e.tensor_copy(out=self.ov[:, s, 0:192], in_=self.ps[pp][:, 0:192]), reads=[self.b_ps[pp]], writes=[self.b_ov[s]])
                P.op("vector", lambda e, s=s, pp=pp: e.tensor_copy(out=self.ov[:, s, 192:384], in_=self.ps[pp][:, 256:448]), reads=[self.b_ps[pp]], writes=[self.b_ov[s]])
                P.op("vector", lambda e, s=s, pp=pp: e.tensor_tensor(out=self.og[:, s, :], in0=self.ps[pp][:, 448:472], in1=self.bg[:], op=ALU.add),
                     reads=[self.b_ps[pp], self.b_bg], writes=[self.b_og[s]])
                P.op("scalar", lambda e, s=s: e.activation(out=self.og[:, s, :], in_=self.og[:, s, :], func=AF.Sigmoid), reads=[self.b_og[s]], writes=[self.b_og[s]])
            for g_ in range(2):
                P.op("gpsimd", lambda e, tok=tok, g_=g_: e.dma_start(out=vs[g_, tok, :].rearrange("(s p) d -> p s d", p=128), in_=self.ov[:, :, 96 * g_:96 * g_ + 96]),
                     reads=self.b_ov, writes=[bd["vs"]], dma=True)
                P.op("gpsimd", lambda e, tok=tok, g_=g_: e.dma_start(out=vw[g_, tok, :].rearrange("(s p) d -> p s d", p=128), in_=self.ov[:, :, 192 + 96 * g_:192 + 96 * g_ + 96]),
                     reads=self.b_ov, writes=[bd["vw"]], dma=True)
            P.op("gpsimd", lambda e, tok=tok: e.dma_start(out=gates[tok, :].rearrange("(s p) d -> p s d", p=128), in_=self.og[:]),
                 reads=self.b_og, writes=[bd["gates"]], dma=True)
        return tcount + NTILE


S = 16384
NQT = S // 512
NEG = -30000.0
VP = 128
GELU_C = float(2.0 * (2.0 / np.pi) ** 0.5)


def b_consts():
    import ml_dtypes
    c = {}
    c["ident_f"] = np.eye(128, dtype=np.float32)
    c["ident_b"] = np.eye(128, dtype=np.float32).astype(ml_dtypes.bfloat16)
    k = np.arange(128)[:, None]
    t = np.arange(512)[None, :]
    masks = np.zeros((16, 128, 512), np.float32)
    for i in range(8):
        kp = 128 * i - 512 + k
        masks[i] = np.where((t - kp >= 0) & (t - kp < 512), 0.0, NEG)
    for r in range(4):
        masks[8 + r] = np.where(16 * k + 31 <= 512 * r + t, 0.0, NEG)
    for i in range(4):
        masks[12 + i] = np.where(128 * i + k <= t, 0.0, NEG)
    c["masks"] = masks.astype(ml_dtypes.bfloat16)
    kk = np.arange(2048)[None, :]
    r = np.arange(32)[:, None]
    c["kpat"] = (((kk // 64) % 32) == r).astype(np.float32).astype(ml_dtypes.bfloat16)
    W = np.zeros((1024, 256), np.float32)
    for j in range(256):
        for n, w in ((4 * j - 1, 0.5), (4 * j, 1.0), (4 * j + 1, 1.0), (4 * j + 2, 1.0), (4 * j + 3, 0.5)):
            if 0 <= n < 1023:
                W[n, j] = w
    c["wimp"] = W.reshape(8, 128, 256).astype(ml_dtypes.bfloat16)
    fp = np.zeros((128, 4), np.float32)
    fp[:64, 0] = 1000.0; fp[:64, 1] = 2000.0
    fp[64:, 1] = 2000.0; fp[64:, 2] = 3000.0
    c["fpat"] = fp
    return c


def pool_consts(j):
    w = 2 ** (j + 1)
    a = np.zeros((64, 4), np.float32)
    a[:, j] = 1.0 / w
    A = np.zeros((64, 4, 16), np.float32)
    tt = np.arange(16)
    A[:, j, :] = 1.0 / np.minimum(tt + 1, w)
    return a, A


def _V(x, e):
    return x(e) if callable(x) else x


class BB:
    def __init__(self, nc, P, dram):
        self.nc = nc
        self.P = P
        self.dram = dram

    def din(self, name, shape, dt):
        return self.dram.din(name, shape, dt)

    def build(self, pfx, src, ymc, b_ymc):
        P = self.P
        nc = self.nc
        op = P.op
        qT, b_qT = src["qT"]; gates, b_gates = src["gates"]
        kcT, b_kcT = src["kcT"]; vcT, b_vcT = src["vcT"]; ksT, b_ksT = src["ksT"]; kwT, b_kwT = src["kwT"]
        vs, b_vs = src["vs"]; vw, b_vw = src["vw"]; uT, b_uT = src["uT"]
        cw = {}
        for kv in ("k", "v"):
            cw[kv + "w1"] = self.din(pfx + "_cmp_%s_w1" % kv, [3072, 96], F32)
            cw[kv + "w2"] = self.din(pfx + "_cmp_%s_w2" % kv, [96, 96], F32)
            cw[kv + "pos"] = self.din(pfx + "_cmp_%s_pos" % kv, [32, 96], F32)
        pool_w = self.din(pfx + "_pool_w", [64, 64], F32)
        pool_sc = self.din(pfx + "_pool_sc", [64, 1], F32)
        pool_a = self.din("pool_a", [64, 4], F32)
        pool_A = self.din("pool_A", [64, 4, 16], F32)
        d_identf = self.din("ident", [128, 128], F32)
        d_identb = self.din("ident_b", [128, 128], BF16)
        d_masks = self.din("masks", [16, 128, 512], BF16)
        d_kpat = self.din("kpat", [32, 2048], BF16)
        d_wimp = self.din("wimp", [8, 128, 256], BF16)
        d_fpat = self.din("fpat", [128, 4], F32)
        d_gsel = self.din("gsel", [6, 24], F32)
        def ysl(r0, r1, t0, n):
            return ymc[t0 // 4096, r0:r1, (t0 % 4096):(t0 % 4096) + n]

        ps_s = [P.psum("ps_s%d" % i, [128, 512]) for i in range(2)]; b_ps_s = [P.pbuf() for i in range(2)]
        ps_o = [P.psum("ps_o%d" % i, [128, 512]) for i in range(2)]; b_ps_o = [P.pbuf() for i in range(2)]
        ps_i = [P.psum("ps_i%d" % i, [128, 512]) for i in range(2)]; b_ps_i = [P.pbuf() for i in range(2)]
        ps_t = [P.psum("ps_t%d" % i, [128, 512]) for i in range(2)]; b_ps_t = [P.pbuf() for i in range(2)]

        idf = P.sbuf("idf", [128, 128], F32); b_idf = P.buf()
        idb = P.sbuf("idb", [128, 128], BF16); b_idb = P.buf()
        masks = P.sbuf("masks", [128, 16, 512], BF16); b_masks = P.buf()
        wimp = P.sbuf("wimp", [128, 8, 256], BF16); b_wimp = P.buf()
        fpat = P.sbuf("fpat", [128, 4], F32); b_fpat = P.buf()
        ksa = P.sbuf("ksa", [128, S], BF16); b_ksa = P.buf()
        vsa = P.sbuf("vsa", [128, 128, VP], BF16); b_vsa = P.buf()
        kcmpT = P.sbuf("kcmpT", [96, 1024], BF16); b_kcmpT = P.buf()
        vcmpa = P.sbuf("vcmpa", [128, 8, VP], BF16); b_vcmpa = P.buf()
        op("sync", lambda e: e.dma_start(out=idf[:], in_=d_identf), writes=[b_idf], dma=True)
        op("sync", lambda e: e.dma_start(out=idb[:], in_=d_identb), writes=[b_idb], dma=True)
        op("sync", lambda e: e.dma_start(out=masks[:], in_=d_masks.rearrange("m p t -> p m t")), writes=[b_masks], dma=True)
        op("sync", lambda e: e.dma_start(out=wimp[:], in_=d_wimp.rearrange("c p j -> p c j")), writes=[b_wimp], dma=True)
        op("sync", lambda e: e.dma_start(out=fpat[:], in_=d_fpat), writes=[b_fpat], dma=True)
        gsel = P.sbuf("gsel", [128, 6, 24], F32); b_gsel = P.buf()
        op("sync", lambda e: e.dma_start(out=gsel[:].rearrange("p a b -> p (a b)"), in_=d_gsel.rearrange("a b -> (a b)").partition_broadcast(128)), writes=[b_gsel], dma=True)
        op("gpsimd", lambda e: e.memset(kcmpT[:], 0.0), writes=[b_kcmpT])
        op("gpsimd", lambda e: e.memset(vcmpa[:], 1.0), writes=[b_vcmpa])
        op("gpsimd", lambda e: e.memset(vsa[:], 1.0), writes=[b_vsa])

        q_sb = [P.sbuf("q_sb%d" % i, [96, 4, 512], BF16) for i in range(2)]; b_q = [P.buf() for i in range(2)]
        g_sb = [P.sbuf("g_sb%d" % i, [128, 4, 6], F32) for i in range(2)]; b_g = [P.buf() for i in range(2)]
        g_full = P.sbuf("g_full", [128, 4, 24], F32); b_gfull = P.buf()
        g_tmp = P.sbuf("g_tmp", [128, 4, 6, 24], F32); b_gtmp = P.buf()
        kw_sb = [P.sbuf("kw_sb%d" % i, [96, 1024], BF16) for i in range(2)]; b_kw = [P.buf() for i in range(2)]
        vwa = [P.sbuf("vwa%d" % i, [128, 8, VP], BF16) for i in range(2)]; b_vwa = [P.buf() for i in range(2)]
        e_all = [P.sbuf("e_all%d" % i, [128, 8, 512], BF16) for i in range(2)]; b_eall = [P.buf() for i in range(2)]
        e_sb = [P.sbuf("e_sb%d" % i, [128, 512], BF16) for i in range(3)]; b_e = [P.buf() for i in range(3)]
        oTs = [P.sbuf("oTs%d" % i, [97, 512], F32) for i in range(2)]; b_oTs = [P.buf() for i in range(2)]
        impacc = P.sbuf("impacc", [128, 4, 258], F32); b_imp = [P.buf() for i in range(4)]
        mr = P.sbuf("mr", [128, 256], F32); b_mr = P.buf()
        m8 = P.sbuf("m8", [128, 3, 8], F32); b_m8 = P.buf()
        selb = P.sbuf("selb", [128, 4, 352], F32); b_selb = [P.buf() for i in range(4)]
        qaug = P.sbuf("qaug", [128, 2, 8, 512], BF16); b_qaug = [[P.buf() for v in range(8)] for h in range(2)]
        oacc = [P.sbuf("oacc%d" % i, [128, 4, 2, 96], F32) for i in range(2)]; b_oacc = [P.buf() for i in range(2)]
        oTo = [P.sbuf("oTo%d" % i, [96, 2, 512], BF16) for i in range(2)]; b_oTo = [P.buf() for i in range(2)]
        rd = P.sbuf("rd", [128, 8, 4], F32); b_rd = P.buf()
        qa_f = qaug[:].rearrange("p a b c -> p (a b c)").bitcast(F32)
        ea0 = e_all[0][:].rearrange("p a b -> p (a b)")
        ea1_f = e_all[1][:].rearrange("p a b -> p (a b)").bitcast(F32)
        w1s = qa_f[0:96, 0:3072].rearrange("d (p e) -> d p e", e=96); b_w1s = P.buf()
        gx = qa_f[0:96, 3072:4096]; b_gx = P.buf()
        w1b = ea0[0:96, 0:3072].rearrange("d (p e) -> d p e", e=96); b_w1b = P.buf()
        gb = ea0[0:96, 3072:4096]; b_gb = P.buf()
        gy = ea1_f[0:96, 0:1024]; b_gy = P.buf()
        w2s = P.sbuf("w2s", [96, 96], F32); b_w2s = P.buf()
        w2b = P.sbuf("w2b", [96, 96], BF16); b_w2b = P.buf()
        poss = P.sbuf("poss", [32, 96], F32); b_poss = P.buf()
        posT = P.sbuf("posT", [96, 32], BF16); b_posT = P.buf()
        cb = P.sbuf("cb", [96, 1], F32); b_cb = P.buf()
        stage = getattr(self, 'stage', 99)
        for i in range(8):
            op("gpsimd", lambda e, i=i: e.dma_start(out=ksa[96:128, i * 2048:(i + 1) * 2048], in_=d_kpat), writes=[b_ksa], dma=True)
        for i in range(16 if stage >= 2 else 0):
            op("gpsimd", lambda e, i=i: e.dma_start(out=vsa[:, i * 8:(i + 1) * 8, 0:96],
                                                                        in_=_V(vs, e)[i * 1024:(i + 1) * 1024, :].rearrange("(k p) d -> p k d", p=128)),
               reads=[b_vs], writes=[b_vsa], dma=True)

        for kv, srcT, b_srcT in ((("k", kcT, b_kcT), ("v", vcT, b_vcT)) if stage >= 1 else ()):
            op("sync", lambda e, srcT=srcT: e.dma_start(out=ksa[0:96, :], in_=srcT), reads=[b_srcT], writes=[b_ksa], dma=True)
            op("sync", lambda e, kv=kv: e.dma_start(out=w1s, in_=cw[kv + "w1"].rearrange("(p d) e -> d p e", d=96)), writes=[b_w1s], dma=True)
            op("sync", lambda e, kv=kv: e.dma_start(out=w2s[:], in_=cw[kv + "w2"]), writes=[b_w2s], dma=True)
            op("sync", lambda e, kv=kv: e.dma_start(out=poss[:], in_=cw[kv + "pos"]), writes=[b_poss], dma=True)
            op("vector", lambda e: e.tensor_copy(out=w1b, in_=w1s), reads=[b_w1s], writes=[b_w1b])
            op("vector", lambda e: e.tensor_copy(out=w2b[:], in_=w2s[:]), reads=[b_w2s], writes=[b_w2b])
            op("tensor", lambda e: e.transpose(out=ps_t[0][0:96, 0:32], in_=poss[:], identity=idf[0:32, 0:32]), reads=[b_poss, b_idf], writes=[b_ps_t[0]], pe_accum=True)
            op("vector", lambda e: e.tensor_copy(out=posT[:], in_=ps_t[0][0:96, 0:32]), reads=[b_ps_t[0]], writes=[b_posT])
            for p in range(32):
                op("tensor", lambda e, p=p: e.matmul(ps_t[1][0:96, 0:1], lhsT=w1b[:, p, :], rhs=posT[:, p:p + 1], start=(p == 0), stop=(p == 31)),
                   reads=[b_w1b, b_posT], writes=[b_ps_t[1]], pe_accum=True)
            op("vector", lambda e: e.tensor_copy(out=cb[:], in_=ps_t[1][0:96, 0:1]), reads=[b_ps_t[1]], writes=[b_cb])
            if stage < 1.2:
                continue
            op("gpsimd", lambda e: e.memset(gb, 0.0), writes=[b_gb])
            kview = ksa[0:96, :].rearrange("d (n s) -> d n s", s=16)
            for half, (n0, N) in enumerate(((0, 512), (512, 511))):
                pp = ps_s[half]
                for p in range(32):
                    rhs = kview[:, n0:n0 + N, p] if p < 16 else kview[:, n0 + 1:n0 + 1 + N, p - 16]
                    op("tensor", lambda e, p=p, rhs=rhs, pp=pp, N=N: e.matmul(pp[0:96, 0:N], lhsT=w1b[:, p, :], rhs=rhs, start=(p == 0), stop=(p == 31)),
                       reads=[b_w1b, b_ksa], writes=[b_ps_s[half]], pe_accum=True)
                sl = slice(n0, n0 + N)
                op("vector", lambda e, pp=pp, N=N, sl=sl: e.tensor_scalar(out=gx[:, sl], in0=pp[0:96, 0:N], scalar1=cb[:, 0:1], scalar2=None, op0=ALU.add),
                   reads=[b_ps_s[half], b_cb], writes=[b_gx])
                op("vector", lambda e, sl=sl: e.tensor_tensor(out=gy[:, sl], in0=gx[:, sl], in1=gx[:, sl], op=ALU.mult), reads=[b_gx], writes=[b_gy])
                op("vector", lambda e, sl=sl: e.tensor_scalar(out=gy[:, sl], in0=gy[:, sl], scalar1=0.044715, scalar2=1.0, op0=ALU.mult, op1=ALU.add), reads=[b_gy], writes=[b_gy])
                op("vector", lambda e, sl=sl: e.tensor_tensor(out=gy[:, sl], in0=gy[:, sl], in1=gx[:, sl], op=ALU.mult), reads=[b_gy, b_gx], writes=[b_gy])
                op("scalar", lambda e, sl=sl: e.activation(out=gy[:, sl], in_=gy[:, sl], func=AF.Sigmoid, scale=GELU_C), reads=[b_gy], writes=[b_gy])
                op("vector", lambda e, sl=sl: e.tensor_tensor(out=gb[:, sl], in0=gy[:, sl], in1=gx[:, sl], op=ALU.mult), reads=[b_gy, b_gx], writes=[b_gb])
            if stage < 1.3:
                continue
            if kv == "k":
                for half, (n0, N) in enumerate(((0, 512), (512, 511))):
                    op("tensor", lambda e, half=half, n0=n0, N=N: e.matmul(ps_o[half][0:96, 0:N], lhsT=w2b[:], rhs=gb[:, n0:n0 + N], start=True, stop=True),
                       reads=[b_w2b, b_gb], writes=[b_ps_o[half]], pe_accum=True)
                    op("vector", lambda e, half=half, n0=n0, N=N: e.tensor_copy(out=kcmpT[:, n0:n0 + N], in_=ps_o[half][0:96, 0:N]), reads=[b_ps_o[half]], writes=[b_kcmpT])
            else:
                for c in range(8):
                    k2 = c % 2
                    op("tensor", lambda e, c=c, k2=k2: e.matmul(ps_o[k2][:, 0:96], lhsT=gb[:, c * 128:(c + 1) * 128], rhs=w2b[:], start=True, stop=True),
                       reads=[b_w2b, b_gb], writes=[b_ps_o[k2]], pe_accum=True)
                    op("vector", lambda e, c=c, k2=k2: e.tensor_copy(out=vcmpa[:, c, 0:96], in_=ps_o[k2][:, 0:96]), reads=[b_ps_o[k2]], writes=[b_vcmpa])

        stage = getattr(self, 'stage', 99)
        op("sync", lambda e: e.dma_start(out=ksa[0:96, :], in_=ksT), reads=[b_ksT], writes=[b_ksa], dma=True)
        CH = 512
        pw_s = P.sbuf("pw_s", [64, 64], F32); b_pw_s = P.buf()
        pw_b = P.sbuf("pw_b", [64, 64], BF16); b_pw_b = P.buf()
        psc = P.sbuf("psc", [64, 1], F32); b_psc = P.buf()
        pa = P.sbuf("pa", [64, 4], F32); b_pa = P.buf()
        pA = P.sbuf("pA", [64, 4, 16], F32); b_pA = P.buf()
        L_ = 16 + CH
        ub = [P.sbuf("ub%d" % i, [64, L_], F32)[:] for i in range(2)]; b_ub = [P.buf() for i in range(2)]
        sw = [P.sbuf("sw%d" % i, [64, L_], F32)[:] for i in range(4)]; b_sw = [P.buf() for i in range(4)]
        acc = P.sbuf("pacc", [64, CH], F32)[:]; b_acc = P.buf()
        a16 = P.sbuf("a16", [64, 2, 16], F32)[:]; b_a16 = P.buf()
        accb = P.sbuf("paccb", [64, CH], BF16)[:]; b_accb = P.buf()
        pout = [P.sbuf("pout%d" % i, [64, CH], BF16)[:] for i in range(2)]; b_pout = [P.buf() for i in range(2)]
        op("sync", lambda e: e.dma_start(out=pw_s[:], in_=pool_w), writes=[b_pw_s], dma=True)
        op("sync", lambda e: e.dma_start(out=psc[:], in_=pool_sc), writes=[b_psc], dma=True)
        op("sync", lambda e: e.dma_start(out=pa[:], in_=pool_a), writes=[b_pa], dma=True)
        op("sync", lambda e: e.dma_start(out=pA[:], in_=pool_A), writes=[b_pA], dma=True)
        op("vector", lambda e: e.tensor_copy(out=pw_b[:], in_=pw_s[:]), reads=[b_pw_s], writes=[b_pw_b])
        def pool_chunk(ci):
            k = ci % 2
            u = ub[k]
            if ci == 0:
                op("gpsimd", lambda e, u=u: e.memset(u[:, 0:16], 0.0), writes=[b_ub[k]])
                op("sync", lambda e, u=u: e.dma_start(out=u[:, 16:], in_=uT[:, 0:CH]), reads=[b_uT], writes=[b_ub[k]], dma=True)
            else:
                op("sync", lambda e, u=u, ci=ci: e.dma_start(out=u, in_=uT[:, ci * CH - 16:(ci + 1) * CH]), reads=[b_uT], writes=[b_ub[k]], dma=True)
            L = 16 + CH
            prev, b_prev = u, b_ub[k]
            for wi, sh in enumerate((1, 2, 4, 8)):
                lo = 2 * sh - 1
                dst = sw[wi]
                op("gpsimd", lambda e, dst=dst, prev=prev, lo=lo, sh=sh, L=L: e.tensor_tensor(out=dst[:, lo:L], in0=prev[:, lo:L], in1=prev[:, lo - sh:L - sh], op=ALU.add),
                   reads=[b_prev], writes=[b_sw[wi]])
                prev, b_prev = dst, b_sw[wi]
            op("vector", lambda e: e.tensor_scalar(out=acc, in0=sw[0][:, 16:], scalar1=pa[:, 0:1], scalar2=None, op0=ALU.mult), reads=[b_sw[0], b_pa], writes=[b_acc])
            for wi in range(1, 4):
                op("vector", lambda e, wi=wi: e.scalar_tensor_tensor(out=acc, in0=sw[wi][:, 16:], scalar=pa[:, wi:wi + 1], in1=acc, op0=ALU.mult, op1=ALU.add),
                   reads=[b_sw[wi], b_pa, b_acc], writes=[b_acc])
            if ci == 0:
                op("vector", lambda e: e.tensor_tensor(out=a16[:, 0, :], in0=sw[0][:, 16:32], in1=pA[:, 0, :], op=ALU.mult), reads=[b_sw[0], b_pA], writes=[b_a16])
                for wi in range(1, 4):
                    op("vector", lambda e, wi=wi: e.tensor_tensor(out=a16[:, 1, :], in0=sw[wi][:, 16:32], in1=pA[:, wi, :], op=ALU.mult), reads=[b_sw[wi], b_pA, b_a16], writes=[b_a16])
                    op("vector", lambda e: e.tensor_tensor(out=a16[:, 0, :], in0=a16[:, 0, :], in1=a16[:, 1, :], op=ALU.add), reads=[b_a16], writes=[b_a16])
                op("vector", lambda e: e.tensor_copy(out=acc[:, 0:16], in_=a16[:, 0, :]), reads=[b_a16, b_acc], writes=[b_acc])
            op("vector", lambda e, u=u: e.tensor_tensor(out=accb, in0=acc, in1=u[:, 16:], op=ALU.subtract), reads=[b_acc, b_ub[k]], writes=[b_accb])
            for hh in range(CH // 512):
                pbi = nxt("t", 2)
                op("tensor", lambda e, hh=hh, pbi=pbi: e.matmul(ps_t[pbi][0:64, :], lhsT=pw_b[:], rhs=accb[:, hh * 512:(hh + 1) * 512], start=True, stop=True),
                   reads=[b_pw_b, b_accb], writes=[b_ps_t[pbi]], pe_accum=True)
                op("vector", lambda e, hh=hh, k=k, pbi=pbi: e.tensor_scalar(out=pout[k][:, hh * 512:(hh + 1) * 512], in0=ps_t[pbi][0:64, :], scalar1=psc[:, 0:1], scalar2=None, op0=ALU.mult),
                   reads=[b_ps_t[pbi], b_psc], writes=[b_pout[k]])
            op("sync", lambda e, k=k, ci=ci: e.dma_start(out=ysl(0, 64, ci * CH, CH), in_=pout[k]), reads=[b_pout[k]], writes=[b_ymc], dma=True)


        P.barrier()

        for i in range(2):
            op("gpsimd", lambda e, i=i: e.memset(vwa[i][:], 1.0), writes=[b_vwa[i]])
        op("gpsimd", lambda e: e.memset(selb[:], 0.0), writes=b_selb)
        op("gpsimd", lambda e: e.memset(impacc[:], 0.0), writes=b_imp)

        cnt = {"s": 0, "o": 0, "e": 0, "t": 0, "rd": 0}

        def nxt(key, n):
            v = cnt[key] % n
            cnt[key] += 1
            return v

        def load_tile(qt):
            k = qt % 2
            tok = slice(qt * 512, (qt + 1) * 512)
            op("sync", lambda e: e.dma_start(out=q_sb[k][:], in_=qT[:, :, tok].rearrange("h d t -> d h t")), reads=[b_qT], writes=[b_q[k]], dma=True)
            op("sync", lambda e: e.dma_start(out=g_full[:], in_=gates[tok, :].rearrange("(s p) c -> p s c", p=128)), reads=[b_gates], writes=[b_gfull], dma=True)
            op("gpsimd", lambda e: e.tensor_tensor(out=g_tmp[:], in0=g_full[:].unsqueeze(2).to_broadcast([128, 4, 6, 24]), in1=gsel[:].unsqueeze(1).to_broadcast([128, 4, 6, 24]), op=ALU.mult),
               reads=[b_gfull, b_gsel], writes=[b_gtmp])
            op("vector", lambda e: e.tensor_reduce(out=g_sb[k][:], in_=g_tmp[:], axis=AX.X, op=ALU.add), reads=[b_gtmp], writes=[b_g[k]])
            if qt == 0:
                op("sync", lambda e: e.dma_start(out=kw_sb[k][:, 512:1024], in_=kwT[:, 0:512]), reads=[b_kwT], writes=[b_kw[k]], dma=True)
                op("sync", lambda e: e.dma_start(out=vwa[k][:, 4:8, 0:96], in_=_V(vw, e)[0:512, :].rearrange("(k p) d -> p k d", p=128)), reads=[b_vw], writes=[b_vwa[k]], dma=True)
            else:
                op("sync", lambda e: e.dma_start(out=kw_sb[k][:], in_=kwT[:, qt * 512 - 512: qt * 512 + 512]), reads=[b_kwT], writes=[b_kw[k]], dma=True)
                op("sync", lambda e: e.dma_start(out=vwa[k][:, :, 0:96], in_=_V(vw, e)[qt * 512 - 512: qt * 512 + 512, :].rearrange("(k p) d -> p k d", p=128)), reads=[b_vw], writes=[b_vwa[k]], dma=True)

        def finish_branch(qt, po, h_own, gcol, first, rd_out=None):
            k = qt % 2
            bi = nxt("t", 2)
            osb = oTs[bi]
            op("vector", lambda e: e.tensor_copy(out=osb[:], in_=ps_o[po][0:97, :]), reads=[b_ps_o[po]], writes=[b_oTs[bi]])
            pt = ps_t[bi]
            for sub in range(4):
                op("tensor", lambda e, sub=sub: e.transpose(out=pt[:, sub * 128: sub * 128 + 97], in_=osb[0:97, sub * 128:(sub + 1) * 128], identity=idf[0:97, 0:97]),
                   reads=[b_oTs[bi], b_idf], writes=[b_ps_t[bi]], pe_accum=True)
            ri = nxt("rd", 8)
            ptv = pt[:].rearrange("p (s c) -> p s c", c=128)
            op("vector", lambda e: e.tensor_scalar(out=rd[:, ri, :], in0=ptv[:, :, 96], scalar1=1e-30, scalar2=None, op0=ALU.max), reads=[b_ps_t[bi]], writes=[b_rd])
            op("vector", lambda e: e.reciprocal(rd[:, ri, :], rd[:, ri, :]), reads=[b_rd], writes=[b_rd])
            if h_own is not None:
                ci = nxt("rd", 8)
                op("vector", lambda e: e.tensor_tensor(out=rd[:, ci, :], in0=rd[:, ri, :], in1=g_sb[k][:, :, h_own * 3 + gcol], op=ALU.mult), reads=[b_rd, b_g[k]], writes=[b_rd])
                dbg = getattr(self, "dbg_branch", None)
                if dbg is not None and dbg != gcol:
                    op("vector", lambda e: e.memset(rd[:, ci, :], 0.0), reads=[b_rd], writes=[b_rd])
                for sub in range(4):
                    if first:
                        op("vector", lambda e, sub=sub: e.tensor_scalar(out=oacc[k][:, sub, h_own, :], in0=ptv[:, sub, 0:96], scalar1=rd[:, ci, sub:sub + 1], scalar2=None, op0=ALU.mult),
                           reads=[b_ps_t[bi], b_rd], writes=[b_oacc[k]])
                    else:
                        op("vector", lambda e, sub=sub: e.scalar_tensor_tensor(out=oacc[k][:, sub, h_own, :], in0=ptv[:, sub, 0:96], scalar=rd[:, ci, sub:sub + 1],
                                                                              in1=oacc[k][:, sub, h_own, :], op0=ALU.mult, op1=ALU.add),
                           reads=[b_ps_t[bi], b_rd, b_oacc[k]], writes=[b_oacc[k]])
            return ri

        def cmp_part(qt):
            k = qt % 2
            nct = min(8, qt // 4 + 1)
            for h in range(4):
                ea = e_all[h % 2]
                for c in range(nct):
                    si = nxt("s", 2)
                    partial = qt < 4 * c + 4
                    op("tensor", lambda e, c=c, h=h, si=si, partial=partial: e.matmul(ps_s[si][:], lhsT=kcmpT[:, c * 128:(c + 1) * 128], rhs=q_sb[k][:, h, :], start=True, stop=not partial),
                       reads=[b_kcmpT, b_q[k]], writes=[b_ps_s[si]], pe_accum=True)
                    if partial:
                        r = qt - 4 * c
                        op("tensor", lambda e, si=si, r=r: e.matmul(ps_s[si][:], lhsT=idb[:], rhs=masks[:, 8 + r, :], start=False, stop=True),
                           reads=[b_idb, b_masks], writes=[b_ps_s[si]], pe_accum=True)
                    op("scalar", lambda e, c=c, si=si, ea=ea: e.activation(out=ea[:, c, :], in_=ps_s[si][:], func=AF.Exp), reads=[b_ps_s[si]], writes=[b_eall[h % 2]])
                po = nxt("o", 2)
                for c in range(nct):
                    op("tensor", lambda e, c=c, ea=ea, po=po: e.matmul(ps_o[po][:, :], lhsT=vcmpa[:, c, 0:128], rhs=ea[:, c, :], start=(c == 0), stop=(c == nct - 1)),
                       reads=[b_vcmpa, b_eall[h % 2]], writes=[b_ps_o[po]], pe_accum=True)
                for sub in range(4):
                    pi = ps_i[sub // 2]
                    for c in range(nct):
                        op("tensor", lambda e, c=c, sub=sub, ea=ea, pi=pi: e.matmul(pi[:, (sub % 2) * 256:(sub % 2) * 256 + 256], lhsT=ea[:, c, sub * 128:(sub + 1) * 128], rhs=wimp[:, c, :],
                                                                               start=(c == 0), stop=(c == nct - 1)),
                           reads=[b_wimp, b_eall[h % 2]], writes=[b_ps_i[sub // 2]], pe_accum=True)
                ri = finish_branch(qt, po, h if h < 2 else None, 0, True)
                for sub in range(4):
                    pi = ps_i[sub // 2]
                    src = pi[:, (sub % 2) * 256:(sub % 2) * 256 + 256]
                    if h == 0:
                        op("vector", lambda e, sub=sub, src=src, ri=ri: e.tensor_scalar(out=impacc[:, sub, 1:257], in0=src, scalar1=rd[:, ri, sub:sub + 1], scalar2=None, op0=ALU.mult),
                           reads=[b_ps_i[sub // 2], b_rd], writes=[b_imp[sub]])
                    else:
                        op("vector", lambda e, sub=sub, src=src, ri=ri: e.scalar_tensor_tensor(out=impacc[:, sub, 1:257], in0=src, scalar=rd[:, ri, sub:sub + 1], in1=impacc[:, sub, 1:257],
                                                                                            op0=ALU.mult, op1=ALU.add),
                           reads=[b_ps_i[sub // 2], b_rd, b_imp[sub]], writes=[b_imp[sub]])

        def topk_part(qt):
            for sub in range(4):
                st = 4 * qt + sub
                sc = impacc[:, sub, 1:257]
                op("vector", lambda e, sub=sub, st=st: e.tensor_tensor(out=impacc[:, sub, 2 * st:2 * st + 3], in0=impacc[:, sub, 2 * st:2 * st + 3], in1=fpat[:, 0:3], op=ALU.add),
                   reads=[b_imp[sub], b_fpat], writes=[b_imp[sub]])
                op("vector", lambda e, sub=sub: e.tensor_scalar(out=impacc[:, sub, 1:2], in0=impacc[:, sub, 1:2], scalar1=4000.0, scalar2=None, op0=ALU.add), reads=[b_imp[sub]], writes=[b_imp[sub]])
                op("vector", lambda e, sc=sc: e.max(out=m8[:, 0, :], in_=sc), reads=[b_imp[sub]], writes=[b_m8])
                op("vector", lambda e, sc=sc: e.match_replace(out=mr[:], in_to_replace=m8[:, 0, :], in_values=sc, imm_value=-1e9), reads=[b_imp[sub], b_m8], writes=[b_mr])
                op("vector", lambda e: e.max(out=m8[:, 1, :], in_=mr[:]), reads=[b_mr], writes=[b_m8])
                op("vector", lambda e: e.tensor_reduce(out=m8[:, 2, 0:1], in_=m8[:, 1, :], axis=AX.X, op=ALU.min), reads=[b_m8], writes=[b_m8])
                op("vector", lambda e, sub=sub, sc=sc: e.tensor_scalar(out=selb[:, sub, 96:352], in0=sc, scalar1=m8[:, 2, 0:1], scalar2=NEG, op0=ALU.is_lt, op1=ALU.mult),
                   reads=[b_imp[sub], b_m8], writes=[b_selb[sub]])

        def qaug_part(qt):
            k = qt % 2
            nv = (qt + 1 + 3) // 4
            for v in range(nv):
                bi = nxt("t", 2)
                pt = ps_t[bi]
                for sub in range(4):
                    op("tensor", lambda e, sub=sub, v=v, pt=pt: e.transpose(out=pt[:, sub * 128:(sub + 1) * 128], in_=selb[:, sub, 32 * v:32 * v + 128], identity=idf[:]),
                       reads=[b_selb[sub], b_idf], writes=[b_ps_t[bi]], pe_accum=True)
                for h in range(2):
                    op("vector", lambda e, h=h, v=v, pt=pt: e.tensor_copy(out=qaug[96:128, h, v, :], in_=pt[96:128, :]),
                       reads=[b_ps_t[bi]], writes=[b_qaug[h][v]])
                    op("gpsimd", lambda e, h=h, v=v: e.tensor_copy(out=qaug[0:96, h, v, :], in_=q_sb[k][:, h, :]), reads=[b_q[k]], writes=[b_qaug[h][v]])

        def win_part(qt):
            k = qt % 2
            tiles = list(range(4, 8)) if qt == 0 else list(range(8))
            items = [(h, n, i) for h in range(2) for n, i in enumerate(tiles)]
            pos = {}
            pend = None

            def pv(item):
                h, n, i = item
                ei, po = pos[item]
                op("tensor", lambda e, i=i, ei=ei, po=po, n=n: e.matmul(ps_o[po][:, :], lhsT=vwa[k][:, i, 0:128], rhs=e_sb[ei][:], start=(n == 0), stop=(n == len(tiles) - 1)),
                   reads=[b_vwa[k], b_e[ei]], writes=[b_ps_o[po]], pe_accum=True)
                if n == len(tiles) - 1:
                    finish_branch(qt, po, h, 2, False)

            po_h = {}
            for item in items:
                h, n, i = item
                if n == 0:
                    po_h[h] = nxt("o", 2)
                si = nxt("s", 2)
                ei = nxt("e", 3)
                pos[item] = (ei, po_h[h])
                op("tensor", lambda e, i=i, si=si, h=h: e.matmul(ps_s[si][:], lhsT=kw_sb[k][:, i * 128:(i + 1) * 128], rhs=q_sb[k][:, h, :], start=True, stop=False),
                   reads=[b_kw[k], b_q[k]], writes=[b_ps_s[si]], pe_accum=True)
                op("tensor", lambda e, i=i, si=si: e.matmul(ps_s[si][:], lhsT=idb[:], rhs=masks[:, i, :], start=False, stop=True),
                   reads=[b_idb, b_masks], writes=[b_ps_s[si]], pe_accum=True)
                op("scalar", lambda e, si=si, ei=ei: e.activation(out=e_sb[ei][:], in_=ps_s[si][:], func=AF.Exp), reads=[b_ps_s[si]], writes=[b_e[ei]])
                if pend is not None:
                    pv(pend)
                pend = item
            pv(pend)

        def slc_part(qt):
            k = qt % 2
            nkt = 4 * (qt + 1)
            items = [(h, kt) for h in range(2) for kt in range(nkt)]
            pos = {}
            pend = None
            po_h = {}

            def pv(item):
                h, kt = item
                ei, po = pos[item]
                op("tensor", lambda e, kt=kt, ei=ei, po=po: e.matmul(ps_o[po][:, :], lhsT=vsa[:, kt, 0:128], rhs=e_sb[ei][:], start=(kt == 0), stop=(kt == nkt - 1)),
                   reads=[b_vsa, b_e[ei]], writes=[b_ps_o[po]], pe_accum=True)
                if kt == nkt - 1:
                    finish_branch(qt, po, h, 1, False)

            for item in items:
                h, kt = item
                if kt == 0:
                    po_h[h] = nxt("o", 2)
                si = nxt("s", 2)
                ei = nxt("e", 3)
                pos[item] = (ei, po_h[h])
                v = kt // 16
                diag = kt >= 4 * qt
                op("tensor", lambda e, kt=kt, si=si, v=v, diag=diag, h=h: e.matmul(ps_s[si][:], lhsT=ksa[:, kt * 128:(kt + 1) * 128], rhs=qaug[:, h, v, :], start=True, stop=not diag),
                   reads=[b_ksa, b_qaug[h][v]], writes=[b_ps_s[si]], pe_accum=True)
                if diag:
                    op("tensor", lambda e, kt=kt, si=si: e.matmul(ps_s[si][:], lhsT=idb[:], rhs=masks[:, 12 + kt - 4 * qt, :], start=False, stop=True),
                       reads=[b_idb, b_masks], writes=[b_ps_s[si]], pe_accum=True)
                op("scalar", lambda e, si=si, ei=ei: e.activation(out=e_sb[ei][:], in_=ps_s[si][:], func=AF.Exp), reads=[b_ps_s[si]], writes=[b_e[ei]])
                if pend is not None:
                    pv(pend)
                pend = item
            pv(pend)

        def out_part(qt):
            k = qt % 2
            tok = slice(qt * 512, (qt + 1) * 512)
            for h in range(2):
                bi = nxt("t", 2)
                pt = ps_t[bi]
                for sub in range(4):
                    op("tensor", lambda e, sub=sub, h=h, pt=pt: e.transpose(out=pt[0:96, sub * 128:(sub + 1) * 128], in_=oacc[k][:, sub, h, :], identity=idf[:]),
                       reads=[b_oacc[k], b_idf], writes=[b_ps_t[bi]], pe_accum=True)
                op("vector", lambda e, h=h, pt=pt: e.tensor_copy(out=oTo[k][:, h, :], in_=pt[0:96, :]), reads=[b_ps_t[bi]], writes=[b_oTo[k]])
            op("sync", lambda e: e.dma_start(out=ysl(64, 256, qt * 512, 512).rearrange("(h d) t -> d h t", d=96), in_=oTo[k][:]), reads=[b_oTo[k]], writes=[b_ymc], dma=True)

        nq = self.nqt if hasattr(self, "nqt") else NQT
        if stage < 4:
            nq = 0
        if nq > 0:
            load_tile(0)
            cmp_part(0)
            if stage >= 4.2:
                topk_part(0)
        for qt in range(nq):
            if stage >= 4.3:
                qaug_part(qt)
            if stage >= 4.4:
                win_part(qt)
            pool_chunk(qt)
            if qt + 1 < nq and stage >= 4.7:
                load_tile(qt + 1)
                cmp_part(qt + 1)
                topk_part(qt + 1)
            if stage >= 4.5:
                slc_part(qt)
            if stage >= 4.6:
                out_part(qt)
            if stage < 4.7:
                break
import ml_dtypes
from concourse.bass_utils import run_bass_kernel_spmd

_BF = ml_dtypes.bfloat16
_NCORE = 8
_GROUPS = [[0, 1, 2, 3], [4, 5, 6, 7]]
_DBG = {}
_CC_CHAIN = Buf("cc_chain")


def _reset_chain():
    _CC_CHAIN.w = None
    _CC_CHAIN.r = {}


def _gather(P, c_ap, b_c, g_ap, b_g):
    if _DBG.get("nocc"):
        rows = c_ap.shape[0]
        for r in range(4):
            P.op("gpsimd", lambda e, r=r: e.dma_start(out=g_ap[r * rows:(r + 1) * rows, :], in_=c_ap), reads=[b_c], writes=[b_g], dma=True)
        return
    P.op("gpsimd", lambda e: e.collective_compute("AllGather", ALU.bypass, replica_groups=_GROUPS, ins=[c_ap], outs=[g_ap]),
         reads=[b_c, _CC_CHAIN], writes=[b_g, _CC_CHAIN], cc=True)


def build_fused():
    nc = bass.Bass("TRN2", target_bir_lowering=False)
    _reset_chain()
    TB.gus_ctr = 0
    P = Prog(nc)
    dram = Dram(nc, P)
    dram.dump = tuple(_DBG.get("dump", ()))
    ident = dram.din("ident", [128, 128], F32)
    xin = dram.din("xin", [NTOK, D], F32)
    xout = dram.dout("xout", [NTOK, D], F32)
    op = P.op

    jcache = {}

    def jexpr(e):
        if "v" not in jcache:
            pid = e.partition_id()
            j = e.snap(pid % 4, min_val=0, max_val=3)
            g = e.snap(j // 2, min_val=0, max_val=1)
            jo = e.snap(g * 2 + (1 - (j % 2)), min_val=0, max_val=3)
            jcache["v"] = dict(j=j, g=g, jo=jo)
        return jcache["v"]

    cur, b_cur = xin, None
    tcount = 0
    ym_loc = b_ym_loc = None
    for l in range(3):
        P.push_scope()
        tb = TB(nc, P, ident, dram)
        if l > 0:
            pl = "l%d" % (l - 1)
            x2 = dram.dscr("x2", [NTOK, D], F32)
            scr2, b_scr2, its2 = tb.prep_for(pl + "_ffn2")

            def hook(t_, its2=its2):
                for it in its2[2 * t_:2 * t_ + 2]:
                    it()
            tcount = tb.stage_mix(pl, ym_loc, b_ym_loc, cur, b_cur, x2, dram.b["x2"], tcount, tile_hook=hook)
            if l == 2:
                tcount = tb.stage_ffn(pl + "_ffn2", x2, dram.b["x2"], xout, dram.b["xout"], tcount, prepped=(scr2, b_scr2))
                P.pop_scope()
                break
            x3 = dram.dscr("x3", [NTOK, D], F32)
            scr1, b_scr1, its1 = tb.prep_for("l%d_ffn1" % l)

            def hook1(t_, its1=its1):
                tb.prep_cast_eng = "gpsimd"
                for it in its1[2 * t_:2 * t_ + 2]:
                    it()
                tb.prep_cast_eng = None
            tcount = tb.stage_ffn(pl + "_ffn2", x2, dram.b["x2"], x3, dram.b["x3"], tcount, prepped=(scr2, b_scr2), tile_hook=hook1)
            cur, b_cur = x3, dram.b["x3"]
            pre1 = (scr1, b_scr1)
        ll = "l%d" % l
        x1 = dram.dscr("x1", [NTOK, D], F32)
        tcount = tb.stage_ffn(ll + "_ffn1", cur, b_cur, x1, dram.b["x1"], tcount, prepped=(pre1 if l > 0 else None))
        cur, b_cur = x1, dram.b["x1"]
        def ctensor(name, shp, dt):
            ap = dram.dscr(name, shp, dt)
            return ap, dram.b[name]
        def gtensor(name, shp, dt):
            if name not in dram.ap:
                dram.ap[name] = nc.dram_tensor(name, list(shp), dt).ap()
                dram.b[name] = P.buf(name)
            return dram.ap[name], dram.b[name]
        C_q, b_Cq = ctensor("c_q", [8, 128, NTOK], BF16)
        C_u, b_Cu = ctensor("c_u", [4, 64, NTOK], F32)
        C_k, b_Ck = ctensor("c_k", [8, 96, NTOK], BF16)
        C_vs, b_Cvs = ctensor("c_vs", [2, NTOK, 96], BF16)
        C_vw, b_Cvw = ctensor("c_vw", [2, NTOK, 96], BF16)
        C_g, b_Cg = ctensor("c_g", [NTOK, 24], F32)
        G_q, b_Gq = gtensor("g_q", [8, 4, 128, NTOK], BF16)
        G_u, b_Gu = gtensor("g_u", [4, 4, 64, NTOK], F32)
        G_k, b_Gk = gtensor("g_k", [8, 4, 96, NTOK], BF16)
        G_vs, b_Gvs = gtensor("g_vs", [2, 4, NTOK, 96], BF16)
        G_vw, b_Gvw = gtensor("g_vw", [2, 4, NTOK, 96], BF16)
        G_g, b_Gg = ctensor("g_g", [4 * NTOK, 24], F32)
        outs = {"gates": (C_g, b_Cg)}
        outs["qT"] = (C_q, b_Cq)
        outs["uT"] = ([C_u[p] for p in range(4)], b_Cu)
        outs["kT"] = (C_k, b_Ck)
        outs["vs"] = (C_vs, b_Cvs)
        outs["vw"] = (C_vw, b_Cvw)
        tcount = tb.stage_proj(ll, cur, b_cur, outs, tcount)
        P.pop_scope()
        if _DBG.get("stop3") and l == 1:
            P.wait_all("sync", list(dram.b.values()))
            P.finish()
            return nc
        if _DBG.get("dump") and l == 0:
            d0 = dram.dscr("dbg_ck0_pre", [96, NTOK], BF16)
            op("sync", lambda e: e.dma_start(out=d0, in_=C_k[0]), reads=[b_Ck], writes=[dram.b["dbg_ck0_pre"]], dma=True)
            dram.outs += [dram.b["dbg_ck0_pre"]]
            P.barrier()
        for i in range(8):
            _gather(P, C_q[i], b_Cq, G_q[i].rearrange("r d t -> (r d) t"), b_Gq)
        for i in range(8):
            _gather(P, C_k[i], b_Ck, G_k[i].rearrange("r d t -> (r d) t"), b_Gk)
        for i in range(4):
            _gather(P, C_u[i], b_Cu, G_u[i].rearrange("r c t -> (r c) t"), b_Gu)
        for i in range(2):
            _gather(P, C_vs[i], b_Cvs, G_vs[i].rearrange("r t d -> (r t) d"), b_Gvs)
            _gather(P, C_vw[i], b_Cvw, G_vw[i].rearrange("r t d -> (r t) d"), b_Gvw)
        _gather(P, C_g, b_Cg, G_g, b_Gg)
        P.barrier()
        if _DBG.get("dump") and l == 0:
            d1 = dram.dscr("dbg_ck0", [96, NTOK], BF16); d2 = dram.dscr("dbg_gk0", [4 * 96, NTOK], BF16)
            op("sync", lambda e: e.dma_start(out=d1, in_=C_k[0]), reads=[b_Ck], writes=[dram.b["dbg_ck0"]], dma=True)
            op("sync", lambda e: e.dma_start(out=d2, in_=G_k[0].rearrange("r d t -> (r d) t")), reads=[b_Gk], writes=[dram.b["dbg_gk0"]], dma=True)
            dram.outs += [dram.b["dbg_ck0"], dram.b["dbg_gk0"]]
        loc = {}

        def mk(name, shp, dt):
            ap = dram.dscr("l_" + name, shp, dt)
            loc[name] = (ap, dram.b["l_" + name])
            return ap, loc[name][1]

        l_qT, b_lq = mk("qT", [4, 96, S], BF16)
        kinds = ["kcT", "vcT", "ksT", "kwT"]
        for name in kinds:
            mk(name, [96, S], BF16)
        mk("uT", [64, S], F32); mk("vs", [S, 96], BF16); mk("vw", [S, 96], BF16)
        gk5 = G_k.rearrange("(k g) r d t -> k g r d t", g=2)
        gq5 = G_q.rearrange("(p h) r d t -> p h r d t", h=2)
        for h2 in range(2):
            def q_own(e, h2=h2):
                return e.dma_start(out=l_qT[h2].rearrange("d (r t) -> d r t", r=4),
                                   in_=gq5[bass.ds(jexpr(e)["j"], 1), h2, :, 0:96, :].rearrange("o r d t -> (o d) r t"))

            def q_oth(e, h2=h2):
                return e.dma_start(out=l_qT[2 + h2].rearrange("d (r t) -> d r t", r=4),
                                   in_=gq5[bass.ds(jexpr(e)["jo"], 1), h2, :, 0:96, :].rearrange("o r d t -> (o d) r t"))
            op("sync", q_own, reads=[b_Gq], writes=[b_lq], dma=True)
            op("sync", q_oth, reads=[b_Gq], writes=[b_lq], dma=True)

        def uf(e):
            return e.dma_start(out=loc["uT"][0].rearrange("c (r t) -> c r t", r=4),
                               in_=G_u[bass.ds(jexpr(e)["j"], 1), :, :, :].rearrange("o r c t -> (o c) r t"))
        op("sync", uf, reads=[b_Gu], writes=[loc["uT"][1]], dma=True)
        for ki, name in enumerate(kinds):
            def kf(e, name=name, ki=ki):
                return e.dma_start(out=loc[name][0].rearrange("d (r t) -> d r t", r=4),
                                   in_=gk5[ki, bass.ds(jexpr(e)["g"], 1), :, :, :].rearrange("o r d t -> (o d) r t"))
            op("sync", kf, reads=[b_Gk], writes=[loc[name][1]], dma=True)
        for name, Gv, b_Gv in (("vs", G_vs, b_Gvs), ("vw", G_vw, b_Gvw)):
            for r in range(4):
                rows = slice(r * NTOK, (r + 1) * NTOK)

                def vf(e, name=name, rows=rows, r=r, Gv=Gv):
                    return e.dma_start(out=loc[name][0][rows, :], in_=Gv[bass.ds(jexpr(e)["g"], 1), r, :, :].rearrange("o t d -> (o t) d"))
                op("sync", vf, reads=[b_Gv], writes=[loc[name][1]], dma=True)
        loc["gates"] = (G_g, b_Gg)
        if _DBG.get("stop1") is not None and _DBG.get("stop1") == l:
            P.wait_all("sync", list(dram.b.values()))
            P.finish()
            return nc
        ymc = dram.dscr("c_ym", [4, 256, NTOK], BF16)
        b_ymc = dram.b["c_ym"]
        P.push_scope()
        bb = BB(nc, P, dram)
        bb.build(ll, loc, ymc, b_ymc)
        P.pop_scope()
        g_ym, b_gym = gtensor("g_ym", [4, 2, 4, 128, NTOK], BF16)
        for tj in range(4):
            for hf in range(2):
                _gather(P, ymc[tj, hf * 128:(hf + 1) * 128, :], b_ymc, g_ym[tj, hf].rearrange("r i t -> (r i) t"), b_gym)
        P.barrier()
        ym_loc = dram.dscr("l_ym", [D, NTOK], BF16)
        b_ym_loc = dram.b["l_ym"]
        for hf in range(2):
            def yf(e, g_ym=g_ym, ym_loc=ym_loc, hf=hf):
                return e.dma_start(out=ym_loc[hf * 512:(hf + 1) * 512, :], in_=g_ym[bass.ds(jexpr(e)["j"], 1), hf, :, :, :].rearrange("o r i t -> (o r i) t"))
            op("sync", yf, reads=[b_gym], writes=[b_ym_loc], dma=True)
        if _DBG.get("stop2") is not None and _DBG.get("stop2") == l:
            P.wait_all("sync", list(dram.b.values()))
            P.finish()
            return nc
    P.wait_all("sync", dram.outs)
    P.finish()
    return nc


def _yperm():
    src_of = lambda r, row: (64 * r + row) if row < 64 else (256 + 192 * r + (row - 64))
    return np.array([src_of(r, hf * 128 + i) for hf in range(2) for r in range(4) for i in range(128)])


_YPERM = _yperm()


def kernel(**inputs):
    inputs = {k: np.asarray(v) for k, v in inputs.items()}
    x = inputs["x"].reshape(2 * S, D).astype(np.float32, copy=False)
    nc = build_fused()
    C = b_consts()
    maps = []
    for c in range(_NCORE):
        b, j = divmod(c, 4)
        m = {"ident": C["ident_f"], "ident_b": C["ident_b"], "masks": C["masks"], "kpat": C["kpat"], "wimp": C["wimp"], "fpat": C["fpat"]}
        a, A = pool_consts(j)
        m["pool_a"] = a
        gs = np.zeros((6, 24), np.float32)
        gs[np.arange(6), 6 * j + np.arange(6)] = 1.0
        m["gsel"] = gs
        m["pool_A"] = A
        m["xin"] = x[c * NTOK:(c + 1) * NTOK]
        for l in range(2):
            p = "l%d" % l
            m[p + "_ffn1_wg"] = inputs["ffn1_w_gate"][l]; m[p + "_ffn1_wu"] = inputs["ffn1_w_up"][l]; m[p + "_ffn1_wd"] = inputs["ffn1_w_down"][l]
            m[p + "_ffn1_lg"] = inputs["ln1_g"][l]; m[p + "_ffn1_lb"] = inputs["ln1_b"][l]
            m[p + "_ffn2_wg"] = inputs["ffn2_w_gate"][l]; m[p + "_ffn2_wu"] = inputs["ffn2_w_up"][l]; m[p + "_ffn2_wd"] = inputs["ffn2_w_down"][l]
            m[p + "_ffn2_lg"] = inputs["ln3_g"][l]; m[p + "_ffn2_lb"] = inputs["ln3_b"][l]
            m[p + "_w_in"] = inputs["w_in"][l]; m[p + "_b_gate"] = inputs["b_gate"][l]
            m[p + "_w_out"] = inputs["w_out"][l][_YPERM]
            m[p + "_ln2_g"] = inputs["ln2_g"][l]; m[p + "_ln2_b"] = inputs["ln2_b"][l]
            m[p + "_cmp_k_w1"] = inputs["cmp_k_w1"][l]; m[p + "_cmp_k_w2"] = inputs["cmp_k_w2"][l]; m[p + "_cmp_k_pos"] = inputs["cmp_pos_k"][l]
            m[p + "_cmp_v_w1"] = inputs["cmp_v_w1"][l]; m[p + "_cmp_v_w2"] = inputs["cmp_v_w2"][l]; m[p + "_cmp_v_pos"] = inputs["cmp_pos_v"][l]
            m[p + "_pool_w"] = inputs["pool_w"][l][j]
            m[p + "_pool_sc"] = inputs["pool_scale"][l][64 * j:64 * j + 64].reshape(64, 1)
        maps.append({k: np.ascontiguousarray(v) for k, v in m.items()})
    res = run_bass_kernel_spmd(nc, maps, core_ids=list(range(_NCORE))).results
    _DBG["res"] = res if _DBG.get("dump") else None
    out = np.concatenate([r["xout"] for r in res], axis=0)
    return out.reshape(2, S, D).astype(np.float32, copy=False)
```
